# Optimizing a Trainium2 kernel written in Bass

```python
import math
import jax, jax.numpy as jnp
from jax import lax
import numpy as np

D_MODEL = 1024
BATCH = 32
SEQ = 256
DEPTH = 4
DEC_BATCH = 2
DEC_SEQ = 1024
PAST_LEN = 512

GRID_W = 64
N_MIXERS = 2
N_RET = (DEPTH + 1) // 2
N_DEL = DEPTH // 2
CHUNK = 64
EPS = 1e-6

RET_HEADS = 4
RET_DK = D_MODEL // RET_HEADS
RET_DV = 2 * RET_DK
RET_QK = RET_HEADS * RET_DK
RET_V = RET_HEADS * RET_DV
RET_IN = 2 * RET_QK + 2 * RET_V
ROPE_BASE = 10000.0

DEL_HEADS = 8
DEL_DK = D_MODEL // DEL_HEADS
DEL_DV = 2 * DEL_DK
DEL_QK = DEL_HEADS * DEL_DK
DEL_V = DEL_HEADS * DEL_DV
DEL_CONV = 2 * DEL_QK + DEL_V
DEL_IN = DEL_CONV + DEL_V + 4 * DEL_HEADS
CONV_W = 3

kernel_name = "bidir_retention_gated_deltanet_prefix_dit"

F32 = jnp.float32


def _rmsnorm_f32(x, w):
    xf = x.astype(F32)
    return xf * lax.rsqrt(jnp.mean(jnp.square(xf), -1, keepdims=True) + EPS) * w.astype(F32)


def _flip(a):
    return jnp.flip(a, axis=1)


def _axial_rope(x):
    B, T, H, dk = x.shape
    rows = T // GRID_W
    r = jnp.broadcast_to(jnp.arange(rows)[:, None], (rows, GRID_W)).reshape(T).astype(F32)
    col = jnp.broadcast_to(jnp.arange(GRID_W)[None, :], (rows, GRID_W)).reshape(T).astype(F32)
    n_pairs = dk // 4
    freqs = ROPE_BASE ** (-jnp.arange(n_pairs, dtype=F32) / n_pairs)
    ang = jnp.concatenate([r[:, None] * freqs, col[:, None] * freqs], -1)
    cos = jnp.cos(ang)[:, None, :]
    sin = jnp.sin(ang)[:, None, :]
    x1, x2 = x[..., : dk // 2], x[..., dk // 2:]
    return jnp.concatenate([x1 * cos - x2 * sin, x1 * sin + x2 * cos], -1)


def _to_chunks(a):
    B, T, H, d = a.shape
    return a.reshape(B, T // CHUNK, CHUNK, H, d).transpose(1, 0, 3, 2, 4)


def _from_chunks(o):
    n, B, H, C, d = o.shape
    return o.transpose(1, 0, 3, 2, 4).reshape(B, n * C, H, d)


def _retention_scan(q, k, v, log_gamma, s0):
    idx = jnp.arange(CHUNK, dtype=F32)
    diff = idx[:, None] - idx[None, :]
    causal = diff >= 0
    lg = log_gamma[:, None, None]
    dmat = jnp.where(causal, jnp.exp(lg * jnp.where(causal, diff, 0.0)), 0.0)
    xi = jnp.exp(log_gamma[:, None] * (idx + 1.0))[..., None]
    zeta = jnp.exp(log_gamma[:, None] * (CHUNK - 1.0 - idx))[..., None]
    chunk_decay = jnp.exp(log_gamma * CHUNK)[:, None, None]

    def step(s, inp):
        qi, ki, vi = inp
        scores = jnp.einsum('bhid,bhjd->bhij', qi, ki) * dmat
        inner = jnp.einsum('bhij,bhjv->bhiv', scores, vi)
        cross = jnp.einsum('bhid,bhdv->bhiv', qi, s) * xi
        s_new = s * chunk_decay + jnp.einsum('bhjd,bhjv->bhdv', ki * zeta, vi)
        return s_new, inner + cross

    s_fin, o = lax.scan(step, s0, (_to_chunks(q), _to_chunks(k), _to_chunks(v)))
    return _from_chunks(o), s_fin


def _gated_delta_scan(q, k, v, beta, g, s0):
    B, T, H, _ = q.shape
    n = T // CHUNK
    bc = beta.reshape(B, n, CHUNK, H).transpose(1, 0, 3, 2)
    gc = g.reshape(B, n, CHUNK, H).transpose(1, 0, 3, 2)
    idx = jnp.arange(CHUNK)
    tril = idx[:, None] >= idx[None, :]
    strict = idx[:, None] > idx[None, :]
    eye = jnp.eye(CHUNK, dtype=F32)

    def step(s, inp):
        qi, ki, vi, bi, gi = inp
        G = jnp.cumsum(gi, axis=-1)
        gdiff = G[..., :, None] - G[..., None, :]
        L = jnp.where(tril, jnp.exp(jnp.where(tril, gdiff, 0.0)), 0.0)
        kk = jnp.einsum('bhid,bhjd->bhij', ki, ki)
        A = jnp.where(strict, bi[..., :, None] * kk * L, 0.0)
        Tm = lax.linalg.triangular_solve(eye + A, jnp.broadcast_to(eye, A.shape),
                                         left_side=True, lower=True, unit_diagonal=True)
        eG = jnp.exp(G)
        u = jnp.einsum('bhij,bhjv->bhiv', Tm, vi * bi[..., None])
        w = jnp.einsum('bhij,bhjd->bhid', Tm, ki * (bi * eG)[..., None])
        v_new = u - jnp.einsum('bhid,bhdv->bhiv', w, s)
        qk = jnp.einsum('bhid,bhjd->bhij', qi, ki) * L
        o = jnp.einsum('bhid,bhdv->bhiv', qi * eG[..., None], s) + jnp.einsum('bhij,bhjv->bhiv', qk, v_new)
        G_last = G[..., -1:]
        s_new = s * jnp.exp(G_last)[..., None] + jnp.einsum(
            'bhjd,bhjv->bhdv', ki * jnp.exp(G_last - G)[..., None], v_new)
        return s_new, o

    s_fin, o = lax.scan(step, s0, (_to_chunks(q), _to_chunks(k), _to_chunks(v), bc, gc))
    return _from_chunks(o), s_fin


def _retention_branch(h, w_in, decay_logit, gn_w, w_out, s0, latent):
    B, T, _ = h.shape
    proj = (h @ w_in).astype(F32)
    q, k, v, z = jnp.split(proj, [RET_QK, 2 * RET_QK, 2 * RET_QK + RET_V], axis=-1)
    q = q.reshape(B, T, RET_HEADS, RET_DK)
    k = k.reshape(B, T, RET_HEADS, RET_DK) * (RET_DK ** -0.5)
    v = v.reshape(B, T, RET_HEADS, RET_DV)
    if latent:
        q, k = _axial_rope(q), _axial_rope(k)
    log_gamma = jax.nn.log_sigmoid(decay_logit.astype(F32))
    s0 = s0.astype(F32)
    o_f, s_f = _retention_scan(q, k, v, log_gamma[0], s0[:, 0])
    o_b, s_b = _retention_scan(_flip(q), _flip(k), _flip(v), log_gamma[1], s0[:, 1])
    o = o_f + _flip(o_b)
    mu = jnp.mean(o, -1, keepdims=True)
    var = jnp.mean(jnp.square(o - mu), -1, keepdims=True)
    o = ((o - mu) * lax.rsqrt(var + EPS)).reshape(B, T, RET_V) * gn_w.astype(F32)
    o = o * jax.nn.silu(z)
    y = o.astype(h.dtype) @ w_out
    return y, jnp.stack([s_f, s_b], axis=1)


def _centred_dwconv(x, w):
    pad = CONV_W // 2
    return lax.conv_general_dilated(x, w[:, None, :].astype(x.dtype), window_strides=(1,),
                                    padding=[(pad, pad)], dimension_numbers=('NWC', 'WIO', 'NWC'),
                                    feature_group_count=x.shape[-1])


def _l2norm(x):
    return x * lax.rsqrt(jnp.sum(jnp.square(x), -1, keepdims=True) + EPS)


def _delta_branch(h, w_in, conv_w, a_log, dt_bias, norm_w, w_out, s0):
    B, T, _ = h.shape
    proj = (h @ w_in).astype(F32)
    qkv, z, a, b = jnp.split(proj, [DEL_CONV, DEL_CONV + DEL_V, DEL_CONV + DEL_V + 2 * DEL_HEADS], axis=-1)
    qkv = jax.nn.silu(_centred_dwconv(qkv, conv_w.astype(F32)))
    q, k, v = jnp.split(qkv, [DEL_QK, 2 * DEL_QK], axis=-1)
    q = _l2norm(q.reshape(B, T, DEL_HEADS, DEL_DK)) * (DEL_DK ** -0.5)
    k = _l2norm(k.reshape(B, T, DEL_HEADS, DEL_DK))
    v = v.reshape(B, T, DEL_HEADS, DEL_DV)
    beta = jax.nn.sigmoid(b.reshape(B, T, 2, DEL_HEADS))
    g = -jnp.exp(a_log.astype(F32)) * jax.nn.softplus(a.reshape(B, T, 2, DEL_HEADS) + dt_bias.astype(F32))
    s0 = s0.astype(F32)
    o_f, s_f = _gated_delta_scan(q, k, v, beta[:, :, 0], g[:, :, 0], s0[:, 0])
    o_b, s_b = _gated_delta_scan(_flip(q), _flip(k), _flip(v), _flip(beta[:, :, 1]), _flip(g[:, :, 1]), s0[:, 1])
    o = o_f + _flip(o_b)
    o = o * lax.rsqrt(jnp.mean(jnp.square(o), -1, keepdims=True) + EPS)
    o = o.reshape(B, T, DEL_V) * norm_w.astype(F32) * jax.nn.silu(z)
    y = o.astype(h.dtype) @ w_out
    return y, jnp.stack([s_f, s_b], axis=1)


def _trunk(x, cond, s_ret, s_del, latent, norm_w, mod_w, mod_b, ret_w_in, ret_decay, ret_gn_w, ret_w_out,
           del_w_in, del_conv_w, del_a_log, del_dt_bias, del_norm_w, del_w_out, final_norm_w):
    sc = jax.nn.silu(cond.astype(F32))
    ret_states, del_states = [], []
    for i in range(DEPTH):
        mod = sc @ mod_w[i].astype(F32) + mod_b[i].astype(F32)
        shift, scale, gate = [m[:, None, :] for m in jnp.split(mod, 3, axis=-1)]
        h = (_rmsnorm_f32(x, norm_w[i]) * (1.0 + scale) + shift).astype(x.dtype)
        j = i // N_MIXERS
        if i % N_MIXERS == 0:
            y, s = _retention_branch(h, ret_w_in[j], ret_decay[j], ret_gn_w[j], ret_w_out[j], s_ret[:, j], latent)
            ret_states.append(s)
        else:
            y, s = _delta_branch(h, del_w_in[j], del_conv_w[j], del_a_log[j], del_dt_bias[j],
                                 del_norm_w[j], del_w_out[j], s_del[:, j])
            del_states.append(s)
        x = x + (gate * y.astype(F32)).astype(x.dtype)
    y_out = _rmsnorm_f32(x, final_norm_w).astype(x.dtype)
    if latent:
        return y_out, None, None
    return y_out, jnp.stack(ret_states, axis=1), jnp.stack(del_states, axis=1)


def setup_inputs(seed: int = 0) -> dict:
    key = jax.random.key(seed)
    ks = jax.random.split(key, 24)
    nrm = jax.random.normal
    D = D_MODEL
    gam = 1.0 - 2.0 ** (-5.0 - np.arange(RET_HEADS))
    gam_logit = jnp.asarray(np.log(gam / (1.0 - gam)), dtype=F32)
    dt = jnp.exp(jax.random.uniform(ks[17], (N_DEL, 2, DEL_HEADS), F32, math.log(1e-3), math.log(1e-1)))
    return {
        "x_prompt": nrm(ks[0], (BATCH, SEQ, D), F32),
        "x_sample": nrm(ks[1], (DEC_BATCH, DEC_SEQ, D), F32),
        "state_ret": 0.5 * nrm(ks[2], (DEC_BATCH, N_RET, 2, RET_HEADS, RET_DK, RET_DV), F32),
        "state_delta": (DEL_DK ** -0.5) * nrm(ks[3], (DEC_BATCH, N_DEL, 2, DEL_HEADS, DEL_DK, DEL_DV), F32),
        "c": nrm(ks[4], (DEC_BATCH, D), F32),
        "c_ctx": nrm(ks[5], (D,), F32),
        "norm_w": 1.0 + 0.02 * nrm(ks[6], (DEPTH, D), F32),
        "mod_w": (D ** -0.5) * nrm(ks[7], (DEPTH, D, 3 * D), F32),
        "mod_b": 0.02 * nrm(ks[8], (DEPTH, 3 * D), F32),
        "ret_w_in": (D ** -0.5) * nrm(ks[9], (N_RET, D, RET_IN), F32),
        "ret_decay": gam_logit[None, None, :] + 0.1 * nrm(ks[10], (N_RET, 2, RET_HEADS), F32),
        "ret_gn_w": 1.0 + 0.02 * nrm(ks[11], (N_RET, RET_V), F32),
        "ret_w_out": (RET_V ** -0.5) * nrm(ks[12], (N_RET, RET_V, D), F32),
        "del_w_in": (D ** -0.5) * nrm(ks[13], (N_DEL, D, DEL_IN), F32),
        "del_conv_w": (CONV_W ** -0.5) * nrm(ks[14], (N_DEL, CONV_W, DEL_CONV), F32),
        "del_a_log": jnp.log(jax.random.uniform(ks[15], (N_DEL, 2, DEL_HEADS), F32, 1.0, 16.0)),
        "del_dt_bias": dt + jnp.log(-jnp.expm1(-dt)),
        "del_norm_w": 1.0 + 0.02 * nrm(ks[18], (N_DEL, DEL_V), F32),
        "del_w_out": (DEL_V ** -0.5) * nrm(ks[19], (N_DEL, DEL_V, D), F32),
        "final_norm_w": 1.0 + 0.02 * nrm(ks[20], (D,), F32),
    }


def reference(x_prompt, x_sample, state_ret, state_delta, c, c_ctx, norm_w, mod_w, mod_b, ret_w_in, ret_decay,
              ret_gn_w, ret_w_out, del_w_in, del_conv_w, del_a_log, del_dt_bias, del_norm_w, del_w_out,
              final_norm_w):
    B = x_prompt.shape[0]
    zero_ret = jnp.zeros((B, N_RET, 2, RET_HEADS, RET_DK, RET_DV), F32)
    zero_del = jnp.zeros((B, N_DEL, 2, DEL_HEADS, DEL_DK, DEL_DV), F32)
    y_prompt, new_state_ret, new_state_delta = _trunk(
        x_prompt, c_ctx[None, :], zero_ret, zero_del, False, norm_w, mod_w, mod_b, ret_w_in, ret_decay,
        ret_gn_w, ret_w_out, del_w_in, del_conv_w, del_a_log, del_dt_bias, del_norm_w, del_w_out, final_norm_w)
    y_sample, _, _ = _trunk(
        x_sample, c, state_ret, state_delta, True, norm_w, mod_w, mod_b, ret_w_in, ret_decay,
        ret_gn_w, ret_w_out, del_w_in, del_conv_w, del_a_log, del_dt_bias, del_norm_w, del_w_out, final_norm_w)
    return (y_prompt, y_sample, new_state_ret, new_state_delta)
```

```python
import contextlib
import numpy as np
import concourse.bass as bass
import concourse.mybir as mybir
from concourse.bass_utils import run_bass_kernel_spmd

F32 = mybir.dt.float32
BF16 = mybir.dt.bfloat16
AF = mybir.ActivationFunctionType
ALU = mybir.AluOpType

D = 1024
NSEG = 5
SEGL = 256
NTOK = NSEG * SEGL
NT = NTOK // 128
DEPTH = 4
EPS = 1e-6
BLOCKS = [(0, 512), (512, 512), (1024, 256)]
import os
EPOCH = int(os.environ.get("K_EPOCH", "2000"))
BIG = 30000.0


class Buf:
    __slots__ = ("name", "lw", "rd", "dsem", "dcnt")

    def __init__(self, name):
        self.name = name
        self.lw = None
        self.rd = {}
        self.dsem = None
        self.dcnt = 0


class Sched:
    def __init__(self, nc, stack):
        self.nc = nc
        self.stack = stack
        self.engs = {"pe": nc.tensor, "act": nc.scalar, "dve": nc.vector, "pool": nc.gpsimd, "sp": nc.sync}
        self.cnt = {e: 0 for e in self.engs}
        self.esems = {e: [] for e in self.engs}
        self.known = {e: {} for e in self.engs}
        self.dsems = {}
        self.dma_bufs = []
        self.nsem = 0
        self.ninstr = 0

    def _newsem(self, name):
        self.nsem += 1
        return self.stack.enter_context(self.nc.semaphore(f"{name}_{self.nsem}"))

    def _esem(self, e, epoch):
        lst = self.esems[e]
        while len(lst) <= epoch:
            lst.append(self._newsem(f"s_{e}_{len(lst)}"))
        return lst[epoch]

    def _need(self, E, ev, raw):
        key, val = ev
        if key == ("e", E) and not raw:
            return
        kn = self.known[E]
        if kn.get(key, 0) >= val:
            return
        kn[key] = val
        eng = self.engs[E]
        if key[0] == "e":
            n = val - 1
            eng.wait_ge(self._esem(key[1], n // EPOCH), n % EPOCH + 1)
        else:
            eng.wait_ge(self.dsems[key], val)
        self.ninstr += 1

    def _deps(self, E, r, w):
        for b in r:
            if b.lw is not None:
                self._need(E, b.lw, True)
        for b in w:
            if b.lw is not None:
                self._need(E, b.lw, False)
            for k, v in b.rd.items():
                self._need(E, (k, v), False)

    def _post(self, ev, r, w):
        k, v = ev
        for b in r:
            if b.rd.get(k, 0) < v:
                b.rd[k] = v
        for b in w:
            b.lw = ev
            b.rd = {}

    def op(self, E, fn, r=(), w=()):
        self._deps(E, r, w)
        ins = fn(self.engs[E])
        n = self.cnt[E]
        ins.then_inc(self._esem(E, n // EPOCH), 1)
        self.cnt[E] = n + 1
        self.ninstr += 1
        self._post((("e", E), n + 1), r, w)
        return ins

    def dma(self, Q, out, in_, r=(), w=(), **kw):
        self._deps(Q, r, w)
        b0 = (list(w) + list(r))[0]
        if b0.dsem is None:
            b0.dsem = self._newsem("d_" + b0.name)
            self.dsems[("d", id(b0))] = b0.dsem
            self.dma_bufs.append(b0)
        ins = self.engs[Q].dma_start(out=out, in_=in_, **kw)
        ins.then_inc(b0.dsem, 16)
        b0.dcnt += 16
        self.ninstr += 1
        self._post((("d", id(b0)), b0.dcnt), r, w)

    def barrier(self):
        for E in self.engs:
            for e2 in self.engs:
                if self.cnt[e2] > 0:
                    self._need(E, (("e", e2), self.cnt[e2]), True)
            for b in self.dma_bufs:
                self._need(E, (("d", id(b)), b.dcnt), True)


def build(depth=DEPTH, dbg=None):
    nc = bass.Bass("TRN2", target_bir_lowering=False)
    dbg = dbg or {}

    def din(name, shape):
        return nc.dram_tensor(name, list(shape), F32, kind="ExternalInput").ap()

    def dout(name, shape):
        return nc.dram_tensor(name, list(shape), F32, kind="ExternalOutput").ap()

    x_d = din("x", [NTOK, D])
    cond_d = din("condT", [128, 40])
    chain_d = din("chainb", [128, 8])
    normw_d = din("normw", [128, 40])
    modb_d = din("modb", [128, 96])
    modw_d = din("mod_w", [4, D, 3 * D])
    rwin_d = din("ret_w_in", [2, D, 6144])
    rwout_d = din("ret_w_out", [2, 2048, D])
    dwin_d = din("del_w_in", [2, D, 6176])
    dwout_d = din("del_w_out", [2, 2048, D])
    rdec_d = din("rdecb", [128, 16])
    gnw_d = din("gnwb", [128, 4096])
    dnw_d = din("dnwb", [128, 4096])
    convw_d = din("convw", [128, 192])
    alog_d = din("alogb", [128, 32])
    dtb_d = din("dtbb", [128, 32])
    s0r_d = din("s0r", [2, 2, 4, 256, 512])
    s0d_d = din("s0d", [2, 2, 8, 128, 256])
    cos_d = din("cosT", [128, NTOK])
    sin_d = din("sinT", [128, NTOK])
    bmask_d = din("bmask", [128, 512])
    y_d = dout("y", [NTOK, D])
    nsr_d = dout("nsr", [NSEG, 2, 2, 4, 256, 512])
    nsd_d = dout("nsd", [NSEG, 2, 2, 8, 128, 256])
    dbg_d = {k: dout("dbg_" + k, shp) for k, shp in dbg.items()}

    with contextlib.ExitStack() as stack:
        S = Sched(nc, stack)

        sb_n = [0]

        def sb(name, shape, dt=F32, st=None):
            sb_n[0] += 1
            return (st or stack).enter_context(nc.sbuf_tensor(f"sb{sb_n[0]}_{name}", list(shape), dt))

        banks = [stack.enter_context(nc.psum_tensor(f"bank{i}", [128, 512], F32)) for i in range(8)]
        bank_bufs = [Buf(f"bank{i}") for i in range(8)]
        bank_rr = [0]

        def psum():
            i = bank_rr[0] % 8
            bank_rr[0] += 1
            return banks[i], bank_bufs[i]

        xT = sb("xT", [128, 8, NTOK])
        xT_b = [Buf(f"xT{s}") for s in range(NSEG)]
        hT = sb("hT", [128, 8, NTOK], BF16)
        hT_b = [Buf(f"hT{s}") for s in range(NSEG)]
        NSLAB = 3
        ring = [sb(f"slab{i}", [128, 4096], BF16) for i in range(NSLAB)]
        ring_b = [Buf(f"slab{i}") for i in range(NSLAB)]
        ring_rr = [0]

        ident_f = sb("ident_f", [128, 128])
        ident_b = sb("ident_b", [128, 128], BF16)
        ones_f = sb("ones_f", [128, 128])
        ones_b = sb("ones_b", [128, 128], BF16)
        cst_b = Buf("consts")
        cpar = sb("cpar", [128, 8])
        prm = sb("prm", [128, 40 + 8 + 40 + 96 + 16 + 192 + 32 + 32])
        prm_b = Buf("prm")
        o_cond, o_chain, o_normw, o_modb, o_rdec, o_convw, o_alog, o_dtb = 0, 40, 48, 88, 184, 200, 392, 424
        modT = sb("modT", [128, 4 * 3 * 40])
        modT_b = Buf("modT")
        sc1 = sb("sc1", [128, 4 * 40])
        scT = sb("scT", [128, 8, NSEG], BF16)
        sqs = [sb("sq0", [128, 8, 256])]
        sqs_b = [Buf("sq0")]
        rstd = [sb("rstd0", [128, 256])]
        rstd_b = [Buf("rstd0")]
        stg = [sb(f"stg{i}", [128, 1024]) for i in range(2)]
        stg_b = [Buf(f"stg{i}") for i in range(2)]
        stg_rr = [0]

        def stage():
            i = stg_rr[0] % 2
            stg_rr[0] += 1
            return stg[i], stg_b[i]

        def chain_ap(s):
            return prm[:, o_chain + s:o_chain + s + 1]

        S.op("pool", lambda e: e.memset(ident_f[:], 1.0), w=[cst_b])
        S.op("pool", lambda e: e.affine_select(out=ident_f[:], in_=ident_f[:], pattern=[[-1, 128]],
                                                compare_op=ALU.is_equal, fill=0.0, base=0, channel_multiplier=1),
             r=[cst_b], w=[cst_b])
        S.op("pool", lambda e: e.tensor_copy(out=ident_b[:], in_=ident_f[:]), r=[cst_b], w=[cst_b])
        S.op("pool", lambda e: e.memset(ones_f[:], 1.0), w=[cst_b])
        S.op("pool", lambda e: e.memset(ones_b[:], 1.0), w=[cst_b])
        S.op("pool", lambda e: e.memset(cpar[:, 0:1], 1024.0 * EPS), w=[cst_b])
        S.op("pool", lambda e: e.memset(cpar[:, 1:2], EPS), w=[cst_b])
        S.op("pool", lambda e: e.memset(cpar[:, 2:3], 1.0), w=[cst_b])
        S.op("pool", lambda e: e.memset(cpar[:, 3:4], 0.0), w=[cst_b])

        for off, n, src in ((o_cond, 40, cond_d), (o_chain, 8, chain_d), (o_normw, 40, normw_d), (o_modb, 96, modb_d),
                            (o_rdec, 16, rdec_d), (o_convw, 192, convw_d), (o_alog, 32, alog_d), (o_dtb, 32, dtb_d)):
            S.dma("sp", prm[:, off:off + n], src, w=[prm_b])

        def v8(t, n=512):
            return t[:, 0:8 * n].rearrange("p (kt c) -> p kt c", kt=8)

        def win_view(w2d):
            return w2d.rearrange("(kt p) c -> p kt c", p=128)

        wspecs = []
        for l in range(depth):
            for part in range(3):
                for hf in range(2):
                    c0 = part * 1024 + hf * 512
                    wspecs.append(lambda t, l=l, c0=c0: [(v8(t), win_view(modw_d[l])[:, :, c0:c0 + 512])])
        for l in range(depth):
            j = l // 2
            if l % 2 == 0:
                wv = win_view(rwin_d[j])
                for h in range(4):
                    wspecs.append(lambda t, h=h, wv=wv: [
                        (v8(t)[:, :, 0:256], wv[:, :, h * 256:(h + 1) * 256]),
                        (v8(t)[:, :, 256:512], wv[:, :, 1024 + h * 256:1024 + (h + 1) * 256])])
                    wspecs.append(lambda t, h=h, wv=wv: [(v8(t), wv[:, :, 2048 + h * 512:2048 + (h + 1) * 512])])
                    wspecs.append(lambda t, h=h, wv=wv: [(v8(t), wv[:, :, 4096 + h * 512:4096 + (h + 1) * 512])])
                    wspecs.append(lambda t, h=h, j=j: [
                        (t[:, :].rearrange("p (v c) -> p v c", v=4),
                         rwout_d[j][h * 512:(h + 1) * 512, :].rearrange("(v p) c -> p v c", p=128))])
            else:
                wv = win_view(dwin_d[j])
                wspecs.append(lambda t, wv=wv: [(v8(t, 32), wv[:, :, 6144:6176])])
                for h in range(8):
                    wspecs.append(lambda t, h=h, wv=wv: [
                        (v8(t)[:, :, 0:128], wv[:, :, h * 128:(h + 1) * 128]),
                        (v8(t)[:, :, 128:256], wv[:, :, 1024 + h * 128:1024 + (h + 1) * 128]),
                        (v8(t)[:, :, 256:512], wv[:, :, 2048 + h * 256:2048 + (h + 1) * 256])])
                    wspecs.append(lambda t, h=h, wv=wv: [(v8(t, 256), wv[:, :, 4096 + h * 256:4096 + (h + 1) * 256])])
                    wspecs.append(lambda t, h=h, j=j: [
                        (t[:, 0:2048].rearrange("p (v c) -> p v c", v=2),
                         dwout_d[j][h * 256:(h + 1) * 256, :].rearrange("(v p) c -> p v c", p=128))])
        ws = {"issued": 0, "released": 0, "got": 0}

        def ws_issue():
            while ws["issued"] < len(wspecs) and ws["issued"] < ws["released"] + NSLAB:
                i = ws["issued"]
                for (dst_view, src_ap) in wspecs[i](ring[i % NSLAB]):
                    S.dma("pool", dst_view, src_ap, w=[ring_b[i % NSLAB]])
                ws["issued"] += 1

        def ws_get():
            ws_issue()
            i = ws["got"]
            assert i < ws["issued"], "weight ring over-subscribed"
            ws["got"] += 1
            return ring[i % NSLAB], ring_b[i % NSLAB]

        def ws_release():
            ws["released"] += 1
            ws_issue()

        for tt in range(NT):
            st_t, st_b = stage()
            S.dma("sp", st_t[:, :], x_d[tt * 128:(tt + 1) * 128, :], w=[st_b])
            for half in range(2):
                bk, bb = psum()
                for q in range(4):
                    dt = half * 4 + q
                    S.op("pe", lambda e, bk=bk, q=q, dt=dt, st_t=st_t: e.transpose(
                        bk[:, q * 128:(q + 1) * 128], st_t[:, dt * 128:(dt + 1) * 128], ident_f[:]),
                        r=[st_b, cst_b], w=[bb])
                eng = "dve" if half == 0 else "act"
                dst = xT[:, half * 4:half * 4 + 4, tt * 128:(tt + 1) * 128]
                src = bk[:, :].rearrange("p (q t) -> p q t", q=4)
                if eng == "dve":
                    S.op("dve", lambda e, dst=dst, src=src: e.tensor_copy(out=dst, in_=src), r=[bb], w=[xT_b[tt // 2]])
                else:
                    S.op("act", lambda e, dst=dst, src=src: e.copy(out=dst, in_=src), r=[bb], w=[xT_b[tt // 2]])

        S.op("act", lambda e: e.activation(out=scT[:].rearrange("p a b -> p (a b)"), in_=prm[:, o_cond:o_cond + 40],
                                           func=AF.Silu), r=[prm_b], w=[modT_b])
        for l in range(depth):
            for part in range(3):
                for hf in range(2):
                    c0 = part * 1024 + hf * 512
                    sl, slb = ws_get()
                    slv = v8(sl)
                    bk, bb = psum()
                    for ct in range(4):
                        for kt in range(8):
                            S.op("pe", lambda e, bk=bk, ct=ct, kt=kt, slv=slv: e.matmul(
                                bk[:, ct * 5:ct * 5 + 5], lhsT=slv[:, kt, ct * 128:(ct + 1) * 128], rhs=scT[:, kt, :],
                                start=(kt == 0), stop=(kt == 7)), r=[slb, modT_b], w=[bb])
                    base = (l * 3 + part) * 40 + hf * 20
                    mb = prm[:, o_modb + (l * 3 + part) * 8 + hf * 4: o_modb + (l * 3 + part) * 8 + hf * 4 + 4]
                    S.op("dve", lambda e, bk=bk, base=base, mb=mb: e.tensor_tensor(
                        out=modT[:, base:base + 20].rearrange("p (a b) -> p a b", a=4),
                        in0=bk[:, 0:20].rearrange("p (a b) -> p a b", a=4),
                        in1=mb.unsqueeze(2).to_broadcast([128, 4, 5]), op=ALU.add), r=[bb, prm_b], w=[modT_b])
                    ws_release()
            sv = modT[:, (l * 3 + 1) * 40:(l * 3 + 1) * 40 + 40]
            S.op("dve", lambda e, l=l, sv=sv: e.tensor_scalar(out=sc1[:, l * 40:(l + 1) * 40], in0=sv, scalar1=1.0,
                                                             scalar2=32.0, op0=ALU.add, op1=ALU.mult),
                 r=[modT_b], w=[modT_b])
            S.op("dve", lambda e, l=l: e.tensor_tensor(
                out=sc1[:, l * 40:(l + 1) * 40].rearrange("p (a b) -> p a b", a=8),
                in0=sc1[:, l * 40:(l + 1) * 40].rearrange("p (a b) -> p a b", a=8),
                in1=prm[:, o_normw + l * 8:o_normw + l * 8 + 8].unsqueeze(2).to_broadcast([128, 8, 5]), op=ALU.mult),
                r=[modT_b, prm_b], w=[modT_b])

        def mod_ap(l, part, dt, s):
            o = (l * 3 + part) * 40 + dt * 5 + s
            return modT[:, o:o + 1]

        nrm_rr = [0]

        def rms_stat(s):
            i = 0
            sq, sqb, rs, rsb = sqs[i], sqs_b[i], rstd[i], rstd_b[i]
            seg = slice(s * SEGL, (s + 1) * SEGL)
            S.op("act", lambda e: e.activation(out=sq[:, :, :], in_=xT[:, :, seg], func=AF.Square), r=[xT_b[s]], w=[sqb])
            bk, bb = psum()
            for dt in range(8):
                S.op("pe", lambda e, dt=dt: e.matmul(bk[:, 0:256], lhsT=ones_f[:], rhs=sq[:, dt, :],
                                                      start=(dt == 0), stop=(dt == 7)), r=[sqb, cst_b], w=[bb])
            S.op("act", lambda e: e.activation(out=rs[:, :], in_=bk[:, 0:256], func=AF.Ln, bias=cpar[:, 0:1], scale=1.0),
                 r=[bb, cst_b], w=[rsb])
            S.op("act", lambda e: e.activation(out=rs[:, :], in_=rs[:, :], func=AF.Exp, scale=-0.5), r=[rsb], w=[rsb])
            return sq, sqb, rs, rsb

        def norm_mod(l):
            for s in range(NSEG):
                sq, sqb, rs, rsb = rms_stat(s)
                seg = slice(s * SEGL, (s + 1) * SEGL)
                S.op("dve", lambda e: e.tensor_tensor(out=sq[:, :, :], in0=xT[:, :, seg],
                                                      in1=rs[:, :].unsqueeze(1).to_broadcast([128, 8, 256]),
                                                      op=ALU.mult), r=[xT_b[s], rsb], w=[sqb])
                for dt in range(8):
                    o = l * 40 + dt * 5 + s
                    S.op("act", lambda e, dt=dt, o=o: e.activation(
                        out=hT[:, dt, seg], in_=sq[:, dt, :], func=AF.Identity, bias=mod_ap(l, 0, dt, s),
                        scale=sc1[:, o:o + 1]), r=[sqb, modT_b], w=[hT_b[s]])

        def segs_of(t0, n):
            return list(range(t0 // SEGL, (t0 + n) // SEGL))

        def out_proj(l, sO, sOb, nvt, oT, oT_b):
            sOv = sO[:, 0:nvt * 1024].rearrange("p (v c) -> p v c", v=nvt)
            for (t0, n) in BLOCKS:
                sg = segs_of(t0, n)
                for dt in range(8):
                    bk, bb = psum()
                    for vt in range(nvt):
                        S.op("pe", lambda e, vt=vt, dt=dt, bk=bk: e.matmul(
                            bk[:, 0:n], lhsT=sOv[:, vt, dt * 128:(dt + 1) * 128], rhs=oT[:, vt, t0:t0 + n],
                            start=(vt == 0), stop=(vt == nvt - 1)), r=[sOb] + [oT_b[s] for s in sg], w=[bb])
                    for s in sg:
                        lo = s * SEGL - t0
                        seg = slice(s * SEGL, (s + 1) * SEGL)
                        S.op("dve", lambda e, bk=bk, lo=lo, seg=seg, dt=dt, s=s: e.scalar_tensor_tensor(
                            out=xT[:, dt, seg], in0=bk[:, lo:lo + SEGL], scalar=mod_ap(l, 2, dt, s), in1=xT[:, dt, seg],
                            op0=ALU.mult, op1=ALU.add), r=[bb, modT_b, xT_b[s]], w=[xT_b[s]])

        def ret_layer(l, j):
            with contextlib.ExitStack() as ls:
                cosT = sb("cosT", [128, NTOK], st=ls)
                sinT = sb("sinT", [128, NTOK], st=ls)
                rope_b = Buf("rope")
                S.dma("sp", cosT[:, :], cos_d, w=[rope_b])
                S.dma("sp", sinT[:, :], sin_d, w=[rope_b])
                lc = sb("lc", [128, 64], st=ls)
                lc_b = Buf("lc")
                iot = sb("iot", [128, 128], st=ls)
                iop = sb("iop", [128, 2], st=ls)
                ioi = sb("ioi", [128, 2, 128], st=ls)
                Mh = sb("Mh", [128, 4, 128], st=ls)
                Xi = sb("Xi", [128, 8, 128], BF16, st=ls)
                tmpm = sb("tmpm", [128, 2, 128], st=ls)
                gnw = sb("gnw", [128, 512], st=ls)
                gnw_b = Buf("gnw")
                qT = sb("qT", [128, 2, NTOK], BF16, st=ls)
                kT = sb("kT", [128, 2, NTOK], BF16, st=ls)
                qk_b = [Buf(f"qk{s}") for s in range(NSEG)]
                sq, sqb = sqs[0], sqs_b[0]
                kzf = sb("kzf", [128, NT, 256], BF16, st=ls)
                kzb = sb("kzb", [128, NT, 256], BF16, st=ls)
                kz_b = [Buf(f"kz{c}") for c in range(NT)]
                v16 = sb("v16", [128, NT, 512], BF16, st=ls)
                v_b = [Buf(f"v{c}") for c in range(NT)]
                sm = sb("sm", [128, NT, 128], BF16, st=ls)
                sm_b = [Buf(f"sm{c}") for c in range(NT)]
                qx = [sb(f"qx{i}", [128, 2, 2, 128], BF16, st=ls) for i in range(2)]
                qx_b = [Buf(f"qx{i}") for i in range(2)]
                zs = [sb(f"zs{i}", [128, 512], BF16, st=ls) for i in range(2)]
                zs_b = [Buf(f"zs{i}") for i in range(2)]
                Sb16 = sb("Sb16", [128, NT, 2, 512], BF16, st=ls)
                Sb16_b = [Buf(f"Sb16_{c}") for c in range(NT)]
                S32 = [sb(f"S32_{d}", [128, 2, 512], st=ls) for d in range(2)]
                S32_b = [Buf(f"S32_{d}") for d in range(2)]
                S32alt = [None, sb("S32_1b", [128, 2, 512], st=ls)]
                S32alt_b = [None, Buf("S32_1b")]
                S16f = sb("S16f", [128, 2, 512], BF16, st=ls)
                S16f_b = Buf("S16f")
                oT = [sb(f"oT{i}", [128, 4, SEGL], BF16, st=ls) for i in range(2)]
                oT_b = [Buf(f"oT{i}") for i in range(2)]
                bst = sb("bst", [128, 16], st=ls)
                bst_b = Buf("bst")
                on = sb("on", [128, 512], st=ls)
                on_b = Buf("on")
                og = sb("og", [128, 512], BF16, st=ls)
                og_b = Buf("og")

                dec = prm[:, o_rdec + j * 8:o_rdec + j * 8 + 8]
                LG, GC, ZF = lc[:, 0:8], lc[:, 8:16], lc[:, 16:24]
                S.op("act", lambda e: e.activation(out=lc[:, 24:32], in_=dec, func=AF.Exp, scale=-1.0), r=[prm_b], w=[lc_b])
                S.op("act", lambda e: e.activation(out=LG, in_=lc[:, 24:32], func=AF.Ln, bias=cpar[:, 2:3], scale=1.0),
                     r=[lc_b, cst_b], w=[lc_b])
                S.op("dve", lambda e: e.tensor_scalar(out=LG, in0=LG, scalar1=-1.0, scalar2=None, op0=ALU.mult),
                     r=[lc_b], w=[lc_b])
                S.op("act", lambda e: e.activation(out=GC, in_=LG, func=AF.Exp, scale=128.0), r=[lc_b], w=[lc_b])
                S.op("pool", lambda e: e.iota(iot[:, :], pattern=[[1, 128]], base=0, channel_multiplier=-1,
                                               allow_small_or_imprecise_dtypes=True), r=[lc_b], w=[lc_b])
                S.op("pool", lambda e: e.iota(iop[:, 0:1], pattern=[[0, 1]], base=127, channel_multiplier=-1,
                                               allow_small_or_imprecise_dtypes=True), r=[lc_b], w=[lc_b])
                S.op("pool", lambda e: e.iota(iop[:, 1:2], pattern=[[0, 1]], base=0, channel_multiplier=1,
                                               allow_small_or_imprecise_dtypes=True), r=[lc_b], w=[lc_b])
                S.op("pool", lambda e: e.iota(ioi[:, 0, :], pattern=[[1, 128]], base=1, channel_multiplier=0,
                                               allow_small_or_imprecise_dtypes=True), r=[lc_b], w=[lc_b])
                S.op("pool", lambda e: e.iota(ioi[:, 1, :], pattern=[[-1, 128]], base=128, channel_multiplier=0,
                                               allow_small_or_imprecise_dtypes=True), r=[lc_b], w=[lc_b])
                for d in range(2):
                    for h in range(4):
                        c = d * 4 + h
                        S.op("act", lambda e, c=c, d=d: e.activation(out=ZF[:, c:c + 1], in_=iop[:, d:d + 1], func=AF.Exp,
                                                                      scale=LG[:, c:c + 1]), r=[lc_b], w=[lc_b])
                S.op("dve", lambda e: e.tensor_scalar(out=ZF, in0=ZF, scalar1=1.0 / 16.0, scalar2=None, op0=ALU.mult),
                     r=[lc_b], w=[lc_b])
                for h in range(4):
                    S.op("dve", lambda e: e.tensor_scalar(out=tmpm[:, 0, :], in0=iot[:, :], scalar1=0.0, scalar2=None,
                                                          op0=ALU.max), r=[lc_b], w=[lc_b])
                    S.op("act", lambda e, h=h: e.activation(out=tmpm[:, 0, :], in_=tmpm[:, 0, :], func=AF.Exp,
                                                             scale=LG[:, h:h + 1]), r=[lc_b], w=[lc_b])
                    S.op("pool", lambda e: e.affine_select(out=tmpm[:, 0, :], in_=tmpm[:, 0, :], pattern=[[1, 128]],
                                                            compare_op=ALU.is_ge, fill=0.0, base=0, channel_multiplier=-1),
                         r=[lc_b], w=[lc_b])
                    S.op("dve", lambda e: e.tensor_scalar(out=tmpm[:, 1, :], in0=iot[:, :], scalar1=-1.0, scalar2=0.0,
                                                          op0=ALU.mult, op1=ALU.max), r=[lc_b], w=[lc_b])
                    S.op("act", lambda e, h=h: e.activation(out=tmpm[:, 1, :], in_=tmpm[:, 1, :], func=AF.Exp,
                                                             scale=LG[:, 4 + h:5 + h]), r=[lc_b], w=[lc_b])
                    S.op("pool", lambda e: e.affine_select(out=tmpm[:, 1, :], in_=tmpm[:, 1, :], pattern=[[-1, 128]],
                                                            compare_op=ALU.is_ge, fill=0.0, base=0, channel_multiplier=1),
                         r=[lc_b], w=[lc_b])
                    S.op("dve", lambda e, h=h: e.tensor_tensor(out=Mh[:, h, :], in0=tmpm[:, 0, :], in1=tmpm[:, 1, :],
                                                               op=ALU.add), r=[lc_b], w=[lc_b])
                    S.op("dve", lambda e, h=h: e.tensor_scalar(out=Mh[:, h, :], in0=Mh[:, h, :], scalar1=1.0 / 16.0,
                                                               scalar2=None, op0=ALU.mult), r=[lc_b], w=[lc_b])
                    for d in range(2):
                        S.op("act", lambda e, h=h, d=d: e.activation(out=Xi[:, d * 4 + h, :], in_=ioi[:, d, :], func=AF.Exp,
                                                                      scale=LG[:, d * 4 + h:d * 4 + h + 1]),
                             r=[lc_b], w=[lc_b])

                norm_mod(l)

                for h in range(4):
                    sQK, sQKb = ws_get()
                    sV, sVb = ws_get()
                    sQKv, sVv = v8(sQK), v8(sV)
                    S.dma("sp", gnw[:, :], gnw_d[:, j * 2048 + h * 512:j * 2048 + (h + 1) * 512], w=[gnw_b])

                    for s in range(NSEG):
                        t0, n = s * SEGL, SEGL
                        for wi, dst in ((0, qT), (1, kT)):
                            pss = []
                            for half in range(2):
                                bk, bb = psum()
                                c0 = wi * 256 + half * 128
                                for kt in range(8):
                                    S.op("pe", lambda e, bk=bk, kt=kt, c0=c0: e.matmul(
                                        bk[:, 0:n], lhsT=sQKv[:, kt, c0:c0 + 128], rhs=hT[:, kt, t0:t0 + n],
                                        start=(kt == 0), stop=(kt == 7)), r=[sQKb, hT_b[s]], w=[bb])
                                pss.append((bk, bb))
                            cs, sn = cosT[:, t0:t0 + n], sinT[:, t0:t0 + n]
                            for half in range(2):
                                S.op("act", lambda e, half=half: e.copy(out=sq[:, half, :], in_=pss[half][0][:, 0:n]),
                                     r=[pss[half][1]], w=[sqb])
                            S.op("dve", lambda e: e.tensor_tensor(out=sq[:, 2, :], in0=sq[:, 0, :], in1=cs, op=ALU.mult),
                                 r=[sqb, rope_b], w=[sqb])
                            S.op("dve", lambda e: e.tensor_tensor(out=sq[:, 3, :], in0=sq[:, 1, :], in1=sn, op=ALU.mult),
                                 r=[sqb, rope_b], w=[sqb])
                            S.op("dve", lambda e: e.tensor_tensor(out=sq[:, 4, :], in0=sq[:, 0, :], in1=sn, op=ALU.mult),
                                 r=[sqb, rope_b], w=[sqb])
                            S.op("dve", lambda e: e.tensor_tensor(out=sq[:, 5, :], in0=sq[:, 1, :], in1=cs, op=ALU.mult),
                                 r=[sqb, rope_b], w=[sqb])
                            S.op("dve", lambda e, dst=dst: e.tensor_tensor(out=dst[:, 0, t0:t0 + n], in0=sq[:, 2, :],
                                                                          in1=sq[:, 3, :], op=ALU.subtract),
                                 r=[sqb], w=[qk_b[s]])
                            S.op("dve", lambda e, dst=dst: e.tensor_tensor(out=dst[:, 1, t0:t0 + n], in0=sq[:, 4, :],
                                                                          in1=sq[:, 5, :], op=ALU.add),
                                 r=[sqb], w=[qk_b[s]])
                    ws_release()
                    for c in range(NT):
                        tk = slice(c * 128, (c + 1) * 128)
                        bk, bb = psum()
                        for kt in range(8):
                            S.op("pe", lambda e, bk=bk, kt=kt: e.matmul(bk[:, :], lhsT=hT[:, kt, tk], rhs=sVv[:, kt, :],
                                                                         start=(kt == 0), stop=(kt == 7)),
                                 r=[sVb, hT_b[c // 2]], w=[bb])
                        S.op("act", lambda e, bk=bk, c=c: e.copy(out=v16[:, c, :], in_=bk[:, :]), r=[bb], w=[v_b[c]])
                    ws_release()
                    sZ, sZb = ws_get()
                    sO, sOb = ws_get()
                    sZv = v8(sZ)
                    sOv = sO[:, :].rearrange("p (v c) -> p v c", v=4)
                    for c in range(NT):
                        tk = slice(c * 128, (c + 1) * 128)
                        bk, bb = psum()
                        bkb = bk[:, :].bitcast(BF16)
                        for dt in range(2):
                            S.op("pe", lambda e, dt=dt, bkb=bkb: e.transpose(bkb[:, dt * 128:(dt + 1) * 128], kT[:, dt, tk],
                                                                             ident_b[:]),
                                 r=[qk_b[c // 2], cst_b], w=[bb])
                        S.op("dve", lambda e, bkb=bkb, c=c: e.tensor_scalar(out=kzf[:, c, :], in0=bkb[:, 0:256],
                                                                            scalar1=ZF[:, h:h + 1], scalar2=None,
                                                                            op0=ALU.mult), r=[bb, lc_b], w=[kz_b[c]])
                        S.op("act", lambda e, bkb=bkb, c=c: e.activation(out=kzb[:, c, :], in_=bkb[:, 0:256], func=AF.Copy,
                                                                         scale=ZF[:, 4 + h:5 + h]), r=[bb, lc_b], w=[kz_b[c]])
                        bk, bb = psum()
                        for dt in range(2):
                            S.op("pe", lambda e, dt=dt, bk=bk: e.matmul(bk[:, 0:128], lhsT=kT[:, dt, tk], rhs=qT[:, dt, tk],
                                                                         start=(dt == 0), stop=(dt == 1)),
                                 r=[qk_b[c // 2]], w=[bb])
                        S.op("dve", lambda e, bk=bk, c=c: e.tensor_tensor(out=sm[:, c, :], in0=bk[:, 0:128], in1=Mh[:, h, :],
                                                                          op=ALU.mult), r=[bb, lc_b], w=[sm_b[c]])

                    def upd_state(d, c, kz):
                        pss = []
                        for dt in range(2):
                            bk, bb = psum()
                            S.op("pe", lambda e, bk=bk, dt=dt: e.matmul(bk[:, :], lhsT=kz[:, c, dt * 128:(dt + 1) * 128],
                                                                         rhs=v16[:, c, :], start=True, stop=True),
                                 r=[kz_b[c], v_b[c]], w=[bb])
                            pss.append((bk, bb))
                        if S32alt[d] is None:
                            dst, dstb = S32[d], S32_b[d]
                        else:
                            dst, dstb = S32alt[d], S32alt_b[d]
                        for dt in range(2):
                            bk, bb = pss[dt]
                            S.op("dve", lambda e, bk=bk, dt=dt: e.scalar_tensor_tensor(
                                out=dst[:, dt, :], in0=S32[d][:, dt, :], scalar=GC[:, d * 4 + h:d * 4 + h + 1],
                                in1=bk[:, :], op0=ALU.mult, op1=ALU.add), r=[bb, lc_b, S32_b[d]], w=[dstb])
                        if S32alt[d] is not None:
                            S32[d], S32alt[d] = S32alt[d], S32[d]
                            S32_b[d], S32alt_b[d] = S32alt_b[d], S32_b[d]

                    def out_state(d, s):
                        st_t, st_bf = stage()
                        S.op("act", lambda e: e.copy(out=st_t[:, :], in_=S32[d][:, :, :].rearrange("p a b -> p (a b)")),
                             r=[S32_b[d]], w=[st_bf])
                        S.dma("sp", nsr_d[s, j, d, h].rearrange("(a p) v -> p a v", p=128),
                              st_t[:, :].rearrange("p (a v) -> p a v", a=2), r=[st_bf])

                    S.op("dve", lambda e: e.memset(S32[1][:, :, :], 0.0), w=[S32_b[1]])
                    for c in range(NT - 1, -1, -1):
                        s = c // 2
                        S.op("act", lambda e, c=c: e.copy(out=Sb16[:, c, :, :], in_=S32[1][:, :, :]),
                             r=[S32_b[1]], w=[Sb16_b[c]])
                        upd_state(1, c, kzb)
                        if c % 2 == 0:
                            out_state(1, s)
                            if c > 0:
                                if s - 1 == 3:
                                    st_t, st_bf = stage()
                                    S.dma("sp", st_t[:, :].rearrange("p (a v) -> p a v", a=2),
                                          s0r_d[j, 1, h].rearrange("(a p) v -> p a v", p=128), w=[st_bf])
                                    S.op("dve", lambda e, s=s, st_t=st_t: e.scalar_tensor_tensor(
                                        out=S32[1][:, :, :], in0=S32[1][:, :, :], scalar=chain_ap(s),
                                        in1=st_t[:, :].rearrange("p (a v) -> p a v", a=2),
                                        op0=ALU.mult, op1=ALU.add), r=[S32_b[1], prm_b, st_bf], w=[S32_b[1]])
                                else:
                                    S.op("dve", lambda e, s=s: e.tensor_scalar(
                                        out=S32[1][:, :, :], in0=S32[1][:, :, :], scalar1=chain_ap(s), scalar2=None,
                                        op0=ALU.mult), r=[S32_b[1], prm_b], w=[S32_b[1]])
                    S.dma("sp", S32[0][:, :, :], s0r_d[j, 0, h].rearrange("(a p) v -> p a v", p=128), w=[S32_b[0]])
                    S.op("act", lambda e: e.copy(out=S16f[:, :, :], in_=S32[0][:, :, :]), r=[S32_b[0]], w=[S16f_b])
                    for c in range(NT):
                        s = c // 2
                        tk = slice(c * 128, (c + 1) * 128)
                        qi = c % 2
                        for d in range(2):
                            S.op("dve", lambda e, d=d, qi=qi: e.tensor_tensor(
                                out=qx[qi][:, d, :, :], in0=qT[:, :, tk],
                                in1=Xi[:, d * 4 + h, :].unsqueeze(1).to_broadcast([128, 2, 128]), op=ALU.mult),
                                r=[qk_b[s], lc_b], w=[qx_b[qi]])
                        bkz, bbz = psum()
                        for kt in range(8):
                            S.op("pe", lambda e, kt=kt: e.matmul(bkz[:, :], lhsT=hT[:, kt, tk], rhs=sZv[:, kt, :],
                                                                  start=(kt == 0), stop=(kt == 7)), r=[sZb, hT_b[s]], w=[bbz])
                        S.op("act", lambda e, qi=qi: e.activation(out=zs[qi][:, :], in_=bkz[:, :], func=AF.Silu),
                             r=[bbz], w=[zs_b[qi]])
                        bk, bb = psum()
                        S.op("pe", lambda e, bk=bk, c=c: e.matmul(bk[:, :], lhsT=sm[:, c, :], rhs=v16[:, c, :],
                                                                   start=True, stop=False), r=[sm_b[c], v_b[c]], w=[bb])
                        for dt in range(2):
                            S.op("pe", lambda e, bk=bk, dt=dt: e.matmul(bk[:, :], lhsT=qx[qi][:, 0, dt, :], rhs=S16f[:, dt, :],
                                                                         start=False, stop=False),
                                 r=[qx_b[qi], S16f_b], w=[bb])
                        for dt in range(2):
                            S.op("pe", lambda e, bk=bk, dt=dt, c=c: e.matmul(bk[:, :], lhsT=qx[qi][:, 1, dt, :],
                                                                              rhs=Sb16[:, c, dt, :], start=False,
                                                                              stop=(dt == 1)),
                                 r=[qx_b[qi], Sb16_b[c]], w=[bb])
                        S.op("dve", lambda e, bk=bk: e.bn_stats(out=bst[:, 0:6], in_=bk[:, :]), r=[bb], w=[bst_b])
                        S.op("dve", lambda e: e.bn_aggr(out=bst[:, 8:10], in_=bst[:, 0:6]), r=[bst_b], w=[bst_b])
                        S.op("act", lambda e: e.activation(out=bst[:, 10:11], in_=bst[:, 9:10], func=AF.Ln, bias=cpar[:, 1:2],
                                                           scale=1.0), r=[bst_b, cst_b], w=[bst_b])
                        S.op("act", lambda e: e.activation(out=bst[:, 10:11], in_=bst[:, 10:11], func=AF.Exp, scale=-0.5),
                             r=[bst_b], w=[bst_b])
                        S.op("dve", lambda e, bk=bk: e.tensor_scalar(out=on[:, :], in0=bk[:, :], scalar1=bst[:, 8:9],
                                                                     scalar2=bst[:, 10:11], op0=ALU.subtract,
                                                                     op1=ALU.mult), r=[bb, bst_b], w=[on_b])
                        S.op("dve", lambda e: e.tensor_tensor(out=on[:, :], in0=on[:, :], in1=gnw[:, :], op=ALU.mult),
                             r=[on_b, gnw_b], w=[on_b])
                        S.op("dve", lambda e, qi=qi: e.tensor_tensor(out=og[:, :], in0=on[:, :], in1=zs[qi][:, :], op=ALU.mult),
                             r=[on_b, zs_b[qi]], w=[og_b])
                        bk2, bb2 = psum()
                        bk2b = bk2[:, :].bitcast(BF16)
                        for vt in range(4):
                            S.op("pe", lambda e, vt=vt, bk2b=bk2b: e.transpose(
                                bk2b[:, vt * 128:(vt + 1) * 128], og[:, vt * 128:(vt + 1) * 128], ident_b[:]),
                                r=[og_b, cst_b], w=[bb2])
                        oi = s % 2
                        lo = (c % 2) * 128
                        S.op("act", lambda e, bk2b=bk2b, oi=oi, lo=lo: e.copy(
                            out=oT[oi][:, :, lo:lo + 128], in_=bk2b[:, 0:512].rearrange("p (v t) -> p v t", v=4)),
                            r=[bb2], w=[oT_b[oi]])
                        upd_state(0, c, kzf)
                        if c % 2 == 1:
                            out_state(0, s)
                            if c < NT - 1:
                                S.op("dve", lambda e, s=s: e.tensor_scalar(
                                    out=S32[0][:, :, :], in0=S32[0][:, :, :], scalar1=chain_ap(s + 1), scalar2=None,
                                    op0=ALU.mult), r=[S32_b[0], prm_b], w=[S32_b[0]])
                        if c < NT - 1:
                            S.op("act", lambda e: e.copy(out=S16f[:, :, :], in_=S32[0][:, :, :]), r=[S32_b[0]], w=[S16f_b])
                        if c % 2 == 1:
                            seg = slice(s * SEGL, (s + 1) * SEGL)
                            for dt in range(8):
                                bk, bb = psum()
                                for vt in range(4):
                                    S.op("pe", lambda e, vt=vt, dt=dt, bk=bk, oi=oi: e.matmul(
                                        bk[:, 0:SEGL], lhsT=sOv[:, vt, dt * 128:(dt + 1) * 128], rhs=oT[oi][:, vt, :],
                                        start=(vt == 0), stop=(vt == 3)), r=[sOb, oT_b[oi]], w=[bb])
                                S.op("dve", lambda e, bk=bk, dt=dt, s=s: e.scalar_tensor_tensor(
                                    out=xT[:, dt, seg], in0=bk[:, 0:SEGL], scalar=mod_ap(l, 2, dt, s), in1=xT[:, dt, seg],
                                    op0=ALU.mult, op1=ALU.add), r=[bb, modT_b, xT_b[s]], w=[xT_b[s]])
                    ws_release()
                    ws_release()
                S.barrier()

        def del_layer(l, j):
            with contextlib.ExitStack() as ls:
                DH = 8
                tri = [sb(f"tri{d}", [128, 128], st=ls) for d in range(2)]
                neg3 = [sb(f"neg3{d}", [128, 128], st=ls) for d in range(2)]
                pos1 = [sb(f"pos1{d}", [128, 128], st=ls) for d in range(2)]
                dc_b = Buf("dconst")
                for d in range(2):
                    S.op("pool", lambda e, d=d: e.memset(tri[d][:, :], 1.0), w=[dc_b])
                    pat, cm = ([[1, 128]], -1) if d == 0 else ([[-1, 128]], 1)
                    S.op("pool", lambda e, d=d, pat=pat, cm=cm: e.affine_select(
                        out=tri[d][:, :], in_=tri[d][:, :], pattern=pat, compare_op=ALU.is_ge, fill=0.0, base=0,
                        channel_multiplier=cm), r=[dc_b], w=[dc_b])
                    S.op("pool", lambda e, d=d: e.memset(neg3[d][:, :], 0.0), w=[dc_b])
                    S.op("pool", lambda e, d=d, pat=pat, cm=cm: e.affine_select(
                        out=neg3[d][:, :], in_=neg3[d][:, :], pattern=pat, compare_op=ALU.is_ge, fill=-BIG, base=0,
                        channel_multiplier=cm), r=[dc_b], w=[dc_b])
                    pat2, cm2 = ([[-1, 128]], 1) if d == 0 else ([[1, 128]], -1)
                    S.op("pool", lambda e, d=d: e.memset(pos1[d][:, :], 0.0), w=[dc_b])
                    S.op("pool", lambda e, d=d, pat2=pat2, cm2=cm2: e.affine_select(
                        out=pos1[d][:, :], in_=pos1[d][:, :], pattern=pat2, compare_op=ALU.is_gt, fill=BIG, base=0,
                        channel_multiplier=cm2), r=[dc_b], w=[dc_b])
                abraw = sb("abraw", [128, NT, 32], st=ls)
                tk_b = Buf("tokscal")
                tsc = sb("tsc", [128, 10, NT, 16], st=ls)
                U, L1, GG, LNB, BB, GT, EGt, NBEG, NEGG, GPL = [tsc[:, i, :, :] for i in range(10)]
                negA = sb("negA", [128, 16], st=ls)
                dnw = sb("dnw", [128, 256], st=ls)
                dnw_b = Buf("dnw")
                XP = [sb("XP0", [128, NSEG, 258], st=ls)] * 2
                XP_b = [Buf("XP0")] * 2
                acc = sqs[0][:, 0:NSEG, :]
                acc_b = sqs_b[0]
                tmb = sb("tmb", [128, NTOK], BF16, st=ls)
                tmb_b = Buf("tmb")
                rinv = sqs[0][:, 5:7, :].rearrange("p a b -> p (a b)")
                rinv_b = sqs_b[0]
                qT = sb("dqT", [128, NTOK], BF16, st=ls)
                kT = sb("dkT", [128, NTOK], BF16, st=ls)
                q_b, k_b = Buf("dq"), Buf("dk")
                bv = sb("bv", [128, NT * 2, 256], BF16, st=ls)
                bv_b = [Buf(f"bv{c}") for c in range(NT)]
                kd = sb("kd", [128, NT * 2, 128], BF16, st=ls)
                kd_b = [Buf(f"kd{c}") for c in range(NT)]
                TT = sb("TT", [128, NT * 2, 128], BF16, st=ls)
                TT_b = [Buf(f"TT{g}") for g in range(NSEG)]
                PTm = sb("PTm", [128, NT * 2, 128], BF16, st=ls)
                qg = sb("qg", [128, NT * 2, 128], BF16, st=ls)
                pq_b = [Buf(f"pq{c}") for c in range(NT)]
                csc = sb("csc", [128, 2, NT * 2], st=ls)
                csc_b = [Buf(f"csc{c}") for c in range(NT)]
                Es = [sb(f"Es{i}", [128, 3, 128], st=ls) for i in range(2)]
                Es_b = [Buf(f"Es{i}") for i in range(2)]
                def g4(nm):
                    return sb(nm, [128, 4, 128], BF16, st=ls), Buf(nm)
                Xg, Xg_b = zip(*[g4(f"Xg{i}") for i in range(2)])
                Yg, Yg_b = zip(*[g4(f"Yg{i}") for i in range(2)])
                Pg, Pg_b = zip(*[g4(f"Pg{i}") for i in range(2)])
                Qg, Qg_b = zip(*[g4(f"Qg{i}") for i in range(2)])
                Ng, Ng_b = zip(*[g4(f"Ng{i}") for i in range(2)])
                Wg, Wg_b = zip(*[g4(f"Wg{i}") for i in range(2)])
                Af, Af_b = g4("Af")
                Bf, Bf_b = g4("Bf")
                bmask = sb("bmask", [128, 4, 128], BF16, st=ls)
                S.dma("pool", bmask[:, :, :].rearrange("p a b -> p (a b)"), bmask_d, w=[dc_b])
                ob = sb("ob", [128, NT, 256], st=ls)
                ob_b = [Buf(f"ob{c}") for c in range(NT)]
                S32 = [sb(f"dS32_{d}", [128, 256], st=ls) for d in range(2)]
                S32_b = [Buf(f"dS32_{d}") for d in range(2)]
                S16 = [sb(f"dS16_{d}", [128, 256], BF16, st=ls) for d in range(2)]
                S16_b = [Buf(f"dS16_{d}") for d in range(2)]
                rr = [sb(f"rr{d}", [128, 256], BF16, st=ls) for d in range(2)]
                rr_b = [Buf(f"rr{d}") for d in range(2)]
                vn16 = [sb(f"vn{d}", [128, 256], BF16, st=ls) for d in range(2)]
                vn_b = [Buf(f"vn{d}") for d in range(2)]
                zs = [sb(f"dzs{i}", [128, 256], BF16, st=ls) for i in range(2)]
                zs_b = [Buf(f"dzs{i}") for i in range(2)]
                on = [sb(f"don{i}", [128, 256], st=ls) for i in range(2)]
                on_b = [Buf(f"don{i}") for i in range(2)]
                og = [sb(f"dog{i}", [128, 256], BF16, st=ls) for i in range(2)]
                og_b = [Buf(f"dog{i}") for i in range(2)]
                oT = [sb(f"doT{i}", [128, 2, SEGL], BF16, st=ls) for i in range(NSEG)]
                oT_b = [Buf(f"doT{i}") for i in range(NSEG)]
                bst = sb("dbst", [128, 2, 8], st=ls)
                bst_b = [Buf("dbst0"), Buf("dbst1")]

                norm_mod(l)

                sAB, sABb = ws_get()
                sABv = v8(sAB, 32)
                for c in range(NT):
                    tk = slice(c * 128, (c + 1) * 128)
                    bk, bb = psum()
                    for kt in range(8):
                        S.op("pe", lambda e, bk=bk, kt=kt: e.matmul(bk[:, 0:32], lhsT=hT[:, kt, tk], rhs=sABv[:, kt, :],
                                                                     start=(kt == 0), stop=(kt == 7)),
                             r=[sABb, hT_b[c // 2]], w=[bb])
                    S.op("act", lambda e, bk=bk, c=c: e.copy(out=abraw[:, c, :], in_=bk[:, 0:32]), r=[bb], w=[tk_b])
                ws_release()
                al = prm[:, o_alog + j * 16:o_alog + j * 16 + 16]
                dtb = prm[:, o_dtb + j * 16:o_dtb + j * 16 + 16]
                T = [tk_b, prm_b, cst_b]
                S.op("act", lambda e: e.activation(out=negA[:, :], in_=al, func=AF.Exp), r=T, w=[tk_b])
                S.op("dve", lambda e: e.tensor_scalar(out=negA[:, :], in0=negA[:, :], scalar1=-1.0, scalar2=None,
                                                      op0=ALU.mult), r=T, w=[tk_b])
                S.op("dve", lambda e: e.tensor_tensor(out=U, in0=abraw[:, :, 0:16],
                                                      in1=dtb.unsqueeze(1).to_broadcast([128, NT, 16]), op=ALU.add),
                     r=T, w=[tk_b])
                S.op("dve", lambda e: e.tensor_scalar(out=L1, in0=U, scalar1=-1.0, scalar2=None, op0=ALU.mult), r=T, w=[tk_b])
                S.op("dve", lambda e: e.tensor_tensor(out=L1, in0=L1, in1=U, op=ALU.max), r=T, w=[tk_b])
                S.op("act", lambda e: e.activation(out=L1, in_=L1, func=AF.Exp, scale=-1.0), r=T, w=[tk_b])
                S.op("act", lambda e: e.activation(out=L1, in_=L1, func=AF.Ln, bias=cpar[:, 2:3], scale=1.0), r=T, w=[tk_b])
                S.op("dve", lambda e: e.tensor_scalar(out=U, in0=U, scalar1=0.0, scalar2=None, op0=ALU.max), r=T, w=[tk_b])
                S.op("dve", lambda e: e.tensor_tensor(out=U, in0=U, in1=L1, op=ALU.add), r=T, w=[tk_b])
                S.op("dve", lambda e: e.tensor_tensor(out=GG, in0=U, in1=negA[:, :].unsqueeze(1).to_broadcast([128, NT, 16]),
                                                      op=ALU.mult), r=T, w=[tk_b])
                S.op("dve", lambda e: e.tensor_scalar(out=L1, in0=abraw[:, :, 16:32], scalar1=-1.0, scalar2=None, op0=ALU.mult),
                     r=T, w=[tk_b])
                S.op("dve", lambda e: e.tensor_tensor(out=L1, in0=L1, in1=abraw[:, :, 16:32], op=ALU.max), r=T, w=[tk_b])
                S.op("act", lambda e: e.activation(out=L1, in_=L1, func=AF.Exp, scale=-1.0), r=T, w=[tk_b])
                S.op("act", lambda e: e.activation(out=L1, in_=L1, func=AF.Ln, bias=cpar[:, 2:3], scale=1.0), r=T, w=[tk_b])
                S.op("dve", lambda e: e.tensor_scalar(out=LNB, in0=abraw[:, :, 16:32], scalar1=0.0, scalar2=None,
                                                      op0=ALU.min), r=T, w=[tk_b])
                S.op("dve", lambda e: e.tensor_tensor(out=LNB, in0=LNB, in1=L1, op=ALU.subtract), r=T, w=[tk_b])
                S.op("act", lambda e: e.activation(out=BB, in_=LNB, func=AF.Exp), r=T, w=[tk_b])
                bk, bb = psum()
                for c in range(NT):
                    for d in range(2):
                        S.op("pe", lambda e, c=c, d=d: e.matmul(bk[:, c * 16 + d * 8:c * 16 + d * 8 + 8], lhsT=tri[d][:, :],
                                                                 rhs=GG[:, c, d * 8:d * 8 + 8], start=True, stop=True),
                             r=[tk_b, dc_b], w=[bb])
                S.op("dve", lambda e: e.tensor_copy(out=GT, in_=bk[:, 0:NT * 16].rearrange("p (c x) -> p c x", c=NT)),
                     r=[bb], w=[tk_b])
                S.op("act", lambda e: e.activation(out=EGt, in_=GT, func=AF.Exp), r=T, w=[tk_b])
                S.op("dve", lambda e: e.tensor_tensor(out=NBEG, in0=BB, in1=EGt, op=ALU.mult), r=T, w=[tk_b])
                S.op("dve", lambda e: e.tensor_scalar(out=NBEG, in0=NBEG, scalar1=-1.0, scalar2=None, op0=ALU.mult),
                     r=T, w=[tk_b])
                S.op("dve", lambda e: e.tensor_scalar(out=NEGG, in0=GT, scalar1=-1.0, scalar2=None, op0=ALU.mult),
                     r=T, w=[tk_b])
                S.op("dve", lambda e: e.tensor_tensor(out=GPL, in0=GT, in1=LNB, op=ALU.add), r=T, w=[tk_b])
                for i in range(2):
                    S.op("dve", lambda e, i=i: e.memset(XP[i][:, 0, 0:1], 0.0), w=[XP_b[i]])
                    S.op("dve", lambda e, i=i: e.memset(XP[i][:, NSEG - 1, 257:258], 0.0), w=[XP_b[i]])

                dstage = float(os.environ.get("K_DSTAGE", "9"))
                if dstage == 0:
                    S.barrier()
                    return
                xp_rr = [0]
                for h in range(int(os.environ.get("K_DHEADS", "8"))):
                    sW, sWb = ws_get()
                    sWv = v8(sW)
                    S.dma("sp", dnw[:, :], dnw_d[:, j * 2048 + h * 256:j * 2048 + (h + 1) * 256], w=[dnw_b])
                    for ct in range(4):
                        xi = xp_rr[0] % 2
                        xp_rr[0] += 1
                        xp, xpb = XP[xi], XP_b[xi]
                        for s in range(NSEG):
                            bk, bb = psum()
                            for kt in range(8):
                                S.op("pe", lambda e, bk=bk, kt=kt, s=s: e.matmul(
                                    bk[:, 0:SEGL], lhsT=sWv[:, kt, ct * 128:(ct + 1) * 128],
                                    rhs=hT[:, kt, s * SEGL:(s + 1) * SEGL], start=(kt == 0), stop=(kt == 7)),
                                    r=[sWb, hT_b[s]], w=[bb])
                            S.op("act", lambda e, bk=bk, s=s: e.copy(out=xp[:, s, 1:257], in_=bk[:, 0:SEGL]), r=[bb], w=[xpb])
                        chn = prm[:, o_chain + 1:o_chain + 5]
                        S.op("dve", lambda e: e.tensor_tensor(out=xp[:, 1:5, 0], in0=xp[:, 0:4, 256], in1=chn, op=ALU.mult),
                             r=[xpb, prm_b], w=[xpb])
                        S.op("dve", lambda e: e.tensor_tensor(out=xp[:, 0:4, 257], in0=xp[:, 1:5, 1], in1=chn, op=ALU.mult),
                             r=[xpb, prm_b], w=[xpb])
                        gct = h if ct == 0 else (8 + h if ct == 1 else 16 + 2 * h + (ct - 2))
                        cw = [prm[:, o_convw + (j * 3 + k) * 32 + gct:o_convw + (j * 3 + k) * 32 + gct + 1] for k in range(3)]
                        S.op("dve", lambda e: e.tensor_scalar(out=acc, in0=xp[:, :, 0:256], scalar1=cw[0], scalar2=None,
                                                              op0=ALU.mult), r=[xpb, prm_b], w=[acc_b])
                        S.op("dve", lambda e: e.scalar_tensor_tensor(out=acc, in0=xp[:, :, 1:257], scalar=cw[1],
                                                                     in1=acc, op0=ALU.mult, op1=ALU.add),
                             r=[xpb, prm_b, acc_b], w=[acc_b])
                        S.op("dve", lambda e: e.scalar_tensor_tensor(out=acc, in0=xp[:, :, 2:258], scalar=cw[2],
                                                                     in1=acc, op0=ALU.mult, op1=ALU.add),
                             r=[xpb, prm_b, acc_b], w=[acc_b])
                        accf = acc.rearrange("p s t -> p (s t)")
                        S.op("act", lambda e: e.activation(out=accf, in_=accf, func=AF.Silu), r=[acc_b], w=[acc_b])
                        if ct < 2:
                            S.op("act", lambda e: e.activation(out=tmb[:, :], in_=accf, func=AF.Square), r=[acc_b], w=[tmb_b])
                            dst, dstb = (qT, q_b) if ct == 0 else (kT, k_b)
                            scl = (128.0 ** -0.5) if ct == 0 else 1.0
                            for (t0, n) in BLOCKS:
                                bk, bb = psum()
                                S.op("pe", lambda e, bk=bk: e.matmul(bk[:, 0:n], lhsT=ones_b[:, :], rhs=tmb[:, t0:t0 + n],
                                                                      start=True, stop=True), r=[tmb_b, cst_b], w=[bb])
                                S.op("act", lambda e, bk=bk: e.activation(out=rinv[:, 0:n], in_=bk[:, 0:n], func=AF.Ln,
                                                                          bias=cpar[:, 1:2], scale=1.0),
                                     r=[bb, cst_b], w=[rinv_b])
                                S.op("act", lambda e: e.activation(out=rinv[:, 0:n], in_=rinv[:, 0:n], func=AF.Exp, scale=-0.5),
                                     r=[rinv_b], w=[rinv_b])
                                S.op("dve", lambda e, dst=dst: e.scalar_tensor_tensor(
                                    out=dst[:, t0:t0 + n], in0=accf[:, t0:t0 + n], scalar=scl, in1=rinv[:, 0:n],
                                    op0=ALU.mult, op1=ALU.mult), r=[acc_b, rinv_b], w=[dstb])
                        else:
                            vt = ct - 2
                            S.op("act", lambda e: e.copy(out=tmb[:, :], in_=accf), r=[acc_b], w=[tmb_b])
                            for c in range(NT):
                                tk = slice(c * 128, (c + 1) * 128)
                                bk, bb = psum()
                                bkb = bk[:, :].bitcast(BF16)
                                S.op("pe", lambda e, bkb=bkb: e.transpose(bkb[:, 0:128], tmb[:, tk], ident_b[:]),
                                     r=[tmb_b, cst_b], w=[bb])
                                S.op("dve", lambda e, bkb=bkb, c=c: e.tensor_scalar(
                                    out=bv[:, c * 2 + 0, vt * 128:(vt + 1) * 128], in0=bkb[:, 0:128],
                                    scalar1=BB[:, c, h:h + 1], scalar2=None, op0=ALU.mult), r=[bb, tk_b], w=[bv_b[c]])
                                S.op("act", lambda e, bkb=bkb, c=c: e.activation(
                                    out=bv[:, c * 2 + 1, vt * 128:(vt + 1) * 128], in_=bkb[:, 0:128], func=AF.Copy,
                                    scale=BB[:, c, 8 + h:9 + h]), r=[bb, tk_b], w=[bv_b[c]])
                    ws_release()
                    if dstage == 1:
                        S.barrier()
                        return
                    sZ, sZb = ws_get()
                    sO, sOb = ws_get()
                    sZv = v8(sZ, 256)
                    sOv = sO[:, 0:2048].rearrange("p (v c) -> p v c", v=2)

                    for g in range(NSEG):
                        kkps = []
                        for ci in range(2):
                            c = 2 * g + ci
                            tk = slice(c * 128, (c + 1) * 128)
                            bk, bb = psum()
                            S.op("pe", lambda e, bk=bk: e.matmul(bk[:, 0:128], lhsT=kT[:, tk], rhs=kT[:, tk], start=True, stop=True),
                                 r=[k_b], w=[bb])
                            S.op("pe", lambda e, bk=bk: e.matmul(bk[:, 128:256], lhsT=kT[:, tk], rhs=qT[:, tk], start=True,
                                                                  stop=True), r=[k_b, q_b], w=[bb])
                            bkt = bk[:, :].bitcast(BF16)
                            S.op("pe", lambda e, bkt=bkt: e.transpose(bkt[:, 512:640], kT[:, tk], ident_b[:]),
                                 r=[k_b, cst_b], w=[bb])
                            kkps.append((bk, bb, bkt))
                        if dstage == 1.1:
                            S.barrier()
                            return
                        xg, xgb = Af, Af_b
                        for ci in range(2):
                            c = 2 * g + ci
                            tk = slice(c * 128, (c + 1) * 128)
                            bk, bb, bkt = kkps[ci]
                            for d in range(2):
                                qd = ci * 2 + d
                                cd = c * 2 + d
                                dh = d * 8 + h
                                ei = qd % 2
                                es, esb = Es[ei], Es_b[ei]
                                be, bbe = psum()
                                gcol = GG[:, c, dh:dh + 1].to_broadcast([128, 128])
                                S.op("pe", lambda e, be=be, d=d: e.matmul(be[:, 0:128], lhsT=gcol, rhs=tri[d][:, :], start=True,
                                                                           stop=True), r=[tk_b, dc_b], w=[bbe])
                                S.op("pe", lambda e, be=be, d=d: e.matmul(be[:, 128:256], lhsT=gcol, rhs=tri[d][:, :], start=True,
                                                                           stop=False), r=[tk_b, dc_b], w=[bbe])
                                S.op("pe", lambda e, be=be, d=d: e.matmul(be[:, 128:256], lhsT=ident_f[:, :], rhs=neg3[d][:, :],
                                                                           start=False, stop=True), r=[cst_b, dc_b], w=[bbe])
                                S.op("pe", lambda e, be=be, d=d: e.matmul(be[:, 256:384], lhsT=gcol, rhs=tri[d][:, :], start=True,
                                                                           stop=False), r=[tk_b, dc_b], w=[bbe])
                                S.op("pe", lambda e, be=be, d=d: e.matmul(be[:, 256:384], lhsT=ident_f[:, :], rhs=pos1[d][:, :],
                                                                           start=False, stop=True), r=[cst_b, dc_b], w=[bbe])
                                S.op("act", lambda e, be=be, es=es: e.activation(out=es[:, 0, :], in_=be[:, 0:128], func=AF.Exp),
                                     r=[bbe], w=[esb])
                                S.op("act", lambda e, be=be, es=es, c=c, dh=dh: e.activation(
                                    out=es[:, 1, :], in_=be[:, 128:256], func=AF.Exp, bias=NEGG[:, c, dh:dh + 1], scale=1.0),
                                    r=[bbe, tk_b], w=[esb])
                                S.op("act", lambda e, be=be, es=es, c=c, dh=dh: e.activation(
                                    out=es[:, 2, :], in_=be[:, 256:384], func=AF.Exp, bias=GPL[:, c, dh:dh + 1], scale=-1.0),
                                    r=[bbe, tk_b], w=[esb])
                                last = 127 if d == 0 else 0
                                S.op("dve", lambda e, es=es, cd=cd, last=last: e.tensor_copy(
                                    out=csc[:, 0, cd:cd + 1], in_=es[:, 0, last:last + 1]), r=[esb], w=[csc_b[c]])
                                S.op("dve", lambda e, es=es, cd=cd, last=last: e.tensor_copy(
                                    out=csc[:, 1, cd:cd + 1], in_=es[:, 1, last:last + 1]), r=[esb], w=[csc_b[c]])
                                S.op("dve", lambda e, es=es, bk=bk, qd=qd: e.tensor_tensor(
                                    out=xg[:, qd, :], in0=bk[:, 0:128], in1=es[:, 2, :], op=ALU.mult), r=[bb, esb], w=[xgb])
                                S.op("dve", lambda e, es=es, bk=bk, cd=cd: e.tensor_tensor(
                                    out=PTm[:, cd, :], in0=bk[:, 128:256], in1=es[:, 1, :], op=ALU.mult), r=[bb, esb], w=[pq_b[c]])
                                S.op("dve", lambda e, es=es, cd=cd: e.tensor_tensor(
                                    out=qg[:, cd, :], in0=qT[:, tk], in1=es[:, 0, :], op=ALU.mult), r=[q_b, esb], w=[pq_b[c]])
                                if d == 0:
                                    S.op("dve", lambda e, bkt=bkt, cd=cd: e.tensor_scalar(
                                        out=kd[:, cd, :], in0=bkt[:, 512:640], scalar1=csc[:, 1, cd:cd + 1], scalar2=None,
                                        op0=ALU.mult), r=[bb, csc_b[c]], w=[kd_b[c]])
                                else:
                                    S.op("act", lambda e, bkt=bkt, cd=cd: e.activation(
                                        out=kd[:, cd, :], in_=bkt[:, 512:640], func=AF.Copy, scale=csc[:, 1, cd:cd + 1]),
                                        r=[bb, csc_b[c]], w=[kd_b[c]])
                        if dstage == 1.2:
                            S.barrier()
                            return
                        bt, bbt = psum()
                        btb = bt[:, :].bitcast(BF16)
                        for qd in range(4):
                            S.op("pe", lambda e, qd=qd, btb=btb: e.transpose(btb[:, qd * 128:(qd + 1) * 128], xg[:, qd, :],
                                                                             ident_b[:]), r=[xgb, cst_b], w=[bbt])
                        bt4 = btb[:, 0:512].rearrange("p (q i) -> p q i", q=4)
                        S.op("act", lambda e: e.copy(out=Bf[:, :, :], in_=bt4), r=[bbt], w=[Bf_b])

                        def mk(k_):
                            return bmask[:, k_, :].unsqueeze(1).to_broadcast([128, 4, 128])

                        def masked(dst, dstb, srcT, srcb, k_):
                            S.op("pool", lambda e: e.tensor_tensor(out=dst[:, :, :], in0=srcT[:, :, :], in1=mk(k_), op=ALU.mult),
                                 r=[srcb, dc_b], w=[dstb])

                        masked(Xg[0], Xg_b[0], xg, xgb, 0)
                        masked(Yg[0], Yg_b[0], Bf, Bf_b, 0)
                        idb = ident_b[:, :].unsqueeze(1).to_broadcast([128, 4, 128])
                        S.op("dve", lambda e: e.scalar_tensor_tensor(out=Pg[0][:, :, :], in0=Yg[0][:, :, :], scalar=-1.0, in1=idb,
                                                                      op0=ALU.mult, op1=ALU.add), r=[Yg_b[0], cst_b], w=[Pg_b[0]])
                        S.op("dve", lambda e: e.scalar_tensor_tensor(out=Qg[0][:, :, :], in0=Xg[0][:, :, :], scalar=-1.0, in1=idb,
                                                                      op0=ALU.mult, op1=ALU.add), r=[Xg_b[0], cst_b], w=[Qg_b[0]])
                        if dstage == 1.3:
                            S.barrier()
                            return

                        def mm4(L, Lb, R, Rb):
                            bk_, bb_ = psum()
                            for qd in range(4):
                                S.op("pe", lambda e, qd=qd: e.matmul(bk_[:, qd * 128:(qd + 1) * 128], lhsT=L[:, qd, :], rhs=R[:, qd, :],
                                                                      start=True, stop=True), r=[Lb, Rb], w=[bb_])
                            return bk_[:, :].rearrange("p (q i) -> p q i", q=4), bb_

                        def ev_copy(dst, dstb, ps, psb):
                            S.op("act", lambda e: e.copy(out=dst[:, :, :], in_=ps), r=[psb], w=[dstb])

                        def ev_comb(dst, dstb, ps, psb, old, oldb, op):
                            if op == "add":
                                S.op("dve", lambda e: e.tensor_tensor(out=dst, in0=ps, in1=old[:, :, :], op=ALU.add),
                                     r=[psb, oldb], w=[dstb])
                            else:
                                S.op("dve", lambda e: e.scalar_tensor_tensor(out=dst, in0=ps, scalar=-1.0, in1=old[:, :, :],
                                                                             op0=ALU.mult, op1=ALU.add), r=[psb, oldb], w=[dstb])

                        pi = 0
                        for m in range(1, 4):
                            a, b2 = (m - 1) % 2, m % 2
                            px, pxb = mm4(Yg[a], Yg_b[a], Xg[a], Xg_b[a])
                            py, pyb = mm4(Xg[a], Xg_b[a], Yg[a], Yg_b[a])
                            ev_copy(Xg[b2], Xg_b[b2], px, pxb)
                            ev_copy(Yg[b2], Yg_b[b2], py, pyb)
                            pp, ppb = mm4(Xg[b2], Xg_b[b2], Pg[pi], Pg_b[pi])
                            pq, pqb = mm4(Yg[b2], Yg_b[b2], Qg[pi], Qg_b[pi])
                            ev_comb(Pg[1 - pi][:, :, :], Pg_b[1 - pi], pp, ppb, Pg[pi], Pg_b[pi], "add")
                            ev_comb(Qg[1 - pi][:, :, :], Qg_b[1 - pi], pq, pqb, Qg[pi], Qg_b[pi], "add")
                            pi = 1 - pi
                        for s_ in range(1, 4):
                            masked(Ng[0], Ng_b[0], xg, xgb, s_)
                            pw1, pw1b = mm4(Ng[0], Ng_b[0], Pg[pi], Pg_b[pi])
                            ev_copy(Wg[0], Wg_b[0], pw1, pw1b)
                            if s_ < 3:
                                masked(Ng[1], Ng_b[1], Bf, Bf_b, s_)
                                pw2, pw2b = mm4(Ng[1], Ng_b[1], Qg[pi], Qg_b[pi])
                                ev_copy(Wg[1], Wg_b[1], pw2, pw2b)
                            pp, ppb = mm4(Qg[pi], Qg_b[pi], Wg[0], Wg_b[0])
                            if s_ < 3:
                                pq, pqb = mm4(Pg[pi], Pg_b[pi], Wg[1], Wg_b[1])
                                ev_comb(Pg[1 - pi][:, :, :], Pg_b[1 - pi], pp, ppb, Pg[pi], Pg_b[pi], "sub")
                                ev_comb(Qg[1 - pi][:, :, :], Qg_b[1 - pi], pq, pqb, Qg[pi], Qg_b[pi], "sub")
                                pi = 1 - pi
                            else:
                                ev_comb(TT[:, g * 4:(g + 1) * 4, :], TT_b[g], pp, ppb, Pg[pi], Pg_b[pi], "sub")

                    if dbg_d and h == 0:
                        def dump(name, ap, bufs):
                            if name in dbg_d:
                                S.dma("pool", dbg_d[name], ap, r=bufs)
                        dump("tsc", tsc[:, :, :, :].rearrange("p a c x -> p (a c x)"), [tk_b])
                        dump("qT", qT[:, :], [q_b])
                        dump("kT", kT[:, :], [k_b])
                        dump("bv", bv[:, :, :].rearrange("p a b -> p (a b)"), bv_b)
                        dump("kd", kd[:, :, :].rearrange("p a b -> p (a b)"), kd_b)
                        dump("TT", TT[:, :, :].rearrange("p a b -> p (a b)"), TT_b)
                        dump("PTm", PTm[:, :, :].rearrange("p a b -> p (a b)"), pq_b)
                        dump("qg", qg[:, :, :].rearrange("p a b -> p (a b)"), pq_b)
                        dump("csc", csc[:, :, :].rearrange("p a b -> p (a b)"), csc_b)
                    if dstage == 2:
                        S.barrier()
                        return
                    S.dma("sp", S32[0][:, :], s0d_d[j, 0, h], w=[S32_b[0]])
                    S.op("dve", lambda e: e.memset(S32[1][:, :], 0.0), w=[S32_b[1]])
                    for d in range(2):
                        S.op("act", lambda e, d=d: e.copy(out=S16[d][:, :], in_=S32[d][:, :]), r=[S32_b[d]], w=[S16_b[d]])
                    arrived = [0] * NT
                    seg_done = [0] * NSEG

                    def consume_o(c, bo, bbo, first):
                        if first:
                            S.op("act", lambda e: e.copy(out=ob[:, c, :], in_=bo[:, 0:256]), r=[bbo], w=[ob_b[c]])
                        else:
                            oi = c % 2
                            S.op("dve", lambda e: e.tensor_tensor(out=on[oi][:, :], in0=bo[:, 0:256], in1=ob[:, c, :], op=ALU.add),
                                 r=[bbo, ob_b[c]], w=[on_b[oi]])

                    def finish_chunk(c, first):
                        s = c // 2
                        if first:
                            return
                        oi = c % 2
                        tk = slice(c * 128, (c + 1) * 128)
                        bkz, bbz = psum()
                        for kt in range(8):
                            S.op("pe", lambda e, kt=kt: e.matmul(bkz[:, 0:256], lhsT=hT[:, kt, tk], rhs=sZv[:, kt, :],
                                                                  start=(kt == 0), stop=(kt == 7)), r=[sZb, hT_b[s]], w=[bbz])
                        S.op("act", lambda e: e.activation(out=zs[oi][:, :], in_=bkz[:, 0:256], func=AF.Silu), r=[bbz], w=[zs_b[oi]])
                        S.op("dve", lambda e: e.bn_stats(out=bst[:, oi, 0:6], in_=on[oi][:, :]), r=[on_b[oi]], w=[bst_b[oi]])
                        S.op("dve", lambda e: e.bn_aggr(out=bst[:, oi, 6:8], in_=bst[:, oi, 0:6]), r=[bst_b[oi]], w=[bst_b[oi]])
                        S.op("dve", lambda e: e.scalar_tensor_tensor(out=bst[:, oi, 0:1], in0=bst[:, oi, 6:7], scalar=bst[:, oi, 6:7],
                                                                     in1=bst[:, oi, 7:8], op0=ALU.mult, op1=ALU.add),
                             r=[bst_b[oi]], w=[bst_b[oi]])
                        S.op("act", lambda e: e.activation(out=bst[:, oi, 1:2], in_=bst[:, oi, 0:1], func=AF.Ln, bias=cpar[:, 1:2],
                                                           scale=1.0), r=[bst_b[oi], cst_b], w=[bst_b[oi]])
                        S.op("act", lambda e: e.activation(out=bst[:, oi, 1:2], in_=bst[:, oi, 1:2], func=AF.Exp, scale=-0.5),
                             r=[bst_b[oi]], w=[bst_b[oi]])
                        S.op("dve", lambda e: e.scalar_tensor_tensor(out=on[oi][:, :], in0=on[oi][:, :], scalar=bst[:, oi, 1:2],
                                                                     in1=dnw[:, :], op0=ALU.mult, op1=ALU.mult),
                             r=[on_b[oi], bst_b[oi], dnw_b], w=[on_b[oi]])
                        S.op("dve", lambda e: e.tensor_tensor(out=og[oi][:, :], in0=on[oi][:, :], in1=zs[oi][:, :], op=ALU.mult),
                             r=[on_b[oi], zs_b[oi]], w=[og_b[oi]])
                        bk2, bb2 = psum()
                        bk2b = bk2[:, :].bitcast(BF16)
                        for vt in range(2):
                            S.op("pe", lambda e, vt=vt: e.transpose(bk2b[:, vt * 128:(vt + 1) * 128],
                                                                    og[oi][:, vt * 128:(vt + 1) * 128], ident_b[:]),
                                 r=[og_b[oi], cst_b], w=[bb2])
                        lo = (c % 2) * 128
                        S.op("act", lambda e: e.copy(out=oT[s][:, :, lo:lo + 128],
                                                     in_=bk2b[:, 0:256].rearrange("p (v t) -> p v t", v=2)), r=[bb2], w=[oT_b[s]])
                        seg_done[s] += 1
                        if seg_done[s] == 2:
                            seg = slice(s * SEGL, (s + 1) * SEGL)
                            for dt in range(8):
                                bk, bb = psum()
                                for vt in range(2):
                                    S.op("pe", lambda e, vt=vt, dt=dt, bk=bk: e.matmul(
                                        bk[:, 0:SEGL], lhsT=sOv[:, vt, dt * 128:(dt + 1) * 128], rhs=oT[s][:, vt, :],
                                        start=(vt == 0), stop=(vt == 1)), r=[sOb, oT_b[s]], w=[bb])
                                S.op("dve", lambda e, bk=bk, dt=dt: e.scalar_tensor_tensor(
                                    out=xT[:, dt, seg], in0=bk[:, 0:SEGL], scalar=mod_ap(l, 2, dt, s), in1=xT[:, dt, seg],
                                    op0=ALU.mult, op1=ALU.add), r=[bb, modT_b, xT_b[s]], w=[xT_b[s]])

                    for step in range(NT):
                        cs_ = [step, NT - 1 - step]
                        st1 = []
                        for d in range(2):
                            c = cs_[d]
                            tk = slice(c * 128, (c + 1) * 128)
                            bk, bb = psum()
                            S.op("pe", lambda e, bk=bk, d=d: e.matmul(bk[:, 0:256], lhsT=kT[:, tk], rhs=S16[d][:, :], start=True,
                                                                       stop=True), r=[k_b, S16_b[d]], w=[bb])
                            st1.append((bk, bb))
                        for d in range(2):
                            c = cs_[d]
                            bk, bb = st1[d]
                            S.op("dve", lambda e, bk=bk, d=d, c=c: e.scalar_tensor_tensor(
                                out=rr[d][:, :], in0=bk[:, 0:256], scalar=NBEG[:, c, d * 8 + h:d * 8 + h + 1],
                                in1=bv[:, c * 2 + d, :], op0=ALU.mult, op1=ALU.add), r=[bb, tk_b, bv_b[c]], w=[rr_b[d]])
                        st3 = []
                        for d in range(2):
                            c = cs_[d]
                            bk, bb = psum()
                            S.op("pe", lambda e, bk=bk, d=d, c=c: e.matmul(bk[:, 0:256], lhsT=TT[:, c * 2 + d, :], rhs=rr[d][:, :],
                                                                            start=True, stop=True), r=[TT_b[c // 2], rr_b[d]], w=[bb])
                            st3.append((bk, bb))
                        for d in range(2):
                            bk, bb = st3[d]
                            S.op("act", lambda e, bk=bk, d=d: e.copy(out=vn16[d][:, :], in_=bk[:, 0:256]), r=[bb], w=[vn_b[d]])
                        st5 = []
                        for d in range(2):
                            c = cs_[d]
                            bo, bbo = psum()
                            S.op("pe", lambda e, bo=bo, d=d, c=c: e.matmul(bo[:, 0:256], lhsT=qg[:, c * 2 + d, :], rhs=S16[d][:, :],
                                                                            start=True, stop=False), r=[pq_b[c], S16_b[d]], w=[bbo])
                            S.op("pe", lambda e, bo=bo, d=d, c=c: e.matmul(bo[:, 0:256], lhsT=PTm[:, c * 2 + d, :], rhs=vn16[d][:, :],
                                                                            start=False, stop=True), r=[pq_b[c], vn_b[d]], w=[bbo])
                            S.op("pe", lambda e, bo=bo, d=d, c=c: e.matmul(bo[:, 256:512], lhsT=kd[:, c * 2 + d, :], rhs=vn16[d][:, :],
                                                                            start=True, stop=True), r=[kd_b[c], vn_b[d]], w=[bbo])
                            st5.append((bo, bbo))
                        for d in range(2):
                            c = cs_[d]
                            s = c // 2
                            bo, bbo = st5[d]
                            cd = c * 2 + d
                            S.op("dve", lambda e, bo=bo, d=d, cd=cd: e.scalar_tensor_tensor(
                                out=S32[d][:, :], in0=S32[d][:, :], scalar=csc[:, 0, cd:cd + 1], in1=bo[:, 256:512],
                                op0=ALU.mult, op1=ALU.add), r=[bbo, csc_b[c], S32_b[d]], w=[S32_b[d]])
                            seg_end = (c % 2 == 1) if d == 0 else (c % 2 == 0)
                            if seg_end:
                                st_t, st_bf = stage()
                                S.op("act", lambda e, st_t=st_t, d=d: e.copy(out=st_t[:, 0:256], in_=S32[d][:, :]),
                                     r=[S32_b[d]], w=[st_bf])
                                S.dma("sp", nsd_d[s, j, d, h], st_t[:, 0:256], r=[st_bf])
                                if d == 0 and c < NT - 1:
                                    S.op("dve", lambda e, s=s: e.tensor_scalar(out=S32[0][:, :], in0=S32[0][:, :],
                                                                               scalar1=chain_ap(s + 1), scalar2=None, op0=ALU.mult),
                                         r=[S32_b[0], prm_b], w=[S32_b[0]])
                                if d == 1 and c > 0:
                                    if s - 1 == 3:
                                        st2, st2b = stage()
                                        S.dma("sp", st2[:, 0:256], s0d_d[j, 1, h], w=[st2b])
                                        S.op("dve", lambda e, s=s, st2=st2: e.scalar_tensor_tensor(
                                            out=S32[1][:, :], in0=S32[1][:, :], scalar=chain_ap(s), in1=st2[:, 0:256],
                                            op0=ALU.mult, op1=ALU.add), r=[S32_b[1], prm_b, st2b], w=[S32_b[1]])
                                    else:
                                        S.op("dve", lambda e, s=s: e.tensor_scalar(out=S32[1][:, :], in0=S32[1][:, :],
                                                                                   scalar1=chain_ap(s), scalar2=None, op0=ALU.mult),
                                             r=[S32_b[1], prm_b], w=[S32_b[1]])
                            if step < NT - 1:
                                S.op("act", lambda e, d=d: e.copy(out=S16[d][:, :], in_=S32[d][:, :]), r=[S32_b[d]], w=[S16_b[d]])
                        for d in range(2):
                            c = cs_[d]
                            bo, bbo = st5[d]
                            arrived[c] += 1
                            consume_o(c, bo, bbo, arrived[c] == 1)
                        for d in range(2):
                            c = cs_[d]
                            finish_chunk(c, arrived[c] == 1)
                    ws_release()
                    ws_release()
                S.barrier()

        def final_out():
            for s in range(NSEG):
                sq, sqb, rs, rsb = rms_stat(s)
                seg = slice(s * SEGL, (s + 1) * SEGL)
                S.op("dve", lambda e: e.tensor_tensor(out=sq[:, :, :], in0=xT[:, :, seg],
                                                      in1=rs[:, :].unsqueeze(1).to_broadcast([128, 8, 256]),
                                                      op=ALU.mult), r=[xT_b[s], rsb], w=[sqb])
                for dt in range(8):
                    S.op("dve", lambda e, dt=dt: e.tensor_scalar(
                        out=sq[:, dt, :], in0=sq[:, dt, :], scalar1=prm[:, o_normw + 32 + dt:o_normw + 33 + dt],
                        scalar2=32.0, op0=ALU.mult, op1=ALU.mult), r=[sqb, prm_b], w=[sqb])
                for half in range(2):
                    tt = s * 2 + half
                    st_t, st_bf = stage()
                    for g in range(2):
                        bk, bb = psum()
                        for q in range(4):
                            dt = g * 4 + q
                            S.op("pe", lambda e, bk=bk, q=q, dt=dt: e.transpose(
                                bk[:, q * 128:(q + 1) * 128], sq[:, dt, half * 128:(half + 1) * 128], ident_f[:]),
                                r=[sqb, cst_b], w=[bb])
                        if g == 0:
                            S.op("dve", lambda e, bk=bk: e.tensor_copy(out=st_t[:, 0:512], in_=bk[:, :]), r=[bb], w=[st_bf])
                        else:
                            S.op("act", lambda e, bk=bk: e.copy(out=st_t[:, 512:1024], in_=bk[:, :]), r=[bb], w=[st_bf])
                    S.dma("sp", y_d[tt * 128:(tt + 1) * 128, :], st_t[:, :], r=[st_bf])

        for l in range(depth):
            if l % 2 == 0:
                ret_layer(l, l // 2)
            else:
                del_layer(l, l // 2)
        final_out()
        S.barrier()
        print(f"[build] instr={S.ninstr} sems={S.nsem} counts={S.cnt}")
    return nc


def _rope_tables():
    pos = np.arange(1024)
    r = (pos // 64).astype(np.float32)
    col = (pos % 64).astype(np.float32)
    freqs = (10000.0 ** (-np.arange(64, dtype=np.float32) / 64.0)).astype(np.float32)
    ang = np.concatenate([r[:, None] * freqs[None, :], col[:, None] * freqs[None, :]], -1)
    return np.cos(ang).T.astype(np.float32), np.sin(ang).T.astype(np.float32)


def _fm(v, nt):
    return np.ascontiguousarray(np.asarray(v, np.float32).reshape(nt, 128).T)


def make_in_maps(inp):
    f = lambda k: np.ascontiguousarray(np.asarray(inp[k], dtype=np.float32))
    xp, xs = f("x_prompt"), f("x_sample")
    c, c_ctx = f("c"), f("c_ctx")
    sr, sd = f("state_ret"), f("state_delta")
    norm_w, fnw, mod_b = f("norm_w"), f("final_norm_w"), f("mod_b")
    normw = np.concatenate([_fm(norm_w[l], 8) for l in range(4)] + [_fm(fnw, 8)], axis=1)
    modb = np.concatenate([_fm(mod_b[l][p * 1024:(p + 1) * 1024], 8) for l in range(4) for p in range(3)], axis=1)
    rdecb = np.ascontiguousarray(np.broadcast_to(f("ret_decay").reshape(1, 16), (128, 16)))
    gnwb = np.ascontiguousarray(np.broadcast_to(f("ret_gn_w").reshape(1, 4096), (128, 4096)))
    dnwb = np.ascontiguousarray(np.broadcast_to(f("del_norm_w").reshape(1, 4096), (128, 4096)))
    cw = f("del_conv_w")
    convw = np.concatenate([_fm(cw[jj, k], 32) for jj in range(2) for k in range(3)], axis=1)
    alogb = np.ascontiguousarray(np.broadcast_to(f("del_a_log").reshape(1, 32), (128, 32)))
    dtbb = np.ascontiguousarray(np.broadcast_to(f("del_dt_bias").reshape(1, 32), (128, 32)))
    cosS, sinS = _rope_tables()
    ii = np.arange(128)
    same = lambda b: (ii[:, None] // b) == (ii[None, :] // b)
    bm = [same(16), same(32) & ~same(16), same(64) & ~same(32), ~same(64)]
    bmask = np.concatenate([m.astype(np.float32) for m in bm], axis=1)
    shared = dict(bmask=bmask, normw=normw, modb=modb, mod_w=f("mod_w"), ret_w_in=f("ret_w_in"), ret_w_out=f("ret_w_out"),
                  del_w_in=f("del_w_in"), del_w_out=f("del_w_out"), rdecb=rdecb, gnwb=gnwb, dnwb=dnwb, convw=convw,
                  alogb=alogb, dtbb=dtbb)
    maps = []
    for core in range(8):
        m = dict(shared)
        chain = np.zeros((128, 8), np.float32)
        cosT = np.ones((128, NTOK), np.float32)
        sinT = np.zeros((128, NTOK), np.float32)
        if core < 6:
            x = xp[core * 5:(core + 1) * 5].reshape(NTOK, D)
            conds = [c_ctx] * 5
            s0r = np.zeros((2, 2, 4, 256, 512), np.float32)
            s0d = np.zeros((2, 2, 8, 128, 256), np.float32)
        else:
            b = core - 6
            x = np.concatenate([xs[b], xp[30 + b]], axis=0)
            conds = [c[b]] * 4 + [c_ctx]
            chain[:, 1:4] = 1.0
            s0r, s0d = sr[b], sd[b]
            cosT[:, 0:1024] = cosS
            sinT[:, 0:1024] = sinS
        condT = np.zeros((128, 40), np.float32)
        for s in range(5):
            condT[:, s::5] = _fm(conds[s], 8)
        m.update(x=np.ascontiguousarray(x), condT=condT, chainb=chain, s0r=np.ascontiguousarray(s0r),
                 s0d=np.ascontiguousarray(s0d), cosT=cosT, sinT=sinT)
        maps.append(m)
    return maps


def assemble(results):
    y_prompt = np.zeros((32, 256, D), np.float32)
    y_sample = np.zeros((2, 1024, D), np.float32)
    nsr = np.zeros((32, 2, 2, 4, 256, 512), np.float32)
    nsd = np.zeros((32, 2, 2, 8, 128, 256), np.float32)
    for core in range(8):
        r = results[core]
        y = r["y"].reshape(5, 256, D)
        if core < 6:
            y_prompt[core * 5:(core + 1) * 5] = y
            nsr[core * 5:(core + 1) * 5] = r["nsr"]
            nsd[core * 5:(core + 1) * 5] = r["nsd"]
        else:
            b = core - 6
            y_sample[b] = y[0:4].reshape(1024, D)
            y_prompt[30 + b] = y[4]
            nsr[30 + b] = r["nsr"][4]
            nsd[30 + b] = r["nsd"][4]
    return y_prompt, y_sample, nsr, nsd


_NC_CACHE = {}


def kernel(**inputs):
    if "nc" not in _NC_CACHE:
        _NC_CACHE["nc"] = build()
    maps = make_in_maps(inputs)
    res = run_bass_kernel_spmd(_NC_CACHE["nc"], maps, core_ids=list(range(8)))
    return assemble(res.results)
```

```python
import contextlib
import numpy as np
import concourse.bass as bass
import concourse.mybir as mybir
from concourse.bass_utils import run_bass_kernel_spmd

F32 = mybir.dt.float32
BF16 = mybir.dt.bfloat16
AF = mybir.ActivationFunctionType
ALU = mybir.AluOpType

D = 1024
NSEG = 5
SEGL = 256
NTOK = NSEG * SEGL
NT = NTOK // 128
DEPTH = 4
EPS = 1e-6
BLOCKS = [(0, 512), (512, 512), (1024, 256)]
import os
EPOCH = int(os.environ.get("K_EPOCH", "2000"))
BIG = 30000.0


class Buf:
    __slots__ = ("name", "lw", "rd", "dsem", "dcnt")

    def __init__(self, name):
        self.name = name
        self.lw = None
        self.rd = {}
        self.dsem = None
        self.dcnt = 0


class Sched:
    def __init__(self, nc, stack):
        self.nc = nc
        self.stack = stack
        self.engs = {"pe": nc.tensor, "act": nc.scalar, "dve": nc.vector, "pool": nc.gpsimd, "sp": nc.sync}
        self.cnt = {e: 0 for e in self.engs}
        self.esems = {e: [] for e in self.engs}
        self.known = {e: {} for e in self.engs}
        self.dsems = {}
        self.dma_bufs = []
        self.nsem = 0
        self.ninstr = 0

    def _newsem(self, name):
        self.nsem += 1
        return self.stack.enter_context(self.nc.semaphore(f"{name}_{self.nsem}"))

    def _esem(self, e, epoch):
        lst = self.esems[e]
        while len(lst) <= epoch:
            lst.append(self._newsem(f"s_{e}_{len(lst)}"))
        return lst[epoch]

    def _need(self, E, ev, raw):
        key, val = ev
        if key == ("e", E) and not raw:
            return
        kn = self.known[E]
        if kn.get(key, 0) >= val:
            return
        kn[key] = val
        eng = self.engs[E]
        if key[0] == "e":
            n = val - 1
            eng.wait_ge(self._esem(key[1], n // EPOCH), n % EPOCH + 1)
        else:
            eng.wait_ge(self.dsems[key], val)
        self.ninstr += 1

    def _deps(self, E, r, w):
        for b in r:
            if b.lw is not None:
                self._need(E, b.lw, True)
        for b in w:
            if b.lw is not None:
                self._need(E, b.lw, False)
            for k, v in b.rd.items():
                self._need(E, (k, v), False)

    def _post(self, ev, r, w):
        k, v = ev
        for b in r:
            if b.rd.get(k, 0) < v:
                b.rd[k] = v
        for b in w:
            b.lw = ev
            b.rd = {}

    def op(self, E, fn, r=(), w=()):
        self._deps(E, r, w)
        ins = fn(self.engs[E])
        n = self.cnt[E]
        ins.then_inc(self._esem(E, n // EPOCH), 1)
        self.cnt[E] = n + 1
        self.ninstr += 1
        self._post((("e", E), n + 1), r, w)
        return ins

    def dma(self, Q, out, in_, r=(), w=(), **kw):
        self._deps(Q, r, w)
        b0 = (list(w) + list(r))[0]
        if b0.dsem is None:
            b0.dsem = self._newsem("d_" + b0.name)
            self.dsems[("d", id(b0))] = b0.dsem
            self.dma_bufs.append(b0)
        ins = self.engs[Q].dma_start(out=out, in_=in_, **kw)
        ins.then_inc(b0.dsem, 16)
        b0.dcnt += 16
        self.ninstr += 1
        self._post((("d", id(b0)), b0.dcnt), r, w)

    def barrier(self):
        for E in self.engs:
            for e2 in self.engs:
                if self.cnt[e2] > 0:
                    self._need(E, (("e", e2), self.cnt[e2]), True)
            for b in self.dma_bufs:
                self._need(E, (("d", id(b)), b.dcnt), True)


def build(depth=DEPTH, dbg=None):
    nc = bass.Bass("TRN2", target_bir_lowering=False)
    dbg = dbg or {}

    def din(name, shape):
        return nc.dram_tensor(name, list(shape), F32, kind="ExternalInput").ap()

    def dout(name, shape):
        return nc.dram_tensor(name, list(shape), F32, kind="ExternalOutput").ap()

    x_d = din("x", [NTOK, D])
    cond_d = din("condT", [128, 40])
    chain_d = din("chainb", [128, 8])
    normw_d = din("normw", [128, 40])
    modb_d = din("modb", [128, 96])
    modw_d = din("mod_w", [4, D, 3 * D])
    rwin_d = din("ret_w_in", [2, D, 6144])
    rwout_d = din("ret_w_out", [2, 2048, D])
    dwin_d = din("del_w_in", [2, D, 6176])
    dwout_d = din("del_w_out", [2, 2048, D])
    rdec_d = din("rdecb", [128, 16])
    gnw_d = din("gnwb", [128, 4096])
    dnw_d = din("dnwb", [128, 4096])
    convw_d = din("convw", [128, 192])
    alog_d = din("alogb", [128, 32])
    dtb_d = din("dtbb", [128, 32])
    s0r_d = din("s0r", [2, 2, 4, 256, 512])
    s0d_d = din("s0d", [2, 2, 8, 128, 256])
    cos_d = din("cosT", [128, NTOK])
    sin_d = din("sinT", [128, NTOK])
    bmask_d = din("bmask", [128, 512])
    y_d = dout("y", [NTOK, D])
    nsr_d = dout("nsr", [NSEG, 2, 2, 4, 256, 512])
    nsd_d = dout("nsd", [NSEG, 2, 2, 8, 128, 256])
    dbg_d = {k: dout("dbg_" + k, shp) for k, shp in dbg.items()}

    with contextlib.ExitStack() as stack:
        S = Sched(nc, stack)

        sb_n = [0]

        def sb(name, shape, dt=F32, st=None):
            sb_n[0] += 1
            return (st or stack).enter_context(nc.sbuf_tensor(f"sb{sb_n[0]}_{name}", list(shape), dt))

        banks = [stack.enter_context(nc.psum_tensor(f"bank{i}", [128, 512], F32)) for i in range(8)]
        bank_bufs = [Buf(f"bank{i}") for i in range(8)]
        bank_rr = [0]

        def psum():
            i = bank_rr[0] % 8
            bank_rr[0] += 1
            return banks[i], bank_bufs[i]

        xT = sb("xT", [128, 8, NTOK])
        xT_b = [Buf(f"xT{s}") for s in range(NSEG)]
        hT = sb("hT", [128, 8, NTOK], BF16)
        hT_b = [Buf(f"hT{s}") for s in range(NSEG)]
        NSLAB = 3
        ring = [sb(f"slab{i}", [128, 4096], BF16) for i in range(NSLAB)]
        ring_b = [Buf(f"slab{i}") for i in range(NSLAB)]
        ring_rr = [0]

        ident_f = sb("ident_f", [128, 128])
        ident_b = sb("ident_b", [128, 128], BF16)
        ones_f = sb("ones_f", [128, 128])
        ones_b = sb("ones_b", [128, 128], BF16)
        cst_b = Buf("consts")
        cpar = sb("cpar", [128, 8])
        prm = sb("prm", [128, 40 + 8 + 40 + 96 + 16 + 192 + 32 + 32])
        prm_b = Buf("prm")
        o_cond, o_chain, o_normw, o_modb, o_rdec, o_convw, o_alog, o_dtb = 0, 40, 48, 88, 184, 200, 392, 424
        modT = sb("modT", [128, 4 * 3 * 40])
        modT_b = Buf("modT")
        sc1 = sb("sc1", [128, 4 * 40])
        scT = sb("scT", [128, 8, NSEG], BF16)
        sqs = [sb("sq0", [128, 8, 256])]
        sqs_b = [Buf("sq0")]
        rstd = [sb("rstd0", [128, 256])]
        rstd_b = [Buf("rstd0")]
        stg = [sb(f"stg{i}", [128, 1024]) for i in range(2)]
        stg_b = [Buf(f"stg{i}") for i in range(2)]
        stg_rr = [0]

        def stage():
            i = stg_rr[0] % 2
            stg_rr[0] += 1
            return stg[i], stg_b[i]

        def chain_ap(s):
            return prm[:, o_chain + s:o_chain + s + 1]

        S.op("pool", lambda e: e.memset(ident_f[:], 1.0), w=[cst_b])
        S.op("pool", lambda e: e.affine_select(out=ident_f[:], in_=ident_f[:], pattern=[[-1, 128]],
                                                compare_op=ALU.is_equal, fill=0.0, base=0, channel_multiplier=1),
             r=[cst_b], w=[cst_b])
        S.op("pool", lambda e: e.tensor_copy(out=ident_b[:], in_=ident_f[:]), r=[cst_b], w=[cst_b])
        S.op("pool", lambda e: e.memset(ones_f[:], 1.0), w=[cst_b])
        S.op("pool", lambda e: e.memset(ones_b[:], 1.0), w=[cst_b])
        S.op("pool", lambda e: e.memset(cpar[:, 0:1], 1024.0 * EPS), w=[cst_b])
        S.op("pool", lambda e: e.memset(cpar[:, 1:2], EPS), w=[cst_b])
        S.op("pool", lambda e: e.memset(cpar[:, 2:3], 1.0), w=[cst_b])
        S.op("pool", lambda e: e.memset(cpar[:, 3:4], 0.0), w=[cst_b])

        for off, n, src in ((o_cond, 40, cond_d), (o_chain, 8, chain_d), (o_normw, 40, normw_d), (o_modb, 96, modb_d),
                            (o_rdec, 16, rdec_d), (o_convw, 192, convw_d), (o_alog, 32, alog_d), (o_dtb, 32, dtb_d)):
            S.dma("sp", prm[:, off:off + n], src, w=[prm_b])

        def v8(t, n=512):
            return t[:, 0:8 * n].rearrange("p (kt c) -> p kt c", kt=8)

        def win_view(w2d):
            return w2d.rearrange("(kt p) c -> p kt c", p=128)

        wspecs = []
        for l in range(depth):
            for part in range(3):
                for hf in range(2):
                    c0 = part * 1024 + hf * 512
                    wspecs.append(lambda t, l=l, c0=c0: [(v8(t), win_view(modw_d[l])[:, :, c0:c0 + 512])])
        for l in range(depth):
            j = l // 2
            if l % 2 == 0:
                wv = win_view(rwin_d[j])
                for h in range(4):
                    wspecs.append(lambda t, h=h, wv=wv: [
                        (v8(t)[:, :, 0:256], wv[:, :, h * 256:(h + 1) * 256]),
                        (v8(t)[:, :, 256:512], wv[:, :, 1024 + h * 256:1024 + (h + 1) * 256])])
                    wspecs.append(lambda t, h=h, wv=wv: [(v8(t), wv[:, :, 2048 + h * 512:2048 + (h + 1) * 512])])
                    wspecs.append(lambda t, h=h, wv=wv: [(v8(t), wv[:, :, 4096 + h * 512:4096 + (h + 1) * 512])])
                    wspecs.append(lambda t, h=h, j=j: [
                        (t[:, :].rearrange("p (v c) -> p v c", v=4),
                         rwout_d[j][h * 512:(h + 1) * 512, :].rearrange("(v p) c -> p v c", p=128))])
            else:
                wv = win_view(dwin_d[j])
                wspecs.append(lambda t, wv=wv: [(v8(t, 32), wv[:, :, 6144:6176])])
                for h in range(8):
                    wspecs.append(lambda t, h=h, wv=wv: [
                        (v8(t)[:, :, 0:128], wv[:, :, h * 128:(h + 1) * 128]),
                        (v8(t)[:, :, 128:256], wv[:, :, 1024 + h * 128:1024 + (h + 1) * 128]),
                        (v8(t)[:, :, 256:512], wv[:, :, 2048 + h * 256:2048 + (h + 1) * 256])])
                    wspecs.append(lambda t, h=h, wv=wv: [(v8(t, 256), wv[:, :, 4096 + h * 256:4096 + (h + 1) * 256])])
                    wspecs.append(lambda t, h=h, j=j: [
                        (t[:, 0:2048].rearrange("p (v c) -> p v c", v=2),
                         dwout_d[j][h * 256:(h + 1) * 256, :].rearrange("(v p) c -> p v c", p=128))])
        ws = {"issued": 0, "released": 0, "got": 0}

        def ws_issue():
            while ws["issued"] < len(wspecs) and ws["issued"] < ws["released"] + NSLAB:
                i = ws["issued"]
                for (dst_view, src_ap) in wspecs[i](ring[i % NSLAB]):
                    S.dma("pool", dst_view, src_ap, w=[ring_b[i % NSLAB]])
                ws["issued"] += 1

        def ws_get():
            ws_issue()
            i = ws["got"]
            assert i < ws["issued"], "weight ring over-subscribed"
            ws["got"] += 1
            return ring[i % NSLAB], ring_b[i % NSLAB]

        def ws_release():
            ws["released"] += 1
            ws_issue()

        for tt in range(NT):
            st_t, st_b = stage()
            S.dma("sp", st_t[:, :], x_d[tt * 128:(tt + 1) * 128, :], w=[st_b])
            for half in range(2):
                bk, bb = psum()
                for q in range(4):
                    dt = half * 4 + q
                    S.op("pe", lambda e, bk=bk, q=q, dt=dt, st_t=st_t: e.transpose(
                        bk[:, q * 128:(q + 1) * 128], st_t[:, dt * 128:(dt + 1) * 128], ident_f[:]),
                        r=[st_b, cst_b], w=[bb])
                eng = "dve" if half == 0 else "act"
                dst = xT[:, half * 4:half * 4 + 4, tt * 128:(tt + 1) * 128]
                src = bk[:, :].rearrange("p (q t) -> p q t", q=4)
                if eng == "dve":
                    S.op("dve", lambda e, dst=dst, src=src: e.tensor_copy(out=dst, in_=src), r=[bb], w=[xT_b[tt // 2]])
                else:
                    S.op("act", lambda e, dst=dst, src=src: e.copy(out=dst, in_=src), r=[bb], w=[xT_b[tt // 2]])

        S.op("act", lambda e: e.activation(out=scT[:].rearrange("p a b -> p (a b)"), in_=prm[:, o_cond:o_cond + 40],
                                           func=AF.Silu), r=[prm_b], w=[modT_b])
        for l in range(depth):
            for part in range(3):
                for hf in range(2):
                    c0 = part * 1024 + hf * 512
                    sl, slb = ws_get()
                    slv = v8(sl)
                    bk, bb = psum()
                    for ct in range(4):
                        for kt in range(8):
                            S.op("pe", lambda e, bk=bk, ct=ct, kt=kt, slv=slv: e.matmul(
                                bk[:, ct * 5:ct * 5 + 5], lhsT=slv[:, kt, ct * 128:(ct + 1) * 128], rhs=scT[:, kt, :],
                                start=(kt == 0), stop=(kt == 7)), r=[slb, modT_b], w=[bb])
                    base = (l * 3 + part) * 40 + hf * 20
                    mb = prm[:, o_modb + (l * 3 + part) * 8 + hf * 4: o_modb + (l * 3 + part) * 8 + hf * 4 + 4]
                    S.op("dve", lambda e, bk=bk, base=base, mb=mb: e.tensor_tensor(
                        out=modT[:, base:base + 20].rearrange("p (a b) -> p a b", a=4),
                        in0=bk[:, 0:20].rearrange("p (a b) -> p a b", a=4),
                        in1=mb.unsqueeze(2).to_broadcast([128, 4, 5]), op=ALU.add), r=[bb, prm_b], w=[modT_b])
                    ws_release()
            sv = modT[:, (l * 3 + 1) * 40:(l * 3 + 1) * 40 + 40]
            S.op("dve", lambda e, l=l, sv=sv: e.tensor_scalar(out=sc1[:, l * 40:(l + 1) * 40], in0=sv, scalar1=1.0,
                                                             scalar2=32.0, op0=ALU.add, op1=ALU.mult),
                 r=[modT_b], w=[modT_b])
            S.op("dve", lambda e, l=l: e.tensor_tensor(
                out=sc1[:, l * 40:(l + 1) * 40].rearrange("p (a b) -> p a b", a=8),
                in0=sc1[:, l * 40:(l + 1) * 40].rearrange("p (a b) -> p a b", a=8),
                in1=prm[:, o_normw + l * 8:o_normw + l * 8 + 8].unsqueeze(2).to_broadcast([128, 8, 5]), op=ALU.mult),
                r=[modT_b, prm_b], w=[modT_b])

        def mod_ap(l, part, dt, s):
            o = (l * 3 + part) * 40 + dt * 5 + s
            return modT[:, o:o + 1]

        nrm_rr = [0]

        def rms_stat(s):
            i = 0
            sq, sqb, rs, rsb = sqs[i], sqs_b[i], rstd[i], rstd_b[i]
            seg = slice(s * SEGL, (s + 1) * SEGL)
            S.op("act", lambda e: e.activation(out=sq[:, :, :], in_=xT[:, :, seg], func=AF.Square), r=[xT_b[s]], w=[sqb])
            bk, bb = psum()
            for dt in range(8):
                S.op("pe", lambda e, dt=dt: e.matmul(bk[:, 0:256], lhsT=ones_f[:], rhs=sq[:, dt, :],
                                                      start=(dt == 0), stop=(dt == 7)), r=[sqb, cst_b], w=[bb])
            S.op("act", lambda e: e.activation(out=rs[:, :], in_=bk[:, 0:256], func=AF.Ln, bias=cpar[:, 0:1], scale=1.0),
                 r=[bb, cst_b], w=[rsb])
            S.op("act", lambda e: e.activation(out=rs[:, :], in_=rs[:, :], func=AF.Exp, scale=-0.5), r=[rsb], w=[rsb])
            return sq, sqb, rs, rsb

        def norm_mod(l):
            for s in range(NSEG):
                sq, sqb, rs, rsb = rms_stat(s)
                seg = slice(s * SEGL, (s + 1) * SEGL)
                S.op("dve", lambda e: e.tensor_tensor(out=sq[:, :, :], in0=xT[:, :, seg],
                                                      in1=rs[:, :].unsqueeze(1).to_broadcast([128, 8, 256]),
                                                      op=ALU.mult), r=[xT_b[s], rsb], w=[sqb])
                for dt in range(8):
                    o = l * 40 + dt * 5 + s
                    S.op("act", lambda e, dt=dt, o=o: e.activation(
                        out=hT[:, dt, seg], in_=sq[:, dt, :], func=AF.Identity, bias=mod_ap(l, 0, dt, s),
                        scale=sc1[:, o:o + 1]), r=[sqb, modT_b], w=[hT_b[s]])

        def segs_of(t0, n):
            return list(range(t0 // SEGL, (t0 + n) // SEGL))

        def out_proj(l, sO, sOb, nvt, oT, oT_b):
            sOv = sO[:, 0:nvt * 1024].rearrange("p (v c) -> p v c", v=nvt)
            for (t0, n) in BLOCKS:
                sg = segs_of(t0, n)
                for dt in range(8):
                    bk, bb = psum()
                    for vt in range(nvt):
                        S.op("pe", lambda e, vt=vt, dt=dt, bk=bk: e.matmul(
                            bk[:, 0:n], lhsT=sOv[:, vt, dt * 128:(dt + 1) * 128], rhs=oT[:, vt, t0:t0 + n],
                            start=(vt == 0), stop=(vt == nvt - 1)), r=[sOb] + [oT_b[s] for s in sg], w=[bb])
                    for s in sg:
                        lo = s * SEGL - t0
                        seg = slice(s * SEGL, (s + 1) * SEGL)
                        S.op("dve", lambda e, bk=bk, lo=lo, seg=seg, dt=dt, s=s: e.scalar_tensor_tensor(
                            out=xT[:, dt, seg], in0=bk[:, lo:lo + SEGL], scalar=mod_ap(l, 2, dt, s), in1=xT[:, dt, seg],
                            op0=ALU.mult, op1=ALU.add), r=[bb, modT_b, xT_b[s]], w=[xT_b[s]])

        def ret_layer(l, j):
            with contextlib.ExitStack() as ls:
                cosT = sb("cosT", [128, NTOK], st=ls)
                sinT = sb("sinT", [128, NTOK], st=ls)
                rope_b = Buf("rope")
                S.dma("sp", cosT[:, :], cos_d, w=[rope_b])
                S.dma("sp", sinT[:, :], sin_d, w=[rope_b])
                lc = sb("lc", [128, 64], st=ls)
                lc_b = Buf("lc")
                iot = sb("iot", [128, 128], st=ls)
                iop = sb("iop", [128, 2], st=ls)
                ioi = sb("ioi", [128, 2, 128], st=ls)
                Mh = sb("Mh", [128, 4, 128], st=ls)
                Xi = sb("Xi", [128, 8, 128], BF16, st=ls)
                tmpm = sb("tmpm", [128, 2, 128], st=ls)
                gnw = sb("gnw", [128, 512], st=ls)
                gnw_b = Buf("gnw")
                qT = sb("qT", [128, 2, NTOK], BF16, st=ls)
                kT = sb("kT", [128, 2, NTOK], BF16, st=ls)
                qk_b = [Buf(f"qk{s}") for s in range(NSEG)]
                sq, sqb = sqs[0], sqs_b[0]
                kzf = sb("kzf", [128, NT, 256], BF16, st=ls)
                kzb = sb("kzb", [128, NT, 256], BF16, st=ls)
                kz_b = [Buf(f"kz{c}") for c in range(NT)]
                v16 = sb("v16", [128, NT, 512], BF16, st=ls)
                v_b = [Buf(f"v{c}") for c in range(NT)]
                sm = sb("sm", [128, NT, 128], BF16, st=ls)
                sm_b = [Buf(f"sm{c}") for c in range(NT)]
                qx = [sb(f"qx{i}", [128, 2, 2, 128], BF16, st=ls) for i in range(2)]
                qx_b = [Buf(f"qx{i}") for i in range(2)]
                zs = [sb(f"zs{i}", [128, 512], st=ls) for i in range(2)]
                zs_b = [Buf(f"zs{i}") for i in range(2)]
                Sb16 = sb("Sb16", [128, NT, 2, 512], BF16, st=ls)
                Sb16_b = [Buf(f"Sb16_{c}") for c in range(NT)]
                S32 = [sb(f"S32_{d}", [128, 2, 512], st=ls) for d in range(2)]
                S32_b = [Buf(f"S32_{d}") for d in range(2)]
                S32alt = [None, sb("S32_1b", [128, 2, 512], st=ls)]
                S32alt_b = [None, Buf("S32_1b")]
                S16f = sb("S16f", [128, 2, 512], BF16, st=ls)
                S16f_b = Buf("S16f")
                oT = [sb(f"oT{i}", [128, 4, SEGL], BF16, st=ls) for i in range(2)]
                oT_b = [Buf(f"oT{i}") for i in range(2)]
                bst = sb("bst", [128, 16], st=ls)
                bst_b = Buf("bst")
                on = sb("on", [128, 512], st=ls)
                on_b = Buf("on")
                og = sb("og", [128, 512], BF16, st=ls)
                og_b = Buf("og")

                dec = prm[:, o_rdec + j * 8:o_rdec + j * 8 + 8]
                LG, GC, ZF = lc[:, 0:8], lc[:, 8:16], lc[:, 16:24]
                S.op("act", lambda e: e.activation(out=lc[:, 24:32], in_=dec, func=AF.Exp, scale=-1.0), r=[prm_b], w=[lc_b])
                S.op("act", lambda e: e.activation(out=LG, in_=lc[:, 24:32], func=AF.Ln, bias=cpar[:, 2:3], scale=1.0),
                     r=[lc_b, cst_b], w=[lc_b])
                S.op("dve", lambda e: e.tensor_scalar(out=LG, in0=LG, scalar1=-1.0, scalar2=None, op0=ALU.mult),
                     r=[lc_b], w=[lc_b])
                S.op("act", lambda e: e.activation(out=GC, in_=LG, func=AF.Exp, scale=128.0), r=[lc_b], w=[lc_b])
                S.op("pool", lambda e: e.iota(iot[:, :], pattern=[[1, 128]], base=0, channel_multiplier=-1,
                                               allow_small_or_imprecise_dtypes=True), r=[lc_b], w=[lc_b])
                S.op("pool", lambda e: e.iota(iop[:, 0:1], pattern=[[0, 1]], base=127, channel_multiplier=-1,
                                               allow_small_or_imprecise_dtypes=True), r=[lc_b], w=[lc_b])
                S.op("pool", lambda e: e.iota(iop[:, 1:2], pattern=[[0, 1]], base=0, channel_multiplier=1,
                                               allow_small_or_imprecise_dtypes=True), r=[lc_b], w=[lc_b])
                S.op("pool", lambda e: e.iota(ioi[:, 0, :], pattern=[[1, 128]], base=1, channel_multiplier=0,
                                               allow_small_or_imprecise_dtypes=True), r=[lc_b], w=[lc_b])
                S.op("pool", lambda e: e.iota(ioi[:, 1, :], pattern=[[-1, 128]], base=128, channel_multiplier=0,
                                               allow_small_or_imprecise_dtypes=True), r=[lc_b], w=[lc_b])
                for d in range(2):
                    for h in range(4):
                        c = d * 4 + h
                        S.op("act", lambda e, c=c, d=d: e.activation(out=ZF[:, c:c + 1], in_=iop[:, d:d + 1], func=AF.Exp,
                                                                      scale=LG[:, c:c + 1]), r=[lc_b], w=[lc_b])
                S.op("dve", lambda e: e.tensor_scalar(out=ZF, in0=ZF, scalar1=1.0 / 16.0, scalar2=None, op0=ALU.mult),
                     r=[lc_b], w=[lc_b])
                for h in range(4):
                    S.op("dve", lambda e: e.tensor_scalar(out=tmpm[:, 0, :], in0=iot[:, :], scalar1=0.0, scalar2=None,
                                                          op0=ALU.max), r=[lc_b], w=[lc_b])
                    S.op("act", lambda e, h=h: e.activation(out=tmpm[:, 0, :], in_=tmpm[:, 0, :], func=AF.Exp,
                                                             scale=LG[:, h:h + 1]), r=[lc_b], w=[lc_b])
                    S.op("pool", lambda e: e.affine_select(out=tmpm[:, 0, :], in_=tmpm[:, 0, :], pattern=[[1, 128]],
                                                            compare_op=ALU.is_ge, fill=0.0, base=0, channel_multiplier=-1),
                         r=[lc_b], w=[lc_b])
                    S.op("dve", lambda e: e.tensor_scalar(out=tmpm[:, 1, :], in0=iot[:, :], scalar1=-1.0, scalar2=0.0,
                                                          op0=ALU.mult, op1=ALU.max), r=[lc_b], w=[lc_b])
                    S.op("act", lambda e, h=h: e.activation(out=tmpm[:, 1, :], in_=tmpm[:, 1, :], func=AF.Exp,
                                                             scale=LG[:, 4 + h:5 + h]), r=[lc_b], w=[lc_b])
                    S.op("pool", lambda e: e.affine_select(out=tmpm[:, 1, :], in_=tmpm[:, 1, :], pattern=[[-1, 128]],
                                                            compare_op=ALU.is_ge, fill=0.0, base=0, channel_multiplier=1),
                         r=[lc_b], w=[lc_b])
                    S.op("dve", lambda e, h=h: e.tensor_tensor(out=Mh[:, h, :], in0=tmpm[:, 0, :], in1=tmpm[:, 1, :],
                                                               op=ALU.add), r=[lc_b], w=[lc_b])
                    S.op("dve", lambda e, h=h: e.tensor_scalar(out=Mh[:, h, :], in0=Mh[:, h, :], scalar1=1.0 / 16.0,
                                                               scalar2=None, op0=ALU.mult), r=[lc_b], w=[lc_b])
                    for d in range(2):
                        S.op("act", lambda e, h=h, d=d: e.activation(out=Xi[:, d * 4 + h, :], in_=ioi[:, d, :], func=AF.Exp,
                                                                      scale=LG[:, d * 4 + h:d * 4 + h + 1]),
                             r=[lc_b], w=[lc_b])

                norm_mod(l)

                for h in range(4):
                    sQK, sQKb = ws_get()
                    sV, sVb = ws_get()
                    sQKv, sVv = v8(sQK), v8(sV)
                    S.dma("sp", gnw[:, :], gnw_d[:, j * 2048 + h * 512:j * 2048 + (h + 1) * 512], w=[gnw_b])

                    for s in range(NSEG):
                        t0, n = s * SEGL, SEGL
                        for wi, dst in ((0, qT), (1, kT)):
                            pss = []
                            for half in range(2):
                                bk, bb = psum()
                                c0 = wi * 256 + half * 128
                                for kt in range(8):
                                    S.op("pe", lambda e, bk=bk, kt=kt, c0=c0: e.matmul(
                                        bk[:, 0:n], lhsT=sQKv[:, kt, c0:c0 + 128], rhs=hT[:, kt, t0:t0 + n],
                                        start=(kt == 0), stop=(kt == 7)), r=[sQKb, hT_b[s]], w=[bb])
                                pss.append((bk, bb))
                            cs, sn = cosT[:, t0:t0 + n], sinT[:, t0:t0 + n]
                            for half in range(2):
                                S.op("act", lambda e, half=half: e.copy(out=sq[:, half, :], in_=pss[half][0][:, 0:n]),
                                     r=[pss[half][1]], w=[sqb])
                            S.op("dve", lambda e: e.tensor_tensor(out=sq[:, 2, :], in0=sq[:, 0, :], in1=cs, op=ALU.mult),
                                 r=[sqb, rope_b], w=[sqb])
                            S.op("dve", lambda e: e.tensor_tensor(out=sq[:, 3, :], in0=sq[:, 1, :], in1=sn, op=ALU.mult),
                                 r=[sqb, rope_b], w=[sqb])
                            S.op("dve", lambda e: e.tensor_tensor(out=sq[:, 4, :], in0=sq[:, 0, :], in1=sn, op=ALU.mult),
                                 r=[sqb, rope_b], w=[sqb])
                            S.op("dve", lambda e: e.tensor_tensor(out=sq[:, 5, :], in0=sq[:, 1, :], in1=cs, op=ALU.mult),
                                 r=[sqb, rope_b], w=[sqb])
                            S.op("dve", lambda e, dst=dst: e.tensor_tensor(out=dst[:, 0, t0:t0 + n], in0=sq[:, 2, :],
                                                                          in1=sq[:, 3, :], op=ALU.subtract),
                                 r=[sqb], w=[qk_b[s]])
                            S.op("dve", lambda e, dst=dst: e.tensor_tensor(out=dst[:, 1, t0:t0 + n], in0=sq[:, 4, :],
                                                                          in1=sq[:, 5, :], op=ALU.add),
                                 r=[sqb], w=[qk_b[s]])
                    ws_release()
                    for c in range(NT):
                        tk = slice(c * 128, (c + 1) * 128)
                        bk, bb = psum()
                        for kt in range(8):
                            S.op("pe", lambda e, bk=bk, kt=kt: e.matmul(bk[:, :], lhsT=hT[:, kt, tk], rhs=sVv[:, kt, :],
                                                                         start=(kt == 0), stop=(kt == 7)),
                                 r=[sVb, hT_b[c // 2]], w=[bb])
                        S.op("act", lambda e, bk=bk, c=c: e.copy(out=v16[:, c, :], in_=bk[:, :]), r=[bb], w=[v_b[c]])
                    ws_release()
                    sZ, sZb = ws_get()
                    sO, sOb = ws_get()
                    sZv = v8(sZ)
                    sOv = sO[:, :].rearrange("p (v c) -> p v c", v=4)
                    for c in range(NT):
                        tk = slice(c * 128, (c + 1) * 128)
                        bk, bb = psum()
                        bkb = bk[:, :].bitcast(BF16)
                        for dt in range(2):
                            S.op("pe", lambda e, dt=dt, bkb=bkb: e.transpose(bkb[:, dt * 128:(dt + 1) * 128], kT[:, dt, tk],
                                                                             ident_b[:]),
                                 r=[qk_b[c // 2], cst_b], w=[bb])
                        S.op("dve", lambda e, bkb=bkb, c=c: e.tensor_scalar(out=kzf[:, c, :], in0=bkb[:, 0:256],
                                                                            scalar1=ZF[:, h:h + 1], scalar2=None,
                                                                            op0=ALU.mult), r=[bb, lc_b], w=[kz_b[c]])
                        S.op("act", lambda e, bkb=bkb, c=c: e.activation(out=kzb[:, c, :], in_=bkb[:, 0:256], func=AF.Copy,
                                                                         scale=ZF[:, 4 + h:5 + h]), r=[bb, lc_b], w=[kz_b[c]])
                        bk, bb = psum()
                        for dt in range(2):
                            S.op("pe", lambda e, dt=dt, bk=bk: e.matmul(bk[:, 0:128], lhsT=kT[:, dt, tk], rhs=qT[:, dt, tk],
                                                                         start=(dt == 0), stop=(dt == 1)),
                                 r=[qk_b[c // 2]], w=[bb])
                        S.op("dve", lambda e, bk=bk, c=c: e.tensor_tensor(out=sm[:, c, :], in0=bk[:, 0:128], in1=Mh[:, h, :],
                                                                          op=ALU.mult), r=[bb, lc_b], w=[sm_b[c]])

                    def upd_state(d, c, kz):
                        pss = []
                        for dt in range(2):
                            bk, bb = psum()
                            S.op("pe", lambda e, bk=bk, dt=dt: e.matmul(bk[:, :], lhsT=kz[:, c, dt * 128:(dt + 1) * 128],
                                                                         rhs=v16[:, c, :], start=True, stop=True),
                                 r=[kz_b[c], v_b[c]], w=[bb])
                            pss.append((bk, bb))
                        if S32alt[d] is None:
                            dst, dstb = S32[d], S32_b[d]
                        else:
                            dst, dstb = S32alt[d], S32alt_b[d]
                        for dt in range(2):
                            bk, bb = pss[dt]
                            S.op("dve", lambda e, bk=bk, dt=dt: e.scalar_tensor_tensor(
                                out=dst[:, dt, :], in0=S32[d][:, dt, :], scalar=GC[:, d * 4 + h:d * 4 + h + 1],
                                in1=bk[:, :], op0=ALU.mult, op1=ALU.add), r=[bb, lc_b, S32_b[d]], w=[dstb])
                        if S32alt[d] is not None:
                            S32[d], S32alt[d] = S32alt[d], S32[d]
                            S32_b[d], S32alt_b[d] = S32alt_b[d], S32_b[d]

                    def out_state(d, s):
                        st_t, st_bf = stage()
                        S.op("act", lambda e: e.copy(out=st_t[:, :], in_=S32[d][:, :, :].rearrange("p a b -> p (a b)")),
                             r=[S32_b[d]], w=[st_bf])
                        S.dma("sp", nsr_d[s, j, d, h].rearrange("(a p) v -> p a v", p=128),
                              st_t[:, :].rearrange("p (a v) -> p a v", a=2), r=[st_bf])

                    S.op("dve", lambda e: e.memset(S32[1][:, :, :], 0.0), w=[S32_b[1]])
                    for c in range(NT - 1, -1, -1):
                        s = c // 2
                        S.op("act", lambda e, c=c: e.copy(out=Sb16[:, c, :, :], in_=S32[1][:, :, :]),
                             r=[S32_b[1]], w=[Sb16_b[c]])
                        upd_state(1, c, kzb)
                        if c % 2 == 0:
                            out_state(1, s)
                            if c > 0:
                                if s - 1 == 3:
                                    st_t, st_bf = stage()
                                    S.dma("sp", st_t[:, :].rearrange("p (a v) -> p a v", a=2),
                                          s0r_d[j, 1, h].rearrange("(a p) v -> p a v", p=128), w=[st_bf])
                                    S.op("dve", lambda e, s=s, st_t=st_t: e.scalar_tensor_tensor(
                                        out=S32[1][:, :, :], in0=S32[1][:, :, :], scalar=chain_ap(s),
                                        in1=st_t[:, :].rearrange("p (a v) -> p a v", a=2),
                                        op0=ALU.mult, op1=ALU.add), r=[S32_b[1], prm_b, st_bf], w=[S32_b[1]])
                                else:
                                    S.op("dve", lambda e, s=s: e.tensor_scalar(
                                        out=S32[1][:, :, :], in0=S32[1][:, :, :], scalar1=chain_ap(s), scalar2=None,
                                        op0=ALU.mult), r=[S32_b[1], prm_b], w=[S32_b[1]])
                    S.dma("sp", S32[0][:, :, :], s0r_d[j, 0, h].rearrange("(a p) v -> p a v", p=128), w=[S32_b[0]])
                    S.op("act", lambda e: e.copy(out=S16f[:, :, :], in_=S32[0][:, :, :]), r=[S32_b[0]], w=[S16f_b])
                    for c in range(NT):
                        s = c // 2
                        tk = slice(c * 128, (c + 1) * 128)
                        qi = c % 2
                        for d in range(2):
                            S.op("dve", lambda e, d=d, qi=qi: e.tensor_tensor(
                                out=qx[qi][:, d, :, :], in0=qT[:, :, tk],
                                in1=Xi[:, d * 4 + h, :].unsqueeze(1).to_broadcast([128, 2, 128]), op=ALU.mult),
                                r=[qk_b[s], lc_b], w=[qx_b[qi]])
                        bkz, bbz = psum()
                        for kt in range(8):
                            S.op("pe", lambda e, kt=kt: e.matmul(bkz[:, :], lhsT=hT[:, kt, tk], rhs=sZv[:, kt, :],
                                                                  start=(kt == 0), stop=(kt == 7)), r=[sZb, hT_b[s]], w=[bbz])
                        S.op("act", lambda e, qi=qi: e.activation(out=zs[qi][:, :], in_=bkz[:, :], func=AF.Exp, scale=-1.0),
                             r=[bbz], w=[zs_b[qi]])
                        S.op("act", lambda e, qi=qi: e.activation(out=zs[qi][:, :], in_=zs[qi][:, :], func=AF.Ln, bias=cpar[:, 2:3],
                                                                  scale=1.0), r=[zs_b[qi], cst_b], w=[zs_b[qi]])
                        S.op("act", lambda e, qi=qi: e.activation(out=zs[qi][:, :], in_=zs[qi][:, :], func=AF.Exp, scale=-1.0),
                             r=[zs_b[qi]], w=[zs_b[qi]])
                        S.op("dve", lambda e, qi=qi: e.tensor_tensor(out=zs[qi][:, :], in0=bkz[:, :], in1=zs[qi][:, :], op=ALU.mult),
                             r=[bbz, zs_b[qi]], w=[zs_b[qi]])
                        bk, bb = psum()
                        S.op("pe", lambda e, bk=bk, c=c: e.matmul(bk[:, :], lhsT=sm[:, c, :], rhs=v16[:, c, :],
                                                                   start=True, stop=False), r=[sm_b[c], v_b[c]], w=[bb])
                        for dt in range(2):
                            S.op("pe", lambda e, bk=bk, dt=dt: e.matmul(bk[:, :], lhsT=qx[qi][:, 0, dt, :], rhs=S16f[:, dt, :],
                                                                         start=False, stop=False),
                                 r=[qx_b[qi], S16f_b], w=[bb])
                        for dt in range(2):
                            S.op("pe", lambda e, bk=bk, dt=dt, c=c: e.matmul(bk[:, :], lhsT=qx[qi][:, 1, dt, :],
                                                                              rhs=Sb16[:, c, dt, :], start=False,
                                                                              stop=(dt == 1)),
                                 r=[qx_b[qi], Sb16_b[c]], w=[bb])
                        S.op("dve", lambda e, bk=bk: e.bn_stats(out=bst[:, 0:6], in_=bk[:, :]), r=[bb], w=[bst_b])
                        S.op("dve", lambda e: e.bn_aggr(out=bst[:, 8:10], in_=bst[:, 0:6]), r=[bst_b], w=[bst_b])
                        S.op("act", lambda e: e.activation(out=bst[:, 10:11], in_=bst[:, 9:10], func=AF.Ln, bias=cpar[:, 1:2],
                                                           scale=1.0), r=[bst_b, cst_b], w=[bst_b])
                        S.op("act", lambda e: e.activation(out=bst[:, 10:11], in_=bst[:, 10:11], func=AF.Exp, scale=-0.5),
                             r=[bst_b], w=[bst_b])
                        S.op("dve", lambda e, bk=bk: e.tensor_scalar(out=on[:, :], in0=bk[:, :], scalar1=bst[:, 8:9],
                                                                     scalar2=bst[:, 10:11], op0=ALU.subtract,
                                                                     op1=ALU.mult), r=[bb, bst_b], w=[on_b])
                        S.op("dve", lambda e: e.tensor_tensor(out=on[:, :], in0=on[:, :], in1=gnw[:, :], op=ALU.mult),
                             r=[on_b, gnw_b], w=[on_b])
                        S.op("dve", lambda e, qi=qi: e.tensor_tensor(out=og[:, :], in0=on[:, :], in1=zs[qi][:, :], op=ALU.mult),
                             r=[on_b, zs_b[qi]], w=[og_b])
                        bk2, bb2 = psum()
                        bk2b = bk2[:, :].bitcast(BF16)
                        for vt in range(4):
                            S.op("pe", lambda e, vt=vt, bk2b=bk2b: e.transpose(
                                bk2b[:, vt * 128:(vt + 1) * 128], og[:, vt * 128:(vt + 1) * 128], ident_b[:]),
                                r=[og_b, cst_b], w=[bb2])
                        oi = s % 2
                        lo = (c % 2) * 128
                        S.op("act", lambda e, bk2b=bk2b, oi=oi, lo=lo: e.copy(
                            out=oT[oi][:, :, lo:lo + 128], in_=bk2b[:, 0:512].rearrange("p (v t) -> p v t", v=4)),
                            r=[bb2], w=[oT_b[oi]])
                        upd_state(0, c, kzf)
                        if c % 2 == 1:
                            out_state(0, s)
                            if c < NT - 1:
                                S.op("dve", lambda e, s=s: e.tensor_scalar(
                                    out=S32[0][:, :, :], in0=S32[0][:, :, :], scalar1=chain_ap(s + 1), scalar2=None,
                                    op0=ALU.mult), r=[S32_b[0], prm_b], w=[S32_b[0]])
                        if c < NT - 1:
                            S.op("act", lambda e: e.copy(out=S16f[:, :, :], in_=S32[0][:, :, :]), r=[S32_b[0]], w=[S16f_b])
                        if c % 2 == 1:
                            seg = slice(s * SEGL, (s + 1) * SEGL)
                            for dt in range(8):
                                bk, bb = psum()
                                for vt in range(4):
                                    S.op("pe", lambda e, vt=vt, dt=dt, bk=bk, oi=oi: e.matmul(
                                        bk[:, 0:SEGL], lhsT=sOv[:, vt, dt * 128:(dt + 1) * 128], rhs=oT[oi][:, vt, :],
                                        start=(vt == 0), stop=(vt == 3)), r=[sOb, oT_b[oi]], w=[bb])
                                S.op("dve", lambda e, bk=bk, dt=dt, s=s: e.scalar_tensor_tensor(
                                    out=xT[:, dt, seg], in0=bk[:, 0:SEGL], scalar=mod_ap(l, 2, dt, s), in1=xT[:, dt, seg],
                                    op0=ALU.mult, op1=ALU.add), r=[bb, modT_b, xT_b[s]], w=[xT_b[s]])
                    ws_release()
                    ws_release()
                S.barrier()

        def del_layer(l, j):
            with contextlib.ExitStack() as ls:
                DH = 8
                tri = [sb(f"tri{d}", [128, 128], st=ls) for d in range(2)]
                neg3 = [sb(f"neg3{d}", [128, 128], st=ls) for d in range(2)]
                pos1 = [sb(f"pos1{d}", [128, 128], st=ls) for d in range(2)]
                dc_b = Buf("dconst")
                for d in range(2):
                    S.op("pool", lambda e, d=d: e.memset(tri[d][:, :], 1.0), w=[dc_b])
                    pat, cm = ([[1, 128]], -1) if d == 0 else ([[-1, 128]], 1)
                    S.op("pool", lambda e, d=d, pat=pat, cm=cm: e.affine_select(
                        out=tri[d][:, :], in_=tri[d][:, :], pattern=pat, compare_op=ALU.is_ge, fill=0.0, base=0,
                        channel_multiplier=cm), r=[dc_b], w=[dc_b])
                    S.op("pool", lambda e, d=d: e.memset(neg3[d][:, :], 0.0), w=[dc_b])
                    S.op("pool", lambda e, d=d, pat=pat, cm=cm: e.affine_select(
                        out=neg3[d][:, :], in_=neg3[d][:, :], pattern=pat, compare_op=ALU.is_ge, fill=-BIG, base=0,
                        channel_multiplier=cm), r=[dc_b], w=[dc_b])
                    pat2, cm2 = ([[-1, 128]], 1) if d == 0 else ([[1, 128]], -1)
                    S.op("pool", lambda e, d=d: e.memset(pos1[d][:, :], 0.0), w=[dc_b])
                    S.op("pool", lambda e, d=d, pat2=pat2, cm2=cm2: e.affine_select(
                        out=pos1[d][:, :], in_=pos1[d][:, :], pattern=pat2, compare_op=ALU.is_gt, fill=BIG, base=0,
                        channel_multiplier=cm2), r=[dc_b], w=[dc_b])
                abraw = sb("abraw", [128, NT, 32], st=ls)
                tk_b = Buf("tokscal")
                tsc = sb("tsc", [128, 10, NT, 16], st=ls)
                U, L1, GG, LNB, BB, GT, EGt, NBEG, NEGG, GPL = [tsc[:, i, :, :] for i in range(10)]
                negA = sb("negA", [128, 16], st=ls)
                dnw = sb("dnw", [128, 256], st=ls)
                dnw_b = Buf("dnw")
                XP = [sb("XP0", [128, NSEG, 258], st=ls)] * 2
                XP_b = [Buf("XP0")] * 2
                acc = sqs[0][:, 0:NSEG, :]
                acc_b = sqs_b[0]
                tmb = sb("tmb", [128, NTOK], BF16, st=ls)
                tmb_b = Buf("tmb")
                rinv = sqs[0][:, 5:7, :].rearrange("p a b -> p (a b)")
                rinv_b = sqs_b[0]
                qT = sb("dqT", [128, NTOK], BF16, st=ls)
                kT = sb("dkT", [128, NTOK], BF16, st=ls)
                q_b, k_b = Buf("dq"), Buf("dk")
                bv = sb("bv", [128, NT * 2, 256], BF16, st=ls)
                bv_b = [Buf(f"bv{c}") for c in range(NT)]
                kd = sb("kd", [128, NT * 2, 128], BF16, st=ls)
                kd_b = [Buf(f"kd{c}") for c in range(NT)]
                TT = sb("TT", [128, NT * 2, 128], BF16, st=ls)
                TT_b = [Buf(f"TT{g}") for g in range(NSEG)]
                PTm = sb("PTm", [128, NT * 2, 128], BF16, st=ls)
                qg = sb("qg", [128, NT * 2, 128], BF16, st=ls)
                pq_b = [Buf(f"pq{c}") for c in range(NT)]
                csc = sb("csc", [128, 2, NT * 2], st=ls)
                csc_b = [Buf(f"csc{c}") for c in range(NT)]
                Es = [sb(f"Es{i}", [128, 3, 128], st=ls) for i in range(2)]
                Es_b = [Buf(f"Es{i}") for i in range(2)]
                def g4(nm):
                    return sb(nm, [128, 4, 128], BF16, st=ls), Buf(nm)
                Xg, Xg_b = zip(*[g4(f"Xg{i}") for i in range(2)])
                Yg, Yg_b = zip(*[g4(f"Yg{i}") for i in range(2)])
                Pg, Pg_b = zip(*[g4(f"Pg{i}") for i in range(2)])
                Qg, Qg_b = zip(*[g4(f"Qg{i}") for i in range(2)])
                Ng, Ng_b = zip(*[g4(f"Ng{i}") for i in range(2)])
                Wg, Wg_b = zip(*[g4(f"Wg{i}") for i in range(2)])
                Af, Af_b = g4("Af")
                Bf, Bf_b = g4("Bf")
                bmask = sb("bmask", [128, 4, 128], BF16, st=ls)
                S.dma("pool", bmask[:, :, :].rearrange("p a b -> p (a b)"), bmask_d, w=[dc_b])
                ob = sb("ob", [128, NT, 256], st=ls)
                ob_b = [Buf(f"ob{c}") for c in range(NT)]
                S32 = [sb(f"dS32_{d}", [128, 256], st=ls) for d in range(2)]
                S32_b = [Buf(f"dS32_{d}") for d in range(2)]
                S16 = [sb(f"dS16_{d}", [128, 256], BF16, st=ls) for d in range(2)]
                S16_b = [Buf(f"dS16_{d}") for d in range(2)]
                rr = [sb(f"rr{d}", [128, 256], BF16, st=ls) for d in range(2)]
                rr_b = [Buf(f"rr{d}") for d in range(2)]
                vn16 = [sb(f"vn{d}", [128, 256], BF16, st=ls) for d in range(2)]
                vn_b = [Buf(f"vn{d}") for d in range(2)]
                zs = [sb(f"dzs{i}", [128, 256], st=ls) for i in range(2)]
                zs_b = [Buf(f"dzs{i}") for i in range(2)]
                on = [sb(f"don{i}", [128, 256], st=ls) for i in range(2)]
                on_b = [Buf(f"don{i}") for i in range(2)]
                og = [sb(f"dog{i}", [128, 256], BF16, st=ls) for i in range(2)]
                og_b = [Buf(f"dog{i}") for i in range(2)]
                oT = [sb(f"doT{i}", [128, 2, SEGL], BF16, st=ls) for i in range(NSEG)]
                oT_b = [Buf(f"doT{i}") for i in range(NSEG)]
                bst = sb("dbst", [128, 2, 8], st=ls)
                bst_b = [Buf("dbst0"), Buf("dbst1")]

                norm_mod(l)

                sAB, sABb = ws_get()
                sABv = v8(sAB, 32)
                for c in range(NT):
                    tk = slice(c * 128, (c + 1) * 128)
                    bk, bb = psum()
                    for kt in range(8):
                        S.op("pe", lambda e, bk=bk, kt=kt: e.matmul(bk[:, 0:32], lhsT=hT[:, kt, tk], rhs=sABv[:, kt, :],
                                                                     start=(kt == 0), stop=(kt == 7)),
                             r=[sABb, hT_b[c // 2]], w=[bb])
                    S.op("act", lambda e, bk=bk, c=c: e.copy(out=abraw[:, c, :], in_=bk[:, 0:32]), r=[bb], w=[tk_b])
                ws_release()
                al = prm[:, o_alog + j * 16:o_alog + j * 16 + 16]
                dtb = prm[:, o_dtb + j * 16:o_dtb + j * 16 + 16]
                T = [tk_b, prm_b, cst_b]
                S.op("act", lambda e: e.activation(out=negA[:, :], in_=al, func=AF.Exp), r=T, w=[tk_b])
                S.op("dve", lambda e: e.tensor_scalar(out=negA[:, :], in0=negA[:, :], scalar1=-1.0, scalar2=None,
                                                      op0=ALU.mult), r=T, w=[tk_b])
                S.op("dve", lambda e: e.tensor_tensor(out=U, in0=abraw[:, :, 0:16],
                                                      in1=dtb.unsqueeze(1).to_broadcast([128, NT, 16]), op=ALU.add),
                     r=T, w=[tk_b])
                S.op("dve", lambda e: e.tensor_scalar(out=L1, in0=U, scalar1=-1.0, scalar2=None, op0=ALU.mult), r=T, w=[tk_b])
                S.op("dve", lambda e: e.tensor_tensor(out=L1, in0=L1, in1=U, op=ALU.max), r=T, w=[tk_b])
                S.op("act", lambda e: e.activation(out=L1, in_=L1, func=AF.Exp, scale=-1.0), r=T, w=[tk_b])
                S.op("act", lambda e: e.activation(out=L1, in_=L1, func=AF.Ln, bias=cpar[:, 2:3], scale=1.0), r=T, w=[tk_b])
                S.op("dve", lambda e: e.tensor_scalar(out=U, in0=U, scalar1=0.0, scalar2=None, op0=ALU.max), r=T, w=[tk_b])
                S.op("dve", lambda e: e.tensor_tensor(out=U, in0=U, in1=L1, op=ALU.add), r=T, w=[tk_b])
                S.op("dve", lambda e: e.tensor_tensor(out=GG, in0=U, in1=negA[:, :].unsqueeze(1).to_broadcast([128, NT, 16]),
                                                      op=ALU.mult), r=T, w=[tk_b])
                S.op("dve", lambda e: e.tensor_scalar(out=L1, in0=abraw[:, :, 16:32], scalar1=-1.0, scalar2=None, op0=ALU.mult),
                     r=T, w=[tk_b])
                S.op("dve", lambda e: e.tensor_tensor(out=L1, in0=L1, in1=abraw[:, :, 16:32], op=ALU.max), r=T, w=[tk_b])
                S.op("act", lambda e: e.activation(out=L1, in_=L1, func=AF.Exp, scale=-1.0), r=T, w=[tk_b])
                S.op("act", lambda e: e.activation(out=L1, in_=L1, func=AF.Ln, bias=cpar[:, 2:3], scale=1.0), r=T, w=[tk_b])
                S.op("dve", lambda e: e.tensor_scalar(out=LNB, in0=abraw[:, :, 16:32], scalar1=0.0, scalar2=None,
                                                      op0=ALU.min), r=T, w=[tk_b])
                S.op("dve", lambda e: e.tensor_tensor(out=LNB, in0=LNB, in1=L1, op=ALU.subtract), r=T, w=[tk_b])
                S.op("act", lambda e: e.activation(out=BB, in_=LNB, func=AF.Exp), r=T, w=[tk_b])
                bk, bb = psum()
                for c in range(NT):
                    for d in range(2):
                        S.op("pe", lambda e, c=c, d=d: e.matmul(bk[:, c * 16 + d * 8:c * 16 + d * 8 + 8], lhsT=tri[d][:, :],
                                                                 rhs=GG[:, c, d * 8:d * 8 + 8], start=True, stop=True),
                             r=[tk_b, dc_b], w=[bb])
                S.op("dve", lambda e: e.tensor_copy(out=GT, in_=bk[:, 0:NT * 16].rearrange("p (c x) -> p c x", c=NT)),
                     r=[bb], w=[tk_b])
                S.op("act", lambda e: e.activation(out=EGt, in_=GT, func=AF.Exp), r=T, w=[tk_b])
                S.op("dve", lambda e: e.tensor_tensor(out=NBEG, in0=BB, in1=EGt, op=ALU.mult), r=T, w=[tk_b])
                S.op("dve", lambda e: e.tensor_scalar(out=NBEG, in0=NBEG, scalar1=-1.0, scalar2=None, op0=ALU.mult),
                     r=T, w=[tk_b])
                S.op("dve", lambda e: e.tensor_scalar(out=NEGG, in0=GT, scalar1=-1.0, scalar2=None, op0=ALU.mult),
                     r=T, w=[tk_b])
                S.op("dve", lambda e: e.tensor_tensor(out=GPL, in0=GT, in1=LNB, op=ALU.add), r=T, w=[tk_b])
                for i in range(2):
                    S.op("dve", lambda e, i=i: e.memset(XP[i][:, 0, 0:1], 0.0), w=[XP_b[i]])
                    S.op("dve", lambda e, i=i: e.memset(XP[i][:, NSEG - 1, 257:258], 0.0), w=[XP_b[i]])

                dstage = float(os.environ.get("K_DSTAGE", "9"))
                if dstage == 0:
                    S.barrier()
                    return
                xp_rr = [0]
                for h in range(int(os.environ.get("K_DHEADS", "8"))):
                    sW, sWb = ws_get()
                    sWv = v8(sW)
                    S.dma("sp", dnw[:, :], dnw_d[:, j * 2048 + h * 256:j * 2048 + (h + 1) * 256], w=[dnw_b])
                    for ct in range(4):
                        xi = xp_rr[0] % 2
                        xp_rr[0] += 1
                        xp, xpb = XP[xi], XP_b[xi]
                        for s in range(NSEG):
                            bk, bb = psum()
                            for kt in range(8):
                                S.op("pe", lambda e, bk=bk, kt=kt, s=s: e.matmul(
                                    bk[:, 0:SEGL], lhsT=sWv[:, kt, ct * 128:(ct + 1) * 128],
                                    rhs=hT[:, kt, s * SEGL:(s + 1) * SEGL], start=(kt == 0), stop=(kt == 7)),
                                    r=[sWb, hT_b[s]], w=[bb])
                            S.op("act", lambda e, bk=bk, s=s: e.copy(out=xp[:, s, 1:257], in_=bk[:, 0:SEGL]), r=[bb], w=[xpb])
                        chn = prm[:, o_chain + 1:o_chain + 5]
                        S.op("dve", lambda e: e.tensor_tensor(out=xp[:, 1:5, 0], in0=xp[:, 0:4, 256], in1=chn, op=ALU.mult),
                             r=[xpb, prm_b], w=[xpb])
                        S.op("dve", lambda e: e.tensor_tensor(out=xp[:, 0:4, 257], in0=xp[:, 1:5, 1], in1=chn, op=ALU.mult),
                             r=[xpb, prm_b], w=[xpb])
                        gct = h if ct == 0 else (8 + h if ct == 1 else 16 + 2 * h + (ct - 2))
                        cw = [prm[:, o_convw + (j * 3 + k) * 32 + gct:o_convw + (j * 3 + k) * 32 + gct + 1] for k in range(3)]
                        S.op("dve", lambda e: e.tensor_scalar(out=acc, in0=xp[:, :, 0:256], scalar1=cw[0], scalar2=None,
                                                              op0=ALU.mult), r=[xpb, prm_b], w=[acc_b])
                        S.op("dve", lambda e: e.scalar_tensor_tensor(out=acc, in0=xp[:, :, 1:257], scalar=cw[1],
                                                                     in1=acc, op0=ALU.mult, op1=ALU.add),
                             r=[xpb, prm_b, acc_b], w=[acc_b])
                        S.op("dve", lambda e: e.scalar_tensor_tensor(out=acc, in0=xp[:, :, 2:258], scalar=cw[2],
                                                                     in1=acc, op0=ALU.mult, op1=ALU.add),
                             r=[xpb, prm_b, acc_b], w=[acc_b])
                        accf = acc.rearrange("p s t -> p (s t)")
                        S.op("act", lambda e: e.activation(out=accf, in_=accf, func=AF.Silu), r=[acc_b], w=[acc_b])
                        if ct < 2:
                            S.op("act", lambda e: e.activation(out=tmb[:, :], in_=accf, func=AF.Square), r=[acc_b], w=[tmb_b])
                            dst, dstb = (qT, q_b) if ct == 0 else (kT, k_b)
                            scl = (128.0 ** -0.5) if ct == 0 else 1.0
                            for (t0, n) in BLOCKS:
                                bk, bb = psum()
                                S.op("pe", lambda e, bk=bk: e.matmul(bk[:, 0:n], lhsT=ones_b[:, :], rhs=tmb[:, t0:t0 + n],
                                                                      start=True, stop=True), r=[tmb_b, cst_b], w=[bb])
                                S.op("act", lambda e, bk=bk: e.activation(out=rinv[:, 0:n], in_=bk[:, 0:n], func=AF.Ln,
                                                                          bias=cpar[:, 1:2], scale=1.0),
                                     r=[bb, cst_b], w=[rinv_b])
                                S.op("act", lambda e: e.activation(out=rinv[:, 0:n], in_=rinv[:, 0:n], func=AF.Exp, scale=-0.5),
                                     r=[rinv_b], w=[rinv_b])
                                S.op("dve", lambda e, dst=dst: e.scalar_tensor_tensor(
                                    out=dst[:, t0:t0 + n], in0=accf[:, t0:t0 + n], scalar=scl, in1=rinv[:, 0:n],
                                    op0=ALU.mult, op1=ALU.mult), r=[acc_b, rinv_b], w=[dstb])
                        else:
                            vt = ct - 2
                            S.op("act", lambda e: e.copy(out=tmb[:, :], in_=accf), r=[acc_b], w=[tmb_b])
                            for c in range(NT):
                                tk = slice(c * 128, (c + 1) * 128)
                                bk, bb = psum()
                                bkb = bk[:, :].bitcast(BF16)
                                S.op("pe", lambda e, bkb=bkb: e.transpose(bkb[:, 0:128], tmb[:, tk], ident_b[:]),
                                     r=[tmb_b, cst_b], w=[bb])
                                S.op("dve", lambda e, bkb=bkb, c=c: e.tensor_scalar(
                                    out=bv[:, c * 2 + 0, vt * 128:(vt + 1) * 128], in0=bkb[:, 0:128],
                                    scalar1=BB[:, c, h:h + 1], scalar2=None, op0=ALU.mult), r=[bb, tk_b], w=[bv_b[c]])
                                S.op("act", lambda e, bkb=bkb, c=c: e.activation(
                                    out=bv[:, c * 2 + 1, vt * 128:(vt + 1) * 128], in_=bkb[:, 0:128], func=AF.Copy,
                                    scale=BB[:, c, 8 + h:9 + h]), r=[bb, tk_b], w=[bv_b[c]])
                    ws_release()
                    if dstage == 1:
                        S.barrier()
                        return
                    sZ, sZb = ws_get()
                    sO, sOb = ws_get()
                    sZv = v8(sZ, 256)
                    sOv = sO[:, 0:2048].rearrange("p (v c) -> p v c", v=2)

                    for g in range(NSEG):
                        kkps = []
                        for ci in range(2):
                            c = 2 * g + ci
                            tk = slice(c * 128, (c + 1) * 128)
                            bk, bb = psum()
                            S.op("pe", lambda e, bk=bk: e.matmul(bk[:, 0:128], lhsT=kT[:, tk], rhs=kT[:, tk], start=True, stop=True),
                                 r=[k_b], w=[bb])
                            S.op("pe", lambda e, bk=bk: e.matmul(bk[:, 128:256], lhsT=kT[:, tk], rhs=qT[:, tk], start=True,
                                                                  stop=True), r=[k_b, q_b], w=[bb])
                            bkt = bk[:, :].bitcast(BF16)
                            S.op("pe", lambda e, bkt=bkt: e.transpose(bkt[:, 512:640], kT[:, tk], ident_b[:]),
                                 r=[k_b, cst_b], w=[bb])
                            kkps.append((bk, bb, bkt))
                        if dstage == 1.1:
                            S.barrier()
                            return
                        xg, xgb = Af, Af_b
                        for ci in range(2):
                            c = 2 * g + ci
                            tk = slice(c * 128, (c + 1) * 128)
                            bk, bb, bkt = kkps[ci]
                            for d in range(2):
                                qd = ci * 2 + d
                                cd = c * 2 + d
                                dh = d * 8 + h
                                ei = qd % 2
                                es, esb = Es[ei], Es_b[ei]
                                be, bbe = psum()
                                gcol = GG[:, c, dh:dh + 1].to_broadcast([128, 128])
                                S.op("pe", lambda e, be=be, d=d: e.matmul(be[:, 0:128], lhsT=gcol, rhs=tri[d][:, :], start=True,
                                                                           stop=True), r=[tk_b, dc_b], w=[bbe])
                                S.op("pe", lambda e, be=be, d=d: e.matmul(be[:, 128:256], lhsT=gcol, rhs=tri[d][:, :], start=True,
                                                                           stop=False), r=[tk_b, dc_b], w=[bbe])
                                S.op("pe", lambda e, be=be, d=d: e.matmul(be[:, 128:256], lhsT=ident_f[:, :], rhs=neg3[d][:, :],
                                                                           start=False, stop=True), r=[cst_b, dc_b], w=[bbe])
                                S.op("pe", lambda e, be=be, d=d: e.matmul(be[:, 256:384], lhsT=gcol, rhs=tri[d][:, :], start=True,
                                                                           stop=False), r=[tk_b, dc_b], w=[bbe])
                                S.op("pe", lambda e, be=be, d=d: e.matmul(be[:, 256:384], lhsT=ident_f[:, :], rhs=pos1[d][:, :],
                                                                           start=False, stop=True), r=[cst_b, dc_b], w=[bbe])
                                S.op("act", lambda e, be=be, es=es: e.activation(out=es[:, 0, :], in_=be[:, 0:128], func=AF.Exp),
                                     r=[bbe], w=[esb])
                                S.op("act", lambda e, be=be, es=es, c=c, dh=dh: e.activation(
                                    out=es[:, 1, :], in_=be[:, 128:256], func=AF.Exp, bias=NEGG[:, c, dh:dh + 1], scale=1.0),
                                    r=[bbe, tk_b], w=[esb])
                                S.op("act", lambda e, be=be, es=es, c=c, dh=dh: e.activation(
                                    out=es[:, 2, :], in_=be[:, 256:384], func=AF.Exp, bias=GPL[:, c, dh:dh + 1], scale=-1.0),
                                    r=[bbe, tk_b], w=[esb])
                                last = 127 if d == 0 else 0
                                S.op("dve", lambda e, es=es, cd=cd, last=last: e.tensor_copy(
                                    out=csc[:, 0, cd:cd + 1], in_=es[:, 0, last:last + 1]), r=[esb], w=[csc_b[c]])
                                S.op("dve", lambda e, es=es, cd=cd, last=last: e.tensor_copy(
                                    out=csc[:, 1, cd:cd + 1], in_=es[:, 1, last:last + 1]), r=[esb], w=[csc_b[c]])
                                S.op("dve", lambda e, es=es, bk=bk, qd=qd: e.tensor_tensor(
                                    out=xg[:, qd, :], in0=bk[:, 0:128], in1=es[:, 2, :], op=ALU.mult), r=[bb, esb], w=[xgb])
                                S.op("dve", lambda e, es=es, bk=bk, cd=cd: e.tensor_tensor(
                                    out=PTm[:, cd, :], in0=bk[:, 128:256], in1=es[:, 1, :], op=ALU.mult), r=[bb, esb], w=[pq_b[c]])
                                S.op("dve", lambda e, es=es, cd=cd: e.tensor_tensor(
                                    out=qg[:, cd, :], in0=qT[:, tk], in1=es[:, 0, :], op=ALU.mult), r=[q_b, esb], w=[pq_b[c]])
                                if d == 0:
                                    S.op("dve", lambda e, bkt=bkt, cd=cd: e.tensor_scalar(
                                        out=kd[:, cd, :], in0=bkt[:, 512:640], scalar1=csc[:, 1, cd:cd + 1], scalar2=None,
                                        op0=ALU.mult), r=[bb, csc_b[c]], w=[kd_b[c]])
                                else:
                                    S.op("act", lambda e, bkt=bkt, cd=cd: e.activation(
                                        out=kd[:, cd, :], in_=bkt[:, 512:640], func=AF.Copy, scale=csc[:, 1, cd:cd + 1]),
                                        r=[bb, csc_b[c]], w=[kd_b[c]])
                        if dstage == 1.2:
                            S.barrier()
                            return
                        bt, bbt = psum()
                        btb = bt[:, :].bitcast(BF16)
                        for qd in range(4):
                            S.op("pe", lambda e, qd=qd, btb=btb: e.transpose(btb[:, qd * 128:(qd + 1) * 128], xg[:, qd, :],
                                                                             ident_b[:]), r=[xgb, cst_b], w=[bbt])
                        bt4 = btb[:, 0:512].rearrange("p (q i) -> p q i", q=4)
                        S.op("act", lambda e: e.copy(out=Bf[:, :, :], in_=bt4), r=[bbt], w=[Bf_b])

                        def mk(k_):
                            return bmask[:, k_, :].unsqueeze(1).to_broadcast([128, 4, 128])

                        def masked(dst, dstb, srcT, srcb, k_):
                            S.op("pool", lambda e: e.tensor_tensor(out=dst[:, :, :], in0=srcT[:, :, :], in1=mk(k_), op=ALU.mult),
                                 r=[srcb, dc_b], w=[dstb])

                        masked(Xg[0], Xg_b[0], xg, xgb, 0)
                        masked(Yg[0], Yg_b[0], Bf, Bf_b, 0)
                        idb = ident_b[:, :].unsqueeze(1).to_broadcast([128, 4, 128])
                        S.op("dve", lambda e: e.scalar_tensor_tensor(out=Pg[0][:, :, :], in0=Yg[0][:, :, :], scalar=-1.0, in1=idb,
                                                                      op0=ALU.mult, op1=ALU.add), r=[Yg_b[0], cst_b], w=[Pg_b[0]])
                        S.op("dve", lambda e: e.scalar_tensor_tensor(out=Qg[0][:, :, :], in0=Xg[0][:, :, :], scalar=-1.0, in1=idb,
                                                                      op0=ALU.mult, op1=ALU.add), r=[Xg_b[0], cst_b], w=[Qg_b[0]])
                        if dstage == 1.3:
                            S.barrier()
                            return

                        def mm4(L, Lb, R, Rb):
                            bk_, bb_ = psum()
                            for qd in range(4):
                                S.op("pe", lambda e, qd=qd: e.matmul(bk_[:, qd * 128:(qd + 1) * 128], lhsT=L[:, qd, :], rhs=R[:, qd, :],
                                                                      start=True, stop=True), r=[Lb, Rb], w=[bb_])
                            return bk_[:, :].rearrange("p (q i) -> p q i", q=4), bb_

                        def ev_copy(dst, dstb, ps, psb):
                            S.op("act", lambda e: e.copy(out=dst[:, :, :], in_=ps), r=[psb], w=[dstb])

                        def ev_comb(dst, dstb, ps, psb, old, oldb, op):
                            if op == "add":
                                S.op("dve", lambda e: e.tensor_tensor(out=dst, in0=ps, in1=old[:, :, :], op=ALU.add),
                                     r=[psb, oldb], w=[dstb])
                            else:
                                S.op("dve", lambda e: e.scalar_tensor_tensor(out=dst, in0=ps, scalar=-1.0, in1=old[:, :, :],
                                                                             op0=ALU.mult, op1=ALU.add), r=[psb, oldb], w=[dstb])

                        pi = 0
                        for m in range(1, 4):
                            a, b2 = (m - 1) % 2, m % 2
                            px, pxb = mm4(Yg[a], Yg_b[a], Xg[a], Xg_b[a])
                            py, pyb = mm4(Xg[a], Xg_b[a], Yg[a], Yg_b[a])
                            ev_copy(Xg[b2], Xg_b[b2], px, pxb)
                            ev_copy(Yg[b2], Yg_b[b2], py, pyb)
                            pp, ppb = mm4(Xg[b2], Xg_b[b2], Pg[pi], Pg_b[pi])
                            pq, pqb = mm4(Yg[b2], Yg_b[b2], Qg[pi], Qg_b[pi])
                            ev_comb(Pg[1 - pi][:, :, :], Pg_b[1 - pi], pp, ppb, Pg[pi], Pg_b[pi], "add")
                            ev_comb(Qg[1 - pi][:, :, :], Qg_b[1 - pi], pq, pqb, Qg[pi], Qg_b[pi], "add")
                            pi = 1 - pi
                        for s_ in range(1, 4):
                            masked(Ng[0], Ng_b[0], xg, xgb, s_)
                            pw1, pw1b = mm4(Ng[0], Ng_b[0], Pg[pi], Pg_b[pi])
                            ev_copy(Wg[0], Wg_b[0], pw1, pw1b)
                            if s_ < 3:
                                masked(Ng[1], Ng_b[1], Bf, Bf_b, s_)
                                pw2, pw2b = mm4(Ng[1], Ng_b[1], Qg[pi], Qg_b[pi])
                                ev_copy(Wg[1], Wg_b[1], pw2, pw2b)
                            pp, ppb = mm4(Qg[pi], Qg_b[pi], Wg[0], Wg_b[0])
                            if s_ < 3:
                                pq, pqb = mm4(Pg[pi], Pg_b[pi], Wg[1], Wg_b[1])
                                ev_comb(Pg[1 - pi][:, :, :], Pg_b[1 - pi], pp, ppb, Pg[pi], Pg_b[pi], "sub")
                                ev_comb(Qg[1 - pi][:, :, :], Qg_b[1 - pi], pq, pqb, Qg[pi], Qg_b[pi], "sub")
                                pi = 1 - pi
                            else:
                                ev_comb(TT[:, g * 4:(g + 1) * 4, :], TT_b[g], pp, ppb, Pg[pi], Pg_b[pi], "sub")

                    if dbg_d and h == 0:
                        def dump(name, ap, bufs):
                            if name in dbg_d:
                                S.dma("pool", dbg_d[name], ap, r=bufs)
                        dump("tsc", tsc[:, :, :, :].rearrange("p a c x -> p (a c x)"), [tk_b])
                        dump("qT", qT[:, :], [q_b])
                        dump("kT", kT[:, :], [k_b])
                        dump("bv", bv[:, :, :].rearrange("p a b -> p (a b)"), bv_b)
                        dump("kd", kd[:, :, :].rearrange("p a b -> p (a b)"), kd_b)
                        dump("TT", TT[:, :, :].rearrange("p a b -> p (a b)"), TT_b)
                        dump("PTm", PTm[:, :, :].rearrange("p a b -> p (a b)"), pq_b)
                        dump("qg", qg[:, :, :].rearrange("p a b -> p (a b)"), pq_b)
                        dump("csc", csc[:, :, :].rearrange("p a b -> p (a b)"), csc_b)
                    if dstage == 2:
                        S.barrier()
                        return
                    S.dma("sp", S32[0][:, :], s0d_d[j, 0, h], w=[S32_b[0]])
                    S.op("dve", lambda e: e.memset(S32[1][:, :], 0.0), w=[S32_b[1]])
                    for d in range(2):
                        S.op("act", lambda e, d=d: e.copy(out=S16[d][:, :], in_=S32[d][:, :]), r=[S32_b[d]], w=[S16_b[d]])
                    arrived = [0] * NT
                    seg_done = [0] * NSEG

                    def consume_o(c, bo, bbo, first):
                        if first:
                            S.op("act", lambda e: e.copy(out=ob[:, c, :], in_=bo[:, 0:256]), r=[bbo], w=[ob_b[c]])
                        else:
                            oi = c % 2
                            S.op("dve", lambda e: e.tensor_tensor(out=on[oi][:, :], in0=bo[:, 0:256], in1=ob[:, c, :], op=ALU.add),
                                 r=[bbo, ob_b[c]], w=[on_b[oi]])

                    def finish_chunk(c, first):
                        s = c // 2
                        if first:
                            return
                        oi = c % 2
                        tk = slice(c * 128, (c + 1) * 128)
                        bkz, bbz = psum()
                        for kt in range(8):
                            S.op("pe", lambda e, kt=kt: e.matmul(bkz[:, 0:256], lhsT=hT[:, kt, tk], rhs=sZv[:, kt, :],
                                                                  start=(kt == 0), stop=(kt == 7)), r=[sZb, hT_b[s]], w=[bbz])
                        S.op("act", lambda e: e.activation(out=zs[oi][:, :], in_=bkz[:, 0:256], func=AF.Exp, scale=-1.0),
                             r=[bbz], w=[zs_b[oi]])
                        S.op("act", lambda e: e.activation(out=zs[oi][:, :], in_=zs[oi][:, :], func=AF.Ln, bias=cpar[:, 2:3], scale=1.0),
                             r=[zs_b[oi], cst_b], w=[zs_b[oi]])
                        S.op("act", lambda e: e.activation(out=zs[oi][:, :], in_=zs[oi][:, :], func=AF.Exp, scale=-1.0),
                             r=[zs_b[oi]], w=[zs_b[oi]])
                        S.op("dve", lambda e: e.tensor_tensor(out=zs[oi][:, :], in0=bkz[:, 0:256], in1=zs[oi][:, :], op=ALU.mult),
                             r=[bbz, zs_b[oi]], w=[zs_b[oi]])
                        S.op("dve", lambda e: e.bn_stats(out=bst[:, oi, 0:6], in_=on[oi][:, :]), r=[on_b[oi]], w=[bst_b[oi]])
                        S.op("dve", lambda e: e.bn_aggr(out=bst[:, oi, 6:8], in_=bst[:, oi, 0:6]), r=[bst_b[oi]], w=[bst_b[oi]])
                        S.op("dve", lambda e: e.scalar_tensor_tensor(out=bst[:, oi, 0:1], in0=bst[:, oi, 6:7], scalar=bst[:, oi, 6:7],
                                                                     in1=bst[:, oi, 7:8], op0=ALU.mult, op1=ALU.add),
                             r=[bst_b[oi]], w=[bst_b[oi]])
                        S.op("act", lambda e: e.activation(out=bst[:, oi, 1:2], in_=bst[:, oi, 0:1], func=AF.Ln, bias=cpar[:, 1:2],
                                                           scale=1.0), r=[bst_b[oi], cst_b], w=[bst_b[oi]])
                        S.op("act", lambda e: e.activation(out=bst[:, oi, 1:2], in_=bst[:, oi, 1:2], func=AF.Exp, scale=-0.5),
                             r=[bst_b[oi]], w=[bst_b[oi]])
                        S.op("dve", lambda e: e.scalar_tensor_tensor(out=on[oi][:, :], in0=on[oi][:, :], scalar=bst[:, oi, 1:2],
                                                                     in1=dnw[:, :], op0=ALU.mult, op1=ALU.mult),
                             r=[on_b[oi], bst_b[oi], dnw_b], w=[on_b[oi]])
                        S.op("dve", lambda e: e.tensor_tensor(out=og[oi][:, :], in0=on[oi][:, :], in1=zs[oi][:, :], op=ALU.mult),
                             r=[on_b[oi], zs_b[oi]], w=[og_b[oi]])
                        bk2, bb2 = psum()
                        bk2b = bk2[:, :].bitcast(BF16)
                        for vt in range(2):
                            S.op("pe", lambda e, vt=vt: e.transpose(bk2b[:, vt * 128:(vt + 1) * 128],
                                                                    og[oi][:, vt * 128:(vt + 1) * 128], ident_b[:]),
                                 r=[og_b[oi], cst_b], w=[bb2])
                        lo = (c % 2) * 128
                        S.op("act", lambda e: e.copy(out=oT[s][:, :, lo:lo + 128],
                                                     in_=bk2b[:, 0:256].rearrange("p (v t) -> p v t", v=2)), r=[bb2], w=[oT_b[s]])
                        seg_done[s] += 1
                        if seg_done[s] == 2:
                            seg = slice(s * SEGL, (s + 1) * SEGL)
                            for dt in range(8):
                                bk, bb = psum()
                                for vt in range(2):
                                    S.op("pe", lambda e, vt=vt, dt=dt, bk=bk: e.matmul(
                                        bk[:, 0:SEGL], lhsT=sOv[:, vt, dt * 128:(dt + 1) * 128], rhs=oT[s][:, vt, :],
                                        start=(vt == 0), stop=(vt == 1)), r=[sOb, oT_b[s]], w=[bb])
                                S.op("dve", lambda e, bk=bk, dt=dt: e.scalar_tensor_tensor(
                                    out=xT[:, dt, seg], in0=bk[:, 0:SEGL], scalar=mod_ap(l, 2, dt, s), in1=xT[:, dt, seg],
                                    op0=ALU.mult, op1=ALU.add), r=[bb, modT_b, xT_b[s]], w=[xT_b[s]])

                    for step in range(NT):
                        cs_ = [step, NT - 1 - step]
                        st1 = []
                        for d in range(2):
                            c = cs_[d]
                            tk = slice(c * 128, (c + 1) * 128)
                            bk, bb = psum()
                            S.op("pe", lambda e, bk=bk, d=d: e.matmul(bk[:, 0:256], lhsT=kT[:, tk], rhs=S16[d][:, :], start=True,
                                                                       stop=True), r=[k_b, S16_b[d]], w=[bb])
                            st1.append((bk, bb))
                        for d in range(2):
                            c = cs_[d]
                            bk, bb = st1[d]
                            S.op("dve", lambda e, bk=bk, d=d, c=c: e.scalar_tensor_tensor(
                                out=rr[d][:, :], in0=bk[:, 0:256], scalar=NBEG[:, c, d * 8 + h:d * 8 + h + 1],
                                in1=bv[:, c * 2 + d, :], op0=ALU.mult, op1=ALU.add), r=[bb, tk_b, bv_b[c]], w=[rr_b[d]])
                        st3 = []
                        for d in range(2):
                            c = cs_[d]
                            bk, bb = psum()
                            S.op("pe", lambda e, bk=bk, d=d, c=c: e.matmul(bk[:, 0:256], lhsT=TT[:, c * 2 + d, :], rhs=rr[d][:, :],
                                                                            start=True, stop=True), r=[TT_b[c // 2], rr_b[d]], w=[bb])
                            st3.append((bk, bb))
                        for d in range(2):
                            bk, bb = st3[d]
                            S.op("act", lambda e, bk=bk, d=d: e.copy(out=vn16[d][:, :], in_=bk[:, 0:256]), r=[bb], w=[vn_b[d]])
                        st5 = []
                        for d in range(2):
                            c = cs_[d]
                            bo, bbo = psum()
                            S.op("pe", lambda e, bo=bo, d=d, c=c: e.matmul(bo[:, 0:256], lhsT=qg[:, c * 2 + d, :], rhs=S16[d][:, :],
                                                                            start=True, stop=False), r=[pq_b[c], S16_b[d]], w=[bbo])
                            S.op("pe", lambda e, bo=bo, d=d, c=c: e.matmul(bo[:, 0:256], lhsT=PTm[:, c * 2 + d, :], rhs=vn16[d][:, :],
                                                                            start=False, stop=True), r=[pq_b[c], vn_b[d]], w=[bbo])
                            S.op("pe", lambda e, bo=bo, d=d, c=c: e.matmul(bo[:, 256:512], lhsT=kd[:, c * 2 + d, :], rhs=vn16[d][:, :],
                                                                            start=True, stop=True), r=[kd_b[c], vn_b[d]], w=[bbo])
                            st5.append((bo, bbo))
                        for d in range(2):
                            c = cs_[d]
                            s = c // 2
                            bo, bbo = st5[d]
                            cd = c * 2 + d
                            S.op("dve", lambda e, bo=bo, d=d, cd=cd: e.scalar_tensor_tensor(
                                out=S32[d][:, :], in0=S32[d][:, :], scalar=csc[:, 0, cd:cd + 1], in1=bo[:, 256:512],
                                op0=ALU.mult, op1=ALU.add), r=[bbo, csc_b[c], S32_b[d]], w=[S32_b[d]])
                            seg_end = (c % 2 == 1) if d == 0 else (c % 2 == 0)
                            if seg_end:
                                st_t, st_bf = stage()
                                S.op("act", lambda e, st_t=st_t, d=d: e.copy(out=st_t[:, 0:256], in_=S32[d][:, :]),
                                     r=[S32_b[d]], w=[st_bf])
                                S.dma("sp", nsd_d[s, j, d, h], st_t[:, 0:256], r=[st_bf])
                                if d == 0 and c < NT - 1:
                                    S.op("dve", lambda e, s=s: e.tensor_scalar(out=S32[0][:, :], in0=S32[0][:, :],
                                                                               scalar1=chain_ap(s + 1), scalar2=None, op0=ALU.mult),
                                         r=[S32_b[0], prm_b], w=[S32_b[0]])
                                if d == 1 and c > 0:
                                    if s - 1 == 3:
                                        st2, st2b = stage()
                                        S.dma("sp", st2[:, 0:256], s0d_d[j, 1, h], w=[st2b])
                                        S.op("dve", lambda e, s=s, st2=st2: e.scalar_tensor_tensor(
                                            out=S32[1][:, :], in0=S32[1][:, :], scalar=chain_ap(s), in1=st2[:, 0:256],
                                            op0=ALU.mult, op1=ALU.add), r=[S32_b[1], prm_b, st2b], w=[S32_b[1]])
                                    else:
                                        S.op("dve", lambda e, s=s: e.tensor_scalar(out=S32[1][:, :], in0=S32[1][:, :],
                                                                                   scalar1=chain_ap(s), scalar2=None, op0=ALU.mult),
                                             r=[S32_b[1], prm_b], w=[S32_b[1]])
                            if step < NT - 1:
                                S.op("act", lambda e, d=d: e.copy(out=S16[d][:, :], in_=S32[d][:, :]), r=[S32_b[d]], w=[S16_b[d]])
                        for d in range(2):
                            c = cs_[d]
                            bo, bbo = st5[d]
                            arrived[c] += 1
                            consume_o(c, bo, bbo, arrived[c] == 1)
                        for d in range(2):
                            c = cs_[d]
                            finish_chunk(c, arrived[c] == 1)
                    ws_release()
                    ws_release()
                S.barrier()

        def final_out():
            for s in range(NSEG):
                sq, sqb, rs, rsb = rms_stat(s)
                seg = slice(s * SEGL, (s + 1) * SEGL)
                S.op("dve", lambda e: e.tensor_tensor(out=sq[:, :, :], in0=xT[:, :, seg],
                                                      in1=rs[:, :].unsqueeze(1).to_broadcast([128, 8, 256]),
                                                      op=ALU.mult), r=[xT_b[s], rsb], w=[sqb])
                for dt in range(8):
                    S.op("dve", lambda e, dt=dt: e.tensor_scalar(
                        out=sq[:, dt, :], in0=sq[:, dt, :], scalar1=prm[:, o_normw + 32 + dt:o_normw + 33 + dt],
                        scalar2=32.0, op0=ALU.mult, op1=ALU.mult), r=[sqb, prm_b], w=[sqb])
                for half in range(2):
                    tt = s * 2 + half
                    st_t, st_bf = stage()
                    for g in range(2):
                        bk, bb = psum()
                        for q in range(4):
                            dt = g * 4 + q
                            S.op("pe", lambda e, bk=bk, q=q, dt=dt: e.transpose(
                                bk[:, q * 128:(q + 1) * 128], sq[:, dt, half * 128:(half + 1) * 128], ident_f[:]),
                                r=[sqb, cst_b], w=[bb])
                        if g == 0:
                            S.op("dve", lambda e, bk=bk: e.tensor_copy(out=st_t[:, 0:512], in_=bk[:, :]), r=[bb], w=[st_bf])
                        else:
                            S.op("act", lambda e, bk=bk: e.copy(out=st_t[:, 512:1024], in_=bk[:, :]), r=[bb], w=[st_bf])
                    S.dma("sp", y_d[tt * 128:(tt + 1) * 128, :], st_t[:, :], r=[st_bf])

        for l in range(depth):
            if l % 2 == 0:
                ret_layer(l, l // 2)
            else:
                del_layer(l, l // 2)
        final_out()
        S.barrier()
        print(f"[build] instr={S.ninstr} sems={S.nsem} counts={S.cnt}")
    return nc


def _rope_tables():
    pos = np.arange(1024)
    r = (pos // 64).astype(np.float32)
    col = (pos % 64).astype(np.float32)
    freqs = (10000.0 ** (-np.arange(64, dtype=np.float32) / 64.0)).astype(np.float32)
    ang = np.concatenate([r[:, None] * freqs[None, :], col[:, None] * freqs[None, :]], -1)
    return np.cos(ang).T.astype(np.float32), np.sin(ang).T.astype(np.float32)


def _fm(v, nt):
    return np.ascontiguousarray(np.asarray(v, np.float32).reshape(nt, 128).T)


def make_in_maps(inp):
    f = lambda k: np.ascontiguousarray(np.asarray(inp[k], dtype=np.float32))
    xp, xs = f("x_prompt"), f("x_sample")
    c, c_ctx = f("c"), f("c_ctx")
    sr, sd = f("state_ret"), f("state_delta")
    norm_w, fnw, mod_b = f("norm_w"), f("final_norm_w"), f("mod_b")
    normw = np.concatenate([_fm(norm_w[l], 8) for l in range(4)] + [_fm(fnw, 8)], axis=1)
    modb = np.concatenate([_fm(mod_b[l][p * 1024:(p + 1) * 1024], 8) for l in range(4) for p in range(3)], axis=1)
    rdecb = np.ascontiguousarray(np.broadcast_to(f("ret_decay").reshape(1, 16), (128, 16)))
    gnwb = np.ascontiguousarray(np.broadcast_to(f("ret_gn_w").reshape(1, 4096), (128, 4096)))
    dnwb = np.ascontiguousarray(np.broadcast_to(f("del_norm_w").reshape(1, 4096), (128, 4096)))
    cw = f("del_conv_w")
    convw = np.concatenate([_fm(cw[jj, k], 32) for jj in range(2) for k in range(3)], axis=1)
    alogb = np.ascontiguousarray(np.broadcast_to(f("del_a_log").reshape(1, 32), (128, 32)))
    dtbb = np.ascontiguousarray(np.broadcast_to(f("del_dt_bias").reshape(1, 32), (128, 32)))
    cosS, sinS = _rope_tables()
    ii = np.arange(128)
    same = lambda b: (ii[:, None] // b) == (ii[None, :] // b)
    bm = [same(16), same(32) & ~same(16), same(64) & ~same(32), ~same(64)]
    bmask = np.concatenate([m.astype(np.float32) for m in bm], axis=1)
    shared = dict(bmask=bmask, normw=normw, modb=modb, mod_w=f("mod_w"), ret_w_in=f("ret_w_in"), ret_w_out=f("ret_w_out"),
                  del_w_in=f("del_w_in"), del_w_out=f("del_w_out"), rdecb=rdecb, gnwb=gnwb, dnwb=dnwb, convw=convw,
                  alogb=alogb, dtbb=dtbb)
    maps = []
    for core in range(8):
        m = dict(shared)
        chain = np.zeros((128, 8), np.float32)
        cosT = np.ones((128, NTOK), np.float32)
        sinT = np.zeros((128, NTOK), np.float32)
        if core < 6:
            x = xp[core * 5:(core + 1) * 5].reshape(NTOK, D)
            conds = [c_ctx] * 5
            s0r = np.zeros((2, 2, 4, 256, 512), np.float32)
            s0d = np.zeros((2, 2, 8, 128, 256), np.float32)
        else:
            b = core - 6
            x = np.concatenate([xs[b], xp[30 + b]], axis=0)
            conds = [c[b]] * 4 + [c_ctx]
            chain[:, 1:4] = 1.0
            s0r, s0d = sr[b], sd[b]
            cosT[:, 0:1024] = cosS
            sinT[:, 0:1024] = sinS
        condT = np.zeros((128, 40), np.float32)
        for s in range(5):
            condT[:, s::5] = _fm(conds[s], 8)
        m.update(x=np.ascontiguousarray(x), condT=condT, chainb=chain, s0r=np.ascontiguousarray(s0r),
                 s0d=np.ascontiguousarray(s0d), cosT=cosT, sinT=sinT)
        maps.append(m)
    return maps


def assemble(results):
    y_prompt = np.zeros((32, 256, D), np.float32)
    y_sample = np.zeros((2, 1024, D), np.float32)
    nsr = np.zeros((32, 2, 2, 4, 256, 512), np.float32)
    nsd = np.zeros((32, 2, 2, 8, 128, 256), np.float32)
    for core in range(8):
        r = results[core]
        y = r["y"].reshape(5, 256, D)
        if core < 6:
            y_prompt[core * 5:(core + 1) * 5] = y
            nsr[core * 5:(core + 1) * 5] = r["nsr"]
            nsd[core * 5:(core + 1) * 5] = r["nsd"]
        else:
            b = core - 6
            y_sample[b] = y[0:4].reshape(1024, D)
            y_prompt[30 + b] = y[4]
            nsr[30 + b] = r["nsr"][4]
            nsd[30 + b] = r["nsd"][4]
    return y_prompt, y_sample, nsr, nsd


_NC_CACHE = {}


def kernel(**inputs):
    if "nc" not in _NC_CACHE:
        _NC_CACHE["nc"] = build()
    maps = make_in_maps(inputs)
    res = run_bass_kernel_spmd(_NC_CACHE["nc"], maps, core_ids=list(range(8)))
    return assemble(res.results)
```

```python
import contextlib
import numpy as np
import concourse.bass as bass
import concourse.mybir as mybir
from concourse.bass_utils import run_bass_kernel_spmd

F32 = mybir.dt.float32
BF16 = mybir.dt.bfloat16
AF = mybir.ActivationFunctionType
ALU = mybir.AluOpType

D = 1024
NSEG = 5
SEGL = 256
NTOK = NSEG * SEGL
NT = NTOK // 128
DEPTH = 4
EPS = 1e-6
BLOCKS = [(0, 512), (512, 512), (1024, 256)]
import os
EPOCH = int(os.environ.get("K_EPOCH", "2000"))
BIG = 30000.0


class Buf:
    __slots__ = ("name", "lw", "rd", "dsem", "dcnt")

    def __init__(self, name):
        self.name = name
        self.lw = None
        self.rd = {}
        self.dsem = None
        self.dcnt = 0


class Sched:
    def __init__(self, nc, stack):
        self.nc = nc
        self.stack = stack
        self.engs = {"pe": nc.tensor, "act": nc.scalar, "dve": nc.vector, "pool": nc.gpsimd, "sp": nc.sync}
        self.cnt = {e: 0 for e in self.engs}
        self.esems = {e: [] for e in self.engs}
        self.known = {e: {} for e in self.engs}
        self.dsems = {}
        self.dma_bufs = []
        self.nsem = 0
        self.ninstr = 0
        self.pe_pending = None

    def _newsem(self, name):
        self.nsem += 1
        return self.stack.enter_context(self.nc.semaphore(f"{name}_{self.nsem}"))

    def _esem(self, e, epoch):
        lst = self.esems[e]
        while len(lst) <= epoch:
            lst.append(self._newsem(f"s_{e}_{len(lst)}"))
        return lst[epoch]

    def _need(self, E, ev, raw):
        key, val = ev
        if key == ("e", E) and not raw:
            return
        kn = self.known[E]
        if kn.get(key, 0) >= val:
            return
        kn[key] = val
        eng = self.engs[E]
        if key[0] == "e":
            n = val - 1
            eng.wait_ge(self._esem(key[1], n // EPOCH), n % EPOCH + 1)
        else:
            eng.wait_ge(self.dsems[key], val)
        self.ninstr += 1

    def _deps(self, E, r, w):
        for b in r:
            if b.lw is not None:
                self._need(E, b.lw, True)
        for b in w:
            if b.lw is not None:
                self._need(E, b.lw, False)
            for k, v in b.rd.items():
                self._need(E, (k, v), False)

    def _post(self, ev, r, w):
        k, v = ev
        for b in r:
            if b.rd.get(k, 0) < v:
                b.rd[k] = v
        for b in w:
            b.lw = ev
            b.rd = {}

    def _commit_pe(self):
        if self.pe_pending is None:
            return
        ins, _ = self.pe_pending
        n = self.cnt["pe"]
        ins.then_inc(self._esem("pe", n // EPOCH), 1)
        self.cnt["pe"] = n + 1
        self.pe_pending = None

    def _touch(self, E, r, w):
        if self.pe_pending is None:
            return
        pw = self.pe_pending[1]
        if E == "pe":
            if tuple(id(b) for b in w) != pw:
                self._commit_pe()
        elif any(id(b) in pw for b in r) or any(id(b) in pw for b in w):
            self._commit_pe()

    def op(self, E, fn, r=(), w=()):
        self._touch(E, r, w)
        self._deps(E, r, w)
        ins = fn(self.engs[E])
        if E == "pe":
            self.pe_pending = (ins, tuple(id(b) for b in w))
            self.ninstr += 1
            self._post((("e", "pe"), self.cnt["pe"] + 1), r, w)
            return ins
        n = self.cnt[E]
        ins.then_inc(self._esem(E, n // EPOCH), 1)
        self.cnt[E] = n + 1
        self.ninstr += 1
        self._post((("e", E), n + 1), r, w)
        return ins

    def dma(self, Q, out, in_, r=(), w=(), **kw):
        self._touch(Q, r, w)
        self._deps(Q, r, w)
        b0 = (list(w) + list(r))[0]
        if b0.dsem is None:
            b0.dsem = self._newsem("d_" + b0.name)
            self.dsems[("d", id(b0))] = b0.dsem
            self.dma_bufs.append(b0)
        ins = self.engs[Q].dma_start(out=out, in_=in_, **kw)
        ins.then_inc(b0.dsem, 16)
        b0.dcnt += 16
        self.ninstr += 1
        self._post((("d", id(b0)), b0.dcnt), r, w)

    def barrier(self):
        self._commit_pe()
        for E in self.engs:
            for e2 in self.engs:
                if self.cnt[e2] > 0:
                    self._need(E, (("e", e2), self.cnt[e2]), True)
            for b in self.dma_bufs:
                self._need(E, (("d", id(b)), b.dcnt), True)


def build(depth=DEPTH, dbg=None):
    nc = bass.Bass("TRN2", target_bir_lowering=False)
    dbg = dbg or {}

    def din(name, shape):
        return nc.dram_tensor(name, list(shape), F32, kind="ExternalInput").ap()

    def dout(name, shape):
        return nc.dram_tensor(name, list(shape), F32, kind="ExternalOutput").ap()

    x_d = din("x", [NTOK, D])
    cond_d = din("condT", [128, 40])
    chain_d = din("chainb", [128, 8])
    normw_d = din("normw", [128, 40])
    modb_d = din("modb", [128, 96])
    modw_d = din("mod_w", [4, D, 3 * D])
    rwin_d = din("ret_w_in", [2, D, 6144])
    rwout_d = din("ret_w_out", [2, 2048, D])
    dwin_d = din("del_w_in", [2, D, 6176])
    dwout_d = din("del_w_out", [2, 2048, D])
    rdec_d = din("rdecb", [128, 16])
    gnw_d = din("gnwb", [128, 4096])
    dnw_d = din("dnwb", [128, 4096])
    convw_d = din("convw", [128, 192])
    alog_d = din("alogb", [128, 32])
    dtb_d = din("dtbb", [128, 32])
    s0r_d = din("s0r", [2, 2, 4, 256, 512])
    s0d_d = din("s0d", [2, 2, 8, 128, 256])
    cos_d = din("cosT", [128, NTOK])
    sin_d = din("sinT", [128, NTOK])
    bmask_d = din("bmask", [128, 512])
    y_d = dout("y", [NTOK, D])
    nsr_d = dout("nsr", [NSEG, 2, 2, 4, 256, 512])
    nsd_d = dout("nsd", [NSEG, 2, 2, 8, 128, 256])
    dbg_d = {k: dout("dbg_" + k, shp) for k, shp in dbg.items()}

    with contextlib.ExitStack() as stack:
        S = Sched(nc, stack)

        sb_n = [0]

        def sb(name, shape, dt=F32, st=None):
            sb_n[0] += 1
            return (st or stack).enter_context(nc.sbuf_tensor(f"sb{sb_n[0]}_{name}", list(shape), dt))

        banks = [stack.enter_context(nc.psum_tensor(f"bank{i}", [128, 512], F32)) for i in range(8)]
        bank_bufs = [Buf(f"bank{i}") for i in range(8)]
        bank_rr = [0]

        def psum():
            i = bank_rr[0] % 8
            bank_rr[0] += 1
            return banks[i], bank_bufs[i]

        xT = sb("xT", [128, 8, NTOK])
        xT_b = [Buf(f"xT{s}") for s in range(NSEG)]
        hT = sb("hT", [128, 8, NTOK], BF16)
        hT_b = [Buf(f"hT{s}") for s in range(NSEG)]
        NSLAB = 3
        ring = [sb(f"slab{i}", [128, 4096], BF16) for i in range(NSLAB)]
        ring_b = [Buf(f"slab{i}") for i in range(NSLAB)]
        ring_rr = [0]

        ident_f = sb("ident_f", [128, 128])
        ident_b = sb("ident_b", [128, 128], BF16)
        ones_f = sb("ones_f", [128, 128])
        ones_b = sb("ones_b", [128, 128], BF16)
        cst_b = Buf("consts")
        cpar = sb("cpar", [128, 8])
        prm = sb("prm", [128, 40 + 8 + 40 + 96 + 16 + 192 + 32 + 32])
        prm_b = Buf("prm")
        o_cond, o_chain, o_normw, o_modb, o_rdec, o_convw, o_alog, o_dtb = 0, 40, 48, 88, 184, 200, 392, 424
        modT = sb("modT", [128, 4 * 3 * 40])
        modT_b = Buf("modT")
        sc1 = sb("sc1", [128, 4 * 40])
        scT = sb("scT", [128, 8, NSEG], BF16)
        sqs = [sb("sq0", [128, 8, 256])]
        sqs_b = [Buf("sq0")]
        rstd = [sb("rstd0", [128, 256])]
        rstd_b = [Buf("rstd0")]
        stg = [sb(f"stg{i}", [128, 1024]) for i in range(2)]
        stg_b = [Buf(f"stg{i}") for i in range(2)]
        stg_rr = [0]

        def stage():
            i = stg_rr[0] % 2
            stg_rr[0] += 1
            return stg[i], stg_b[i]

        def chain_ap(s):
            return prm[:, o_chain + s:o_chain + s + 1]

        S.op("pool", lambda e: e.memset(ident_f[:], 1.0), w=[cst_b])
        S.op("pool", lambda e: e.affine_select(out=ident_f[:], in_=ident_f[:], pattern=[[-1, 128]],
                                                compare_op=ALU.is_equal, fill=0.0, base=0, channel_multiplier=1),
             r=[cst_b], w=[cst_b])
        S.op("pool", lambda e: e.tensor_copy(out=ident_b[:], in_=ident_f[:]), r=[cst_b], w=[cst_b])
        S.op("pool", lambda e: e.memset(ones_f[:], 1.0), w=[cst_b])
        S.op("pool", lambda e: e.memset(ones_b[:], 1.0), w=[cst_b])
        S.op("pool", lambda e: e.memset(cpar[:, 0:1], 1024.0 * EPS), w=[cst_b])
        S.op("pool", lambda e: e.memset(cpar[:, 1:2], EPS), w=[cst_b])
        S.op("pool", lambda e: e.memset(cpar[:, 2:3], 1.0), w=[cst_b])
        S.op("pool", lambda e: e.memset(cpar[:, 3:4], 0.0), w=[cst_b])

        for off, n, src in ((o_cond, 40, cond_d), (o_chain, 8, chain_d), (o_normw, 40, normw_d), (o_modb, 96, modb_d),
                            (o_rdec, 16, rdec_d), (o_convw, 192, convw_d), (o_alog, 32, alog_d), (o_dtb, 32, dtb_d)):
            S.dma("sp", prm[:, off:off + n], src, w=[prm_b])

        def v8(t, n=512):
            return t[:, 0:8 * n].rearrange("p (kt c) -> p kt c", kt=8)

        def win_view(w2d):
            return w2d.rearrange("(kt p) c -> p kt c", p=128)

        wspecs = []
        for l in range(depth):
            for part in range(3):
                for hf in range(2):
                    c0 = part * 1024 + hf * 512
                    wspecs.append(lambda t, l=l, c0=c0: [(v8(t), win_view(modw_d[l])[:, :, c0:c0 + 512])])
        for l in range(depth):
            j = l // 2
            if l % 2 == 0:
                wv = win_view(rwin_d[j])
                for h in range(4):
                    wspecs.append(lambda t, h=h, wv=wv: [
                        (v8(t)[:, :, 0:256], wv[:, :, h * 256:(h + 1) * 256]),
                        (v8(t)[:, :, 256:512], wv[:, :, 1024 + h * 256:1024 + (h + 1) * 256])])
                    wspecs.append(lambda t, h=h, wv=wv: [(v8(t), wv[:, :, 2048 + h * 512:2048 + (h + 1) * 512])])
                    wspecs.append(lambda t, h=h, wv=wv: [(v8(t), wv[:, :, 4096 + h * 512:4096 + (h + 1) * 512])])
                    wspecs.append(lambda t, h=h, j=j: [
                        (t[:, :].rearrange("p (v c) -> p v c", v=4),
                         rwout_d[j][h * 512:(h + 1) * 512, :].rearrange("(v p) c -> p v c", p=128))])
            else:
                wv = win_view(dwin_d[j])
                wspecs.append(lambda t, wv=wv: [(v8(t, 32), wv[:, :, 6144:6176])])
                for h in range(8):
                    wspecs.append(lambda t, h=h, wv=wv: [
                        (v8(t)[:, :, 0:128], wv[:, :, h * 128:(h + 1) * 128]),
                        (v8(t)[:, :, 128:256], wv[:, :, 1024 + h * 128:1024 + (h + 1) * 128]),
                        (v8(t)[:, :, 256:512], wv[:, :, 2048 + h * 256:2048 + (h + 1) * 256])])
                    wspecs.append(lambda t, h=h, wv=wv: [(v8(t, 256), wv[:, :, 4096 + h * 256:4096 + (h + 1) * 256])])
                    wspecs.append(lambda t, h=h, j=j: [
                        (t[:, 0:2048].rearrange("p (v c) -> p v c", v=2),
                         dwout_d[j][h * 256:(h + 1) * 256, :].rearrange("(v p) c -> p v c", p=128))])
        ws = {"issued": 0, "released": 0, "got": 0}

        def ws_issue():
            while ws["issued"] < len(wspecs) and ws["issued"] < ws["released"] + NSLAB:
                i = ws["issued"]
                for (dst_view, src_ap) in wspecs[i](ring[i % NSLAB]):
                    S.dma("pool", dst_view, src_ap, w=[ring_b[i % NSLAB]])
                ws["issued"] += 1

        def ws_get():
            ws_issue()
            i = ws["got"]
            assert i < ws["issued"], "weight ring over-subscribed"
            ws["got"] += 1
            return ring[i % NSLAB], ring_b[i % NSLAB]

        def ws_release():
            ws["released"] += 1
            ws_issue()

        for tt in range(NT):
            st_t, st_b = stage()
            S.dma("sp", st_t[:, :], x_d[tt * 128:(tt + 1) * 128, :], w=[st_b])
            for half in range(2):
                bk, bb = psum()
                for q in range(4):
                    dt = half * 4 + q
                    S.op("pe", lambda e, bk=bk, q=q, dt=dt, st_t=st_t: e.transpose(
                        bk[:, q * 128:(q + 1) * 128], st_t[:, dt * 128:(dt + 1) * 128], ident_f[:]),
                        r=[st_b, cst_b], w=[bb])
                eng = "dve" if half == 0 else "act"
                dst = xT[:, half * 4:half * 4 + 4, tt * 128:(tt + 1) * 128]
                src = bk[:, :].rearrange("p (q t) -> p q t", q=4)
                if eng == "dve":
                    S.op("dve", lambda e, dst=dst, src=src: e.tensor_copy(out=dst, in_=src), r=[bb], w=[xT_b[tt // 2]])
                else:
                    S.op("act", lambda e, dst=dst, src=src: e.copy(out=dst, in_=src), r=[bb], w=[xT_b[tt // 2]])

        S.op("act", lambda e: e.activation(out=scT[:].rearrange("p a b -> p (a b)"), in_=prm[:, o_cond:o_cond + 40],
                                           func=AF.Silu), r=[prm_b], w=[modT_b])
        for l in range(depth):
            for part in range(3):
                for hf in range(2):
                    c0 = part * 1024 + hf * 512
                    sl, slb = ws_get()
                    slv = v8(sl)
                    bk, bb = psum()
                    for ct in range(4):
                        for kt in range(8):
                            S.op("pe", lambda e, bk=bk, ct=ct, kt=kt, slv=slv: e.matmul(
                                bk[:, ct * 5:ct * 5 + 5], lhsT=slv[:, kt, ct * 128:(ct + 1) * 128], rhs=scT[:, kt, :],
                                start=(kt == 0), stop=(kt == 7)), r=[slb, modT_b], w=[bb])
                    base = (l * 3 + part) * 40 + hf * 20
                    mb = prm[:, o_modb + (l * 3 + part) * 8 + hf * 4: o_modb + (l * 3 + part) * 8 + hf * 4 + 4]
                    S.op("dve", lambda e, bk=bk, base=base, mb=mb: e.tensor_tensor(
                        out=modT[:, base:base + 20].rearrange("p (a b) -> p a b", a=4),
                        in0=bk[:, 0:20].rearrange("p (a b) -> p a b", a=4),
                        in1=mb.unsqueeze(2).to_broadcast([128, 4, 5]), op=ALU.add), r=[bb, prm_b], w=[modT_b])
                    ws_release()
            sv = modT[:, (l * 3 + 1) * 40:(l * 3 + 1) * 40 + 40]
            S.op("dve", lambda e, l=l, sv=sv: e.tensor_scalar(out=sc1[:, l * 40:(l + 1) * 40], in0=sv, scalar1=1.0,
                                                             scalar2=32.0, op0=ALU.add, op1=ALU.mult),
                 r=[modT_b], w=[modT_b])
            S.op("dve", lambda e, l=l: e.tensor_tensor(
                out=sc1[:, l * 40:(l + 1) * 40].rearrange("p (a b) -> p a b", a=8),
                in0=sc1[:, l * 40:(l + 1) * 40].rearrange("p (a b) -> p a b", a=8),
                in1=prm[:, o_normw + l * 8:o_normw + l * 8 + 8].unsqueeze(2).to_broadcast([128, 8, 5]), op=ALU.mult),
                r=[modT_b, prm_b], w=[modT_b])

        def mod_ap(l, part, dt, s):
            o = (l * 3 + part) * 40 + dt * 5 + s
            return modT[:, o:o + 1]

        nrm_rr = [0]

        def rms_stat(s):
            i = 0
            sq, sqb, rs, rsb = sqs[i], sqs_b[i], rstd[i], rstd_b[i]
            seg = slice(s * SEGL, (s + 1) * SEGL)
            S.op("act", lambda e: e.activation(out=sq[:, :, :], in_=xT[:, :, seg], func=AF.Square), r=[xT_b[s]], w=[sqb])
            bk, bb = psum()
            for dt in range(8):
                S.op("pe", lambda e, dt=dt: e.matmul(bk[:, 0:256], lhsT=ones_f[:], rhs=sq[:, dt, :],
                                                      start=(dt == 0), stop=(dt == 7)), r=[sqb, cst_b], w=[bb])
            S.op("act", lambda e: e.activation(out=rs[:, :], in_=bk[:, 0:256], func=AF.Ln, bias=cpar[:, 0:1], scale=1.0),
                 r=[bb, cst_b], w=[rsb])
            S.op("act", lambda e: e.activation(out=rs[:, :], in_=rs[:, :], func=AF.Exp, scale=-0.5), r=[rsb], w=[rsb])
            return sq, sqb, rs, rsb

        def norm_mod(l):
            for s in range(NSEG):
                sq, sqb, rs, rsb = rms_stat(s)
                seg = slice(s * SEGL, (s + 1) * SEGL)
                S.op("dve", lambda e: e.tensor_tensor(out=sq[:, :, :], in0=xT[:, :, seg],
                                                      in1=rs[:, :].unsqueeze(1).to_broadcast([128, 8, 256]),
                                                      op=ALU.mult), r=[xT_b[s], rsb], w=[sqb])
                for dt in range(8):
                    o = l * 40 + dt * 5 + s
                    S.op("act", lambda e, dt=dt, o=o: e.activation(
                        out=hT[:, dt, seg], in_=sq[:, dt, :], func=AF.Identity, bias=mod_ap(l, 0, dt, s),
                        scale=sc1[:, o:o + 1]), r=[sqb, modT_b], w=[hT_b[s]])

        def segs_of(t0, n):
            return list(range(t0 // SEGL, (t0 + n) // SEGL))

        def out_proj(l, sO, sOb, nvt, oT, oT_b):
            sOv = sO[:, 0:nvt * 1024].rearrange("p (v c) -> p v c", v=nvt)
            for (t0, n) in BLOCKS:
                sg = segs_of(t0, n)
                for dt in range(8):
                    bk, bb = psum()
                    for vt in range(nvt):
                        S.op("pe", lambda e, vt=vt, dt=dt, bk=bk: e.matmul(
                            bk[:, 0:n], lhsT=sOv[:, vt, dt * 128:(dt + 1) * 128], rhs=oT[:, vt, t0:t0 + n],
                            start=(vt == 0), stop=(vt == nvt - 1)), r=[sOb] + [oT_b[s] for s in sg], w=[bb])
                    for s in sg:
                        lo = s * SEGL - t0
                        seg = slice(s * SEGL, (s + 1) * SEGL)
                        S.op("dve", lambda e, bk=bk, lo=lo, seg=seg, dt=dt, s=s: e.scalar_tensor_tensor(
                            out=xT[:, dt, seg], in0=bk[:, lo:lo + SEGL], scalar=mod_ap(l, 2, dt, s), in1=xT[:, dt, seg],
                            op0=ALU.mult, op1=ALU.add), r=[bb, modT_b, xT_b[s]], w=[xT_b[s]])

        def ret_layer(l, j):
            with contextlib.ExitStack() as ls:
                cosT = sb("cosT", [128, NTOK], st=ls)
                sinT = sb("sinT", [128, NTOK], st=ls)
                rope_b = Buf("rope")
                S.dma("sp", cosT[:, :], cos_d, w=[rope_b])
                S.dma("sp", sinT[:, :], sin_d, w=[rope_b])
                lc = sb("lc", [128, 64], st=ls)
                lc_b = Buf("lc")
                iot = sb("iot", [128, 128], st=ls)
                iop = sb("iop", [128, 2], st=ls)
                ioi = sb("ioi", [128, 2, 128], st=ls)
                Mh = sb("Mh", [128, 4, 128], st=ls)
                Xi = sb("Xi", [128, 8, 128], BF16, st=ls)
                tmpm = sb("tmpm", [128, 2, 128], st=ls)
                gnw = sb("gnw", [128, 512], st=ls)
                gnw_b = Buf("gnw")
                qT = sb("qT", [128, 2, NTOK], BF16, st=ls)
                kT = sb("kT", [128, 2, NTOK], BF16, st=ls)
                qk_b = [Buf(f"qk{s}") for s in range(NSEG)]
                sq, sqb = sqs[0], sqs_b[0]
                kzf = sb("kzf", [128, NT, 256], BF16, st=ls)
                kzb = sb("kzb", [128, NT, 256], BF16, st=ls)
                kz_b = [Buf(f"kz{c}") for c in range(NT)]
                v16 = sb("v16", [128, NT, 512], BF16, st=ls)
                v_b = [Buf(f"v{c}") for c in range(NT)]
                sm = sb("sm", [128, NT, 128], BF16, st=ls)
                sm_b = [Buf(f"sm{c}") for c in range(NT)]
                qx = [sb(f"qx{i}", [128, 2, 2, 128], BF16, st=ls) for i in range(2)]
                qx_b = [Buf(f"qx{i}") for i in range(2)]
                zs = [sb(f"zs{i}", [128, 512], st=ls) for i in range(2)]
                zs_b = [Buf(f"zs{i}") for i in range(2)]
                Sb16 = sb("Sb16", [128, NT, 2, 512], BF16, st=ls)
                Sb16_b = [Buf(f"Sb16_{c}") for c in range(NT)]
                S32 = [sb(f"S32_{d}", [128, 2, 512], st=ls) for d in range(2)]
                S32_b = [Buf(f"S32_{d}") for d in range(2)]
                S32alt = [None, sb("S32_1b", [128, 2, 512], st=ls)]
                S32alt_b = [None, Buf("S32_1b")]
                S16f = sb("S16f", [128, 2, 512], BF16, st=ls)
                S16f_b = Buf("S16f")
                oT = [sb(f"oT{i}", [128, 4, SEGL], BF16, st=ls) for i in range(2)]
                oT_b = [Buf(f"oT{i}") for i in range(2)]
                bst = sb("bst", [128, 16], st=ls)
                bst_b = Buf("bst")
                on = sb("on", [128, 512], st=ls)
                on_b = Buf("on")
                og = sb("og", [128, 512], BF16, st=ls)
                og_b = Buf("og")

                dec = prm[:, o_rdec + j * 8:o_rdec + j * 8 + 8]
                LG, GC, ZF = lc[:, 0:8], lc[:, 8:16], lc[:, 16:24]
                S.op("act", lambda e: e.activation(out=lc[:, 24:32], in_=dec, func=AF.Exp, scale=-1.0), r=[prm_b], w=[lc_b])
                S.op("act", lambda e: e.activation(out=LG, in_=lc[:, 24:32], func=AF.Ln, bias=cpar[:, 2:3], scale=1.0),
                     r=[lc_b, cst_b], w=[lc_b])
                S.op("dve", lambda e: e.tensor_scalar(out=LG, in0=LG, scalar1=-1.0, scalar2=None, op0=ALU.mult),
                     r=[lc_b], w=[lc_b])
                S.op("act", lambda e: e.activation(out=GC, in_=LG, func=AF.Exp, scale=128.0), r=[lc_b], w=[lc_b])
                S.op("pool", lambda e: e.iota(iot[:, :], pattern=[[1, 128]], base=0, channel_multiplier=-1,
                                               allow_small_or_imprecise_dtypes=True), r=[lc_b], w=[lc_b])
                S.op("pool", lambda e: e.iota(iop[:, 0:1], pattern=[[0, 1]], base=127, channel_multiplier=-1,
                                               allow_small_or_imprecise_dtypes=True), r=[lc_b], w=[lc_b])
                S.op("pool", lambda e: e.iota(iop[:, 1:2], pattern=[[0, 1]], base=0, channel_multiplier=1,
                                               allow_small_or_imprecise_dtypes=True), r=[lc_b], w=[lc_b])
                S.op("pool", lambda e: e.iota(ioi[:, 0, :], pattern=[[1, 128]], base=1, channel_multiplier=0,
                                               allow_small_or_imprecise_dtypes=True), r=[lc_b], w=[lc_b])
                S.op("pool", lambda e: e.iota(ioi[:, 1, :], pattern=[[-1, 128]], base=128, channel_multiplier=0,
                                               allow_small_or_imprecise_dtypes=True), r=[lc_b], w=[lc_b])
                for d in range(2):
                    for h in range(4):
                        c = d * 4 + h
                        S.op("act", lambda e, c=c, d=d: e.activation(out=ZF[:, c:c + 1], in_=iop[:, d:d + 1], func=AF.Exp,
                                                                      scale=LG[:, c:c + 1]), r=[lc_b], w=[lc_b])
                S.op("dve", lambda e: e.tensor_scalar(out=ZF, in0=ZF, scalar1=1.0 / 16.0, scalar2=None, op0=ALU.mult),
                     r=[lc_b], w=[lc_b])
                for h in range(4):
                    S.op("dve", lambda e: e.tensor_scalar(out=tmpm[:, 0, :], in0=iot[:, :], scalar1=0.0, scalar2=None,
                                                          op0=ALU.max), r=[lc_b], w=[lc_b])
                    S.op("act", lambda e, h=h: e.activation(out=tmpm[:, 0, :], in_=tmpm[:, 0, :], func=AF.Exp,
                                                             scale=LG[:, h:h + 1]), r=[lc_b], w=[lc_b])
                    S.op("pool", lambda e: e.affine_select(out=tmpm[:, 0, :], in_=tmpm[:, 0, :], pattern=[[1, 128]],
                                                            compare_op=ALU.is_ge, fill=0.0, base=0, channel_multiplier=-1),
                         r=[lc_b], w=[lc_b])
                    S.op("dve", lambda e: e.tensor_scalar(out=tmpm[:, 1, :], in0=iot[:, :], scalar1=-1.0, scalar2=0.0,
                                                          op0=ALU.mult, op1=ALU.max), r=[lc_b], w=[lc_b])
                    S.op("act", lambda e, h=h: e.activation(out=tmpm[:, 1, :], in_=tmpm[:, 1, :], func=AF.Exp,
                                                             scale=LG[:, 4 + h:5 + h]), r=[lc_b], w=[lc_b])
                    S.op("pool", lambda e: e.affine_select(out=tmpm[:, 1, :], in_=tmpm[:, 1, :], pattern=[[-1, 128]],
                                                            compare_op=ALU.is_ge, fill=0.0, base=0, channel_multiplier=1),
                         r=[lc_b], w=[lc_b])
                    S.op("dve", lambda e, h=h: e.tensor_tensor(out=Mh[:, h, :], in0=tmpm[:, 0, :], in1=tmpm[:, 1, :],
                                                               op=ALU.add), r=[lc_b], w=[lc_b])
                    S.op("dve", lambda e, h=h: e.tensor_scalar(out=Mh[:, h, :], in0=Mh[:, h, :], scalar1=1.0 / 16.0,
                                                               scalar2=None, op0=ALU.mult), r=[lc_b], w=[lc_b])
                    for d in range(2):
                        S.op("act", lambda e, h=h, d=d: e.activation(out=Xi[:, d * 4 + h, :], in_=ioi[:, d, :], func=AF.Exp,
                                                                      scale=LG[:, d * 4 + h:d * 4 + h + 1]),
                             r=[lc_b], w=[lc_b])

                norm_mod(l)

                for h in range(4):
                    sQK, sQKb = ws_get()
                    sV, sVb = ws_get()
                    sQKv, sVv = v8(sQK), v8(sV)
                    S.dma("sp", gnw[:, :], gnw_d[:, j * 2048 + h * 512:j * 2048 + (h + 1) * 512], w=[gnw_b])

                    for s in range(NSEG):
                        t0, n = s * SEGL, SEGL
                        for wi, dst in ((0, qT), (1, kT)):
                            pss = []
                            for half in range(2):
                                bk, bb = psum()
                                c0 = wi * 256 + half * 128
                                for kt in range(8):
                                    S.op("pe", lambda e, bk=bk, kt=kt, c0=c0: e.matmul(
                                        bk[:, 0:n], lhsT=sQKv[:, kt, c0:c0 + 128], rhs=hT[:, kt, t0:t0 + n],
                                        start=(kt == 0), stop=(kt == 7)), r=[sQKb, hT_b[s]], w=[bb])
                                pss.append((bk, bb))
                            cs, sn = cosT[:, t0:t0 + n], sinT[:, t0:t0 + n]
                            for half in range(2):
                                S.op("act", lambda e, half=half: e.copy(out=sq[:, half, :], in_=pss[half][0][:, 0:n]),
                                     r=[pss[half][1]], w=[sqb])
                            S.op("dve", lambda e: e.tensor_tensor(out=sq[:, 2, :], in0=sq[:, 0, :], in1=cs, op=ALU.mult),
                                 r=[sqb, rope_b], w=[sqb])
                            S.op("dve", lambda e: e.tensor_tensor(out=sq[:, 3, :], in0=sq[:, 1, :], in1=sn, op=ALU.mult),
                                 r=[sqb, rope_b], w=[sqb])
                            S.op("dve", lambda e: e.tensor_tensor(out=sq[:, 4, :], in0=sq[:, 0, :], in1=sn, op=ALU.mult),
                                 r=[sqb, rope_b], w=[sqb])
                            S.op("dve", lambda e: e.tensor_tensor(out=sq[:, 5, :], in0=sq[:, 1, :], in1=cs, op=ALU.mult),
                                 r=[sqb, rope_b], w=[sqb])
                            S.op("dve", lambda e, dst=dst: e.tensor_tensor(out=dst[:, 0, t0:t0 + n], in0=sq[:, 2, :],
                                                                          in1=sq[:, 3, :], op=ALU.subtract),
                                 r=[sqb], w=[qk_b[s]])
                            S.op("dve", lambda e, dst=dst: e.tensor_tensor(out=dst[:, 1, t0:t0 + n], in0=sq[:, 4, :],
                                                                          in1=sq[:, 5, :], op=ALU.add),
                                 r=[sqb], w=[qk_b[s]])
                    ws_release()
                    for c in range(NT):
                        tk = slice(c * 128, (c + 1) * 128)
                        bk, bb = psum()
                        for kt in range(8):
                            S.op("pe", lambda e, bk=bk, kt=kt: e.matmul(bk[:, :], lhsT=hT[:, kt, tk], rhs=sVv[:, kt, :],
                                                                         start=(kt == 0), stop=(kt == 7)),
                                 r=[sVb, hT_b[c // 2]], w=[bb])
                        S.op("act", lambda e, bk=bk, c=c: e.copy(out=v16[:, c, :], in_=bk[:, :]), r=[bb], w=[v_b[c]])
                    ws_release()
                    sZ, sZb = ws_get()
                    sO, sOb = ws_get()
                    sZv = v8(sZ)
                    sOv = sO[:, :].rearrange("p (v c) -> p v c", v=4)
                    for c in range(NT):
                        tk = slice(c * 128, (c + 1) * 128)
                        bk, bb = psum()
                        bkb = bk[:, :].bitcast(BF16)
                        for dt in range(2):
                            S.op("pe", lambda e, dt=dt, bkb=bkb: e.transpose(bkb[:, dt * 128:(dt + 1) * 128], kT[:, dt, tk],
                                                                             ident_b[:]),
                                 r=[qk_b[c // 2], cst_b], w=[bb])
                        S.op("dve", lambda e, bkb=bkb, c=c: e.tensor_scalar(out=kzf[:, c, :], in0=bkb[:, 0:256],
                                                                            scalar1=ZF[:, h:h + 1], scalar2=None,
                                                                            op0=ALU.mult), r=[bb, lc_b], w=[kz_b[c]])
                        S.op("act", lambda e, bkb=bkb, c=c: e.activation(out=kzb[:, c, :], in_=bkb[:, 0:256], func=AF.Copy,
                                                                         scale=ZF[:, 4 + h:5 + h]), r=[bb, lc_b], w=[kz_b[c]])
                        bk, bb = psum()
                        for dt in range(2):
                            S.op("pe", lambda e, dt=dt, bk=bk: e.matmul(bk[:, 0:128], lhsT=kT[:, dt, tk], rhs=qT[:, dt, tk],
                                                                         start=(dt == 0), stop=(dt == 1)),
                                 r=[qk_b[c // 2]], w=[bb])
                        S.op("dve", lambda e, bk=bk, c=c: e.tensor_tensor(out=sm[:, c, :], in0=bk[:, 0:128], in1=Mh[:, h, :],
                                                                          op=ALU.mult), r=[bb, lc_b], w=[sm_b[c]])

                    def upd_state(d, c, kz):
                        pss = []
                        for dt in range(2):
                            bk, bb = psum()
                            S.op("pe", lambda e, bk=bk, dt=dt: e.matmul(bk[:, :], lhsT=kz[:, c, dt * 128:(dt + 1) * 128],
                                                                         rhs=v16[:, c, :], start=True, stop=True),
                                 r=[kz_b[c], v_b[c]], w=[bb])
                            pss.append((bk, bb))
                        if S32alt[d] is None:
                            dst, dstb = S32[d], S32_b[d]
                        else:
                            dst, dstb = S32alt[d], S32alt_b[d]
                        for dt in range(2):
                            bk, bb = pss[dt]
                            S.op("dve", lambda e, bk=bk, dt=dt: e.scalar_tensor_tensor(
                                out=dst[:, dt, :], in0=S32[d][:, dt, :], scalar=GC[:, d * 4 + h:d * 4 + h + 1],
                                in1=bk[:, :], op0=ALU.mult, op1=ALU.add), r=[bb, lc_b, S32_b[d]], w=[dstb])
                        if S32alt[d] is not None:
                            S32[d], S32alt[d] = S32alt[d], S32[d]
                            S32_b[d], S32alt_b[d] = S32alt_b[d], S32_b[d]

                    def out_state(d, s):
                        st_t, st_bf = stage()
                        S.op("act", lambda e: e.copy(out=st_t[:, :], in_=S32[d][:, :, :].rearrange("p a b -> p (a b)")),
                             r=[S32_b[d]], w=[st_bf])
                        S.dma("sp", nsr_d[s, j, d, h].rearrange("(a p) v -> p a v", p=128),
                              st_t[:, :].rearrange("p (a v) -> p a v", a=2), r=[st_bf])

                    S.op("dve", lambda e: e.memset(S32[1][:, :, :], 0.0), w=[S32_b[1]])
                    for c in range(NT - 1, -1, -1):
                        s = c // 2
                        S.op("act", lambda e, c=c: e.copy(out=Sb16[:, c, :, :], in_=S32[1][:, :, :]),
                             r=[S32_b[1]], w=[Sb16_b[c]])
                        upd_state(1, c, kzb)
                        if c % 2 == 0:
                            out_state(1, s)
                            if c > 0:
                                if s - 1 == 3:
                                    st_t, st_bf = stage()
                                    S.dma("sp", st_t[:, :].rearrange("p (a v) -> p a v", a=2),
                                          s0r_d[j, 1, h].rearrange("(a p) v -> p a v", p=128), w=[st_bf])
                                    S.op("dve", lambda e, s=s, st_t=st_t: e.scalar_tensor_tensor(
                                        out=S32[1][:, :, :], in0=S32[1][:, :, :], scalar=chain_ap(s),
                                        in1=st_t[:, :].rearrange("p (a v) -> p a v", a=2),
                                        op0=ALU.mult, op1=ALU.add), r=[S32_b[1], prm_b, st_bf], w=[S32_b[1]])
                                else:
                                    S.op("dve", lambda e, s=s: e.tensor_scalar(
                                        out=S32[1][:, :, :], in0=S32[1][:, :, :], scalar1=chain_ap(s), scalar2=None,
                                        op0=ALU.mult), r=[S32_b[1], prm_b], w=[S32_b[1]])
                    S.dma("sp", S32[0][:, :, :], s0r_d[j, 0, h].rearrange("(a p) v -> p a v", p=128), w=[S32_b[0]])
                    S.op("act", lambda e: e.copy(out=S16f[:, :, :], in_=S32[0][:, :, :]), r=[S32_b[0]], w=[S16f_b])
                    for c in range(NT):
                        s = c // 2
                        tk = slice(c * 128, (c + 1) * 128)
                        qi = c % 2
                        for d in range(2):
                            S.op("dve", lambda e, d=d, qi=qi: e.tensor_tensor(
                                out=qx[qi][:, d, :, :], in0=qT[:, :, tk],
                                in1=Xi[:, d * 4 + h, :].unsqueeze(1).to_broadcast([128, 2, 128]), op=ALU.mult),
                                r=[qk_b[s], lc_b], w=[qx_b[qi]])
                        bkz, bbz = psum()
                        for kt in range(8):
                            S.op("pe", lambda e, kt=kt: e.matmul(bkz[:, :], lhsT=hT[:, kt, tk], rhs=sZv[:, kt, :],
                                                                  start=(kt == 0), stop=(kt == 7)), r=[sZb, hT_b[s]], w=[bbz])
                        S.op("act", lambda e, qi=qi: e.activation(out=zs[qi][:, :], in_=bkz[:, :], func=AF.Exp, scale=-1.0),
                             r=[bbz], w=[zs_b[qi]])
                        S.op("act", lambda e, qi=qi: e.activation(out=zs[qi][:, :], in_=zs[qi][:, :], func=AF.Ln, bias=cpar[:, 2:3],
                                                                  scale=1.0), r=[zs_b[qi], cst_b], w=[zs_b[qi]])
                        S.op("act", lambda e, qi=qi: e.activation(out=zs[qi][:, :], in_=zs[qi][:, :], func=AF.Exp, scale=-1.0),
                             r=[zs_b[qi]], w=[zs_b[qi]])
                        S.op("dve", lambda e, qi=qi: e.tensor_tensor(out=zs[qi][:, :], in0=bkz[:, :], in1=zs[qi][:, :], op=ALU.mult),
                             r=[bbz, zs_b[qi]], w=[zs_b[qi]])
                        bk, bb = psum()
                        S.op("pe", lambda e, bk=bk, c=c: e.matmul(bk[:, :], lhsT=sm[:, c, :], rhs=v16[:, c, :],
                                                                   start=True, stop=False), r=[sm_b[c], v_b[c]], w=[bb])
                        for dt in range(2):
                            S.op("pe", lambda e, bk=bk, dt=dt: e.matmul(bk[:, :], lhsT=qx[qi][:, 0, dt, :], rhs=S16f[:, dt, :],
                                                                         start=False, stop=False),
                                 r=[qx_b[qi], S16f_b], w=[bb])
                        for dt in range(2):
                            S.op("pe", lambda e, bk=bk, dt=dt, c=c: e.matmul(bk[:, :], lhsT=qx[qi][:, 1, dt, :],
                                                                              rhs=Sb16[:, c, dt, :], start=False,
                                                                              stop=(dt == 1)),
                                 r=[qx_b[qi], Sb16_b[c]], w=[bb])
                        S.op("dve", lambda e, bk=bk: e.bn_stats(out=bst[:, 0:6], in_=bk[:, :]), r=[bb], w=[bst_b])
                        S.op("dve", lambda e: e.bn_aggr(out=bst[:, 8:10], in_=bst[:, 0:6]), r=[bst_b], w=[bst_b])
                        S.op("act", lambda e: e.activation(out=bst[:, 10:11], in_=bst[:, 9:10], func=AF.Ln, bias=cpar[:, 1:2],
                                                           scale=1.0), r=[bst_b, cst_b], w=[bst_b])
                        S.op("act", lambda e: e.activation(out=bst[:, 10:11], in_=bst[:, 10:11], func=AF.Exp, scale=-0.5),
                             r=[bst_b], w=[bst_b])
                        S.op("dve", lambda e, bk=bk: e.tensor_scalar(out=on[:, :], in0=bk[:, :], scalar1=bst[:, 8:9],
                                                                     scalar2=bst[:, 10:11], op0=ALU.subtract,
                                                                     op1=ALU.mult), r=[bb, bst_b], w=[on_b])
                        S.op("dve", lambda e: e.tensor_tensor(out=on[:, :], in0=on[:, :], in1=gnw[:, :], op=ALU.mult),
                             r=[on_b, gnw_b], w=[on_b])
                        S.op("dve", lambda e, qi=qi: e.tensor_tensor(out=og[:, :], in0=on[:, :], in1=zs[qi][:, :], op=ALU.mult),
                             r=[on_b, zs_b[qi]], w=[og_b])
                        bk2, bb2 = psum()
                        bk2b = bk2[:, :].bitcast(BF16)
                        for vt in range(4):
                            S.op("pe", lambda e, vt=vt, bk2b=bk2b: e.transpose(
                                bk2b[:, vt * 128:(vt + 1) * 128], og[:, vt * 128:(vt + 1) * 128], ident_b[:]),
                                r=[og_b, cst_b], w=[bb2])
                        oi = s % 2
                        lo = (c % 2) * 128
                        S.op("act", lambda e, bk2b=bk2b, oi=oi, lo=lo: e.copy(
                            out=oT[oi][:, :, lo:lo + 128], in_=bk2b[:, 0:512].rearrange("p (v t) -> p v t", v=4)),
                            r=[bb2], w=[oT_b[oi]])
                        upd_state(0, c, kzf)
                        if c % 2 == 1:
                            out_state(0, s)
                            if c < NT - 1:
                                S.op("dve", lambda e, s=s: e.tensor_scalar(
                                    out=S32[0][:, :, :], in0=S32[0][:, :, :], scalar1=chain_ap(s + 1), scalar2=None,
                                    op0=ALU.mult), r=[S32_b[0], prm_b], w=[S32_b[0]])
                        if c < NT - 1:
                            S.op("act", lambda e: e.copy(out=S16f[:, :, :], in_=S32[0][:, :, :]), r=[S32_b[0]], w=[S16f_b])
                        if c % 2 == 1:
                            seg = slice(s * SEGL, (s + 1) * SEGL)
                            for dt in range(8):
                                bk, bb = psum()
                                for vt in range(4):
                                    S.op("pe", lambda e, vt=vt, dt=dt, bk=bk, oi=oi: e.matmul(
                                        bk[:, 0:SEGL], lhsT=sOv[:, vt, dt * 128:(dt + 1) * 128], rhs=oT[oi][:, vt, :],
                                        start=(vt == 0), stop=(vt == 3)), r=[sOb, oT_b[oi]], w=[bb])
                                S.op("dve", lambda e, bk=bk, dt=dt, s=s: e.scalar_tensor_tensor(
                                    out=xT[:, dt, seg], in0=bk[:, 0:SEGL], scalar=mod_ap(l, 2, dt, s), in1=xT[:, dt, seg],
                                    op0=ALU.mult, op1=ALU.add), r=[bb, modT_b, xT_b[s]], w=[xT_b[s]])
                    ws_release()
                    ws_release()
                S.barrier()

        def del_layer(l, j):
            with contextlib.ExitStack() as ls:
                DH = 8
                tri = [sb(f"tri{d}", [128, 128], st=ls) for d in range(2)]
                neg3 = [sb(f"neg3{d}", [128, 128], st=ls) for d in range(2)]
                pos1 = [sb(f"pos1{d}", [128, 128], st=ls) for d in range(2)]
                dc_b = Buf("dconst")
                for d in range(2):
                    S.op("pool", lambda e, d=d: e.memset(tri[d][:, :], 1.0), w=[dc_b])
                    pat, cm = ([[1, 128]], -1) if d == 0 else ([[-1, 128]], 1)
                    S.op("pool", lambda e, d=d, pat=pat, cm=cm: e.affine_select(
                        out=tri[d][:, :], in_=tri[d][:, :], pattern=pat, compare_op=ALU.is_ge, fill=0.0, base=0,
                        channel_multiplier=cm), r=[dc_b], w=[dc_b])
                    S.op("pool", lambda e, d=d: e.memset(neg3[d][:, :], 0.0), w=[dc_b])
                    S.op("pool", lambda e, d=d, pat=pat, cm=cm: e.affine_select(
                        out=neg3[d][:, :], in_=neg3[d][:, :], pattern=pat, compare_op=ALU.is_ge, fill=-BIG, base=0,
                        channel_multiplier=cm), r=[dc_b], w=[dc_b])
                    pat2, cm2 = ([[-1, 128]], 1) if d == 0 else ([[1, 128]], -1)
                    S.op("pool", lambda e, d=d: e.memset(pos1[d][:, :], 0.0), w=[dc_b])
                    S.op("pool", lambda e, d=d, pat2=pat2, cm2=cm2: e.affine_select(
                        out=pos1[d][:, :], in_=pos1[d][:, :], pattern=pat2, compare_op=ALU.is_gt, fill=BIG, base=0,
                        channel_multiplier=cm2), r=[dc_b], w=[dc_b])
                abraw = sb("abraw", [128, NT, 32], st=ls)
                tk_b = Buf("tokscal")
                tsc = sb("tsc", [128, 10, NT, 16], st=ls)
                U, L1, GG, LNB, BB, GT, EGt, NBEG, NEGG, GPL = [tsc[:, i, :, :] for i in range(10)]
                negA = sb("negA", [128, 16], st=ls)
                dnw = sb("dnw", [128, 256], st=ls)
                dnw_b = Buf("dnw")
                XP = [sb("XP0", [128, NSEG, 258], st=ls)] * 2
                XP_b = [Buf("XP0")] * 2
                acc = sqs[0][:, 0:NSEG, :]
                acc_b = sqs_b[0]
                tmb = sb("tmb", [128, NTOK], BF16, st=ls)
                tmb_b = Buf("tmb")
                rinv = sqs[0][:, 5:7, :].rearrange("p a b -> p (a b)")
                rinv_b = sqs_b[0]
                qT = sb("dqT", [128, NTOK], BF16, st=ls)
                kT = sb("dkT", [128, NTOK], BF16, st=ls)
                q_b, k_b = Buf("dq"), Buf("dk")
                bv = sb("bv", [128, NT * 2, 256], BF16, st=ls)
                bv_b = [Buf(f"bv{c}") for c in range(NT)]
                kd = sb("kd", [128, NT * 2, 128], BF16, st=ls)
                kd_b = [Buf(f"kd{c}") for c in range(NT)]
                TT = sb("TT", [128, NT * 2, 128], BF16, st=ls)
                TT_b = [Buf(f"TT{g}") for g in range(NSEG)]
                PTm = sb("PTm", [128, NT * 2, 128], BF16, st=ls)
                qg = sb("qg", [128, NT * 2, 128], BF16, st=ls)
                pq_b = [Buf(f"pq{c}") for c in range(NT)]
                csc = sb("csc", [128, 2, NT * 2], st=ls)
                csc_b = [Buf(f"csc{c}") for c in range(NT)]
                Es = [sb(f"Es{i}", [128, 3, 128], st=ls) for i in range(2)]
                Es_b = [Buf(f"Es{i}") for i in range(2)]
                def g4(nm):
                    return sb(nm, [128, 4, 128], BF16, st=ls), Buf(nm)
                Xg, Xg_b = zip(*[g4(f"Xg{i}") for i in range(2)])
                Yg, Yg_b = zip(*[g4(f"Yg{i}") for i in range(2)])
                Pg, Pg_b = zip(*[g4(f"Pg{i}") for i in range(2)])
                Qg, Qg_b = zip(*[g4(f"Qg{i}") for i in range(2)])
                Ng, Ng_b = zip(*[g4(f"Ng{i}") for i in range(2)])
                Wg, Wg_b = zip(*[g4(f"Wg{i}") for i in range(2)])
                Af, Af_b = g4("Af")
                Bf, Bf_b = g4("Bf")
                bmask = sb("bmask", [128, 4, 128], BF16, st=ls)
                S.dma("pool", bmask[:, :, :].rearrange("p a b -> p (a b)"), bmask_d, w=[dc_b])
                ob = sb("ob", [128, NT, 256], st=ls)
                ob_b = [Buf(f"ob{c}") for c in range(NT)]
                S32 = [sb(f"dS32_{d}", [128, 256], st=ls) for d in range(2)]
                S32_b = [Buf(f"dS32_{d}") for d in range(2)]
                S16 = [sb(f"dS16_{d}", [128, 256], BF16, st=ls) for d in range(2)]
                S16_b = [Buf(f"dS16_{d}") for d in range(2)]
                rr = [sb(f"rr{d}", [128, 256], BF16, st=ls) for d in range(2)]
                rr_b = [Buf(f"rr{d}") for d in range(2)]
                vn16 = [sb(f"vn{d}", [128, 256], BF16, st=ls) for d in range(2)]
                vn_b = [Buf(f"vn{d}") for d in range(2)]
                zs = [sb(f"dzs{i}", [128, 256], st=ls) for i in range(2)]
                zs_b = [Buf(f"dzs{i}") for i in range(2)]
                on = [sb(f"don{i}", [128, 256], st=ls) for i in range(2)]
                on_b = [Buf(f"don{i}") for i in range(2)]
                og = [sb(f"dog{i}", [128, 256], BF16, st=ls) for i in range(2)]
                og_b = [Buf(f"dog{i}") for i in range(2)]
                oT = [sb(f"doT{i}", [128, 2, SEGL], BF16, st=ls) for i in range(NSEG)]
                oT_b = [Buf(f"doT{i}") for i in range(NSEG)]
                bst = sb("dbst", [128, 2, 8], st=ls)
                bst_b = [Buf("dbst0"), Buf("dbst1")]

                norm_mod(l)

                sAB, sABb = ws_get()
                sABv = v8(sAB, 32)
                for c in range(NT):
                    tk = slice(c * 128, (c + 1) * 128)
                    bk, bb = psum()
                    for kt in range(8):
                        S.op("pe", lambda e, bk=bk, kt=kt: e.matmul(bk[:, 0:32], lhsT=hT[:, kt, tk], rhs=sABv[:, kt, :],
                                                                     start=(kt == 0), stop=(kt == 7)),
                             r=[sABb, hT_b[c // 2]], w=[bb])
                    S.op("act", lambda e, bk=bk, c=c: e.copy(out=abraw[:, c, :], in_=bk[:, 0:32]), r=[bb], w=[tk_b])
                ws_release()
                al = prm[:, o_alog + j * 16:o_alog + j * 16 + 16]
                dtb = prm[:, o_dtb + j * 16:o_dtb + j * 16 + 16]
                T = [tk_b, prm_b, cst_b]
                S.op("act", lambda e: e.activation(out=negA[:, :], in_=al, func=AF.Exp), r=T, w=[tk_b])
                S.op("dve", lambda e: e.tensor_scalar(out=negA[:, :], in0=negA[:, :], scalar1=-1.0, scalar2=None,
                                                      op0=ALU.mult), r=T, w=[tk_b])
                S.op("dve", lambda e: e.tensor_tensor(out=U, in0=abraw[:, :, 0:16],
                                                      in1=dtb.unsqueeze(1).to_broadcast([128, NT, 16]), op=ALU.add),
                     r=T, w=[tk_b])
                S.op("dve", lambda e: e.tensor_scalar(out=L1, in0=U, scalar1=-1.0, scalar2=None, op0=ALU.mult), r=T, w=[tk_b])
                S.op("dve", lambda e: e.tensor_tensor(out=L1, in0=L1, in1=U, op=ALU.max), r=T, w=[tk_b])
                S.op("act", lambda e: e.activation(out=L1, in_=L1, func=AF.Exp, scale=-1.0), r=T, w=[tk_b])
                S.op("act", lambda e: e.activation(out=L1, in_=L1, func=AF.Ln, bias=cpar[:, 2:3], scale=1.0), r=T, w=[tk_b])
                S.op("dve", lambda e: e.tensor_scalar(out=U, in0=U, scalar1=0.0, scalar2=None, op0=ALU.max), r=T, w=[tk_b])
                S.op("dve", lambda e: e.tensor_tensor(out=U, in0=U, in1=L1, op=ALU.add), r=T, w=[tk_b])
                S.op("dve", lambda e: e.tensor_tensor(out=GG, in0=U, in1=negA[:, :].unsqueeze(1).to_broadcast([128, NT, 16]),
                                                      op=ALU.mult), r=T, w=[tk_b])
                S.op("dve", lambda e: e.tensor_scalar(out=L1, in0=abraw[:, :, 16:32], scalar1=-1.0, scalar2=None, op0=ALU.mult),
                     r=T, w=[tk_b])
                S.op("dve", lambda e: e.tensor_tensor(out=L1, in0=L1, in1=abraw[:, :, 16:32], op=ALU.max), r=T, w=[tk_b])
                S.op("act", lambda e: e.activation(out=L1, in_=L1, func=AF.Exp, scale=-1.0), r=T, w=[tk_b])
                S.op("act", lambda e: e.activation(out=L1, in_=L1, func=AF.Ln, bias=cpar[:, 2:3], scale=1.0), r=T, w=[tk_b])
                S.op("dve", lambda e: e.tensor_scalar(out=LNB, in0=abraw[:, :, 16:32], scalar1=0.0, scalar2=None,
                                                      op0=ALU.min), r=T, w=[tk_b])
                S.op("dve", lambda e: e.tensor_tensor(out=LNB, in0=LNB, in1=L1, op=ALU.subtract), r=T, w=[tk_b])
                S.op("act", lambda e: e.activation(out=BB, in_=LNB, func=AF.Exp), r=T, w=[tk_b])
                bk, bb = psum()
                for c in range(NT):
                    for d in range(2):
                        S.op("pe", lambda e, c=c, d=d: e.matmul(bk[:, c * 16 + d * 8:c * 16 + d * 8 + 8], lhsT=tri[d][:, :],
                                                                 rhs=GG[:, c, d * 8:d * 8 + 8], start=True, stop=True),
                             r=[tk_b, dc_b], w=[bb])
                S.op("dve", lambda e: e.tensor_copy(out=GT, in_=bk[:, 0:NT * 16].rearrange("p (c x) -> p c x", c=NT)),
                     r=[bb], w=[tk_b])
                S.op("act", lambda e: e.activation(out=EGt, in_=GT, func=AF.Exp), r=T, w=[tk_b])
                S.op("dve", lambda e: e.tensor_tensor(out=NBEG, in0=BB, in1=EGt, op=ALU.mult), r=T, w=[tk_b])
                S.op("dve", lambda e: e.tensor_scalar(out=NBEG, in0=NBEG, scalar1=-1.0, scalar2=None, op0=ALU.mult),
                     r=T, w=[tk_b])
                S.op("dve", lambda e: e.tensor_scalar(out=NEGG, in0=GT, scalar1=-1.0, scalar2=None, op0=ALU.mult),
                     r=T, w=[tk_b])
                S.op("dve", lambda e: e.tensor_tensor(out=GPL, in0=GT, in1=LNB, op=ALU.add), r=T, w=[tk_b])
                for i in range(2):
                    S.op("dve", lambda e, i=i: e.memset(XP[i][:, 0, 0:1], 0.0), w=[XP_b[i]])
                    S.op("dve", lambda e, i=i: e.memset(XP[i][:, NSEG - 1, 257:258], 0.0), w=[XP_b[i]])

                dstage = float(os.environ.get("K_DSTAGE", "9"))
                if dstage == 0:
                    S.barrier()
                    return
                xp_rr = [0]
                for h in range(int(os.environ.get("K_DHEADS", "8"))):
                    sW, sWb = ws_get()
                    sWv = v8(sW)
                    S.dma("sp", dnw[:, :], dnw_d[:, j * 2048 + h * 256:j * 2048 + (h + 1) * 256], w=[dnw_b])
                    for ct in range(4):
                        xi = xp_rr[0] % 2
                        xp_rr[0] += 1
                        xp, xpb = XP[xi], XP_b[xi]
                        for s in range(NSEG):
                            bk, bb = psum()
                            for kt in range(8):
                                S.op("pe", lambda e, bk=bk, kt=kt, s=s: e.matmul(
                                    bk[:, 0:SEGL], lhsT=sWv[:, kt, ct * 128:(ct + 1) * 128],
                                    rhs=hT[:, kt, s * SEGL:(s + 1) * SEGL], start=(kt == 0), stop=(kt == 7)),
                                    r=[sWb, hT_b[s]], w=[bb])
                            S.op("act", lambda e, bk=bk, s=s: e.copy(out=xp[:, s, 1:257], in_=bk[:, 0:SEGL]), r=[bb], w=[xpb])
                        chn = prm[:, o_chain + 1:o_chain + 5]
                        S.op("dve", lambda e: e.tensor_tensor(out=xp[:, 1:5, 0], in0=xp[:, 0:4, 256], in1=chn, op=ALU.mult),
                             r=[xpb, prm_b], w=[xpb])
                        S.op("dve", lambda e: e.tensor_tensor(out=xp[:, 0:4, 257], in0=xp[:, 1:5, 1], in1=chn, op=ALU.mult),
                             r=[xpb, prm_b], w=[xpb])
                        gct = h if ct == 0 else (8 + h if ct == 1 else 16 + 2 * h + (ct - 2))
                        cw = [prm[:, o_convw + (j * 3 + k) * 32 + gct:o_convw + (j * 3 + k) * 32 + gct + 1] for k in range(3)]
                        S.op("dve", lambda e: e.tensor_scalar(out=acc, in0=xp[:, :, 0:256], scalar1=cw[0], scalar2=None,
                                                              op0=ALU.mult), r=[xpb, prm_b], w=[acc_b])
                        S.op("dve", lambda e: e.scalar_tensor_tensor(out=acc, in0=xp[:, :, 1:257], scalar=cw[1],
                                                                     in1=acc, op0=ALU.mult, op1=ALU.add),
                             r=[xpb, prm_b, acc_b], w=[acc_b])
                        S.op("dve", lambda e: e.scalar_tensor_tensor(out=acc, in0=xp[:, :, 2:258], scalar=cw[2],
                                                                     in1=acc, op0=ALU.mult, op1=ALU.add),
                             r=[xpb, prm_b, acc_b], w=[acc_b])
                        accf = acc.rearrange("p s t -> p (s t)")
                        S.op("act", lambda e: e.activation(out=accf, in_=accf, func=AF.Silu), r=[acc_b], w=[acc_b])
                        if ct < 2:
                            S.op("act", lambda e: e.activation(out=tmb[:, :], in_=accf, func=AF.Square), r=[acc_b], w=[tmb_b])
                            dst, dstb = (qT, q_b) if ct == 0 else (kT, k_b)
                            scl = (128.0 ** -0.5) if ct == 0 else 1.0
                            for (t0, n) in BLOCKS:
                                bk, bb = psum()
                                S.op("pe", lambda e, bk=bk: e.matmul(bk[:, 0:n], lhsT=ones_b[:, :], rhs=tmb[:, t0:t0 + n],
                                                                      start=True, stop=True), r=[tmb_b, cst_b], w=[bb])
                                S.op("act", lambda e, bk=bk: e.activation(out=rinv[:, 0:n], in_=bk[:, 0:n], func=AF.Ln,
                                                                          bias=cpar[:, 1:2], scale=1.0),
                                     r=[bb, cst_b], w=[rinv_b])
                                S.op("act", lambda e: e.activation(out=rinv[:, 0:n], in_=rinv[:, 0:n], func=AF.Exp, scale=-0.5),
                                     r=[rinv_b], w=[rinv_b])
                                S.op("dve", lambda e, dst=dst: e.scalar_tensor_tensor(
                                    out=dst[:, t0:t0 + n], in0=accf[:, t0:t0 + n], scalar=scl, in1=rinv[:, 0:n],
                                    op0=ALU.mult, op1=ALU.mult), r=[acc_b, rinv_b], w=[dstb])
                        else:
                            vt = ct - 2
                            S.op("act", lambda e: e.copy(out=tmb[:, :], in_=accf), r=[acc_b], w=[tmb_b])
                            for c in range(NT):
                                tk = slice(c * 128, (c + 1) * 128)
                                bk, bb = psum()
                                bkb = bk[:, :].bitcast(BF16)
                                S.op("pe", lambda e, bkb=bkb: e.transpose(bkb[:, 0:128], tmb[:, tk], ident_b[:]),
                                     r=[tmb_b, cst_b], w=[bb])
                                S.op("dve", lambda e, bkb=bkb, c=c: e.tensor_scalar(
                                    out=bv[:, c * 2 + 0, vt * 128:(vt + 1) * 128], in0=bkb[:, 0:128],
                                    scalar1=BB[:, c, h:h + 1], scalar2=None, op0=ALU.mult), r=[bb, tk_b], w=[bv_b[c]])
                                S.op("act", lambda e, bkb=bkb, c=c: e.activation(
                                    out=bv[:, c * 2 + 1, vt * 128:(vt + 1) * 128], in_=bkb[:, 0:128], func=AF.Copy,
                                    scale=BB[:, c, 8 + h:9 + h]), r=[bb, tk_b], w=[bv_b[c]])
                    ws_release()
                    if dstage == 1:
                        S.barrier()
                        return
                    sZ, sZb = ws_get()
                    sO, sOb = ws_get()
                    sZv = v8(sZ, 256)
                    sOv = sO[:, 0:2048].rearrange("p (v c) -> p v c", v=2)

                    for g in range(NSEG):
                        kkps = []
                        for ci in range(2):
                            c = 2 * g + ci
                            tk = slice(c * 128, (c + 1) * 128)
                            bk, bb = psum()
                            S.op("pe", lambda e, bk=bk: e.matmul(bk[:, 0:128], lhsT=kT[:, tk], rhs=kT[:, tk], start=True, stop=True),
                                 r=[k_b], w=[bb])
                            S.op("pe", lambda e, bk=bk: e.matmul(bk[:, 128:256], lhsT=kT[:, tk], rhs=qT[:, tk], start=True,
                                                                  stop=True), r=[k_b, q_b], w=[bb])
                            bkt = bk[:, :].bitcast(BF16)
                            S.op("pe", lambda e, bkt=bkt: e.transpose(bkt[:, 512:640], kT[:, tk], ident_b[:]),
                                 r=[k_b, cst_b], w=[bb])
                            kkps.append((bk, bb, bkt))
                        if dstage == 1.1:
                            S.barrier()
                            return
                        xg, xgb = Af, Af_b
                        for ci in range(2):
                            c = 2 * g + ci
                            tk = slice(c * 128, (c + 1) * 128)
                            bk, bb, bkt = kkps[ci]
                            for d in range(2):
                                qd = ci * 2 + d
                                cd = c * 2 + d
                                dh = d * 8 + h
                                ei = qd % 2
                                es, esb = Es[ei], Es_b[ei]
                                be, bbe = psum()
                                gcol = GG[:, c, dh:dh + 1].to_broadcast([128, 128])
                                S.op("pe", lambda e, be=be, d=d: e.matmul(be[:, 0:128], lhsT=gcol, rhs=tri[d][:, :], start=True,
                                                                           stop=True), r=[tk_b, dc_b], w=[bbe])
                                S.op("pe", lambda e, be=be, d=d: e.matmul(be[:, 128:256], lhsT=gcol, rhs=tri[d][:, :], start=True,
                                                                           stop=False), r=[tk_b, dc_b], w=[bbe])
                                S.op("pe", lambda e, be=be, d=d: e.matmul(be[:, 128:256], lhsT=ident_f[:, :], rhs=neg3[d][:, :],
                                                                           start=False, stop=True), r=[cst_b, dc_b], w=[bbe])
                                S.op("pe", lambda e, be=be, d=d: e.matmul(be[:, 256:384], lhsT=gcol, rhs=tri[d][:, :], start=True,
                                                                           stop=False), r=[tk_b, dc_b], w=[bbe])
                                S.op("pe", lambda e, be=be, d=d: e.matmul(be[:, 256:384], lhsT=ident_f[:, :], rhs=pos1[d][:, :],
                                                                           start=False, stop=True), r=[cst_b, dc_b], w=[bbe])
                                S.op("act", lambda e, be=be, es=es: e.activation(out=es[:, 0, :], in_=be[:, 0:128], func=AF.Exp),
                                     r=[bbe], w=[esb])
                                S.op("act", lambda e, be=be, es=es, c=c, dh=dh: e.activation(
                                    out=es[:, 1, :], in_=be[:, 128:256], func=AF.Exp, bias=NEGG[:, c, dh:dh + 1], scale=1.0),
                                    r=[bbe, tk_b], w=[esb])
                                S.op("act", lambda e, be=be, es=es, c=c, dh=dh: e.activation(
                                    out=es[:, 2, :], in_=be[:, 256:384], func=AF.Exp, bias=GPL[:, c, dh:dh + 1], scale=-1.0),
                                    r=[bbe, tk_b], w=[esb])
                                last = 127 if d == 0 else 0
                                S.op("dve", lambda e, es=es, cd=cd, last=last: e.tensor_copy(
                                    out=csc[:, 0, cd:cd + 1], in_=es[:, 0, last:last + 1]), r=[esb], w=[csc_b[c]])
                                S.op("dve", lambda e, es=es, cd=cd, last=last: e.tensor_copy(
                                    out=csc[:, 1, cd:cd + 1], in_=es[:, 1, last:last + 1]), r=[esb], w=[csc_b[c]])
                                S.op("dve", lambda e, es=es, bk=bk, qd=qd: e.tensor_tensor(
                                    out=xg[:, qd, :], in0=bk[:, 0:128], in1=es[:, 2, :], op=ALU.mult), r=[bb, esb], w=[xgb])
                                S.op("dve", lambda e, es=es, bk=bk, cd=cd: e.tensor_tensor(
                                    out=PTm[:, cd, :], in0=bk[:, 128:256], in1=es[:, 1, :], op=ALU.mult), r=[bb, esb], w=[pq_b[c]])
                                S.op("dve", lambda e, es=es, cd=cd: e.tensor_tensor(
                                    out=qg[:, cd, :], in0=qT[:, tk], in1=es[:, 0, :], op=ALU.mult), r=[q_b, esb], w=[pq_b[c]])
                                if d == 0:
                                    S.op("dve", lambda e, bkt=bkt, cd=cd: e.tensor_scalar(
                                        out=kd[:, cd, :], in0=bkt[:, 512:640], scalar1=csc[:, 1, cd:cd + 1], scalar2=None,
                                        op0=ALU.mult), r=[bb, csc_b[c]], w=[kd_b[c]])
                                else:
                                    S.op("act", lambda e, bkt=bkt, cd=cd: e.activation(
                                        out=kd[:, cd, :], in_=bkt[:, 512:640], func=AF.Copy, scale=csc[:, 1, cd:cd + 1]),
                                        r=[bb, csc_b[c]], w=[kd_b[c]])
                        if dstage == 1.2:
                            S.barrier()
                            return
                        bt, bbt = psum()
                        btb = bt[:, :].bitcast(BF16)
                        for qd in range(4):
                            S.op("pe", lambda e, qd=qd, btb=btb: e.transpose(btb[:, qd * 128:(qd + 1) * 128], xg[:, qd, :],
                                                                             ident_b[:]), r=[xgb, cst_b], w=[bbt])
                        bt4 = btb[:, 0:512].rearrange("p (q i) -> p q i", q=4)
                        S.op("act", lambda e: e.copy(out=Bf[:, :, :], in_=bt4), r=[bbt], w=[Bf_b])

                        def mk(k_):
                            return bmask[:, k_, :].unsqueeze(1).to_broadcast([128, 4, 128])

                        def masked(dst, dstb, srcT, srcb, k_):
                            S.op("pool", lambda e: e.tensor_tensor(out=dst[:, :, :], in0=srcT[:, :, :], in1=mk(k_), op=ALU.mult),
                                 r=[srcb, dc_b], w=[dstb])

                        masked(Xg[0], Xg_b[0], xg, xgb, 0)
                        masked(Yg[0], Yg_b[0], Bf, Bf_b, 0)
                        idb = ident_b[:, :].unsqueeze(1).to_broadcast([128, 4, 128])
                        S.op("dve", lambda e: e.scalar_tensor_tensor(out=Pg[0][:, :, :], in0=Yg[0][:, :, :], scalar=-1.0, in1=idb,
                                                                      op0=ALU.mult, op1=ALU.add), r=[Yg_b[0], cst_b], w=[Pg_b[0]])
                        S.op("dve", lambda e: e.scalar_tensor_tensor(out=Qg[0][:, :, :], in0=Xg[0][:, :, :], scalar=-1.0, in1=idb,
                                                                      op0=ALU.mult, op1=ALU.add), r=[Xg_b[0], cst_b], w=[Qg_b[0]])
                        if dstage == 1.3:
                            S.barrier()
                            return

                        def mm4(L, Lb, R, Rb):
                            bk_, bb_ = psum()
                            for qd in range(4):
                                S.op("pe", lambda e, qd=qd: e.matmul(bk_[:, qd * 128:(qd + 1) * 128], lhsT=L[:, qd, :], rhs=R[:, qd, :],
                                                                      start=True, stop=True), r=[Lb, Rb], w=[bb_])
                            return bk_[:, :].rearrange("p (q i) -> p q i", q=4), bb_

                        def ev_copy(dst, dstb, ps, psb):
                            S.op("act", lambda e: e.copy(out=dst[:, :, :], in_=ps), r=[psb], w=[dstb])

                        def ev_comb(dst, dstb, ps, psb, old, oldb, op):
                            if op == "add":
                                S.op("dve", lambda e: e.tensor_tensor(out=dst, in0=ps, in1=old[:, :, :], op=ALU.add),
                                     r=[psb, oldb], w=[dstb])
                            else:
                                S.op("dve", lambda e: e.scalar_tensor_tensor(out=dst, in0=ps, scalar=-1.0, in1=old[:, :, :],
                                                                             op0=ALU.mult, op1=ALU.add), r=[psb, oldb], w=[dstb])

                        pi = 0
                        for m in range(1, 4):
                            a, b2 = (m - 1) % 2, m % 2
                            px, pxb = mm4(Yg[a], Yg_b[a], Xg[a], Xg_b[a])
                            py, pyb = mm4(Xg[a], Xg_b[a], Yg[a], Yg_b[a])
                            ev_copy(Xg[b2], Xg_b[b2], px, pxb)
                            ev_copy(Yg[b2], Yg_b[b2], py, pyb)
                            pp, ppb = mm4(Xg[b2], Xg_b[b2], Pg[pi], Pg_b[pi])
                            pq, pqb = mm4(Yg[b2], Yg_b[b2], Qg[pi], Qg_b[pi])
                            ev_comb(Pg[1 - pi][:, :, :], Pg_b[1 - pi], pp, ppb, Pg[pi], Pg_b[pi], "add")
                            ev_comb(Qg[1 - pi][:, :, :], Qg_b[1 - pi], pq, pqb, Qg[pi], Qg_b[pi], "add")
                            pi = 1 - pi
                        for s_ in range(1, 4):
                            masked(Ng[0], Ng_b[0], xg, xgb, s_)
                            pw1, pw1b = mm4(Ng[0], Ng_b[0], Pg[pi], Pg_b[pi])
                            ev_copy(Wg[0], Wg_b[0], pw1, pw1b)
                            if s_ < 3:
                                masked(Ng[1], Ng_b[1], Bf, Bf_b, s_)
                                pw2, pw2b = mm4(Ng[1], Ng_b[1], Qg[pi], Qg_b[pi])
                                ev_copy(Wg[1], Wg_b[1], pw2, pw2b)
                            pp, ppb = mm4(Qg[pi], Qg_b[pi], Wg[0], Wg_b[0])
                            if s_ < 3:
                                pq, pqb = mm4(Pg[pi], Pg_b[pi], Wg[1], Wg_b[1])
                                ev_comb(Pg[1 - pi][:, :, :], Pg_b[1 - pi], pp, ppb, Pg[pi], Pg_b[pi], "sub")
                                ev_comb(Qg[1 - pi][:, :, :], Qg_b[1 - pi], pq, pqb, Qg[pi], Qg_b[pi], "sub")
                                pi = 1 - pi
                            else:
                                ev_comb(TT[:, g * 4:(g + 1) * 4, :], TT_b[g], pp, ppb, Pg[pi], Pg_b[pi], "sub")

                    if dbg_d and h == 0:
                        def dump(name, ap, bufs):
                            if name in dbg_d:
                                S.dma("pool", dbg_d[name], ap, r=bufs)
                        dump("tsc", tsc[:, :, :, :].rearrange("p a c x -> p (a c x)"), [tk_b])
                        dump("qT", qT[:, :], [q_b])
                        dump("kT", kT[:, :], [k_b])
                        dump("bv", bv[:, :, :].rearrange("p a b -> p (a b)"), bv_b)
                        dump("kd", kd[:, :, :].rearrange("p a b -> p (a b)"), kd_b)
                        dump("TT", TT[:, :, :].rearrange("p a b -> p (a b)"), TT_b)
                        dump("PTm", PTm[:, :, :].rearrange("p a b -> p (a b)"), pq_b)
                        dump("qg", qg[:, :, :].rearrange("p a b -> p (a b)"), pq_b)
                        dump("csc", csc[:, :, :].rearrange("p a b -> p (a b)"), csc_b)
                    if dstage == 2:
                        S.barrier()
                        return
                    S.dma("sp", S32[0][:, :], s0d_d[j, 0, h], w=[S32_b[0]])
                    S.op("dve", lambda e: e.memset(S32[1][:, :], 0.0), w=[S32_b[1]])
                    for d in range(2):
                        S.op("act", lambda e, d=d: e.copy(out=S16[d][:, :], in_=S32[d][:, :]), r=[S32_b[d]], w=[S16_b[d]])
                    arrived = [0] * NT
                    seg_done = [0] * NSEG

                    def consume_o(c, bo, bbo, first):
                        if first:
                            S.op("act", lambda e: e.copy(out=ob[:, c, :], in_=bo[:, 0:256]), r=[bbo], w=[ob_b[c]])
                        else:
                            oi = c % 2
                            S.op("dve", lambda e: e.tensor_tensor(out=on[oi][:, :], in0=bo[:, 0:256], in1=ob[:, c, :], op=ALU.add),
                                 r=[bbo, ob_b[c]], w=[on_b[oi]])

                    def finish_chunk(c, first):
                        s = c // 2
                        if first:
                            return
                        oi = c % 2
                        tk = slice(c * 128, (c + 1) * 128)
                        bkz, bbz = psum()
                        for kt in range(8):
                            S.op("pe", lambda e, kt=kt: e.matmul(bkz[:, 0:256], lhsT=hT[:, kt, tk], rhs=sZv[:, kt, :],
                                                                  start=(kt == 0), stop=(kt == 7)), r=[sZb, hT_b[s]], w=[bbz])
                        S.op("act", lambda e: e.activation(out=zs[oi][:, :], in_=bkz[:, 0:256], func=AF.Exp, scale=-1.0),
                             r=[bbz], w=[zs_b[oi]])
                        S.op("act", lambda e: e.activation(out=zs[oi][:, :], in_=zs[oi][:, :], func=AF.Ln, bias=cpar[:, 2:3], scale=1.0),
                             r=[zs_b[oi], cst_b], w=[zs_b[oi]])
                        S.op("act", lambda e: e.activation(out=zs[oi][:, :], in_=zs[oi][:, :], func=AF.Exp, scale=-1.0),
                             r=[zs_b[oi]], w=[zs_b[oi]])
                        S.op("dve", lambda e: e.tensor_tensor(out=zs[oi][:, :], in0=bkz[:, 0:256], in1=zs[oi][:, :], op=ALU.mult),
                             r=[bbz, zs_b[oi]], w=[zs_b[oi]])
                        S.op("dve", lambda e: e.bn_stats(out=bst[:, oi, 0:6], in_=on[oi][:, :]), r=[on_b[oi]], w=[bst_b[oi]])
                        S.op("dve", lambda e: e.bn_aggr(out=bst[:, oi, 6:8], in_=bst[:, oi, 0:6]), r=[bst_b[oi]], w=[bst_b[oi]])
                        S.op("dve", lambda e: e.scalar_tensor_tensor(out=bst[:, oi, 0:1], in0=bst[:, oi, 6:7], scalar=bst[:, oi, 6:7],
                                                                     in1=bst[:, oi, 7:8], op0=ALU.mult, op1=ALU.add),
                             r=[bst_b[oi]], w=[bst_b[oi]])
                        S.op("act", lambda e: e.activation(out=bst[:, oi, 1:2], in_=bst[:, oi, 0:1], func=AF.Ln, bias=cpar[:, 1:2],
                                                           scale=1.0), r=[bst_b[oi], cst_b], w=[bst_b[oi]])
                        S.op("act", lambda e: e.activation(out=bst[:, oi, 1:2], in_=bst[:, oi, 1:2], func=AF.Exp, scale=-0.5),
                             r=[bst_b[oi]], w=[bst_b[oi]])
                        S.op("dve", lambda e: e.scalar_tensor_tensor(out=on[oi][:, :], in0=on[oi][:, :], scalar=bst[:, oi, 1:2],
                                                                     in1=dnw[:, :], op0=ALU.mult, op1=ALU.mult),
                             r=[on_b[oi], bst_b[oi], dnw_b], w=[on_b[oi]])
                        S.op("dve", lambda e: e.tensor_tensor(out=og[oi][:, :], in0=on[oi][:, :], in1=zs[oi][:, :], op=ALU.mult),
                             r=[on_b[oi], zs_b[oi]], w=[og_b[oi]])
                        bk2, bb2 = psum()
                        bk2b = bk2[:, :].bitcast(BF16)
                        for vt in range(2):
                            S.op("pe", lambda e, vt=vt: e.transpose(bk2b[:, vt * 128:(vt + 1) * 128],
                                                                    og[oi][:, vt * 128:(vt + 1) * 128], ident_b[:]),
                                 r=[og_b[oi], cst_b], w=[bb2])
                        lo = (c % 2) * 128
                        S.op("act", lambda e: e.copy(out=oT[s][:, :, lo:lo + 128],
                                                     in_=bk2b[:, 0:256].rearrange("p (v t) -> p v t", v=2)), r=[bb2], w=[oT_b[s]])
                        seg_done[s] += 1
                        if seg_done[s] == 2:
                            seg = slice(s * SEGL, (s + 1) * SEGL)
                            for dt in range(8):
                                bk, bb = psum()
                                for vt in range(2):
                                    S.op("pe", lambda e, vt=vt, dt=dt, bk=bk: e.matmul(
                                        bk[:, 0:SEGL], lhsT=sOv[:, vt, dt * 128:(dt + 1) * 128], rhs=oT[s][:, vt, :],
                                        start=(vt == 0), stop=(vt == 1)), r=[sOb, oT_b[s]], w=[bb])
                                S.op("dve", lambda e, bk=bk, dt=dt: e.scalar_tensor_tensor(
                                    out=xT[:, dt, seg], in0=bk[:, 0:SEGL], scalar=mod_ap(l, 2, dt, s), in1=xT[:, dt, seg],
                                    op0=ALU.mult, op1=ALU.add), r=[bb, modT_b, xT_b[s]], w=[xT_b[s]])

                    for step in range(NT):
                        cs_ = [step, NT - 1 - step]
                        st1 = []
                        for d in range(2):
                            c = cs_[d]
                            tk = slice(c * 128, (c + 1) * 128)
                            bk, bb = psum()
                            S.op("pe", lambda e, bk=bk, d=d: e.matmul(bk[:, 0:256], lhsT=kT[:, tk], rhs=S16[d][:, :], start=True,
                                                                       stop=True), r=[k_b, S16_b[d]], w=[bb])
                            st1.append((bk, bb))
                        for d in range(2):
                            c = cs_[d]
                            bk, bb = st1[d]
                            S.op("dve", lambda e, bk=bk, d=d, c=c: e.scalar_tensor_tensor(
                                out=rr[d][:, :], in0=bk[:, 0:256], scalar=NBEG[:, c, d * 8 + h:d * 8 + h + 1],
                                in1=bv[:, c * 2 + d, :], op0=ALU.mult, op1=ALU.add), r=[bb, tk_b, bv_b[c]], w=[rr_b[d]])
                        st3 = []
                        for d in range(2):
                            c = cs_[d]
                            bk, bb = psum()
                            S.op("pe", lambda e, bk=bk, d=d, c=c: e.matmul(bk[:, 0:256], lhsT=TT[:, c * 2 + d, :], rhs=rr[d][:, :],
                                                                            start=True, stop=True), r=[TT_b[c // 2], rr_b[d]], w=[bb])
                            st3.append((bk, bb))
                        for d in range(2):
                            bk, bb = st3[d]
                            S.op("act", lambda e, bk=bk, d=d: e.copy(out=vn16[d][:, :], in_=bk[:, 0:256]), r=[bb], w=[vn_b[d]])
                        st5 = []
                        for d in range(2):
                            c = cs_[d]
                            bo, bbo = psum()
                            S.op("pe", lambda e, bo=bo, d=d, c=c: e.matmul(bo[:, 0:256], lhsT=qg[:, c * 2 + d, :], rhs=S16[d][:, :],
                                                                            start=True, stop=False), r=[pq_b[c], S16_b[d]], w=[bbo])
                            S.op("pe", lambda e, bo=bo, d=d, c=c: e.matmul(bo[:, 0:256], lhsT=PTm[:, c * 2 + d, :], rhs=vn16[d][:, :],
                                                                            start=False, stop=True), r=[pq_b[c], vn_b[d]], w=[bbo])
                            S.op("pe", lambda e, bo=bo, d=d, c=c: e.matmul(bo[:, 256:512], lhsT=kd[:, c * 2 + d, :], rhs=vn16[d][:, :],
                                                                            start=True, stop=True), r=[kd_b[c], vn_b[d]], w=[bbo])
                            st5.append((bo, bbo))
                        for d in range(2):
                            c = cs_[d]
                            s = c // 2
                            bo, bbo = st5[d]
                            cd = c * 2 + d
                            S.op("dve", lambda e, bo=bo, d=d, cd=cd: e.scalar_tensor_tensor(
                                out=S32[d][:, :], in0=S32[d][:, :], scalar=csc[:, 0, cd:cd + 1], in1=bo[:, 256:512],
                                op0=ALU.mult, op1=ALU.add), r=[bbo, csc_b[c], S32_b[d]], w=[S32_b[d]])
                            seg_end = (c % 2 == 1) if d == 0 else (c % 2 == 0)
                            if seg_end:
                                st_t, st_bf = stage()
                                S.op("act", lambda e, st_t=st_t, d=d: e.copy(out=st_t[:, 0:256], in_=S32[d][:, :]),
                                     r=[S32_b[d]], w=[st_bf])
                                S.dma("sp", nsd_d[s, j, d, h], st_t[:, 0:256], r=[st_bf])
                                if d == 0 and c < NT - 1:
                                    S.op("dve", lambda e, s=s: e.tensor_scalar(out=S32[0][:, :], in0=S32[0][:, :],
                                                                               scalar1=chain_ap(s + 1), scalar2=None, op0=ALU.mult),
                                         r=[S32_b[0], prm_b], w=[S32_b[0]])
                                if d == 1 and c > 0:
                                    if s - 1 == 3:
                                        st2, st2b = stage()
                                        S.dma("sp", st2[:, 0:256], s0d_d[j, 1, h], w=[st2b])
                                        S.op("dve", lambda e, s=s, st2=st2: e.scalar_tensor_tensor(
                                            out=S32[1][:, :], in0=S32[1][:, :], scalar=chain_ap(s), in1=st2[:, 0:256],
                                            op0=ALU.mult, op1=ALU.add), r=[S32_b[1], prm_b, st2b], w=[S32_b[1]])
                                    else:
                                        S.op("dve", lambda e, s=s: e.tensor_scalar(out=S32[1][:, :], in0=S32[1][:, :],
                                                                                   scalar1=chain_ap(s), scalar2=None, op0=ALU.mult),
                                             r=[S32_b[1], prm_b], w=[S32_b[1]])
                            if step < NT - 1:
                                S.op("act", lambda e, d=d: e.copy(out=S16[d][:, :], in_=S32[d][:, :]), r=[S32_b[d]], w=[S16_b[d]])
                        for d in range(2):
                            c = cs_[d]
                            bo, bbo = st5[d]
                            arrived[c] += 1
                            consume_o(c, bo, bbo, arrived[c] == 1)
                        for d in range(2):
                            c = cs_[d]
                            finish_chunk(c, arrived[c] == 1)
                    ws_release()
                    ws_release()
                S.barrier()

        def final_out():
            for s in range(NSEG):
                sq, sqb, rs, rsb = rms_stat(s)
                seg = slice(s * SEGL, (s + 1) * SEGL)
                S.op("dve", lambda e: e.tensor_tensor(out=sq[:, :, :], in0=xT[:, :, seg],
                                                      in1=rs[:, :].unsqueeze(1).to_broadcast([128, 8, 256]),
                                                      op=ALU.mult), r=[xT_b[s], rsb], w=[sqb])
                for dt in range(8):
                    S.op("dve", lambda e, dt=dt: e.tensor_scalar(
                        out=sq[:, dt, :], in0=sq[:, dt, :], scalar1=prm[:, o_normw + 32 + dt:o_normw + 33 + dt],
                        scalar2=32.0, op0=ALU.mult, op1=ALU.mult), r=[sqb, prm_b], w=[sqb])
                for half in range(2):
                    tt = s * 2 + half
                    st_t, st_bf = stage()
                    for g in range(2):
                        bk, bb = psum()
                        for q in range(4):
                            dt = g * 4 + q
                            S.op("pe", lambda e, bk=bk, q=q, dt=dt: e.transpose(
                                bk[:, q * 128:(q + 1) * 128], sq[:, dt, half * 128:(half + 1) * 128], ident_f[:]),
                                r=[sqb, cst_b], w=[bb])
                        if g == 0:
                            S.op("dve", lambda e, bk=bk: e.tensor_copy(out=st_t[:, 0:512], in_=bk[:, :]), r=[bb], w=[st_bf])
                        else:
                            S.op("act", lambda e, bk=bk: e.copy(out=st_t[:, 512:1024], in_=bk[:, :]), r=[bb], w=[st_bf])
                    S.dma("sp", y_d[tt * 128:(tt + 1) * 128, :], st_t[:, :], r=[st_bf])

        for l in range(depth):
            if l % 2 == 0:
                ret_layer(l, l // 2)
            else:
                del_layer(l, l // 2)
        final_out()
        S.barrier()
        print(f"[build] instr={S.ninstr} sems={S.nsem} counts={S.cnt}")
    return nc


def _rope_tables():
    pos = np.arange(1024)
    r = (pos // 64).astype(np.float32)
    col = (pos % 64).astype(np.float32)
    freqs = (10000.0 ** (-np.arange(64, dtype=np.float32) / 64.0)).astype(np.float32)
    ang = np.concatenate([r[:, None] * freqs[None, :], col[:, None] * freqs[None, :]], -1)
    return np.cos(ang).T.astype(np.float32), np.sin(ang).T.astype(np.float32)


def _fm(v, nt):
    return np.ascontiguousarray(np.asarray(v, np.float32).reshape(nt, 128).T)


def make_in_maps(inp):
    f = lambda k: np.ascontiguousarray(np.asarray(inp[k], dtype=np.float32))
    xp, xs = f("x_prompt"), f("x_sample")
    c, c_ctx = f("c"), f("c_ctx")
    sr, sd = f("state_ret"), f("state_delta")
    norm_w, fnw, mod_b = f("norm_w"), f("final_norm_w"), f("mod_b")
    normw = np.concatenate([_fm(norm_w[l], 8) for l in range(4)] + [_fm(fnw, 8)], axis=1)
    modb = np.concatenate([_fm(mod_b[l][p * 1024:(p + 1) * 1024], 8) for l in range(4) for p in range(3)], axis=1)
    rdecb = np.ascontiguousarray(np.broadcast_to(f("ret_decay").reshape(1, 16), (128, 16)))
    gnwb = np.ascontiguousarray(np.broadcast_to(f("ret_gn_w").reshape(1, 4096), (128, 4096)))
    dnwb = np.ascontiguousarray(np.broadcast_to(f("del_norm_w").reshape(1, 4096), (128, 4096)))
    cw = f("del_conv_w")
    convw = np.concatenate([_fm(cw[jj, k], 32) for jj in range(2) for k in range(3)], axis=1)
    alogb = np.ascontiguousarray(np.broadcast_to(f("del_a_log").reshape(1, 32), (128, 32)))
    dtbb = np.ascontiguousarray(np.broadcast_to(f("del_dt_bias").reshape(1, 32), (128, 32)))
    cosS, sinS = _rope_tables()
    ii = np.arange(128)
    same = lambda b: (ii[:, None] // b) == (ii[None, :] // b)
    bm = [same(16), same(32) & ~same(16), same(64) & ~same(32), ~same(64)]
    bmask = np.concatenate([m.astype(np.float32) for m in bm], axis=1)
    shared = dict(bmask=bmask, normw=normw, modb=modb, mod_w=f("mod_w"), ret_w_in=f("ret_w_in"), ret_w_out=f("ret_w_out"),
                  del_w_in=f("del_w_in"), del_w_out=f("del_w_out"), rdecb=rdecb, gnwb=gnwb, dnwb=dnwb, convw=convw,
                  alogb=alogb, dtbb=dtbb)
    maps = []
    for core in range(8):
        m = dict(shared)
        chain = np.zeros((128, 8), np.float32)
        cosT = np.ones((128, NTOK), np.float32)
        sinT = np.zeros((128, NTOK), np.float32)
        if core < 6:
            x = xp[core * 5:(core + 1) * 5].reshape(NTOK, D)
            conds = [c_ctx] * 5
            s0r = np.zeros((2, 2, 4, 256, 512), np.float32)
            s0d = np.zeros((2, 2, 8, 128, 256), np.float32)
        else:
            b = core - 6
            x = np.concatenate([xs[b], xp[30 + b]], axis=0)
            conds = [c[b]] * 4 + [c_ctx]
            chain[:, 1:4] = 1.0
            s0r, s0d = sr[b], sd[b]
            cosT[:, 0:1024] = cosS
            sinT[:, 0:1024] = sinS
        condT = np.zeros((128, 40), np.float32)
        for s in range(5):
            condT[:, s::5] = _fm(conds[s], 8)
        m.update(x=np.ascontiguousarray(x), condT=condT, chainb=chain, s0r=np.ascontiguousarray(s0r),
                 s0d=np.ascontiguousarray(s0d), cosT=cosT, sinT=sinT)
        maps.append(m)
    return maps


def assemble(results):
    y_prompt = np.zeros((32, 256, D), np.float32)
    y_sample = np.zeros((2, 1024, D), np.float32)
    nsr = np.zeros((32, 2, 2, 4, 256, 512), np.float32)
    nsd = np.zeros((32, 2, 2, 8, 128, 256), np.float32)
    for core in range(8):
        r = results[core]
        y = r["y"].reshape(5, 256, D)
        if core < 6:
            y_prompt[core * 5:(core + 1) * 5] = y
            nsr[core * 5:(core + 1) * 5] = r["nsr"]
            nsd[core * 5:(core + 1) * 5] = r["nsd"]
        else:
            b = core - 6
            y_sample[b] = y[0:4].reshape(1024, D)
            y_prompt[30 + b] = y[4]
            nsr[30 + b] = r["nsr"][4]
            nsd[30 + b] = r["nsd"][4]
    return y_prompt, y_sample, nsr, nsd


_NC_CACHE = {}


def kernel(**inputs):
    if "nc" not in _NC_CACHE:
        _NC_CACHE["nc"] = build()
    maps = make_in_maps(inputs)
    res = run_bass_kernel_spmd(_NC_CACHE["nc"], maps, core_ids=list(range(8)))
    return assemble(res.results)
```

```python
import contextlib
import numpy as np
import concourse.bass as bass
import concourse.mybir as mybir
from concourse.bass_utils import run_bass_kernel_spmd

F32 = mybir.dt.float32
BF16 = mybir.dt.bfloat16
AF = mybir.ActivationFunctionType
ALU = mybir.AluOpType

D = 1024
NSEG = 5
SEGL = 256
NTOK = NSEG * SEGL
NT = NTOK // 128
DEPTH = 4
EPS = 1e-6
BLOCKS = [(0, 512), (512, 512), (1024, 256)]
import os
EPOCH = int(os.environ.get("K_EPOCH", "4000"))
BIG = 30000.0


class Buf:
    __slots__ = ("name", "lw", "rd", "dsem", "dcnt")

    def __init__(self, name):
        self.name = name
        self.lw = None
        self.rd = {}
        self.dsem = None
        self.dcnt = 0


class Sched:
    def __init__(self, nc, stack):
        self.nc = nc
        self.stack = stack
        self.engs = {"pe": nc.tensor, "act": nc.scalar, "dve": nc.vector, "pool": nc.gpsimd, "sp": nc.sync}
        self.cnt = {e: 0 for e in self.engs}
        self.esems = {e: [] for e in self.engs}
        self.known = {e: {} for e in self.engs}
        self.dsems = {}
        self.dma_bufs = []
        self.nsem = 0
        self.ninstr = 0
        self.pe_pending = None

    def _newsem(self, name):
        self.nsem += 1
        return self.stack.enter_context(self.nc.semaphore(f"{name}_{self.nsem}"))

    def _esem(self, e, epoch):
        lst = self.esems[e]
        while len(lst) <= epoch:
            lst.append(self._newsem(f"s_{e}_{len(lst)}"))
        return lst[epoch]

    def _need(self, E, ev, raw):
        key, val = ev
        if key == ("e", E) and not raw:
            return
        kn = self.known[E]
        if kn.get(key, 0) >= val:
            return
        kn[key] = val
        eng = self.engs[E]
        if key[0] == "e":
            n = val - 1
            eng.wait_ge(self._esem(key[1], n // EPOCH), n % EPOCH + 1)
        else:
            eng.wait_ge(self.dsems[key], val)
        self.ninstr += 1

    def _deps(self, E, r, w):
        for b in r:
            if b.lw is not None:
                self._need(E, b.lw, True)
        for b in w:
            if b.lw is not None:
                self._need(E, b.lw, False)
            for k, v in b.rd.items():
                self._need(E, (k, v), False)

    def _post(self, ev, r, w):
        k, v = ev
        for b in r:
            if b.rd.get(k, 0) < v:
                b.rd[k] = v
        for b in w:
            b.lw = ev
            b.rd = {}

    def _commit_pe(self):
        if self.pe_pending is None:
            return
        ins, _ = self.pe_pending
        n = self.cnt["pe"]
        ins.then_inc(self._esem("pe", n // EPOCH), 1)
        self.cnt["pe"] = n + 1
        self.pe_pending = None

    def _touch(self, E, r, w):
        if self.pe_pending is None:
            return
        pw = self.pe_pending[1]
        if E == "pe":
            if tuple(id(b) for b in w) != pw:
                self._commit_pe()
        elif any(id(b) in pw for b in r) or any(id(b) in pw for b in w):
            self._commit_pe()

    def op(self, E, fn, r=(), w=()):
        self._touch(E, r, w)
        self._deps(E, r, w)
        ins = fn(self.engs[E])
        if E == "pe":
            self.pe_pending = (ins, tuple(id(b) for b in w))
            self.ninstr += 1
            self._post((("e", "pe"), self.cnt["pe"] + 1), r, w)
            return ins
        n = self.cnt[E]
        ins.then_inc(self._esem(E, n // EPOCH), 1)
        self.cnt[E] = n + 1
        self.ninstr += 1
        self._post((("e", E), n + 1), r, w)
        return ins

    def dma(self, Q, out, in_, r=(), w=(), **kw):
        self._touch(Q, r, w)
        self._deps(Q, r, w)
        b0 = (list(w) + list(r))[0]
        if b0.dsem is None:
            b0.dsem = self._newsem("d_" + b0.name)
            self.dsems[("d", id(b0))] = b0.dsem
            self.dma_bufs.append(b0)
        ins = self.engs[Q].dma_start(out=out, in_=in_, **kw)
        ins.then_inc(b0.dsem, 16)
        b0.dcnt += 16
        self.ninstr += 1
        self._post((("d", id(b0)), b0.dcnt), r, w)

    def barrier(self):
        self._commit_pe()
        for E in self.engs:
            for e2 in self.engs:
                if self.cnt[e2] > 0:
                    self._need(E, (("e", e2), self.cnt[e2]), True)
            for b in self.dma_bufs:
                self._need(E, (("d", id(b)), b.dcnt), True)


def build(depth=DEPTH, dbg=None):
    nc = bass.Bass("TRN2", target_bir_lowering=False)
    dbg = dbg or {}

    def din(name, shape):
        return nc.dram_tensor(name, list(shape), F32, kind="ExternalInput").ap()

    def dout(name, shape):
        return nc.dram_tensor(name, list(shape), F32, kind="ExternalOutput").ap()

    x_d = din("x", [NTOK, D])
    cond_d = din("condT", [128, 40])
    chain_d = din("chainb", [128, 8])
    normw_d = din("normw", [128, 40])
    modb_d = din("modb", [128, 96])
    modw_d = din("mod_w", [4, D, 3 * D])
    rwin_d = din("ret_w_in", [2, D, 6144])
    rwout_d = din("ret_w_out", [2, 2048, D])
    dwin_d = din("del_w_in", [2, D, 6176])
    dwout_d = din("del_w_out", [2, 2048, D])
    rdec_d = din("rdecb", [128, 16])
    gnw_d = din("gnwb", [128, 4096])
    dnw_d = din("dnwb", [128, 4096])
    convw_d = din("convw", [128, 192])
    alog_d = din("alogb", [128, 32])
    dtb_d = din("dtbb", [128, 32])
    s0r_d = din("s0r", [2, 2, 4, 256, 512])
    s0d_d = din("s0d", [2, 2, 8, 128, 256])
    cos_d = din("cosT", [128, NTOK])
    sin_d = din("sinT", [128, NTOK])
    bmask_d = din("bmask", [128, 512])
    y_d = dout("y", [NTOK, D])
    nsr_d = dout("nsr", [NSEG, 2, 2, 4, 256, 512])
    nsd_d = dout("nsd", [NSEG, 2, 2, 8, 128, 256])
    dbg_d = {k: dout("dbg_" + k, shp) for k, shp in dbg.items()}

    with contextlib.ExitStack() as stack:
        S = Sched(nc, stack)

        sb_n = [0]

        def sb(name, shape, dt=F32, st=None):
            sb_n[0] += 1
            return (st or stack).enter_context(nc.sbuf_tensor(f"sb{sb_n[0]}_{name}", list(shape), dt))

        banks = [stack.enter_context(nc.psum_tensor(f"bank{i}", [128, 512], F32)) for i in range(8)]
        bank_bufs = [Buf(f"bank{i}") for i in range(8)]
        bank_rr = [0]

        def psum():
            i = bank_rr[0] % 8
            bank_rr[0] += 1
            return banks[i], bank_bufs[i]

        xT = sb("xT", [128, 8, NTOK])
        xT_b = [Buf(f"xT{s}") for s in range(NSEG)]
        hT = sb("hT", [128, 8, NTOK], BF16)
        hT_b = [Buf(f"hT{s}") for s in range(NSEG)]
        NSLAB = 3
        ring = [sb(f"slab{i}", [128, 4096], BF16) for i in range(NSLAB)]
        ring_b = [Buf(f"slab{i}") for i in range(NSLAB)]
        ring_rr = [0]

        ident_f = sb("ident_f", [128, 128])
        ident_b = sb("ident_b", [128, 128], BF16)
        ones_f = sb("ones_f", [128, 128])
        ones_b = sb("ones_b", [128, 128], BF16)
        cst_b = Buf("consts")
        cpar = sb("cpar", [128, 8])
        prm = sb("prm", [128, 40 + 8 + 40 + 96 + 16 + 192 + 32 + 32])
        prm_b = Buf("prm")
        o_cond, o_chain, o_normw, o_modb, o_rdec, o_convw, o_alog, o_dtb = 0, 40, 48, 88, 184, 200, 392, 424
        modT = sb("modT", [128, 4 * 3 * 40])
        modT_b = Buf("modT")
        sc1 = sb("sc1", [128, 4 * 40])
        scT = sb("scT", [128, 8, NSEG], BF16)
        sqs = [sb("sq0", [128, 8, 256])]
        sqs_b = [Buf("sq0")]
        rstd = [sb("rstd0", [128, 256])]
        rstd_b = [Buf("rstd0")]
        stg = [sb(f"stg{i}", [128, 1024]) for i in range(2)]
        stg_b = [Buf(f"stg{i}") for i in range(2)]
        stg_rr = [0]

        def stage():
            i = stg_rr[0] % 2
            stg_rr[0] += 1
            return stg[i], stg_b[i]

        def chain_ap(s):
            return prm[:, o_chain + s:o_chain + s + 1]

        S.op("pool", lambda e: e.memset(ident_f[:], 1.0), w=[cst_b])
        S.op("pool", lambda e: e.affine_select(out=ident_f[:], in_=ident_f[:], pattern=[[-1, 128]],
                                                compare_op=ALU.is_equal, fill=0.0, base=0, channel_multiplier=1),
             r=[cst_b], w=[cst_b])
        S.op("pool", lambda e: e.tensor_copy(out=ident_b[:], in_=ident_f[:]), r=[cst_b], w=[cst_b])
        S.op("pool", lambda e: e.memset(ones_f[:], 1.0), w=[cst_b])
        S.op("pool", lambda e: e.memset(ones_b[:], 1.0), w=[cst_b])
        S.op("pool", lambda e: e.memset(cpar[:, 0:1], 1024.0 * EPS), w=[cst_b])
        S.op("pool", lambda e: e.memset(cpar[:, 1:2], EPS), w=[cst_b])
        S.op("pool", lambda e: e.memset(cpar[:, 2:3], 1.0), w=[cst_b])
        S.op("pool", lambda e: e.memset(cpar[:, 3:4], 0.0), w=[cst_b])
        S.op("pool", lambda e: e.memset(cpar[:, 4:5], -0.5), w=[cst_b])

        for off, n, src in ((o_cond, 40, cond_d), (o_chain, 8, chain_d), (o_normw, 40, normw_d), (o_modb, 96, modb_d),
                            (o_rdec, 16, rdec_d), (o_convw, 192, convw_d), (o_alog, 32, alog_d), (o_dtb, 32, dtb_d)):
            S.dma("sp", prm[:, off:off + n], src, w=[prm_b])

        def v8(t, n=512):
            return t[:, 0:8 * n].rearrange("p (kt c) -> p kt c", kt=8)

        def win_view(w2d):
            return w2d.rearrange("(kt p) c -> p kt c", p=128)

        wspecs = []
        for l in range(depth):
            for part in range(3):
                for hf in range(2):
                    c0 = part * 1024 + hf * 512
                    wspecs.append(lambda t, l=l, c0=c0: [(v8(t), win_view(modw_d[l])[:, :, c0:c0 + 512])])
        for l in range(depth):
            j = l // 2
            if l % 2 == 0:
                wv = win_view(rwin_d[j])
                for h in range(4):
                    wspecs.append(lambda t, h=h, wv=wv: [
                        (v8(t)[:, :, 0:256], wv[:, :, h * 256:(h + 1) * 256]),
                        (v8(t)[:, :, 256:512], wv[:, :, 1024 + h * 256:1024 + (h + 1) * 256])])
                    wspecs.append(lambda t, h=h, wv=wv: [(v8(t), wv[:, :, 2048 + h * 512:2048 + (h + 1) * 512])])
                    wspecs.append(lambda t, h=h, wv=wv: [(v8(t), wv[:, :, 4096 + h * 512:4096 + (h + 1) * 512])])
                    wspecs.append(lambda t, h=h, j=j: [
                        (t[:, :].rearrange("p (v c) -> p v c", v=4),
                         rwout_d[j][h * 512:(h + 1) * 512, :].rearrange("(v p) c -> p v c", p=128))])
            else:
                wv = win_view(dwin_d[j])
                wspecs.append(lambda t, wv=wv: [(v8(t, 32), wv[:, :, 6144:6176])])
                for h in range(8):
                    wspecs.append(lambda t, h=h, wv=wv: [
                        (v8(t)[:, :, 0:128], wv[:, :, h * 128:(h + 1) * 128]),
                        (v8(t)[:, :, 128:256], wv[:, :, 1024 + h * 128:1024 + (h + 1) * 128]),
                        (v8(t)[:, :, 256:512], wv[:, :, 2048 + h * 256:2048 + (h + 1) * 256])])
                    wspecs.append(lambda t, h=h, wv=wv: [(v8(t, 256), wv[:, :, 4096 + h * 256:4096 + (h + 1) * 256])])
                    wspecs.append(lambda t, h=h, j=j: [
                        (t[:, 0:2048].rearrange("p (v c) -> p v c", v=2),
                         dwout_d[j][h * 256:(h + 1) * 256, :].rearrange("(v p) c -> p v c", p=128))])
        ws = {"issued": 0, "released": 0, "got": 0}

        def ws_issue():
            while ws["issued"] < len(wspecs) and ws["issued"] < ws["released"] + NSLAB:
                i = ws["issued"]
                for (dst_view, src_ap) in wspecs[i](ring[i % NSLAB]):
                    S.dma("pool", dst_view, src_ap, w=[ring_b[i % NSLAB]])
                ws["issued"] += 1

        def ws_get():
            ws_issue()
            i = ws["got"]
            assert i < ws["issued"], "weight ring over-subscribed"
            ws["got"] += 1
            return ring[i % NSLAB], ring_b[i % NSLAB]

        def ws_release():
            ws["released"] += 1
            ws_issue()

        for tt in range(NT):
            st_t, st_b = stage()
            S.dma("sp", st_t[:, :], x_d[tt * 128:(tt + 1) * 128, :], w=[st_b])
            for half in range(2):
                bk, bb = psum()
                for q in range(4):
                    dt = half * 4 + q
                    S.op("pe", lambda e, bk=bk, q=q, dt=dt, st_t=st_t: e.transpose(
                        bk[:, q * 128:(q + 1) * 128], st_t[:, dt * 128:(dt + 1) * 128], ident_f[:]),
                        r=[st_b, cst_b], w=[bb])
                eng = "dve" if half == 0 else "act"
                dst = xT[:, half * 4:half * 4 + 4, tt * 128:(tt + 1) * 128]
                src = bk[:, :].rearrange("p (q t) -> p q t", q=4)
                if eng == "dve":
                    S.op("dve", lambda e, dst=dst, src=src: e.tensor_copy(out=dst, in_=src), r=[bb], w=[xT_b[tt // 2]])
                else:
                    S.op("act", lambda e, dst=dst, src=src: e.copy(out=dst, in_=src), r=[bb], w=[xT_b[tt // 2]])

        S.op("act", lambda e: e.activation(out=scT[:].rearrange("p a b -> p (a b)"), in_=prm[:, o_cond:o_cond + 40],
                                           func=AF.Silu), r=[prm_b], w=[modT_b])
        for l in range(depth):
            for part in range(3):
                for hf in range(2):
                    c0 = part * 1024 + hf * 512
                    sl, slb = ws_get()
                    slv = v8(sl)
                    bk, bb = psum()
                    for ct in range(4):
                        for kt in range(8):
                            S.op("pe", lambda e, bk=bk, ct=ct, kt=kt, slv=slv: e.matmul(
                                bk[:, ct * 5:ct * 5 + 5], lhsT=slv[:, kt, ct * 128:(ct + 1) * 128], rhs=scT[:, kt, :],
                                start=(kt == 0), stop=(kt == 7)), r=[slb, modT_b], w=[bb])
                    base = (l * 3 + part) * 40 + hf * 20
                    mb = prm[:, o_modb + (l * 3 + part) * 8 + hf * 4: o_modb + (l * 3 + part) * 8 + hf * 4 + 4]
                    S.op("dve", lambda e, bk=bk, base=base, mb=mb: e.tensor_tensor(
                        out=modT[:, base:base + 20].rearrange("p (a b) -> p a b", a=4),
                        in0=bk[:, 0:20].rearrange("p (a b) -> p a b", a=4),
                        in1=mb.unsqueeze(2).to_broadcast([128, 4, 5]), op=ALU.add), r=[bb, prm_b], w=[modT_b])
                    ws_release()
            sv = modT[:, (l * 3 + 1) * 40:(l * 3 + 1) * 40 + 40]
            S.op("dve", lambda e, l=l, sv=sv: e.tensor_scalar(out=sc1[:, l * 40:(l + 1) * 40], in0=sv, scalar1=1.0,
                                                             scalar2=32.0, op0=ALU.add, op1=ALU.mult),
                 r=[modT_b], w=[modT_b])
            S.op("dve", lambda e, l=l: e.tensor_tensor(
                out=sc1[:, l * 40:(l + 1) * 40].rearrange("p (a b) -> p a b", a=8),
                in0=sc1[:, l * 40:(l + 1) * 40].rearrange("p (a b) -> p a b", a=8),
                in1=prm[:, o_normw + l * 8:o_normw + l * 8 + 8].unsqueeze(2).to_broadcast([128, 8, 5]), op=ALU.mult),
                r=[modT_b, prm_b], w=[modT_b])

        def mod_ap(l, part, dt, s):
            o = (l * 3 + part) * 40 + dt * 5 + s
            return modT[:, o:o + 1]

        nrm_rr = [0]

        def rms_stat(s):
            i = 0
            sq, sqb, rs, rsb = sqs[i], sqs_b[i], rstd[i], rstd_b[i]
            seg = slice(s * SEGL, (s + 1) * SEGL)
            S.op("act", lambda e: e.activation(out=sq[:, :, :], in_=xT[:, :, seg], func=AF.Square), r=[xT_b[s]], w=[sqb])
            bk, bb = psum()
            for dt in range(8):
                S.op("pe", lambda e, dt=dt: e.matmul(bk[:, 0:256], lhsT=ones_f[:], rhs=sq[:, dt, :],
                                                      start=(dt == 0), stop=(dt == 7)), r=[sqb, cst_b], w=[bb])
            S.op("act", lambda e: e.activation(out=rs[:, :], in_=bk[:, 0:256], func=AF.Ln, bias=cpar[:, 0:1], scale=1.0),
                 r=[bb, cst_b], w=[rsb])
            S.op("act", lambda e: e.activation(out=rs[:, :], in_=rs[:, :], func=AF.Exp, scale=-0.5), r=[rsb], w=[rsb])
            return sq, sqb, rs, rsb

        def norm_mod(l):
            for s in range(NSEG):
                sq, sqb, rs, rsb = rms_stat(s)
                seg = slice(s * SEGL, (s + 1) * SEGL)
                S.op("dve", lambda e: e.tensor_tensor(out=sq[:, :, :], in0=xT[:, :, seg],
                                                      in1=rs[:, :].unsqueeze(1).to_broadcast([128, 8, 256]),
                                                      op=ALU.mult), r=[xT_b[s], rsb], w=[sqb])
                for dt in range(8):
                    o = l * 40 + dt * 5 + s
                    S.op("act", lambda e, dt=dt, o=o: e.activation(
                        out=hT[:, dt, seg], in_=sq[:, dt, :], func=AF.Identity, bias=mod_ap(l, 0, dt, s),
                        scale=sc1[:, o:o + 1]), r=[sqb, modT_b], w=[hT_b[s]])

        def segs_of(t0, n):
            return list(range(t0 // SEGL, (t0 + n) // SEGL))

        def out_proj(l, sO, sOb, nvt, oT, oT_b):
            sOv = sO[:, 0:nvt * 1024].rearrange("p (v c) -> p v c", v=nvt)
            for (t0, n) in BLOCKS:
                sg = segs_of(t0, n)
                for dt in range(8):
                    bk, bb = psum()
                    for vt in range(nvt):
                        S.op("pe", lambda e, vt=vt, dt=dt, bk=bk: e.matmul(
                            bk[:, 0:n], lhsT=sOv[:, vt, dt * 128:(dt + 1) * 128], rhs=oT[:, vt, t0:t0 + n],
                            start=(vt == 0), stop=(vt == nvt - 1)), r=[sOb] + [oT_b[s] for s in sg], w=[bb])
                    for s in sg:
                        lo = s * SEGL - t0
                        seg = slice(s * SEGL, (s + 1) * SEGL)
                        S.op("dve", lambda e, bk=bk, lo=lo, seg=seg, dt=dt, s=s: e.scalar_tensor_tensor(
                            out=xT[:, dt, seg], in0=bk[:, lo:lo + SEGL], scalar=mod_ap(l, 2, dt, s), in1=xT[:, dt, seg],
                            op0=ALU.mult, op1=ALU.add), r=[bb, modT_b, xT_b[s]], w=[xT_b[s]])

        def ret_layer(l, j):
            with contextlib.ExitStack() as ls:
                cosT = sb("cosT", [128, NTOK], st=ls)
                sinT = sb("sinT", [128, NTOK], st=ls)
                rope_b = Buf("rope")
                S.dma("sp", cosT[:, :], cos_d, w=[rope_b])
                S.dma("sp", sinT[:, :], sin_d, w=[rope_b])
                lc = sb("lc", [128, 64], st=ls)
                lc_b = Buf("lc")
                iot = sb("iot", [128, 128], st=ls)
                iop = sb("iop", [128, 2], st=ls)
                ioi = sb("ioi", [128, 2, 128], st=ls)
                Mh = sb("Mh", [128, 4, 128], st=ls)
                Xi = sb("Xi", [128, 8, 128], BF16, st=ls)
                tmpm = sb("tmpm", [128, 2, 128], st=ls)
                gnw = sb("gnw", [128, 512], st=ls)
                gnw_b = Buf("gnw")
                qT = sb("qT", [128, 2, NTOK], BF16, st=ls)
                kT = sb("kT", [128, 2, NTOK], BF16, st=ls)
                qk_b = [Buf(f"qk{s}") for s in range(NSEG)]
                sq, sqb = sqs[0], sqs_b[0]
                kzf = sb("kzf", [128, NT, 256], BF16, st=ls)
                kzb = sb("kzb", [128, NT, 256], BF16, st=ls)
                kz_b = [Buf(f"kz{c}") for c in range(NT)]
                v16 = sb("v16", [128, NT, 512], BF16, st=ls)
                v_b = [Buf(f"v{c}") for c in range(NT)]
                sm = sb("sm", [128, NT, 128], BF16, st=ls)
                sm_b = [Buf(f"sm{c}") for c in range(NT)]
                qx = [sb(f"qx{i}", [128, 2, 2, 128], BF16, st=ls) for i in range(2)]
                qx_b = [Buf(f"qx{i}") for i in range(2)]
                zs = [sb(f"zs{i}", [128, 512], st=ls) for i in range(2)]
                zs_b = [Buf(f"zs{i}") for i in range(2)]
                Sb16 = sb("Sb16", [128, NT, 2, 512], BF16, st=ls)
                Sb16_b = [Buf(f"Sb16_{c}") for c in range(NT)]
                S32 = [sb(f"S32_{d}", [128, 2, 512], st=ls) for d in range(2)]
                S32_b = [Buf(f"S32_{d}") for d in range(2)]
                S32alt = [None, sb("S32_1b", [128, 2, 512], st=ls)]
                S32alt_b = [None, Buf("S32_1b")]
                S16f = sb("S16f", [128, 2, 512], BF16, st=ls)
                S16f_b = Buf("S16f")
                oT = [sb(f"oT{i}", [128, 4, SEGL], BF16, st=ls) for i in range(2)]
                oT_b = [Buf(f"oT{i}") for i in range(2)]
                bst = sb("bst", [128, 16], st=ls)
                bst_b = Buf("bst")
                on = sb("on", [128, 512], st=ls)
                on_b = Buf("on")
                og = sb("og", [128, 512], BF16, st=ls)
                og_b = Buf("og")

                dec = prm[:, o_rdec + j * 8:o_rdec + j * 8 + 8]
                LG, GC, ZF = lc[:, 0:8], lc[:, 8:16], lc[:, 16:24]
                S.op("act", lambda e: e.activation(out=lc[:, 24:32], in_=dec, func=AF.Exp, scale=-1.0), r=[prm_b], w=[lc_b])
                S.op("act", lambda e: e.activation(out=LG, in_=lc[:, 24:32], func=AF.Ln, bias=cpar[:, 2:3], scale=1.0),
                     r=[lc_b, cst_b], w=[lc_b])
                S.op("dve", lambda e: e.tensor_scalar(out=LG, in0=LG, scalar1=-1.0, scalar2=None, op0=ALU.mult),
                     r=[lc_b], w=[lc_b])
                S.op("act", lambda e: e.activation(out=GC, in_=LG, func=AF.Exp, scale=128.0), r=[lc_b], w=[lc_b])
                S.op("pool", lambda e: e.iota(iot[:, :], pattern=[[1, 128]], base=0, channel_multiplier=-1,
                                               allow_small_or_imprecise_dtypes=True), r=[lc_b], w=[lc_b])
                S.op("pool", lambda e: e.iota(iop[:, 0:1], pattern=[[0, 1]], base=127, channel_multiplier=-1,
                                               allow_small_or_imprecise_dtypes=True), r=[lc_b], w=[lc_b])
                S.op("pool", lambda e: e.iota(iop[:, 1:2], pattern=[[0, 1]], base=0, channel_multiplier=1,
                                               allow_small_or_imprecise_dtypes=True), r=[lc_b], w=[lc_b])
                S.op("pool", lambda e: e.iota(ioi[:, 0, :], pattern=[[1, 128]], base=1, channel_multiplier=0,
                                               allow_small_or_imprecise_dtypes=True), r=[lc_b], w=[lc_b])
                S.op("pool", lambda e: e.iota(ioi[:, 1, :], pattern=[[-1, 128]], base=128, channel_multiplier=0,
                                               allow_small_or_imprecise_dtypes=True), r=[lc_b], w=[lc_b])
                for d in range(2):
                    for h in range(4):
                        c = d * 4 + h
                        S.op("act", lambda e, c=c, d=d: e.activation(out=ZF[:, c:c + 1], in_=iop[:, d:d + 1], func=AF.Exp,
                                                                      scale=LG[:, c:c + 1]), r=[lc_b], w=[lc_b])
                S.op("dve", lambda e: e.tensor_scalar(out=ZF, in0=ZF, scalar1=1.0 / 16.0, scalar2=None, op0=ALU.mult),
                     r=[lc_b], w=[lc_b])
                for h in range(4):
                    S.op("dve", lambda e: e.tensor_scalar(out=tmpm[:, 0, :], in0=iot[:, :], scalar1=0.0, scalar2=None,
                                                          op0=ALU.max), r=[lc_b], w=[lc_b])
                    S.op("act", lambda e, h=h: e.activation(out=tmpm[:, 0, :], in_=tmpm[:, 0, :], func=AF.Exp,
                                                             scale=LG[:, h:h + 1]), r=[lc_b], w=[lc_b])
                    S.op("pool", lambda e: e.affine_select(out=tmpm[:, 0, :], in_=tmpm[:, 0, :], pattern=[[1, 128]],
                                                            compare_op=ALU.is_ge, fill=0.0, base=0, channel_multiplier=-1),
                         r=[lc_b], w=[lc_b])
                    S.op("dve", lambda e: e.tensor_scalar(out=tmpm[:, 1, :], in0=iot[:, :], scalar1=-1.0, scalar2=0.0,
                                                          op0=ALU.mult, op1=ALU.max), r=[lc_b], w=[lc_b])
                    S.op("act", lambda e, h=h: e.activation(out=tmpm[:, 1, :], in_=tmpm[:, 1, :], func=AF.Exp,
                                                             scale=LG[:, 4 + h:5 + h]), r=[lc_b], w=[lc_b])
                    S.op("pool", lambda e: e.affine_select(out=tmpm[:, 1, :], in_=tmpm[:, 1, :], pattern=[[-1, 128]],
                                                            compare_op=ALU.is_ge, fill=0.0, base=0, channel_multiplier=1),
                         r=[lc_b], w=[lc_b])
                    S.op("dve", lambda e, h=h: e.tensor_tensor(out=Mh[:, h, :], in0=tmpm[:, 0, :], in1=tmpm[:, 1, :],
                                                               op=ALU.add), r=[lc_b], w=[lc_b])
                    S.op("dve", lambda e, h=h: e.tensor_scalar(out=Mh[:, h, :], in0=Mh[:, h, :], scalar1=1.0 / 16.0,
                                                               scalar2=None, op0=ALU.mult), r=[lc_b], w=[lc_b])
                    for d in range(2):
                        S.op("act", lambda e, h=h, d=d: e.activation(out=Xi[:, d * 4 + h, :], in_=ioi[:, d, :], func=AF.Exp,
                                                                      scale=LG[:, d * 4 + h:d * 4 + h + 1]),
                             r=[lc_b], w=[lc_b])

                norm_mod(l)

                for h in range(4):
                    sQK, sQKb = ws_get()
                    sV, sVb = ws_get()
                    sQKv, sVv = v8(sQK), v8(sV)
                    S.dma("sp", gnw[:, :], gnw_d[:, j * 2048 + h * 512:j * 2048 + (h + 1) * 512], w=[gnw_b])

                    for s in range(NSEG):
                        t0, n = s * SEGL, SEGL
                        for wi, dst in ((0, qT), (1, kT)):
                            pss = []
                            for half in range(2):
                                bk, bb = psum()
                                c0 = wi * 256 + half * 128
                                for kt in range(8):
                                    S.op("pe", lambda e, bk=bk, kt=kt, c0=c0: e.matmul(
                                        bk[:, 0:n], lhsT=sQKv[:, kt, c0:c0 + 128], rhs=hT[:, kt, t0:t0 + n],
                                        start=(kt == 0), stop=(kt == 7)), r=[sQKb, hT_b[s]], w=[bb])
                                pss.append((bk, bb))
                            cs, sn = cosT[:, t0:t0 + n], sinT[:, t0:t0 + n]
                            for half in range(2):
                                S.op("act", lambda e, half=half: e.copy(out=sq[:, half, :], in_=pss[half][0][:, 0:n]),
                                     r=[pss[half][1]], w=[sqb])
                            S.op("dve", lambda e: e.tensor_tensor(out=sq[:, 2, :], in0=sq[:, 0, :], in1=cs, op=ALU.mult),
                                 r=[sqb, rope_b], w=[sqb])
                            S.op("dve", lambda e: e.tensor_tensor(out=sq[:, 3, :], in0=sq[:, 1, :], in1=sn, op=ALU.mult),
                                 r=[sqb, rope_b], w=[sqb])
                            S.op("dve", lambda e: e.tensor_tensor(out=sq[:, 4, :], in0=sq[:, 0, :], in1=sn, op=ALU.mult),
                                 r=[sqb, rope_b], w=[sqb])
                            S.op("dve", lambda e: e.tensor_tensor(out=sq[:, 5, :], in0=sq[:, 1, :], in1=cs, op=ALU.mult),
                                 r=[sqb, rope_b], w=[sqb])
                            S.op("dve", lambda e, dst=dst: e.tensor_tensor(out=dst[:, 0, t0:t0 + n], in0=sq[:, 2, :],
                                                                          in1=sq[:, 3, :], op=ALU.subtract),
                                 r=[sqb], w=[qk_b[s]])
                            S.op("dve", lambda e, dst=dst: e.tensor_tensor(out=dst[:, 1, t0:t0 + n], in0=sq[:, 4, :],
                                                                          in1=sq[:, 5, :], op=ALU.add),
                                 r=[sqb], w=[qk_b[s]])
                    ws_release()
                    for c in range(NT):
                        tk = slice(c * 128, (c + 1) * 128)
                        bk, bb = psum()
                        for kt in range(8):
                            S.op("pe", lambda e, bk=bk, kt=kt: e.matmul(bk[:, :], lhsT=hT[:, kt, tk], rhs=sVv[:, kt, :],
                                                                         start=(kt == 0), stop=(kt == 7)),
                                 r=[sVb, hT_b[c // 2]], w=[bb])
                        S.op("act", lambda e, bk=bk, c=c: e.copy(out=v16[:, c, :], in_=bk[:, :]), r=[bb], w=[v_b[c]])
                    ws_release()
                    sZ, sZb = ws_get()
                    sO, sOb = ws_get()
                    sZv = v8(sZ)
                    sOv = sO[:, :].rearrange("p (v c) -> p v c", v=4)
                    for c in range(NT):
                        tk = slice(c * 128, (c + 1) * 128)
                        bk, bb = psum()
                        bkb = bk[:, :].bitcast(BF16)
                        for dt in range(2):
                            S.op("pe", lambda e, dt=dt, bkb=bkb: e.transpose(bkb[:, dt * 128:(dt + 1) * 128], kT[:, dt, tk],
                                                                             ident_b[:]),
                                 r=[qk_b[c // 2], cst_b], w=[bb])
                        S.op("dve", lambda e, bkb=bkb, c=c: e.tensor_scalar(out=kzf[:, c, :], in0=bkb[:, 0:256],
                                                                            scalar1=ZF[:, h:h + 1], scalar2=None,
                                                                            op0=ALU.mult), r=[bb, lc_b], w=[kz_b[c]])
                        S.op("act", lambda e, bkb=bkb, c=c: e.activation(out=kzb[:, c, :], in_=bkb[:, 0:256], func=AF.Copy,
                                                                         scale=ZF[:, 4 + h:5 + h]), r=[bb, lc_b], w=[kz_b[c]])
                        bk, bb = psum()
                        for dt in range(2):
                            S.op("pe", lambda e, dt=dt, bk=bk: e.matmul(bk[:, 0:128], lhsT=kT[:, dt, tk], rhs=qT[:, dt, tk],
                                                                         start=(dt == 0), stop=(dt == 1)),
                                 r=[qk_b[c // 2]], w=[bb])
                        S.op("dve", lambda e, bk=bk, c=c: e.tensor_tensor(out=sm[:, c, :], in0=bk[:, 0:128], in1=Mh[:, h, :],
                                                                          op=ALU.mult), r=[bb, lc_b], w=[sm_b[c]])

                    def upd_state(d, c, kz):
                        pss = []
                        for dt in range(2):
                            bk, bb = psum()
                            S.op("pe", lambda e, bk=bk, dt=dt: e.matmul(bk[:, :], lhsT=kz[:, c, dt * 128:(dt + 1) * 128],
                                                                         rhs=v16[:, c, :], start=True, stop=True),
                                 r=[kz_b[c], v_b[c]], w=[bb])
                            pss.append((bk, bb))
                        if S32alt[d] is None:
                            dst, dstb = S32[d], S32_b[d]
                        else:
                            dst, dstb = S32alt[d], S32alt_b[d]
                        for dt in range(2):
                            bk, bb = pss[dt]
                            S.op("dve", lambda e, bk=bk, dt=dt: e.scalar_tensor_tensor(
                                out=dst[:, dt, :], in0=S32[d][:, dt, :], scalar=GC[:, d * 4 + h:d * 4 + h + 1],
                                in1=bk[:, :], op0=ALU.mult, op1=ALU.add), r=[bb, lc_b, S32_b[d]], w=[dstb])
                        if S32alt[d] is not None:
                            S32[d], S32alt[d] = S32alt[d], S32[d]
                            S32_b[d], S32alt_b[d] = S32alt_b[d], S32_b[d]

                    def out_state(d, s):
                        st_t, st_bf = stage()
                        S.op("act", lambda e: e.copy(out=st_t[:, :], in_=S32[d][:, :, :].rearrange("p a b -> p (a b)")),
                             r=[S32_b[d]], w=[st_bf])
                        S.dma("sp", nsr_d[s, j, d, h].rearrange("(a p) v -> p a v", p=128),
                              st_t[:, :].rearrange("p (a v) -> p a v", a=2), r=[st_bf])

                    S.op("dve", lambda e: e.memset(S32[1][:, :, :], 0.0), w=[S32_b[1]])
                    for c in range(NT - 1, -1, -1):
                        s = c // 2
                        S.op("act", lambda e, c=c: e.copy(out=Sb16[:, c, :, :], in_=S32[1][:, :, :]),
                             r=[S32_b[1]], w=[Sb16_b[c]])
                        upd_state(1, c, kzb)
                        if c % 2 == 0:
                            out_state(1, s)
                            if c > 0:
                                if s - 1 == 3:
                                    st_t, st_bf = stage()
                                    S.dma("sp", st_t[:, :].rearrange("p (a v) -> p a v", a=2),
                                          s0r_d[j, 1, h].rearrange("(a p) v -> p a v", p=128), w=[st_bf])
                                    S.op("dve", lambda e, s=s, st_t=st_t: e.scalar_tensor_tensor(
                                        out=S32[1][:, :, :], in0=S32[1][:, :, :], scalar=chain_ap(s),
                                        in1=st_t[:, :].rearrange("p (a v) -> p a v", a=2),
                                        op0=ALU.mult, op1=ALU.add), r=[S32_b[1], prm_b, st_bf], w=[S32_b[1]])
                                else:
                                    S.op("dve", lambda e, s=s: e.tensor_scalar(
                                        out=S32[1][:, :, :], in0=S32[1][:, :, :], scalar1=chain_ap(s), scalar2=None,
                                        op0=ALU.mult), r=[S32_b[1], prm_b], w=[S32_b[1]])
                    S.dma("sp", S32[0][:, :, :], s0r_d[j, 0, h].rearrange("(a p) v -> p a v", p=128), w=[S32_b[0]])
                    S.op("act", lambda e: e.copy(out=S16f[:, :, :], in_=S32[0][:, :, :]), r=[S32_b[0]], w=[S16f_b])
                    for c in range(NT):
                        s = c // 2
                        tk = slice(c * 128, (c + 1) * 128)
                        qi = c % 2
                        for d in range(2):
                            S.op("dve", lambda e, d=d, qi=qi: e.tensor_tensor(
                                out=qx[qi][:, d, :, :], in0=qT[:, :, tk],
                                in1=Xi[:, d * 4 + h, :].unsqueeze(1).to_broadcast([128, 2, 128]), op=ALU.mult),
                                r=[qk_b[s], lc_b], w=[qx_b[qi]])
                        bkz, bbz = psum()
                        for kt in range(8):
                            S.op("pe", lambda e, kt=kt: e.matmul(bkz[:, :], lhsT=hT[:, kt, tk], rhs=sZv[:, kt, :],
                                                                  start=(kt == 0), stop=(kt == 7)), r=[sZb, hT_b[s]], w=[bbz])
                        S.op("act", lambda e, qi=qi: e.activation(out=zs[qi][:, :], in_=bkz[:, :], func=AF.Exp, scale=-1.0),
                             r=[bbz], w=[zs_b[qi]])
                        S.op("act", lambda e, qi=qi: e.activation(out=zs[qi][:, :], in_=zs[qi][:, :], func=AF.Ln, bias=cpar[:, 2:3],
                                                                  scale=1.0), r=[zs_b[qi], cst_b], w=[zs_b[qi]])
                        S.op("act", lambda e, qi=qi: e.activation(out=zs[qi][:, :], in_=zs[qi][:, :], func=AF.Exp, scale=-1.0),
                             r=[zs_b[qi]], w=[zs_b[qi]])
                        S.op("dve", lambda e, qi=qi: e.tensor_tensor(out=zs[qi][:, :], in0=bkz[:, :], in1=zs[qi][:, :], op=ALU.mult),
                             r=[bbz, zs_b[qi]], w=[zs_b[qi]])
                        bk, bb = psum()
                        S.op("pe", lambda e, bk=bk, c=c: e.matmul(bk[:, :], lhsT=sm[:, c, :], rhs=v16[:, c, :],
                                                                   start=True, stop=False), r=[sm_b[c], v_b[c]], w=[bb])
                        for dt in range(2):
                            S.op("pe", lambda e, bk=bk, dt=dt: e.matmul(bk[:, :], lhsT=qx[qi][:, 0, dt, :], rhs=S16f[:, dt, :],
                                                                         start=False, stop=False),
                                 r=[qx_b[qi], S16f_b], w=[bb])
                        for dt in range(2):
                            S.op("pe", lambda e, bk=bk, dt=dt, c=c: e.matmul(bk[:, :], lhsT=qx[qi][:, 1, dt, :],
                                                                              rhs=Sb16[:, c, dt, :], start=False,
                                                                              stop=(dt == 1)),
                                 r=[qx_b[qi], Sb16_b[c]], w=[bb])
                        S.op("dve", lambda e, bk=bk: e.bn_stats(out=bst[:, 0:6], in_=bk[:, :]), r=[bb], w=[bst_b])
                        S.op("dve", lambda e: e.bn_aggr(out=bst[:, 8:10], in_=bst[:, 0:6]), r=[bst_b], w=[bst_b])
                        S.op("act", lambda e: e.activation(out=bst[:, 10:11], in_=bst[:, 9:10], func=AF.Ln, bias=cpar[:, 1:2],
                                                           scale=1.0), r=[bst_b, cst_b], w=[bst_b])
                        S.op("act", lambda e: e.activation(out=bst[:, 10:11], in_=bst[:, 10:11], func=AF.Exp, scale=-0.5),
                             r=[bst_b], w=[bst_b])
                        S.op("dve", lambda e, bk=bk: e.tensor_scalar(out=on[:, :], in0=bk[:, :], scalar1=bst[:, 8:9],
                                                                     scalar2=bst[:, 10:11], op0=ALU.subtract,
                                                                     op1=ALU.mult), r=[bb, bst_b], w=[on_b])
                        S.op("dve", lambda e: e.tensor_tensor(out=on[:, :], in0=on[:, :], in1=gnw[:, :], op=ALU.mult),
                             r=[on_b, gnw_b], w=[on_b])
                        S.op("dve", lambda e, qi=qi: e.tensor_tensor(out=og[:, :], in0=on[:, :], in1=zs[qi][:, :], op=ALU.mult),
                             r=[on_b, zs_b[qi]], w=[og_b])
                        bk2, bb2 = psum()
                        bk2b = bk2[:, :].bitcast(BF16)
                        for vt in range(4):
                            S.op("pe", lambda e, vt=vt, bk2b=bk2b: e.transpose(
                                bk2b[:, vt * 128:(vt + 1) * 128], og[:, vt * 128:(vt + 1) * 128], ident_b[:]),
                                r=[og_b, cst_b], w=[bb2])
                        oi = s % 2
                        lo = (c % 2) * 128
                        S.op("act", lambda e, bk2b=bk2b, oi=oi, lo=lo: e.copy(
                            out=oT[oi][:, :, lo:lo + 128], in_=bk2b[:, 0:512].rearrange("p (v t) -> p v t", v=4)),
                            r=[bb2], w=[oT_b[oi]])
                        upd_state(0, c, kzf)
                        if c % 2 == 1:
                            out_state(0, s)
                            if c < NT - 1:
                                S.op("dve", lambda e, s=s: e.tensor_scalar(
                                    out=S32[0][:, :, :], in0=S32[0][:, :, :], scalar1=chain_ap(s + 1), scalar2=None,
                                    op0=ALU.mult), r=[S32_b[0], prm_b], w=[S32_b[0]])
                        if c < NT - 1:
                            S.op("act", lambda e: e.copy(out=S16f[:, :, :], in_=S32[0][:, :, :]), r=[S32_b[0]], w=[S16f_b])
                        if c % 2 == 1:
                            seg = slice(s * SEGL, (s + 1) * SEGL)
                            for dt in range(8):
                                bk, bb = psum()
                                for vt in range(4):
                                    S.op("pe", lambda e, vt=vt, dt=dt, bk=bk, oi=oi: e.matmul(
                                        bk[:, 0:SEGL], lhsT=sOv[:, vt, dt * 128:(dt + 1) * 128], rhs=oT[oi][:, vt, :],
                                        start=(vt == 0), stop=(vt == 3)), r=[sOb, oT_b[oi]], w=[bb])
                                S.op("dve", lambda e, bk=bk, dt=dt, s=s: e.scalar_tensor_tensor(
                                    out=xT[:, dt, seg], in0=bk[:, 0:SEGL], scalar=mod_ap(l, 2, dt, s), in1=xT[:, dt, seg],
                                    op0=ALU.mult, op1=ALU.add), r=[bb, modT_b, xT_b[s]], w=[xT_b[s]])
                    ws_release()
                    ws_release()
                S.barrier()

        def del_layer(l, j):
            with contextlib.ExitStack() as ls:
                DH = 8
                tri = [sb(f"tri{d}", [128, 128], st=ls) for d in range(2)]
                neg3 = [sb(f"neg3{d}", [128, 128], st=ls) for d in range(2)]
                pos1 = [sb(f"pos1{d}", [128, 128], st=ls) for d in range(2)]
                dc_b = Buf("dconst")
                for d in range(2):
                    S.op("pool", lambda e, d=d: e.memset(tri[d][:, :], 1.0), w=[dc_b])
                    pat, cm = ([[1, 128]], -1) if d == 0 else ([[-1, 128]], 1)
                    S.op("pool", lambda e, d=d, pat=pat, cm=cm: e.affine_select(
                        out=tri[d][:, :], in_=tri[d][:, :], pattern=pat, compare_op=ALU.is_ge, fill=0.0, base=0,
                        channel_multiplier=cm), r=[dc_b], w=[dc_b])
                    S.op("pool", lambda e, d=d: e.memset(neg3[d][:, :], 0.0), w=[dc_b])
                    S.op("pool", lambda e, d=d, pat=pat, cm=cm: e.affine_select(
                        out=neg3[d][:, :], in_=neg3[d][:, :], pattern=pat, compare_op=ALU.is_ge, fill=-BIG, base=0,
                        channel_multiplier=cm), r=[dc_b], w=[dc_b])
                    pat2, cm2 = ([[-1, 128]], 1) if d == 0 else ([[1, 128]], -1)
                    S.op("pool", lambda e, d=d: e.memset(pos1[d][:, :], 0.0), w=[dc_b])
                    S.op("pool", lambda e, d=d, pat2=pat2, cm2=cm2: e.affine_select(
                        out=pos1[d][:, :], in_=pos1[d][:, :], pattern=pat2, compare_op=ALU.is_gt, fill=BIG, base=0,
                        channel_multiplier=cm2), r=[dc_b], w=[dc_b])
                abraw = sb("abraw", [128, NT, 32], st=ls)
                tk_b = Buf("tokscal")
                tsc = sb("tsc", [128, 10, NT, 16], st=ls)
                U, L1, GG, LNB, BB, GT, EGt, NBEG, NEGG, GPL = [tsc[:, i, :, :] for i in range(10)]
                negA = sb("negA", [128, 16], st=ls)
                dnw = sb("dnw", [128, 256], st=ls)
                dnw_b = Buf("dnw")
                XP = [sb("XP0", [128, NSEG, 258], st=ls)] * 2
                XP_b = [Buf("XP0")] * 2
                acc = sqs[0][:, 0:NSEG, :]
                acc_b = sqs_b[0]
                tmb = sb("tmb", [128, NTOK], BF16, st=ls)
                tmb_b = Buf("tmb")
                rinv = sqs[0][:, 5:7, :].rearrange("p a b -> p (a b)")
                rinv_b = sqs_b[0]
                qT = sb("dqT", [128, NTOK], BF16, st=ls)
                kT = sb("dkT", [128, NTOK], BF16, st=ls)
                q_b, k_b = Buf("dq"), Buf("dk")
                bv = sb("bv", [128, NT * 2, 256], BF16, st=ls)
                bv_b = [Buf(f"bv{c}") for c in range(NT)]
                kd = sb("kd", [128, NT * 2, 128], BF16, st=ls)
                kd_b = [Buf(f"kd{c}") for c in range(NT)]
                TT = sb("TT", [128, NT * 2, 128], BF16, st=ls)
                TT_b = [Buf(f"TT{g}") for g in range(NSEG)]
                PTm = sb("PTm", [128, NT * 2, 128], BF16, st=ls)
                qg = sb("qg", [128, NT * 2, 128], BF16, st=ls)
                pq_b = [Buf(f"pq{c}") for c in range(NT)]
                csc = sb("csc", [128, 2, NT * 2], st=ls)
                csc_b = [Buf(f"csc{c}") for c in range(NT)]
                Es = [sb(f"Es{i}", [128, 3, 128], st=ls) for i in range(2)]
                Es_b = [Buf(f"Es{i}") for i in range(2)]
                def g4(nm):
                    return sb(nm, [128, 4, 128], BF16, st=ls), Buf(nm)
                Xg, Xg_b = zip(*[g4(f"Xg{i}") for i in range(2)])
                Yg, Yg_b = zip(*[g4(f"Yg{i}") for i in range(2)])
                Pg, Pg_b = zip(*[g4(f"Pg{i}") for i in range(2)])
                Qg, Qg_b = zip(*[g4(f"Qg{i}") for i in range(2)])
                id4, _ = g4("id4")
                Wg, Wg_b = zip(*[g4(f"Wg{i}") for i in range(2)])
                Af, Af_b = g4("Af")
                Bf, Bf_b = g4("Bf")
                bmask = sb("bmask", [128, 4, 128], BF16, st=ls)
                S.dma("pool", bmask[:, :, :].rearrange("p a b -> p (a b)"), bmask_d, w=[dc_b])
                bmaskn = sb("bmaskn", [128, 4, 4, 128], BF16, st=ls)
                for q_ in range(4):
                    S.op("dve", lambda e, q_=q_: e.tensor_scalar(out=bmaskn[:, :, q_, :], in0=bmask[:, :, :], scalar1=-1.0,
                                                                 scalar2=None, op0=ALU.mult), r=[dc_b], w=[dc_b])
                for q_ in range(4):
                    S.op("dve", lambda e, q_=q_: e.tensor_copy(out=id4[:, q_, :], in_=ident_b[:, :]), r=[cst_b, dc_b], w=[dc_b])
                ob = sb("ob", [128, NT, 256], st=ls)
                ob_b = [Buf(f"ob{c}") for c in range(NT)]
                S32 = [sb(f"dS32_{d}", [128, 256], st=ls) for d in range(2)]
                S32_b = [Buf(f"dS32_{d}") for d in range(2)]
                S16 = [sb(f"dS16_{d}", [128, 256], BF16, st=ls) for d in range(2)]
                S16_b = [Buf(f"dS16_{d}") for d in range(2)]
                rr = [sb(f"rr{d}", [128, 256], BF16, st=ls) for d in range(2)]
                rr_b = [Buf(f"rr{d}") for d in range(2)]
                vn16 = [sb(f"vn{d}", [128, 256], BF16, st=ls) for d in range(2)]
                vn_b = [Buf(f"vn{d}") for d in range(2)]
                zs = [sb(f"dzs{i}", [128, 256], st=ls) for i in range(2)]
                zs_b = [Buf(f"dzs{i}") for i in range(2)]
                on = [sb(f"don{i}", [128, 256], st=ls) for i in range(2)]
                on_b = [Buf(f"don{i}") for i in range(2)]
                og = [sb(f"dog{i}", [128, 256], BF16, st=ls) for i in range(2)]
                og_b = [Buf(f"dog{i}") for i in range(2)]
                oT = [sb(f"doT{i}", [128, 2, SEGL], BF16, st=ls) for i in range(NSEG)]
                oT_b = [Buf(f"doT{i}") for i in range(NSEG)]
                bst = sb("dbst", [128, 2, 8], st=ls)
                bst_b = [Buf("dbst0"), Buf("dbst1")]

                norm_mod(l)

                sAB, sABb = ws_get()
                sABv = v8(sAB, 32)
                for c in range(NT):
                    tk = slice(c * 128, (c + 1) * 128)
                    bk, bb = psum()
                    for kt in range(8):
                        S.op("pe", lambda e, bk=bk, kt=kt: e.matmul(bk[:, 0:32], lhsT=hT[:, kt, tk], rhs=sABv[:, kt, :],
                                                                     start=(kt == 0), stop=(kt == 7)),
                             r=[sABb, hT_b[c // 2]], w=[bb])
                    S.op("act", lambda e, bk=bk, c=c: e.copy(out=abraw[:, c, :], in_=bk[:, 0:32]), r=[bb], w=[tk_b])
                ws_release()
                al = prm[:, o_alog + j * 16:o_alog + j * 16 + 16]
                dtb = prm[:, o_dtb + j * 16:o_dtb + j * 16 + 16]
                T = [tk_b, prm_b, cst_b]
                S.op("act", lambda e: e.activation(out=negA[:, :], in_=al, func=AF.Exp), r=T, w=[tk_b])
                S.op("dve", lambda e: e.tensor_scalar(out=negA[:, :], in0=negA[:, :], scalar1=-1.0, scalar2=None,
                                                      op0=ALU.mult), r=T, w=[tk_b])
                S.op("dve", lambda e: e.tensor_tensor(out=U, in0=abraw[:, :, 0:16],
                                                      in1=dtb.unsqueeze(1).to_broadcast([128, NT, 16]), op=ALU.add),
                     r=T, w=[tk_b])
                S.op("dve", lambda e: e.tensor_scalar(out=L1, in0=U, scalar1=-1.0, scalar2=None, op0=ALU.mult), r=T, w=[tk_b])
                S.op("dve", lambda e: e.tensor_tensor(out=L1, in0=L1, in1=U, op=ALU.max), r=T, w=[tk_b])
                S.op("act", lambda e: e.activation(out=L1, in_=L1, func=AF.Exp, scale=-1.0), r=T, w=[tk_b])
                S.op("act", lambda e: e.activation(out=L1, in_=L1, func=AF.Ln, bias=cpar[:, 2:3], scale=1.0), r=T, w=[tk_b])
                S.op("dve", lambda e: e.tensor_scalar(out=U, in0=U, scalar1=0.0, scalar2=None, op0=ALU.max), r=T, w=[tk_b])
                S.op("dve", lambda e: e.tensor_tensor(out=U, in0=U, in1=L1, op=ALU.add), r=T, w=[tk_b])
                S.op("dve", lambda e: e.tensor_tensor(out=GG, in0=U, in1=negA[:, :].unsqueeze(1).to_broadcast([128, NT, 16]),
                                                      op=ALU.mult), r=T, w=[tk_b])
                S.op("dve", lambda e: e.tensor_scalar(out=L1, in0=abraw[:, :, 16:32], scalar1=-1.0, scalar2=None, op0=ALU.mult),
                     r=T, w=[tk_b])
                S.op("dve", lambda e: e.tensor_tensor(out=L1, in0=L1, in1=abraw[:, :, 16:32], op=ALU.max), r=T, w=[tk_b])
                S.op("act", lambda e: e.activation(out=L1, in_=L1, func=AF.Exp, scale=-1.0), r=T, w=[tk_b])
                S.op("act", lambda e: e.activation(out=L1, in_=L1, func=AF.Ln, bias=cpar[:, 2:3], scale=1.0), r=T, w=[tk_b])
                S.op("dve", lambda e: e.tensor_scalar(out=LNB, in0=abraw[:, :, 16:32], scalar1=0.0, scalar2=None,
                                                      op0=ALU.min), r=T, w=[tk_b])
                S.op("dve", lambda e: e.tensor_tensor(out=LNB, in0=LNB, in1=L1, op=ALU.subtract), r=T, w=[tk_b])
                S.op("act", lambda e: e.activation(out=BB, in_=LNB, func=AF.Exp), r=T, w=[tk_b])
                bk, bb = psum()
                for c in range(NT):
                    for d in range(2):
                        S.op("pe", lambda e, c=c, d=d: e.matmul(bk[:, c * 16 + d * 8:c * 16 + d * 8 + 8], lhsT=tri[d][:, :],
                                                                 rhs=GG[:, c, d * 8:d * 8 + 8], start=True, stop=True),
                             r=[tk_b, dc_b], w=[bb])
                S.op("dve", lambda e: e.tensor_copy(out=GT, in_=bk[:, 0:NT * 16].rearrange("p (c x) -> p c x", c=NT)),
                     r=[bb], w=[tk_b])
                S.op("act", lambda e: e.activation(out=EGt, in_=GT, func=AF.Exp), r=T, w=[tk_b])
                S.op("dve", lambda e: e.tensor_tensor(out=NBEG, in0=BB, in1=EGt, op=ALU.mult), r=T, w=[tk_b])
                S.op("dve", lambda e: e.tensor_scalar(out=NBEG, in0=NBEG, scalar1=-1.0, scalar2=None, op0=ALU.mult),
                     r=T, w=[tk_b])
                S.op("dve", lambda e: e.tensor_scalar(out=NEGG, in0=GT, scalar1=-1.0, scalar2=None, op0=ALU.mult),
                     r=T, w=[tk_b])
                S.op("dve", lambda e: e.tensor_tensor(out=GPL, in0=GT, in1=LNB, op=ALU.add), r=T, w=[tk_b])
                for i in range(2):
                    S.op("dve", lambda e, i=i: e.memset(XP[i][:, 0, 0:1], 0.0), w=[XP_b[i]])
                    S.op("dve", lambda e, i=i: e.memset(XP[i][:, NSEG - 1, 257:258], 0.0), w=[XP_b[i]])

                dstage = float(os.environ.get("K_DSTAGE", "9"))
                if dstage == 0:
                    S.barrier()
                    return
                xp_rr = [0]
                for h in range(int(os.environ.get("K_DHEADS", "8"))):
                    sW, sWb = ws_get()
                    sWv = v8(sW)
                    S.dma("sp", dnw[:, :], dnw_d[:, j * 2048 + h * 256:j * 2048 + (h + 1) * 256], w=[dnw_b])
                    for ct in range(4):
                        xi = xp_rr[0] % 2
                        xp_rr[0] += 1
                        xp, xpb = XP[xi], XP_b[xi]
                        for s in range(NSEG):
                            bk, bb = psum()
                            for kt in range(8):
                                S.op("pe", lambda e, bk=bk, kt=kt, s=s: e.matmul(
                                    bk[:, 0:SEGL], lhsT=sWv[:, kt, ct * 128:(ct + 1) * 128],
                                    rhs=hT[:, kt, s * SEGL:(s + 1) * SEGL], start=(kt == 0), stop=(kt == 7)),
                                    r=[sWb, hT_b[s]], w=[bb])
                            S.op("act", lambda e, bk=bk, s=s: e.copy(out=xp[:, s, 1:257], in_=bk[:, 0:SEGL]), r=[bb], w=[xpb])
                        chn = prm[:, o_chain + 1:o_chain + 5]
                        S.op("dve", lambda e: e.tensor_tensor(out=xp[:, 1:5, 0], in0=xp[:, 0:4, 256], in1=chn, op=ALU.mult),
                             r=[xpb, prm_b], w=[xpb])
                        S.op("dve", lambda e: e.tensor_tensor(out=xp[:, 0:4, 257], in0=xp[:, 1:5, 1], in1=chn, op=ALU.mult),
                             r=[xpb, prm_b], w=[xpb])
                        gct = h if ct == 0 else (8 + h if ct == 1 else 16 + 2 * h + (ct - 2))
                        cw = [prm[:, o_convw + (j * 3 + k) * 32 + gct:o_convw + (j * 3 + k) * 32 + gct + 1] for k in range(3)]
                        S.op("dve", lambda e: e.tensor_scalar(out=acc, in0=xp[:, :, 0:256], scalar1=cw[0], scalar2=None,
                                                              op0=ALU.mult), r=[xpb, prm_b], w=[acc_b])
                        S.op("dve", lambda e: e.scalar_tensor_tensor(out=acc, in0=xp[:, :, 1:257], scalar=cw[1],
                                                                     in1=acc, op0=ALU.mult, op1=ALU.add),
                             r=[xpb, prm_b, acc_b], w=[acc_b])
                        S.op("dve", lambda e: e.scalar_tensor_tensor(out=acc, in0=xp[:, :, 2:258], scalar=cw[2],
                                                                     in1=acc, op0=ALU.mult, op1=ALU.add),
                             r=[xpb, prm_b, acc_b], w=[acc_b])
                        accf = acc.rearrange("p s t -> p (s t)")
                        S.op("act", lambda e: e.activation(out=accf, in_=accf, func=AF.Silu), r=[acc_b], w=[acc_b])
                        if ct < 2:
                            S.op("act", lambda e: e.activation(out=tmb[:, :], in_=accf, func=AF.Square), r=[acc_b], w=[tmb_b])
                            dst, dstb = (qT, q_b) if ct == 0 else (kT, k_b)
                            scl = (128.0 ** -0.5) if ct == 0 else 1.0
                            for (t0, n) in BLOCKS:
                                bk, bb = psum()
                                S.op("pe", lambda e, bk=bk: e.matmul(bk[:, 0:n], lhsT=ones_b[:, :], rhs=tmb[:, t0:t0 + n],
                                                                      start=True, stop=True), r=[tmb_b, cst_b], w=[bb])
                                S.op("act", lambda e, bk=bk: e.activation(out=rinv[:, 0:n], in_=bk[:, 0:n], func=AF.Ln,
                                                                          bias=cpar[:, 1:2], scale=1.0),
                                     r=[bb, cst_b], w=[rinv_b])
                                S.op("act", lambda e: e.activation(out=rinv[:, 0:n], in_=rinv[:, 0:n], func=AF.Exp, scale=-0.5),
                                     r=[rinv_b], w=[rinv_b])
                                S.op("dve", lambda e, dst=dst: e.scalar_tensor_tensor(
                                    out=dst[:, t0:t0 + n], in0=accf[:, t0:t0 + n], scalar=scl, in1=rinv[:, 0:n],
                                    op0=ALU.mult, op1=ALU.mult), r=[acc_b, rinv_b], w=[dstb])
                        else:
                            vt = ct - 2
                            S.op("act", lambda e: e.copy(out=tmb[:, :], in_=accf), r=[acc_b], w=[tmb_b])
                            for c in range(NT):
                                tk = slice(c * 128, (c + 1) * 128)
                                bk, bb = psum()
                                bkb = bk[:, :].bitcast(BF16)
                                S.op("pe", lambda e, bkb=bkb: e.transpose(bkb[:, 0:128], tmb[:, tk], ident_b[:]),
                                     r=[tmb_b, cst_b], w=[bb])
                                S.op("dve", lambda e, bkb=bkb, c=c: e.tensor_scalar(
                                    out=bv[:, c * 2 + 0, vt * 128:(vt + 1) * 128], in0=bkb[:, 0:128],
                                    scalar1=BB[:, c, h:h + 1], scalar2=None, op0=ALU.mult), r=[bb, tk_b], w=[bv_b[c]])
                                S.op("act", lambda e, bkb=bkb, c=c: e.activation(
                                    out=bv[:, c * 2 + 1, vt * 128:(vt + 1) * 128], in_=bkb[:, 0:128], func=AF.Copy,
                                    scale=BB[:, c, 8 + h:9 + h]), r=[bb, tk_b], w=[bv_b[c]])
                    ws_release()
                    if dstage == 1:
                        S.barrier()
                        return
                    sZ, sZb = ws_get()
                    sO, sOb = ws_get()
                    sZv = v8(sZ, 256)
                    sOv = sO[:, 0:2048].rearrange("p (v c) -> p v c", v=2)

                    for g in range(NSEG):
                        kkps = []
                        for ci in range(2):
                            c = 2 * g + ci
                            tk = slice(c * 128, (c + 1) * 128)
                            bk, bb = psum()
                            S.op("pe", lambda e, bk=bk: e.matmul(bk[:, 0:128], lhsT=kT[:, tk], rhs=kT[:, tk], start=True, stop=True),
                                 r=[k_b], w=[bb])
                            S.op("pe", lambda e, bk=bk: e.matmul(bk[:, 128:256], lhsT=kT[:, tk], rhs=qT[:, tk], start=True,
                                                                  stop=True), r=[k_b, q_b], w=[bb])
                            bkt = bk[:, :].bitcast(BF16)
                            S.op("pe", lambda e, bkt=bkt: e.transpose(bkt[:, 512:640], kT[:, tk], ident_b[:]),
                                 r=[k_b, cst_b], w=[bb])
                            kkps.append((bk, bb, bkt))
                        if dstage == 1.1:
                            S.barrier()
                            return
                        xg, xgb = Af, Af_b
                        for ci in range(2):
                            c = 2 * g + ci
                            tk = slice(c * 128, (c + 1) * 128)
                            bk, bb, bkt = kkps[ci]
                            for d in range(2):
                                qd = ci * 2 + d
                                cd = c * 2 + d
                                dh = d * 8 + h
                                ei = qd % 2
                                es, esb = Es[ei], Es_b[ei]
                                be, bbe = psum()
                                gcol = GG[:, c, dh:dh + 1].to_broadcast([128, 128])
                                S.op("pe", lambda e, be=be, d=d: e.matmul(be[:, 0:128], lhsT=gcol, rhs=tri[d][:, :], start=True,
                                                                           stop=True), r=[tk_b, dc_b], w=[bbe])
                                S.op("pe", lambda e, be=be, d=d: e.matmul(be[:, 128:256], lhsT=gcol, rhs=tri[d][:, :], start=True,
                                                                           stop=False), r=[tk_b, dc_b], w=[bbe])
                                S.op("pe", lambda e, be=be, d=d: e.matmul(be[:, 128:256], lhsT=ident_f[:, :], rhs=neg3[d][:, :],
                                                                           start=False, stop=True), r=[cst_b, dc_b], w=[bbe])
                                S.op("pe", lambda e, be=be, d=d: e.matmul(be[:, 256:384], lhsT=gcol, rhs=tri[d][:, :], start=True,
                                                                           stop=False), r=[tk_b, dc_b], w=[bbe])
                                S.op("pe", lambda e, be=be, d=d: e.matmul(be[:, 256:384], lhsT=ident_f[:, :], rhs=pos1[d][:, :],
                                                                           start=False, stop=True), r=[cst_b, dc_b], w=[bbe])
                                S.op("act", lambda e, be=be, es=es: e.activation(out=es[:, 0, :], in_=be[:, 0:128], func=AF.Exp),
                                     r=[bbe], w=[esb])
                                S.op("act", lambda e, be=be, es=es, c=c, dh=dh: e.activation(
                                    out=es[:, 1, :], in_=be[:, 128:256], func=AF.Exp, bias=NEGG[:, c, dh:dh + 1], scale=1.0),
                                    r=[bbe, tk_b], w=[esb])
                                S.op("act", lambda e, be=be, es=es, c=c, dh=dh: e.activation(
                                    out=es[:, 2, :], in_=be[:, 256:384], func=AF.Exp, bias=GPL[:, c, dh:dh + 1], scale=-1.0),
                                    r=[bbe, tk_b], w=[esb])
                                last = 127 if d == 0 else 0
                                S.op("dve", lambda e, es=es, cd=cd, last=last: e.tensor_copy(
                                    out=csc[:, 0, cd:cd + 1], in_=es[:, 0, last:last + 1]), r=[esb], w=[csc_b[c]])
                                S.op("dve", lambda e, es=es, cd=cd, last=last: e.tensor_copy(
                                    out=csc[:, 1, cd:cd + 1], in_=es[:, 1, last:last + 1]), r=[esb], w=[csc_b[c]])
                                S.op("dve", lambda e, es=es, bk=bk, qd=qd: e.tensor_tensor(
                                    out=xg[:, qd, :], in0=bk[:, 0:128], in1=es[:, 2, :], op=ALU.mult), r=[bb, esb], w=[xgb])
                                S.op("dve", lambda e, es=es, bk=bk, cd=cd: e.tensor_tensor(
                                    out=PTm[:, cd, :], in0=bk[:, 128:256], in1=es[:, 1, :], op=ALU.mult), r=[bb, esb], w=[pq_b[c]])
                                S.op("dve", lambda e, es=es, cd=cd: e.tensor_tensor(
                                    out=qg[:, cd, :], in0=qT[:, tk], in1=es[:, 0, :], op=ALU.mult), r=[q_b, esb], w=[pq_b[c]])
                                if d == 0:
                                    S.op("dve", lambda e, bkt=bkt, cd=cd: e.tensor_scalar(
                                        out=kd[:, cd, :], in0=bkt[:, 512:640], scalar1=csc[:, 1, cd:cd + 1], scalar2=None,
                                        op0=ALU.mult), r=[bb, csc_b[c]], w=[kd_b[c]])
                                else:
                                    S.op("act", lambda e, bkt=bkt, cd=cd: e.activation(
                                        out=kd[:, cd, :], in_=bkt[:, 512:640], func=AF.Copy, scale=csc[:, 1, cd:cd + 1]),
                                        r=[bb, csc_b[c]], w=[kd_b[c]])
                        if dstage == 1.2:
                            S.barrier()
                            return
                        bt, bbt = psum()
                        btb = bt[:, :].bitcast(BF16)
                        for qd in range(4):
                            S.op("pe", lambda e, qd=qd, btb=btb: e.transpose(btb[:, qd * 128:(qd + 1) * 128], xg[:, qd, :],
                                                                             ident_b[:]), r=[xgb, cst_b], w=[bbt])
                        bt4 = btb[:, 0:512].rearrange("p (q i) -> p q i", q=4)
                        S.op("act", lambda e: e.copy(out=Bf[:, :, :], in_=bt4), r=[bbt], w=[Bf_b])

                        def mk(k_, neg=False):
                            if neg:
                                return bmaskn[:, k_, :, :]
                            return bmask[:, k_, :].unsqueeze(1).to_broadcast([128, 4, 128])

                        S.op("dve", lambda e: e.tensor_tensor(out=Xg[0][:, :, :], in0=xg[:, :, :], in1=mk(0), op=ALU.mult),
                             r=[xgb, dc_b], w=[Xg_b[0]])
                        S.op("dve", lambda e: e.tensor_tensor(out=Yg[0][:, :, :], in0=Bf[:, :, :], in1=mk(0), op=ALU.mult),
                             r=[Bf_b, dc_b], w=[Yg_b[0]])
                        S.op("dve", lambda e: e.scalar_tensor_tensor(out=Pg[0][:, :, :], in0=Yg[0][:, :, :], scalar=-1.0, in1=id4[:, :, :],
                                                                     op0=ALU.mult, op1=ALU.add), r=[Yg_b[0], dc_b], w=[Pg_b[0]])
                        S.op("dve", lambda e: e.scalar_tensor_tensor(out=Qg[0][:, :, :], in0=Xg[0][:, :, :], scalar=-1.0, in1=id4[:, :, :],
                                                                     op0=ALU.mult, op1=ALU.add), r=[Xg_b[0], dc_b], w=[Qg_b[0]])
                        if dstage == 1.3:
                            S.barrier()
                            return

                        def mm4(L, Lb, R, Rb, acc=None):
                            bk_, bb_ = psum()
                            for qd in range(4):
                                o_ = bk_[:, qd * 128:(qd + 1) * 128]
                                if acc is not None:
                                    S.op("pe", lambda e, qd=qd: e.matmul(o_, lhsT=ident_b[:, :], rhs=acc[0][:, qd, :], start=True,
                                                                          stop=False), r=[cst_b, acc[1]], w=[bb_])
                                S.op("pe", lambda e, qd=qd: e.matmul(o_, lhsT=L[:, qd, :], rhs=R[:, qd, :], start=(acc is None),
                                                                      stop=True), r=[Lb, Rb], w=[bb_])
                            return bk_[:, :].rearrange("p (q i) -> p q i", q=4), bb_

                        def ev_copy(eng, dst, dstb, ps, psb):
                            if eng == "act":
                                S.op("act", lambda e: e.copy(out=dst, in_=ps), r=[psb], w=[dstb])
                            else:
                                S.op("dve", lambda e: e.tensor_copy(out=dst, in_=ps), r=[psb], w=[dstb])

                        def ev_add(dst, dstb, ps, psb, old_, oldb):
                            S.op("dve", lambda e: e.tensor_tensor(out=dst, in0=ps, in1=old_[:, :, :], op=ALU.add),
                                 r=[psb, oldb], w=[dstb])

                        def ev_mask(dst, dstb, ps, psb, k_):
                            S.op("dve", lambda e: e.tensor_tensor(out=dst[:, :, :], in0=ps, in1=mk(k_, True), op=ALU.mult),
                                 r=[psb, dc_b], w=[dstb])

                        pi = 0
                        for m in range(1, 4):
                            a, b2 = (m - 1) % 2, m % 2
                            px, pxb = mm4(Yg[a], Yg_b[a], Xg[a], Xg_b[a])
                            py, pyb = mm4(Xg[a], Xg_b[a], Yg[a], Yg_b[a])
                            ev_copy("act", Xg[b2][:, :, :], Xg_b[b2], px, pxb)
                            ev_copy("act", Yg[b2][:, :, :], Yg_b[b2], py, pyb)
                            pp, ppb = mm4(Xg[b2], Xg_b[b2], Pg[pi], Pg_b[pi])
                            pq, pqb = mm4(Yg[b2], Yg_b[b2], Qg[pi], Qg_b[pi])
                            ev_add(Pg[1 - pi][:, :, :], Pg_b[1 - pi], pp, ppb, Pg[pi], Pg_b[pi])
                            ev_add(Qg[1 - pi][:, :, :], Qg_b[1 - pi], pq, pqb, Qg[pi], Qg_b[pi])
                            pi = 1 - pi
                        for s_ in range(1, 4):
                            pw1, pw1b = mm4(xg, xgb, Pg[pi], Pg_b[pi])
                            ev_mask(Wg[0], Wg_b[0], pw1, pw1b, s_)
                            if s_ < 3:
                                pw2, pw2b = mm4(Bf, Bf_b, Qg[pi], Qg_b[pi])
                                ev_mask(Wg[1], Wg_b[1], pw2, pw2b, s_)
                            pp, ppb = mm4(Qg[pi], Qg_b[pi], Wg[0], Wg_b[0])
                            if s_ < 3:
                                pq, pqb = mm4(Pg[pi], Pg_b[pi], Wg[1], Wg_b[1])
                                ev_add(Pg[1 - pi][:, :, :], Pg_b[1 - pi], pp, ppb, Pg[pi], Pg_b[pi])
                                ev_add(Qg[1 - pi][:, :, :], Qg_b[1 - pi], pq, pqb, Qg[pi], Qg_b[pi])
                                pi = 1 - pi
                            else:
                                ev_add(TT[:, g * 4:(g + 1) * 4, :], TT_b[g], pp, ppb, Pg[pi], Pg_b[pi])

                    if dbg_d and h == 0:
                        def dump(name, ap, bufs):
                            if name in dbg_d:
                                S.dma("pool", dbg_d[name], ap, r=bufs)
                        dump("tsc", tsc[:, :, :, :].rearrange("p a c x -> p (a c x)"), [tk_b])
                        dump("qT", qT[:, :], [q_b])
                        dump("kT", kT[:, :], [k_b])
                        dump("bv", bv[:, :, :].rearrange("p a b -> p (a b)"), bv_b)
                        dump("kd", kd[:, :, :].rearrange("p a b -> p (a b)"), kd_b)
                        dump("TT", TT[:, :, :].rearrange("p a b -> p (a b)"), TT_b)
                        dump("PTm", PTm[:, :, :].rearrange("p a b -> p (a b)"), pq_b)
                        dump("qg", qg[:, :, :].rearrange("p a b -> p (a b)"), pq_b)
                        dump("csc", csc[:, :, :].rearrange("p a b -> p (a b)"), csc_b)
                    if dstage == 2:
                        S.barrier()
                        return
                    S.dma("sp", S32[0][:, :], s0d_d[j, 0, h], w=[S32_b[0]])
                    S.op("dve", lambda e: e.memset(S32[1][:, :], 0.0), w=[S32_b[1]])
                    for d in range(2):
                        S.op("act", lambda e, d=d: e.copy(out=S16[d][:, :], in_=S32[d][:, :]), r=[S32_b[d]], w=[S16_b[d]])
                    arrived = [0] * NT
                    seg_done = [0] * NSEG

                    def consume_o(c, bo, bbo, first):
                        if first:
                            S.op("act", lambda e: e.copy(out=ob[:, c, :], in_=bo[:, 0:256]), r=[bbo], w=[ob_b[c]])
                        else:
                            oi = c % 2
                            S.op("dve", lambda e: e.tensor_tensor(out=on[oi][:, :], in0=bo[:, 0:256], in1=ob[:, c, :], op=ALU.add),
                                 r=[bbo, ob_b[c]], w=[on_b[oi]])

                    def finish_chunk(c, first):
                        s = c // 2
                        if first:
                            return
                        oi = c % 2
                        tk = slice(c * 128, (c + 1) * 128)
                        bkz, bbz = psum()
                        for kt in range(8):
                            S.op("pe", lambda e, kt=kt: e.matmul(bkz[:, 0:256], lhsT=hT[:, kt, tk], rhs=sZv[:, kt, :],
                                                                  start=(kt == 0), stop=(kt == 7)), r=[sZb, hT_b[s]], w=[bbz])
                        S.op("act", lambda e: e.activation(out=zs[oi][:, :], in_=bkz[:, 0:256], func=AF.Exp, scale=-1.0),
                             r=[bbz], w=[zs_b[oi]])
                        S.op("act", lambda e: e.activation(out=zs[oi][:, :], in_=zs[oi][:, :], func=AF.Ln, bias=cpar[:, 2:3], scale=1.0),
                             r=[zs_b[oi], cst_b], w=[zs_b[oi]])
                        S.op("act", lambda e: e.activation(out=zs[oi][:, :], in_=zs[oi][:, :], func=AF.Exp, scale=-1.0),
                             r=[zs_b[oi]], w=[zs_b[oi]])
                        S.op("dve", lambda e: e.tensor_tensor(out=zs[oi][:, :], in0=bkz[:, 0:256], in1=zs[oi][:, :], op=ALU.mult),
                             r=[bbz, zs_b[oi]], w=[zs_b[oi]])
                        S.op("dve", lambda e: e.bn_stats(out=bst[:, oi, 0:6], in_=on[oi][:, :]), r=[on_b[oi]], w=[bst_b[oi]])
                        S.op("dve", lambda e: e.bn_aggr(out=bst[:, oi, 6:8], in_=bst[:, oi, 0:6]), r=[bst_b[oi]], w=[bst_b[oi]])
                        S.op("dve", lambda e: e.scalar_tensor_tensor(out=bst[:, oi, 0:1], in0=bst[:, oi, 6:7], scalar=bst[:, oi, 6:7],
                                                                     in1=bst[:, oi, 7:8], op0=ALU.mult, op1=ALU.add),
                             r=[bst_b[oi]], w=[bst_b[oi]])
                        S.op("act", lambda e: e.activation(out=bst[:, oi, 1:2], in_=bst[:, oi, 0:1], func=AF.Ln, bias=cpar[:, 1:2],
                                                           scale=1.0), r=[bst_b[oi], cst_b], w=[bst_b[oi]])
                        S.op("act", lambda e: e.activation(out=bst[:, oi, 1:2], in_=bst[:, oi, 1:2], func=AF.Exp, scale=-0.5),
                             r=[bst_b[oi]], w=[bst_b[oi]])
                        S.op("dve", lambda e: e.scalar_tensor_tensor(out=on[oi][:, :], in0=on[oi][:, :], scalar=bst[:, oi, 1:2],
                                                                     in1=dnw[:, :], op0=ALU.mult, op1=ALU.mult),
                             r=[on_b[oi], bst_b[oi], dnw_b], w=[on_b[oi]])
                        S.op("dve", lambda e: e.tensor_tensor(out=og[oi][:, :], in0=on[oi][:, :], in1=zs[oi][:, :], op=ALU.mult),
                             r=[on_b[oi], zs_b[oi]], w=[og_b[oi]])
                        bk2, bb2 = psum()
                        bk2b = bk2[:, :].bitcast(BF16)
                        for vt in range(2):
                            S.op("pe", lambda e, vt=vt: e.transpose(bk2b[:, vt * 128:(vt + 1) * 128],
                                                                    og[oi][:, vt * 128:(vt + 1) * 128], ident_b[:]),
                                 r=[og_b[oi], cst_b], w=[bb2])
                        lo = (c % 2) * 128
                        S.op("act", lambda e: e.copy(out=oT[s][:, :, lo:lo + 128],
                                                     in_=bk2b[:, 0:256].rearrange("p (v t) -> p v t", v=2)), r=[bb2], w=[oT_b[s]])
                        seg_done[s] += 1
                        if seg_done[s] == 2:
                            seg = slice(s * SEGL, (s + 1) * SEGL)
                            for dt in range(8):
                                bk, bb = psum()
                                for vt in range(2):
                                    S.op("pe", lambda e, vt=vt, dt=dt, bk=bk: e.matmul(
                                        bk[:, 0:SEGL], lhsT=sOv[:, vt, dt * 128:(dt + 1) * 128], rhs=oT[s][:, vt, :],
                                        start=(vt == 0), stop=(vt == 1)), r=[sOb, oT_b[s]], w=[bb])
                                S.op("dve", lambda e, bk=bk, dt=dt: e.scalar_tensor_tensor(
                                    out=xT[:, dt, seg], in0=bk[:, 0:SEGL], scalar=mod_ap(l, 2, dt, s), in1=xT[:, dt, seg],
                                    op0=ALU.mult, op1=ALU.add), r=[bb, modT_b, xT_b[s]], w=[xT_b[s]])

                    for step in range(NT):
                        cs_ = [step, NT - 1 - step]
                        st1 = []
                        for d in range(2):
                            c = cs_[d]
                            tk = slice(c * 128, (c + 1) * 128)
                            bk, bb = psum()
                            S.op("pe", lambda e, bk=bk, d=d: e.matmul(bk[:, 0:256], lhsT=kT[:, tk], rhs=S16[d][:, :], start=True,
                                                                       stop=True), r=[k_b, S16_b[d]], w=[bb])
                            st1.append((bk, bb))
                        for d in range(2):
                            c = cs_[d]
                            bk, bb = st1[d]
                            S.op("dve", lambda e, bk=bk, d=d, c=c: e.scalar_tensor_tensor(
                                out=rr[d][:, :], in0=bk[:, 0:256], scalar=NBEG[:, c, d * 8 + h:d * 8 + h + 1],
                                in1=bv[:, c * 2 + d, :], op0=ALU.mult, op1=ALU.add), r=[bb, tk_b, bv_b[c]], w=[rr_b[d]])
                        st3 = []
                        for d in range(2):
                            c = cs_[d]
                            bk, bb = psum()
                            S.op("pe", lambda e, bk=bk, d=d, c=c: e.matmul(bk[:, 0:256], lhsT=TT[:, c * 2 + d, :], rhs=rr[d][:, :],
                                                                            start=True, stop=True), r=[TT_b[c // 2], rr_b[d]], w=[bb])
                            st3.append((bk, bb))
                        for d in range(2):
                            bk, bb = st3[d]
                            S.op("act", lambda e, bk=bk, d=d: e.copy(out=vn16[d][:, :], in_=bk[:, 0:256]), r=[bb], w=[vn_b[d]])
                        st5 = []
                        for d in range(2):
                            c = cs_[d]
                            bo, bbo = psum()
                            S.op("pe", lambda e, bo=bo, d=d, c=c: e.matmul(bo[:, 0:256], lhsT=qg[:, c * 2 + d, :], rhs=S16[d][:, :],
                                                                            start=True, stop=False), r=[pq_b[c], S16_b[d]], w=[bbo])
                            S.op("pe", lambda e, bo=bo, d=d, c=c: e.matmul(bo[:, 0:256], lhsT=PTm[:, c * 2 + d, :], rhs=vn16[d][:, :],
                                                                            start=False, stop=True), r=[pq_b[c], vn_b[d]], w=[bbo])
                            S.op("pe", lambda e, bo=bo, d=d, c=c: e.matmul(bo[:, 256:512], lhsT=kd[:, c * 2 + d, :], rhs=vn16[d][:, :],
                                                                            start=True, stop=True), r=[kd_b[c], vn_b[d]], w=[bbo])
                            st5.append((bo, bbo))
                        for d in range(2):
                            c = cs_[d]
                            s = c // 2
                            bo, bbo = st5[d]
                            cd = c * 2 + d
                            S.op("dve", lambda e, bo=bo, d=d, cd=cd: e.scalar_tensor_tensor(
                                out=S32[d][:, :], in0=S32[d][:, :], scalar=csc[:, 0, cd:cd + 1], in1=bo[:, 256:512],
                                op0=ALU.mult, op1=ALU.add), r=[bbo, csc_b[c], S32_b[d]], w=[S32_b[d]])
                            seg_end = (c % 2 == 1) if d == 0 else (c % 2 == 0)
                            if seg_end:
                                st_t, st_bf = stage()
                                S.op("act", lambda e, st_t=st_t, d=d: e.copy(out=st_t[:, 0:256], in_=S32[d][:, :]),
                                     r=[S32_b[d]], w=[st_bf])
                                S.dma("sp", nsd_d[s, j, d, h], st_t[:, 0:256], r=[st_bf])
                                if d == 0 and c < NT - 1:
                                    S.op("dve", lambda e, s=s: e.tensor_scalar(out=S32[0][:, :], in0=S32[0][:, :],
                                                                               scalar1=chain_ap(s + 1), scalar2=None, op0=ALU.mult),
                                         r=[S32_b[0], prm_b], w=[S32_b[0]])
                                if d == 1 and c > 0:
                                    if s - 1 == 3:
                                        st2, st2b = stage()
                                        S.dma("sp", st2[:, 0:256], s0d_d[j, 1, h], w=[st2b])
                                        S.op("dve", lambda e, s=s, st2=st2: e.scalar_tensor_tensor(
                                            out=S32[1][:, :], in0=S32[1][:, :], scalar=chain_ap(s), in1=st2[:, 0:256],
                                            op0=ALU.mult, op1=ALU.add), r=[S32_b[1], prm_b, st2b], w=[S32_b[1]])
                                    else:
                                        S.op("dve", lambda e, s=s: e.tensor_scalar(out=S32[1][:, :], in0=S32[1][:, :],
                                                                                   scalar1=chain_ap(s), scalar2=None, op0=ALU.mult),
                                             r=[S32_b[1], prm_b], w=[S32_b[1]])
                            if step < NT - 1:
                                S.op("act", lambda e, d=d: e.copy(out=S16[d][:, :], in_=S32[d][:, :]), r=[S32_b[d]], w=[S16_b[d]])
                        for d in range(2):
                            c = cs_[d]
                            bo, bbo = st5[d]
                            arrived[c] += 1
                            consume_o(c, bo, bbo, arrived[c] == 1)
                        for d in range(2):
                            c = cs_[d]
                            finish_chunk(c, arrived[c] == 1)
                    ws_release()
                    ws_release()
                S.barrier()

        def final_out():
            for s in range(NSEG):
                sq, sqb, rs, rsb = rms_stat(s)
                seg = slice(s * SEGL, (s + 1) * SEGL)
                S.op("dve", lambda e: e.tensor_tensor(out=sq[:, :, :], in0=xT[:, :, seg],
                                                      in1=rs[:, :].unsqueeze(1).to_broadcast([128, 8, 256]),
                                                      op=ALU.mult), r=[xT_b[s], rsb], w=[sqb])
                for dt in range(8):
                    S.op("dve", lambda e, dt=dt: e.tensor_scalar(
                        out=sq[:, dt, :], in0=sq[:, dt, :], scalar1=prm[:, o_normw + 32 + dt:o_normw + 33 + dt],
                        scalar2=32.0, op0=ALU.mult, op1=ALU.mult), r=[sqb, prm_b], w=[sqb])
                for half in range(2):
                    tt = s * 2 + half
                    st_t, st_bf = stage()
                    for g in range(2):
                        bk, bb = psum()
                        for q in range(4):
                            dt = g * 4 + q
                            S.op("pe", lambda e, bk=bk, q=q, dt=dt: e.transpose(
                                bk[:, q * 128:(q + 1) * 128], sq[:, dt, half * 128:(half + 1) * 128], ident_f[:]),
                                r=[sqb, cst_b], w=[bb])
                        if g == 0:
                            S.op("dve", lambda e, bk=bk: e.tensor_copy(out=st_t[:, 0:512], in_=bk[:, :]), r=[bb], w=[st_bf])
                        else:
                            S.op("act", lambda e, bk=bk: e.copy(out=st_t[:, 512:1024], in_=bk[:, :]), r=[bb], w=[st_bf])
                    S.dma("sp", y_d[tt * 128:(tt + 1) * 128, :], st_t[:, :], r=[st_bf])

        for l in range(depth):
            if l % 2 == 0:
                ret_layer(l, l // 2)
            else:
                del_layer(l, l // 2)
        final_out()
        S.barrier()
        print(f"[build] instr={S.ninstr} sems={S.nsem} counts={S.cnt}")
    return nc


def _rope_tables():
    pos = np.arange(1024)
    r = (pos // 64).astype(np.float32)
    col = (pos % 64).astype(np.float32)
    freqs = (10000.0 ** (-np.arange(64, dtype=np.float32) / 64.0)).astype(np.float32)
    ang = np.concatenate([r[:, None] * freqs[None, :], col[:, None] * freqs[None, :]], -1)
    return np.cos(ang).T.astype(np.float32), np.sin(ang).T.astype(np.float32)


def _fm(v, nt):
    return np.ascontiguousarray(np.asarray(v, np.float32).reshape(nt, 128).T)


def make_in_maps(inp):
    f = lambda k: np.ascontiguousarray(np.asarray(inp[k], dtype=np.float32))
    xp, xs = f("x_prompt"), f("x_sample")
    c, c_ctx = f("c"), f("c_ctx")
    sr, sd = f("state_ret"), f("state_delta")
    norm_w, fnw, mod_b = f("norm_w"), f("final_norm_w"), f("mod_b")
    normw = np.concatenate([_fm(norm_w[l], 8) for l in range(4)] + [_fm(fnw, 8)], axis=1)
    modb = np.concatenate([_fm(mod_b[l][p * 1024:(p + 1) * 1024], 8) for l in range(4) for p in range(3)], axis=1)
    rdecb = np.ascontiguousarray(np.broadcast_to(f("ret_decay").reshape(1, 16), (128, 16)))
    gnwb = np.ascontiguousarray(np.broadcast_to(f("ret_gn_w").reshape(1, 4096), (128, 4096)))
    dnwb = np.ascontiguousarray(np.broadcast_to(f("del_norm_w").reshape(1, 4096), (128, 4096)))
    cw = f("del_conv_w")
    convw = np.concatenate([_fm(cw[jj, k], 32) for jj in range(2) for k in range(3)], axis=1)
    alogb = np.ascontiguousarray(np.broadcast_to(f("del_a_log").reshape(1, 32), (128, 32)))
    dtbb = np.ascontiguousarray(np.broadcast_to(f("del_dt_bias").reshape(1, 32), (128, 32)))
    cosS, sinS = _rope_tables()
    ii = np.arange(128)
    same = lambda b: (ii[:, None] // b) == (ii[None, :] // b)
    bm = [same(16), same(32) & ~same(16), same(64) & ~same(32), ~same(64)]
    bmask = np.concatenate([m.astype(np.float32) for m in bm], axis=1)
    shared = dict(bmask=bmask, normw=normw, modb=modb, mod_w=f("mod_w"), ret_w_in=f("ret_w_in"), ret_w_out=f("ret_w_out"),
                  del_w_in=f("del_w_in"), del_w_out=f("del_w_out"), rdecb=rdecb, gnwb=gnwb, dnwb=dnwb, convw=convw,
                  alogb=alogb, dtbb=dtbb)
    maps = []
    for core in range(8):
        m = dict(shared)
        chain = np.zeros((128, 8), np.float32)
        cosT = np.ones((128, NTOK), np.float32)
        sinT = np.zeros((128, NTOK), np.float32)
        if core < 6:
            x = xp[core * 5:(core + 1) * 5].reshape(NTOK, D)
            conds = [c_ctx] * 5
            s0r = np.zeros((2, 2, 4, 256, 512), np.float32)
            s0d = np.zeros((2, 2, 8, 128, 256), np.float32)
        else:
            b = core - 6
            x = np.concatenate([xs[b], xp[30 + b]], axis=0)
            conds = [c[b]] * 4 + [c_ctx]
            chain[:, 1:4] = 1.0
            s0r, s0d = sr[b], sd[b]
            cosT[:, 0:1024] = cosS
            sinT[:, 0:1024] = sinS
        condT = np.zeros((128, 40), np.float32)
        for s in range(5):
            condT[:, s::5] = _fm(conds[s], 8)
        m.update(x=np.ascontiguousarray(x), condT=condT, chainb=chain, s0r=np.ascontiguousarray(s0r),
                 s0d=np.ascontiguousarray(s0d), cosT=cosT, sinT=sinT)
        maps.append(m)
    return maps


def assemble(results):
    y_prompt = np.zeros((32, 256, D), np.float32)
    y_sample = np.zeros((2, 1024, D), np.float32)
    nsr = np.zeros((32, 2, 2, 4, 256, 512), np.float32)
    nsd = np.zeros((32, 2, 2, 8, 128, 256), np.float32)
    for core in range(8):
        r = results[core]
        y = r["y"].reshape(5, 256, D)
        if core < 6:
            y_prompt[core * 5:(core + 1) * 5] = y
            nsr[core * 5:(core + 1) * 5] = r["nsr"]
            nsd[core * 5:(core + 1) * 5] = r["nsd"]
        else:
            b = core - 6
            y_sample[b] = y[0:4].reshape(1024, D)
            y_prompt[30 + b] = y[4]
            nsr[30 + b] = r["nsr"][4]
            nsd[30 + b] = r["nsd"][4]
    return y_prompt, y_sample, nsr, nsd


_NC_CACHE = {}


def kernel(**inputs):
    if "nc" not in _NC_CACHE:
        _NC_CACHE["nc"] = build()
    maps = make_in_maps(inputs)
    res = run_bass_kernel_spmd(_NC_CACHE["nc"], maps, core_ids=list(range(8)))
    return assemble(res.results)
```

```python
import contextlib
import numpy as np
import concourse.bass as bass
import concourse.mybir as mybir
from concourse.bass_utils import run_bass_kernel_spmd

F32 = mybir.dt.float32
BF16 = mybir.dt.bfloat16
AF = mybir.ActivationFunctionType
ALU = mybir.AluOpType

D = 1024
NSEG = 5
SEGL = 256
NTOK = NSEG * SEGL
NT = NTOK // 128
DEPTH = 4
EPS = 1e-6
BLOCKS = [(0, 512), (512, 512), (1024, 256)]
import os
EPOCH = int(os.environ.get("K_EPOCH", "4000"))
BIG = 30000.0


class Buf:
    __slots__ = ("name", "lw", "rd", "dsem", "dcnt")

    def __init__(self, name):
        self.name = name
        self.lw = None
        self.rd = {}
        self.dsem = None
        self.dcnt = 0


class Sched:
    def __init__(self, nc, stack):
        self.nc = nc
        self.stack = stack
        self.engs = {"pe": nc.tensor, "act": nc.scalar, "dve": nc.vector, "pool": nc.gpsimd, "sp": nc.sync}
        self.cnt = {e: 0 for e in self.engs}
        self.esems = {e: [] for e in self.engs}
        self.known = {e: {} for e in self.engs}
        self.dsems = {}
        self.dma_bufs = []
        self.nsem = 0
        self.ninstr = 0
        self.pe_pending = None

    def _newsem(self, name):
        self.nsem += 1
        return self.stack.enter_context(self.nc.semaphore(f"{name}_{self.nsem}"))

    def _esem(self, e, epoch):
        lst = self.esems[e]
        while len(lst) <= epoch:
            lst.append(self._newsem(f"s_{e}_{len(lst)}"))
        return lst[epoch]

    def _need(self, E, ev, raw):
        key, val = ev
        if key == ("e", E) and not raw:
            return
        kn = self.known[E]
        if kn.get(key, 0) >= val:
            return
        kn[key] = val
        eng = self.engs[E]
        if key[0] == "e":
            n = val - 1
            eng.wait_ge(self._esem(key[1], n // EPOCH), n % EPOCH + 1)
        else:
            eng.wait_ge(self.dsems[key], val)
        self.ninstr += 1

    def _deps(self, E, r, w):
        for b in r:
            if b.lw is not None:
                self._need(E, b.lw, True)
        for b in w:
            if b.lw is not None:
                self._need(E, b.lw, False)
            for k, v in b.rd.items():
                self._need(E, (k, v), False)

    def _post(self, ev, r, w):
        k, v = ev
        for b in r:
            if b.rd.get(k, 0) < v:
                b.rd[k] = v
        for b in w:
            b.lw = ev
            b.rd = {}

    def _commit_pe(self):
        if self.pe_pending is None:
            return
        ins, _ = self.pe_pending
        n = self.cnt["pe"]
        ins.then_inc(self._esem("pe", n // EPOCH), 1)
        self.cnt["pe"] = n + 1
        self.pe_pending = None

    def _touch(self, E, r, w):
        if self.pe_pending is None:
            return
        pw = self.pe_pending[1]
        if E == "pe":
            if tuple(id(b) for b in w) != pw:
                self._commit_pe()
        elif any(id(b) in pw for b in r) or any(id(b) in pw for b in w):
            self._commit_pe()

    def op(self, E, fn, r=(), w=()):
        self._touch(E, r, w)
        self._deps(E, r, w)
        ins = fn(self.engs[E])
        if E == "pe":
            self.pe_pending = (ins, tuple(id(b) for b in w))
            self.ninstr += 1
            self._post((("e", "pe"), self.cnt["pe"] + 1), r, w)
            return ins
        n = self.cnt[E]
        ins.then_inc(self._esem(E, n // EPOCH), 1)
        self.cnt[E] = n + 1
        self.ninstr += 1
        self._post((("e", E), n + 1), r, w)
        return ins

    def dma(self, Q, out, in_, r=(), w=(), **kw):
        self._touch(Q, r, w)
        self._deps(Q, r, w)
        b0 = (list(w) + list(r))[0]
        if b0.dsem is None:
            b0.dsem = self._newsem("d_" + b0.name)
            self.dsems[("d", id(b0))] = b0.dsem
            self.dma_bufs.append(b0)
        ins = self.engs[Q].dma_start(out=out, in_=in_, **kw)
        ins.then_inc(b0.dsem, 16)
        b0.dcnt += 16
        self.ninstr += 1
        self._post((("d", id(b0)), b0.dcnt), r, w)

    def barrier(self):
        self._commit_pe()
        for E in self.engs:
            for e2 in self.engs:
                if self.cnt[e2] > 0:
                    self._need(E, (("e", e2), self.cnt[e2]), True)
            for b in self.dma_bufs:
                self._need(E, (("d", id(b)), b.dcnt), True)


def build(depth=DEPTH, dbg=None):
    nc = bass.Bass("TRN2", target_bir_lowering=False)
    dbg = dbg or {}

    def din(name, shape):
        return nc.dram_tensor(name, list(shape), F32, kind="ExternalInput").ap()

    def dout(name, shape):
        return nc.dram_tensor(name, list(shape), F32, kind="ExternalOutput").ap()

    x_d = din("x", [NTOK, D])
    cond_d = din("condT", [128, 40])
    chain_d = din("chainb", [128, 8])
    normw_d = din("normw", [128, 40])
    modb_d = din("modb", [128, 96])
    modw_d = din("mod_w", [4, D, 3 * D])
    rwin_d = din("ret_w_in", [2, D, 6144])
    rwout_d = din("ret_w_out", [2, 2048, D])
    dwin_d = din("del_w_in", [2, D, 6176])
    dwout_d = din("del_w_out", [2, 2048, D])
    rdec_d = din("rdecb", [128, 16])
    gnw_d = din("gnwb", [128, 4096])
    dnw_d = din("dnwb", [128, 4096])
    convw_d = din("convw", [128, 192])
    alog_d = din("alogb", [128, 32])
    dtb_d = din("dtbb", [128, 32])
    s0r_d = din("s0r", [2, 2, 4, 256, 512])
    s0d_d = din("s0d", [2, 2, 8, 128, 256])
    cos_d = din("cosT", [128, NTOK])
    sin_d = din("sinT", [128, NTOK])
    bmask_d = din("bmask", [128, 512])
    y_d = dout("y", [NTOK, D])
    nsr_d = dout("nsr", [NSEG, 2, 2, 4, 256, 512])
    nsd_d = dout("nsd", [NSEG, 2, 2, 8, 128, 256])
    dbg_d = {k: dout("dbg_" + k, shp) for k, shp in dbg.items()}

    with contextlib.ExitStack() as stack:
        S = Sched(nc, stack)

        sb_n = [0]

        def sb(name, shape, dt=F32, st=None):
            sb_n[0] += 1
            return (st or stack).enter_context(nc.sbuf_tensor(f"sb{sb_n[0]}_{name}", list(shape), dt))

        banks = [stack.enter_context(nc.psum_tensor(f"bank{i}", [128, 512], F32)) for i in range(8)]
        bank_bufs = [Buf(f"bank{i}") for i in range(8)]
        bank_rr = [0]

        def psum():
            i = bank_rr[0] % 8
            bank_rr[0] += 1
            return banks[i], bank_bufs[i]

        xT = sb("xT", [128, 8, NTOK])
        xT_b = [Buf(f"xT{s}") for s in range(NSEG)]
        hT = sb("hT", [128, 8, NTOK], BF16)
        hT_b = [Buf(f"hT{s}") for s in range(NSEG)]
        NSLAB = 3
        ring = [sb(f"slab{i}", [128, 4096], BF16) for i in range(NSLAB)]
        ring_b = [Buf(f"slab{i}") for i in range(NSLAB)]
        ring_rr = [0]

        ident_f = sb("ident_f", [128, 128])
        ident_b = sb("ident_b", [128, 128], BF16)
        ones_f = sb("ones_f", [128, 128])
        ones_b = sb("ones_b", [128, 128], BF16)
        cst_b = Buf("consts")
        cpar = sb("cpar", [128, 8])
        prm = sb("prm", [128, 40 + 8 + 40 + 96 + 16 + 192 + 32 + 32])
        prm_b = Buf("prm")
        o_cond, o_chain, o_normw, o_modb, o_rdec, o_convw, o_alog, o_dtb = 0, 40, 48, 88, 184, 200, 392, 424
        modT = sb("modT", [128, 4 * 3 * 40])
        modT_b = Buf("modT")
        sc1 = sb("sc1", [128, 4 * 40])
        scT = sb("scT", [128, 8, NSEG], BF16)
        sqs = [sb("sq0", [128, 8, 256])]
        sqs_b = [Buf("sq0")]
        rstd = [sb("rstd0", [128, 256])]
        rstd_b = [Buf("rstd0")]
        stg = [sb(f"stg{i}", [128, 1024]) for i in range(2)]
        stg_b = [Buf(f"stg{i}") for i in range(2)]
        stg_rr = [0]

        def stage():
            i = stg_rr[0] % 2
            stg_rr[0] += 1
            return stg[i], stg_b[i]

        def chain_ap(s):
            return prm[:, o_chain + s:o_chain + s + 1]

        S.op("pool", lambda e: e.memset(ident_f[:], 1.0), w=[cst_b])
        S.op("pool", lambda e: e.affine_select(out=ident_f[:], in_=ident_f[:], pattern=[[-1, 128]],
                                                compare_op=ALU.is_equal, fill=0.0, base=0, channel_multiplier=1),
             r=[cst_b], w=[cst_b])
        S.op("pool", lambda e: e.tensor_copy(out=ident_b[:], in_=ident_f[:]), r=[cst_b], w=[cst_b])
        S.op("pool", lambda e: e.memset(ones_f[:], 1.0), w=[cst_b])
        S.op("pool", lambda e: e.memset(ones_b[:], 1.0), w=[cst_b])
        S.op("pool", lambda e: e.memset(cpar[:, 0:1], 1024.0 * EPS), w=[cst_b])
        S.op("pool", lambda e: e.memset(cpar[:, 1:2], EPS), w=[cst_b])
        S.op("pool", lambda e: e.memset(cpar[:, 2:3], 1.0), w=[cst_b])
        S.op("pool", lambda e: e.memset(cpar[:, 3:4], 0.0), w=[cst_b])
        S.op("pool", lambda e: e.memset(cpar[:, 4:5], -0.5), w=[cst_b])

        for off, n, src in ((o_cond, 40, cond_d), (o_chain, 8, chain_d), (o_normw, 40, normw_d), (o_modb, 96, modb_d),
                            (o_rdec, 16, rdec_d), (o_convw, 192, convw_d), (o_alog, 32, alog_d), (o_dtb, 32, dtb_d)):
            S.dma("sp", prm[:, off:off + n], src, w=[prm_b])

        def v8(t, n=512):
            return t[:, 0:8 * n].rearrange("p (kt c) -> p kt c", kt=8)

        def win_view(w2d):
            return w2d.rearrange("(kt p) c -> p kt c", p=128)

        wspecs = []
        for l in range(depth):
            for part in range(3):
                for hf in range(2):
                    c0 = part * 1024 + hf * 512
                    wspecs.append(lambda t, l=l, c0=c0: [(v8(t), win_view(modw_d[l])[:, :, c0:c0 + 512])])
        for l in range(depth):
            j = l // 2
            if l % 2 == 0:
                wv = win_view(rwin_d[j])
                for h in range(4):
                    wspecs.append(lambda t, h=h, wv=wv: [
                        (v8(t)[:, :, 0:256], wv[:, :, h * 256:(h + 1) * 256]),
                        (v8(t)[:, :, 256:512], wv[:, :, 1024 + h * 256:1024 + (h + 1) * 256])])
                    wspecs.append(lambda t, h=h, wv=wv: [(v8(t), wv[:, :, 2048 + h * 512:2048 + (h + 1) * 512])])
                    wspecs.append(lambda t, h=h, wv=wv: [(v8(t), wv[:, :, 4096 + h * 512:4096 + (h + 1) * 512])])
                    wspecs.append(lambda t, h=h, j=j: [
                        (t[:, :].rearrange("p (v c) -> p v c", v=4),
                         rwout_d[j][h * 512:(h + 1) * 512, :].rearrange("(v p) c -> p v c", p=128))])
            else:
                wv = win_view(dwin_d[j])
                wspecs.append(lambda t, wv=wv: [(v8(t, 32), wv[:, :, 6144:6176])])
                for h in range(8):
                    wspecs.append(lambda t, h=h, wv=wv: [
                        (v8(t)[:, :, 0:128], wv[:, :, h * 128:(h + 1) * 128]),
                        (v8(t)[:, :, 128:256], wv[:, :, 1024 + h * 128:1024 + (h + 1) * 128]),
                        (v8(t)[:, :, 256:512], wv[:, :, 2048 + h * 256:2048 + (h + 1) * 256])])
                    wspecs.append(lambda t, h=h, wv=wv: [(v8(t, 256), wv[:, :, 4096 + h * 256:4096 + (h + 1) * 256])])
                    wspecs.append(lambda t, h=h, j=j: [
                        (t[:, 0:2048].rearrange("p (v c) -> p v c", v=2),
                         dwout_d[j][h * 256:(h + 1) * 256, :].rearrange("(v p) c -> p v c", p=128))])
        ws = {"issued": 0, "released": 0, "got": 0}

        def ws_issue():
            while ws["issued"] < len(wspecs) and ws["issued"] < ws["released"] + NSLAB:
                i = ws["issued"]
                for (dst_view, src_ap) in wspecs[i](ring[i % NSLAB]):
                    S.dma("pool", dst_view, src_ap, w=[ring_b[i % NSLAB]])
                ws["issued"] += 1

        def ws_get():
            ws_issue()
            i = ws["got"]
            assert i < ws["issued"], "weight ring over-subscribed"
            ws["got"] += 1
            return ring[i % NSLAB], ring_b[i % NSLAB]

        def ws_release():
            ws["released"] += 1
            ws_issue()

        for tt in range(NT):
            st_t, st_b = stage()
            S.dma("sp", st_t[:, :], x_d[tt * 128:(tt + 1) * 128, :], w=[st_b])
            for half in range(2):
                bk, bb = psum()
                for q in range(4):
                    dt = half * 4 + q
                    S.op("pe", lambda e, bk=bk, q=q, dt=dt, st_t=st_t: e.transpose(
                        bk[:, q * 128:(q + 1) * 128], st_t[:, dt * 128:(dt + 1) * 128], ident_f[:]),
                        r=[st_b, cst_b], w=[bb])
                eng = "dve" if half == 0 else "act"
                dst = xT[:, half * 4:half * 4 + 4, tt * 128:(tt + 1) * 128]
                src = bk[:, :].rearrange("p (q t) -> p q t", q=4)
                if eng == "dve":
                    S.op("dve", lambda e, dst=dst, src=src: e.tensor_copy(out=dst, in_=src), r=[bb], w=[xT_b[tt // 2]])
                else:
                    S.op("act", lambda e, dst=dst, src=src: e.copy(out=dst, in_=src), r=[bb], w=[xT_b[tt // 2]])

        S.op("act", lambda e: e.activation(out=scT[:].rearrange("p a b -> p (a b)"), in_=prm[:, o_cond:o_cond + 40],
                                           func=AF.Silu), r=[prm_b], w=[modT_b])
        for l in range(depth):
            for part in range(3):
                for hf in range(2):
                    c0 = part * 1024 + hf * 512
                    sl, slb = ws_get()
                    slv = v8(sl)
                    bk, bb = psum()
                    for ct in range(4):
                        for kt in range(8):
                            S.op("pe", lambda e, bk=bk, ct=ct, kt=kt, slv=slv: e.matmul(
                                bk[:, ct * 5:ct * 5 + 5], lhsT=slv[:, kt, ct * 128:(ct + 1) * 128], rhs=scT[:, kt, :],
                                start=(kt == 0), stop=(kt == 7)), r=[slb, modT_b], w=[bb])
                    base = (l * 3 + part) * 40 + hf * 20
                    mb = prm[:, o_modb + (l * 3 + part) * 8 + hf * 4: o_modb + (l * 3 + part) * 8 + hf * 4 + 4]
                    S.op("dve", lambda e, bk=bk, base=base, mb=mb: e.tensor_tensor(
                        out=modT[:, base:base + 20].rearrange("p (a b) -> p a b", a=4),
                        in0=bk[:, 0:20].rearrange("p (a b) -> p a b", a=4),
                        in1=mb.unsqueeze(2).to_broadcast([128, 4, 5]), op=ALU.add), r=[bb, prm_b], w=[modT_b])
                    ws_release()
            sv = modT[:, (l * 3 + 1) * 40:(l * 3 + 1) * 40 + 40]
            S.op("dve", lambda e, l=l, sv=sv: e.tensor_scalar(out=sc1[:, l * 40:(l + 1) * 40], in0=sv, scalar1=1.0,
                                                             scalar2=32.0, op0=ALU.add, op1=ALU.mult),
                 r=[modT_b], w=[modT_b])
            S.op("dve", lambda e, l=l: e.tensor_tensor(
                out=sc1[:, l * 40:(l + 1) * 40].rearrange("p (a b) -> p a b", a=8),
                in0=sc1[:, l * 40:(l + 1) * 40].rearrange("p (a b) -> p a b", a=8),
                in1=prm[:, o_normw + l * 8:o_normw + l * 8 + 8].unsqueeze(2).to_broadcast([128, 8, 5]), op=ALU.mult),
                r=[modT_b, prm_b], w=[modT_b])

        def mod_ap(l, part, dt, s):
            o = (l * 3 + part) * 40 + dt * 5 + s
            return modT[:, o:o + 1]

        nrm_rr = [0]

        def rms_stat(s):
            i = 0
            sq, sqb, rs, rsb = sqs[i], sqs_b[i], rstd[i], rstd_b[i]
            seg = slice(s * SEGL, (s + 1) * SEGL)
            S.op("act", lambda e: e.activation(out=sq[:, :, :], in_=xT[:, :, seg], func=AF.Square), r=[xT_b[s]], w=[sqb])
            bk, bb = psum()
            for dt in range(8):
                S.op("pe", lambda e, dt=dt: e.matmul(bk[:, 0:256], lhsT=ones_f[:], rhs=sq[:, dt, :],
                                                      start=(dt == 0), stop=(dt == 7)), r=[sqb, cst_b], w=[bb])
            S.op("act", lambda e: e.activation(out=rs[:, :], in_=bk[:, 0:256], func=AF.Ln, bias=cpar[:, 0:1], scale=1.0),
                 r=[bb, cst_b], w=[rsb])
            S.op("act", lambda e: e.activation(out=rs[:, :], in_=rs[:, :], func=AF.Exp, scale=-0.5), r=[rsb], w=[rsb])
            return sq, sqb, rs, rsb

        def norm_mod(l):
            for s in range(NSEG):
                sq, sqb, rs, rsb = rms_stat(s)
                seg = slice(s * SEGL, (s + 1) * SEGL)
                S.op("dve", lambda e: e.tensor_tensor(out=sq[:, :, :], in0=xT[:, :, seg],
                                                      in1=rs[:, :].unsqueeze(1).to_broadcast([128, 8, 256]),
                                                      op=ALU.mult), r=[xT_b[s], rsb], w=[sqb])
                for dt in range(8):
                    o = l * 40 + dt * 5 + s
                    S.op("act", lambda e, dt=dt, o=o: e.activation(
                        out=hT[:, dt, seg], in_=sq[:, dt, :], func=AF.Identity, bias=mod_ap(l, 0, dt, s),
                        scale=sc1[:, o:o + 1]), r=[sqb, modT_b], w=[hT_b[s]])

        def segs_of(t0, n):
            return list(range(t0 // SEGL, (t0 + n) // SEGL))

        def out_proj(l, sO, sOb, nvt, oT, oT_b):
            sOv = sO[:, 0:nvt * 1024].rearrange("p (v c) -> p v c", v=nvt)
            for (t0, n) in BLOCKS:
                sg = segs_of(t0, n)
                for dt in range(8):
                    bk, bb = psum()
                    for vt in range(nvt):
                        S.op("pe", lambda e, vt=vt, dt=dt, bk=bk: e.matmul(
                            bk[:, 0:n], lhsT=sOv[:, vt, dt * 128:(dt + 1) * 128], rhs=oT[:, vt, t0:t0 + n],
                            start=(vt == 0), stop=(vt == nvt - 1)), r=[sOb] + [oT_b[s] for s in sg], w=[bb])
                    for s in sg:
                        lo = s * SEGL - t0
                        seg = slice(s * SEGL, (s + 1) * SEGL)
                        S.op("dve", lambda e, bk=bk, lo=lo, seg=seg, dt=dt, s=s: e.scalar_tensor_tensor(
                            out=xT[:, dt, seg], in0=bk[:, lo:lo + SEGL], scalar=mod_ap(l, 2, dt, s), in1=xT[:, dt, seg],
                            op0=ALU.mult, op1=ALU.add), r=[bb, modT_b, xT_b[s]], w=[xT_b[s]])

        def ret_layer(l, j):
            with contextlib.ExitStack() as ls:
                cosT = sb("cosT", [128, NTOK], st=ls)
                sinT = sb("sinT", [128, NTOK], st=ls)
                rope_b = Buf("rope")
                S.dma("sp", cosT[:, :], cos_d, w=[rope_b])
                S.dma("sp", sinT[:, :], sin_d, w=[rope_b])
                lc = sb("lc", [128, 64], st=ls)
                lc_b = Buf("lc")
                iot = sb("iot", [128, 128], st=ls)
                iop = sb("iop", [128, 2], st=ls)
                ioi = sb("ioi", [128, 2, 128], st=ls)
                Mh = sb("Mh", [128, 4, 128], st=ls)
                Xi = sb("Xi", [128, 8, 128], BF16, st=ls)
                tmpm = sb("tmpm", [128, 2, 128], st=ls)
                gnw = sb("gnw", [128, 512], st=ls)
                gnw_b = Buf("gnw")
                qT = sb("qT", [128, 2, NTOK], BF16, st=ls)
                kT = sb("kT", [128, 2, NTOK], BF16, st=ls)
                qk_b = [Buf(f"qk{s}") for s in range(NSEG)]
                sq, sqb = sqs[0], sqs_b[0]
                kzf = sb("kzf", [128, NT, 256], BF16, st=ls)
                kzb = sb("kzb", [128, NT, 256], BF16, st=ls)
                kz_b = [Buf(f"kz{c}") for c in range(NT)]
                v16 = sb("v16", [128, NT, 512], BF16, st=ls)
                v_b = [Buf(f"v{c}") for c in range(NT)]
                sm = sb("sm", [128, NT, 128], BF16, st=ls)
                sm_b = [Buf(f"sm{c}") for c in range(NT)]
                qx = [sb(f"qx{i}", [128, 2, 2, 128], BF16, st=ls) for i in range(2)]
                qx_b = [Buf(f"qx{i}") for i in range(2)]
                zs = [sb(f"zs{i}", [128, 512], st=ls) for i in range(2)]
                zs_b = [Buf(f"zs{i}") for i in range(2)]
                Sb16 = sb("Sb16", [128, NT, 2, 512], BF16, st=ls)
                Sb16_b = [Buf(f"Sb16_{c}") for c in range(NT)]
                S32 = [sb(f"S32_{d}", [128, 2, 512], st=ls) for d in range(2)]
                S32_b = [Buf(f"S32_{d}") for d in range(2)]
                S32alt = [None, sb("S32_1b", [128, 2, 512], st=ls)]
                S32alt_b = [None, Buf("S32_1b")]
                S16f = sb("S16f", [128, 2, 512], BF16, st=ls)
                S16f_b = Buf("S16f")
                oT = [sb(f"oT{i}", [128, 4, SEGL], BF16, st=ls) for i in range(2)]
                oT_b = [Buf(f"oT{i}") for i in range(2)]
                bst = sb("bst", [128, 16], st=ls)
                bst_b = Buf("bst")
                on = sb("on", [128, 512], st=ls)
                on_b = Buf("on")
                ogs = [sb(f"og{i}", [128, 512], BF16, st=ls) for i in range(2)]
                ogs_b = [Buf(f"og{i}") for i in range(2)]

                dec = prm[:, o_rdec + j * 8:o_rdec + j * 8 + 8]
                LG, GC, ZF = lc[:, 0:8], lc[:, 8:16], lc[:, 16:24]
                S.op("act", lambda e: e.activation(out=lc[:, 24:32], in_=dec, func=AF.Exp, scale=-1.0), r=[prm_b], w=[lc_b])
                S.op("act", lambda e: e.activation(out=LG, in_=lc[:, 24:32], func=AF.Ln, bias=cpar[:, 2:3], scale=1.0),
                     r=[lc_b, cst_b], w=[lc_b])
                S.op("dve", lambda e: e.tensor_scalar(out=LG, in0=LG, scalar1=-1.0, scalar2=None, op0=ALU.mult),
                     r=[lc_b], w=[lc_b])
                S.op("act", lambda e: e.activation(out=GC, in_=LG, func=AF.Exp, scale=128.0), r=[lc_b], w=[lc_b])
                S.op("pool", lambda e: e.iota(iot[:, :], pattern=[[1, 128]], base=0, channel_multiplier=-1,
                                               allow_small_or_imprecise_dtypes=True), r=[lc_b], w=[lc_b])
                S.op("pool", lambda e: e.iota(iop[:, 0:1], pattern=[[0, 1]], base=127, channel_multiplier=-1,
                                               allow_small_or_imprecise_dtypes=True), r=[lc_b], w=[lc_b])
                S.op("pool", lambda e: e.iota(iop[:, 1:2], pattern=[[0, 1]], base=0, channel_multiplier=1,
                                               allow_small_or_imprecise_dtypes=True), r=[lc_b], w=[lc_b])
                S.op("pool", lambda e: e.iota(ioi[:, 0, :], pattern=[[1, 128]], base=1, channel_multiplier=0,
                                               allow_small_or_imprecise_dtypes=True), r=[lc_b], w=[lc_b])
                S.op("pool", lambda e: e.iota(ioi[:, 1, :], pattern=[[-1, 128]], base=128, channel_multiplier=0,
                                               allow_small_or_imprecise_dtypes=True), r=[lc_b], w=[lc_b])
                for d in range(2):
                    for h in range(4):
                        c = d * 4 + h
                        S.op("act", lambda e, c=c, d=d: e.activation(out=ZF[:, c:c + 1], in_=iop[:, d:d + 1], func=AF.Exp,
                                                                      scale=LG[:, c:c + 1]), r=[lc_b], w=[lc_b])
                S.op("dve", lambda e: e.tensor_scalar(out=ZF, in0=ZF, scalar1=1.0 / 16.0, scalar2=None, op0=ALU.mult),
                     r=[lc_b], w=[lc_b])
                for h in range(4):
                    S.op("dve", lambda e: e.tensor_scalar(out=tmpm[:, 0, :], in0=iot[:, :], scalar1=0.0, scalar2=None,
                                                          op0=ALU.max), r=[lc_b], w=[lc_b])
                    S.op("act", lambda e, h=h: e.activation(out=tmpm[:, 0, :], in_=tmpm[:, 0, :], func=AF.Exp,
                                                             scale=LG[:, h:h + 1]), r=[lc_b], w=[lc_b])
                    S.op("pool", lambda e: e.affine_select(out=tmpm[:, 0, :], in_=tmpm[:, 0, :], pattern=[[1, 128]],
                                                            compare_op=ALU.is_ge, fill=0.0, base=0, channel_multiplier=-1),
                         r=[lc_b], w=[lc_b])
                    S.op("dve", lambda e: e.tensor_scalar(out=tmpm[:, 1, :], in0=iot[:, :], scalar1=-1.0, scalar2=0.0,
                                                          op0=ALU.mult, op1=ALU.max), r=[lc_b], w=[lc_b])
                    S.op("act", lambda e, h=h: e.activation(out=tmpm[:, 1, :], in_=tmpm[:, 1, :], func=AF.Exp,
                                                             scale=LG[:, 4 + h:5 + h]), r=[lc_b], w=[lc_b])
                    S.op("pool", lambda e: e.affine_select(out=tmpm[:, 1, :], in_=tmpm[:, 1, :], pattern=[[-1, 128]],
                                                            compare_op=ALU.is_ge, fill=0.0, base=0, channel_multiplier=1),
                         r=[lc_b], w=[lc_b])
                    S.op("dve", lambda e, h=h: e.tensor_tensor(out=Mh[:, h, :], in0=tmpm[:, 0, :], in1=tmpm[:, 1, :],
                                                               op=ALU.add), r=[lc_b], w=[lc_b])
                    S.op("dve", lambda e, h=h: e.tensor_scalar(out=Mh[:, h, :], in0=Mh[:, h, :], scalar1=1.0 / 16.0,
                                                               scalar2=None, op0=ALU.mult), r=[lc_b], w=[lc_b])
                    for d in range(2):
                        S.op("act", lambda e, h=h, d=d: e.activation(out=Xi[:, d * 4 + h, :], in_=ioi[:, d, :], func=AF.Exp,
                                                                      scale=LG[:, d * 4 + h:d * 4 + h + 1]),
                             r=[lc_b], w=[lc_b])

                norm_mod(l)

                for h in range(4):
                    sQK, sQKb = ws_get()
                    sV, sVb = ws_get()
                    sQKv, sVv = v8(sQK), v8(sV)
                    S.dma("sp", gnw[:, :], gnw_d[:, j * 2048 + h * 512:j * 2048 + (h + 1) * 512], w=[gnw_b])

                    for s in range(NSEG):
                        t0, n = s * SEGL, SEGL
                        for wi, dst in ((0, qT), (1, kT)):
                            pss = []
                            for half in range(2):
                                bk, bb = psum()
                                c0 = wi * 256 + half * 128
                                for kt in range(8):
                                    S.op("pe", lambda e, bk=bk, kt=kt, c0=c0: e.matmul(
                                        bk[:, 0:n], lhsT=sQKv[:, kt, c0:c0 + 128], rhs=hT[:, kt, t0:t0 + n],
                                        start=(kt == 0), stop=(kt == 7)), r=[sQKb, hT_b[s]], w=[bb])
                                pss.append((bk, bb))
                            cs, sn = cosT[:, t0:t0 + n], sinT[:, t0:t0 + n]
                            for half in range(2):
                                S.op("act", lambda e, half=half: e.copy(out=sq[:, half, :], in_=pss[half][0][:, 0:n]),
                                     r=[pss[half][1]], w=[sqb])
                            S.op("dve", lambda e: e.tensor_tensor(out=sq[:, 2, :], in0=sq[:, 0, :], in1=cs, op=ALU.mult),
                                 r=[sqb, rope_b], w=[sqb])
                            S.op("dve", lambda e: e.tensor_tensor(out=sq[:, 3, :], in0=sq[:, 1, :], in1=sn, op=ALU.mult),
                                 r=[sqb, rope_b], w=[sqb])
                            S.op("dve", lambda e: e.tensor_tensor(out=sq[:, 4, :], in0=sq[:, 0, :], in1=sn, op=ALU.mult),
                                 r=[sqb, rope_b], w=[sqb])
                            S.op("dve", lambda e: e.tensor_tensor(out=sq[:, 5, :], in0=sq[:, 1, :], in1=cs, op=ALU.mult),
                                 r=[sqb, rope_b], w=[sqb])
                            S.op("dve", lambda e, dst=dst: e.tensor_tensor(out=dst[:, 0, t0:t0 + n], in0=sq[:, 2, :],
                                                                          in1=sq[:, 3, :], op=ALU.subtract),
                                 r=[sqb], w=[qk_b[s]])
                            S.op("dve", lambda e, dst=dst: e.tensor_tensor(out=dst[:, 1, t0:t0 + n], in0=sq[:, 4, :],
                                                                          in1=sq[:, 5, :], op=ALU.add),
                                 r=[sqb], w=[qk_b[s]])
                    ws_release()
                    for c in range(NT):
                        tk = slice(c * 128, (c + 1) * 128)
                        bk, bb = psum()
                        for kt in range(8):
                            S.op("pe", lambda e, bk=bk, kt=kt: e.matmul(bk[:, :], lhsT=hT[:, kt, tk], rhs=sVv[:, kt, :],
                                                                         start=(kt == 0), stop=(kt == 7)),
                                 r=[sVb, hT_b[c // 2]], w=[bb])
                        S.op("act", lambda e, bk=bk, c=c: e.copy(out=v16[:, c, :], in_=bk[:, :]), r=[bb], w=[v_b[c]])
                    ws_release()
                    sZ, sZb = ws_get()
                    sO, sOb = ws_get()
                    sZv = v8(sZ)
                    sOv = sO[:, :].rearrange("p (v c) -> p v c", v=4)
                    for c in range(NT):
                        tk = slice(c * 128, (c + 1) * 128)
                        bk, bb = psum()
                        bkb = bk[:, :].bitcast(BF16)
                        for dt in range(2):
                            S.op("pe", lambda e, dt=dt, bkb=bkb: e.transpose(bkb[:, dt * 128:(dt + 1) * 128], kT[:, dt, tk],
                                                                             ident_b[:]),
                                 r=[qk_b[c // 2], cst_b], w=[bb])
                        S.op("dve", lambda e, bkb=bkb, c=c: e.tensor_scalar(out=kzf[:, c, :], in0=bkb[:, 0:256],
                                                                            scalar1=ZF[:, h:h + 1], scalar2=None,
                                                                            op0=ALU.mult), r=[bb, lc_b], w=[kz_b[c]])
                        S.op("act", lambda e, bkb=bkb, c=c: e.activation(out=kzb[:, c, :], in_=bkb[:, 0:256], func=AF.Copy,
                                                                         scale=ZF[:, 4 + h:5 + h]), r=[bb, lc_b], w=[kz_b[c]])
                        bk, bb = psum()
                        for dt in range(2):
                            S.op("pe", lambda e, dt=dt, bk=bk: e.matmul(bk[:, 0:128], lhsT=kT[:, dt, tk], rhs=qT[:, dt, tk],
                                                                         start=(dt == 0), stop=(dt == 1)),
                                 r=[qk_b[c // 2]], w=[bb])
                        S.op("dve", lambda e, bk=bk, c=c: e.tensor_tensor(out=sm[:, c, :], in0=bk[:, 0:128], in1=Mh[:, h, :],
                                                                          op=ALU.mult), r=[bb, lc_b], w=[sm_b[c]])

                    def upd_state(d, c, kz):
                        pss = []
                        for dt in range(2):
                            bk, bb = psum()
                            S.op("pe", lambda e, bk=bk, dt=dt: e.matmul(bk[:, :], lhsT=kz[:, c, dt * 128:(dt + 1) * 128],
                                                                         rhs=v16[:, c, :], start=True, stop=True),
                                 r=[kz_b[c], v_b[c]], w=[bb])
                            pss.append((bk, bb))
                        if S32alt[d] is None:
                            dst, dstb = S32[d], S32_b[d]
                        else:
                            dst, dstb = S32alt[d], S32alt_b[d]
                        for dt in range(2):
                            bk, bb = pss[dt]
                            S.op("dve", lambda e, bk=bk, dt=dt: e.scalar_tensor_tensor(
                                out=dst[:, dt, :], in0=S32[d][:, dt, :], scalar=GC[:, d * 4 + h:d * 4 + h + 1],
                                in1=bk[:, :], op0=ALU.mult, op1=ALU.add), r=[bb, lc_b, S32_b[d]], w=[dstb])
                        if S32alt[d] is not None:
                            S32[d], S32alt[d] = S32alt[d], S32[d]
                            S32_b[d], S32alt_b[d] = S32alt_b[d], S32_b[d]

                    def out_state(d, s):
                        st_t, st_bf = stage()
                        S.op("act", lambda e: e.copy(out=st_t[:, :], in_=S32[d][:, :, :].rearrange("p a b -> p (a b)")),
                             r=[S32_b[d]], w=[st_bf])
                        S.dma("sp", nsr_d[s, j, d, h].rearrange("(a p) v -> p a v", p=128),
                              st_t[:, :].rearrange("p (a v) -> p a v", a=2), r=[st_bf])

                    S.op("dve", lambda e: e.memset(S32[1][:, :, :], 0.0), w=[S32_b[1]])
                    for c in range(NT - 1, -1, -1):
                        s = c // 2
                        S.op("act", lambda e, c=c: e.copy(out=Sb16[:, c, :, :], in_=S32[1][:, :, :]),
                             r=[S32_b[1]], w=[Sb16_b[c]])
                        upd_state(1, c, kzb)
                        if c % 2 == 0:
                            out_state(1, s)
                            if c > 0:
                                if s - 1 == 3:
                                    st_t, st_bf = stage()
                                    S.dma("sp", st_t[:, :].rearrange("p (a v) -> p a v", a=2),
                                          s0r_d[j, 1, h].rearrange("(a p) v -> p a v", p=128), w=[st_bf])
                                    S.op("dve", lambda e, s=s, st_t=st_t: e.scalar_tensor_tensor(
                                        out=S32[1][:, :, :], in0=S32[1][:, :, :], scalar=chain_ap(s),
                                        in1=st_t[:, :].rearrange("p (a v) -> p a v", a=2),
                                        op0=ALU.mult, op1=ALU.add), r=[S32_b[1], prm_b, st_bf], w=[S32_b[1]])
                                else:
                                    S.op("dve", lambda e, s=s: e.tensor_scalar(
                                        out=S32[1][:, :, :], in0=S32[1][:, :, :], scalar1=chain_ap(s), scalar2=None,
                                        op0=ALU.mult), r=[S32_b[1], prm_b], w=[S32_b[1]])
                    S.dma("sp", S32[0][:, :, :], s0r_d[j, 0, h].rearrange("(a p) v -> p a v", p=128), w=[S32_b[0]])
                    S.op("act", lambda e: e.copy(out=S16f[:, :, :], in_=S32[0][:, :, :]), r=[S32_b[0]], w=[S16f_b])
                    for c in range(NT):
                        s = c // 2
                        tk = slice(c * 128, (c + 1) * 128)
                        qi = c % 2
                        for d in range(2):
                            S.op("dve", lambda e, d=d, qi=qi: e.tensor_tensor(
                                out=qx[qi][:, d, :, :], in0=qT[:, :, tk],
                                in1=Xi[:, d * 4 + h, :].unsqueeze(1).to_broadcast([128, 2, 128]), op=ALU.mult),
                                r=[qk_b[s], lc_b], w=[qx_b[qi]])
                        bkz, bbz = psum()
                        for kt in range(8):
                            S.op("pe", lambda e, kt=kt: e.matmul(bkz[:, :], lhsT=hT[:, kt, tk], rhs=sZv[:, kt, :],
                                                                  start=(kt == 0), stop=(kt == 7)), r=[sZb, hT_b[s]], w=[bbz])
                        S.op("act", lambda e, qi=qi: e.activation(out=zs[qi][:, :], in_=bkz[:, :], func=AF.Exp, scale=-1.0),
                             r=[bbz], w=[zs_b[qi]])
                        S.op("act", lambda e, qi=qi: e.activation(out=zs[qi][:, :], in_=zs[qi][:, :], func=AF.Ln, bias=cpar[:, 2:3],
                                                                  scale=1.0), r=[zs_b[qi], cst_b], w=[zs_b[qi]])
                        S.op("act", lambda e, qi=qi: e.activation(out=zs[qi][:, :], in_=zs[qi][:, :], func=AF.Exp, scale=-1.0),
                             r=[zs_b[qi]], w=[zs_b[qi]])
                        S.op("dve", lambda e, qi=qi: e.tensor_tensor(out=zs[qi][:, :], in0=bkz[:, :], in1=zs[qi][:, :], op=ALU.mult),
                             r=[bbz, zs_b[qi]], w=[zs_b[qi]])
                        bk, bb = psum()
                        S.op("pe", lambda e, bk=bk, c=c: e.matmul(bk[:, :], lhsT=sm[:, c, :], rhs=v16[:, c, :],
                                                                   start=True, stop=False), r=[sm_b[c], v_b[c]], w=[bb])
                        for dt in range(2):
                            S.op("pe", lambda e, bk=bk, dt=dt: e.matmul(bk[:, :], lhsT=qx[qi][:, 0, dt, :], rhs=S16f[:, dt, :],
                                                                         start=False, stop=False),
                                 r=[qx_b[qi], S16f_b], w=[bb])
                        for dt in range(2):
                            S.op("pe", lambda e, bk=bk, dt=dt, c=c: e.matmul(bk[:, :], lhsT=qx[qi][:, 1, dt, :],
                                                                              rhs=Sb16[:, c, dt, :], start=False,
                                                                              stop=(dt == 1)),
                                 r=[qx_b[qi], Sb16_b[c]], w=[bb])
                        S.op("dve", lambda e, bk=bk: e.bn_stats(out=bst[:, 0:6], in_=bk[:, :]), r=[bb], w=[bst_b])
                        S.op("dve", lambda e: e.bn_aggr(out=bst[:, 8:10], in_=bst[:, 0:6]), r=[bst_b], w=[bst_b])
                        S.op("act", lambda e: e.activation(out=bst[:, 10:11], in_=bst[:, 9:10], func=AF.Ln, bias=cpar[:, 1:2],
                                                           scale=1.0), r=[bst_b, cst_b], w=[bst_b])
                        S.op("act", lambda e: e.activation(out=bst[:, 10:11], in_=bst[:, 10:11], func=AF.Exp, scale=-0.5),
                             r=[bst_b], w=[bst_b])
                        S.op("dve", lambda e, bk=bk: e.tensor_scalar(out=on[:, :], in0=bk[:, :], scalar1=bst[:, 8:9],
                                                                     scalar2=bst[:, 10:11], op0=ALU.subtract,
                                                                     op1=ALU.mult), r=[bb, bst_b], w=[on_b])
                        S.op("dve", lambda e: e.tensor_tensor(out=on[:, :], in0=on[:, :], in1=gnw[:, :], op=ALU.mult),
                             r=[on_b, gnw_b], w=[on_b])
                        og, og_b = ogs[qi], ogs_b[qi]
                        S.op("dve", lambda e, qi=qi, og=og: e.tensor_tensor(out=og[:, :], in0=on[:, :], in1=zs[qi][:, :], op=ALU.mult),
                             r=[on_b, zs_b[qi]], w=[og_b])
                        bk2, bb2 = psum()
                        bk2b = bk2[:, :].bitcast(BF16)
                        for vt in range(4):
                            S.op("pe", lambda e, vt=vt, bk2b=bk2b: e.transpose(
                                bk2b[:, vt * 128:(vt + 1) * 128], og[:, vt * 128:(vt + 1) * 128], ident_b[:]),
                                r=[og_b, cst_b], w=[bb2])
                        oi = s % 2
                        lo = (c % 2) * 128
                        S.op("act", lambda e, bk2b=bk2b, oi=oi, lo=lo: e.copy(
                            out=oT[oi][:, :, lo:lo + 128], in_=bk2b[:, 0:512].rearrange("p (v t) -> p v t", v=4)),
                            r=[bb2], w=[oT_b[oi]])
                        upd_state(0, c, kzf)
                        if c % 2 == 1:
                            out_state(0, s)
                            if c < NT - 1:
                                S.op("dve", lambda e, s=s: e.tensor_scalar(
                                    out=S32[0][:, :, :], in0=S32[0][:, :, :], scalar1=chain_ap(s + 1), scalar2=None,
                                    op0=ALU.mult), r=[S32_b[0], prm_b], w=[S32_b[0]])
                        if c < NT - 1:
                            S.op("act", lambda e: e.copy(out=S16f[:, :, :], in_=S32[0][:, :, :]), r=[S32_b[0]], w=[S16f_b])
                        if c % 2 == 1:
                            seg = slice(s * SEGL, (s + 1) * SEGL)
                            for dt in range(8):
                                bk, bb = psum()
                                for vt in range(4):
                                    S.op("pe", lambda e, vt=vt, dt=dt, bk=bk, oi=oi: e.matmul(
                                        bk[:, 0:SEGL], lhsT=sOv[:, vt, dt * 128:(dt + 1) * 128], rhs=oT[oi][:, vt, :],
                                        start=(vt == 0), stop=(vt == 3)), r=[sOb, oT_b[oi]], w=[bb])
                                S.op("dve", lambda e, bk=bk, dt=dt, s=s: e.scalar_tensor_tensor(
                                    out=xT[:, dt, seg], in0=bk[:, 0:SEGL], scalar=mod_ap(l, 2, dt, s), in1=xT[:, dt, seg],
                                    op0=ALU.mult, op1=ALU.add), r=[bb, modT_b, xT_b[s]], w=[xT_b[s]])
                    ws_release()
                    ws_release()
                S.barrier()

        def del_layer(l, j):
            with contextlib.ExitStack() as ls:
                DH = 8
                tri = [sb(f"tri{d}", [128, 128], st=ls) for d in range(2)]
                neg3 = [sb(f"neg3{d}", [128, 128], st=ls) for d in range(2)]
                pos1 = [sb(f"pos1{d}", [128, 128], st=ls) for d in range(2)]
                dc_b = Buf("dconst")
                for d in range(2):
                    S.op("pool", lambda e, d=d: e.memset(tri[d][:, :], 1.0), w=[dc_b])
                    pat, cm = ([[1, 128]], -1) if d == 0 else ([[-1, 128]], 1)
                    S.op("pool", lambda e, d=d, pat=pat, cm=cm: e.affine_select(
                        out=tri[d][:, :], in_=tri[d][:, :], pattern=pat, compare_op=ALU.is_ge, fill=0.0, base=0,
                        channel_multiplier=cm), r=[dc_b], w=[dc_b])
                    S.op("pool", lambda e, d=d: e.memset(neg3[d][:, :], 0.0), w=[dc_b])
                    S.op("pool", lambda e, d=d, pat=pat, cm=cm: e.affine_select(
                        out=neg3[d][:, :], in_=neg3[d][:, :], pattern=pat, compare_op=ALU.is_ge, fill=-BIG, base=0,
                        channel_multiplier=cm), r=[dc_b], w=[dc_b])
                    pat2, cm2 = ([[-1, 128]], 1) if d == 0 else ([[1, 128]], -1)
                    S.op("pool", lambda e, d=d: e.memset(pos1[d][:, :], 0.0), w=[dc_b])
                    S.op("pool", lambda e, d=d, pat2=pat2, cm2=cm2: e.affine_select(
                        out=pos1[d][:, :], in_=pos1[d][:, :], pattern=pat2, compare_op=ALU.is_gt, fill=BIG, base=0,
                        channel_multiplier=cm2), r=[dc_b], w=[dc_b])
                abraw = sb("abraw", [128, NT, 32], st=ls)
                tk_b = Buf("tokscal")
                tsc = sb("tsc", [128, 10, NT, 16], st=ls)
                U, L1, GG, LNB, BB, GT, EGt, NBEG, NEGG, GPL = [tsc[:, i, :, :] for i in range(10)]
                negA = sb("negA", [128, 16], st=ls)
                dnw = sb("dnw", [128, 256], st=ls)
                dnw_b = Buf("dnw")
                XP = [sb("XP0", [128, NSEG, 258], st=ls)] * 2
                XP_b = [Buf("XP0")] * 2
                acc = sqs[0][:, 0:NSEG, :]
                acc_b = sqs_b[0]
                tmb = sb("tmb", [128, NTOK], BF16, st=ls)
                tmb_b = Buf("tmb")
                rinv = sqs[0][:, 5:7, :].rearrange("p a b -> p (a b)")
                rinv_b = sqs_b[0]
                qT = sb("dqT", [128, NTOK], BF16, st=ls)
                kT = sb("dkT", [128, NTOK], BF16, st=ls)
                q_b, k_b = Buf("dq"), Buf("dk")
                bv = sb("bv", [128, NT * 2, 256], BF16, st=ls)
                bv_b = [Buf(f"bv{c}") for c in range(NT)]
                kd = sb("kd", [128, NT * 2, 128], BF16, st=ls)
                kd_b = [Buf(f"kd{c}") for c in range(NT)]
                TT = sb("TT", [128, NT * 2, 128], BF16, st=ls)
                TT_b = [Buf(f"TT{g}") for g in range(NSEG)]
                PTm = sb("PTm", [128, NT * 2, 128], BF16, st=ls)
                qg = sb("qg", [128, NT * 2, 128], BF16, st=ls)
                pq_b = [Buf(f"pq{c}") for c in range(NT)]
                csc = sb("csc", [128, 2, NT * 2], st=ls)
                csc_b = [Buf(f"csc{c}") for c in range(NT)]
                Es = [sb(f"Es{i}", [128, 3, 128], st=ls) for i in range(2)]
                Es_b = [Buf(f"Es{i}") for i in range(2)]
                def g4(nm):
                    return sb(nm, [128, 4, 128], BF16, st=ls), Buf(nm)
                Xg, Xg_b = zip(*[g4(f"Xg{i}") for i in range(2)])
                Yg, Yg_b = zip(*[g4(f"Yg{i}") for i in range(2)])
                Pg, Pg_b = zip(*[g4(f"Pg{i}") for i in range(2)])
                Qg, Qg_b = zip(*[g4(f"Qg{i}") for i in range(2)])
                id4, _ = g4("id4")
                Wg, Wg_b = zip(*[g4(f"Wg{i}") for i in range(2)])
                Af, Af_b = g4("Af")
                Bf, Bf_b = g4("Bf")
                bmask = sb("bmask", [128, 4, 128], BF16, st=ls)
                S.dma("pool", bmask[:, :, :].rearrange("p a b -> p (a b)"), bmask_d, w=[dc_b])
                bmaskn = sb("bmaskn", [128, 4, 4, 128], BF16, st=ls)
                for q_ in range(4):
                    S.op("dve", lambda e, q_=q_: e.tensor_scalar(out=bmaskn[:, :, q_, :], in0=bmask[:, :, :], scalar1=-1.0,
                                                                 scalar2=None, op0=ALU.mult), r=[dc_b], w=[dc_b])
                for q_ in range(4):
                    S.op("dve", lambda e, q_=q_: e.tensor_copy(out=id4[:, q_, :], in_=ident_b[:, :]), r=[cst_b, dc_b], w=[dc_b])
                ob = sb("ob", [128, NT, 256], st=ls)
                ob_b = [Buf(f"ob{c}") for c in range(NT)]
                S32 = [sb(f"dS32_{d}", [128, 256], st=ls) for d in range(2)]
                S32_b = [Buf(f"dS32_{d}") for d in range(2)]
                S16 = [sb(f"dS16_{d}", [128, 256], BF16, st=ls) for d in range(2)]
                S16_b = [Buf(f"dS16_{d}") for d in range(2)]
                rr = [sb(f"rr{d}", [128, 256], BF16, st=ls) for d in range(2)]
                rr_b = [Buf(f"rr{d}") for d in range(2)]
                vn16 = [sb(f"vn{d}", [128, 256], BF16, st=ls) for d in range(2)]
                vn_b = [Buf(f"vn{d}") for d in range(2)]
                zs = [sb(f"dzs{i}", [128, 256], st=ls) for i in range(2)]
                zs_b = [Buf(f"dzs{i}") for i in range(2)]
                on = [sb(f"don{i}", [128, 256], st=ls) for i in range(2)]
                on_b = [Buf(f"don{i}") for i in range(2)]
                og = [sb(f"dog{i}", [128, 256], BF16, st=ls) for i in range(2)]
                og_b = [Buf(f"dog{i}") for i in range(2)]
                oT = [sb(f"doT{i}", [128, 2, SEGL], BF16, st=ls) for i in range(NSEG)]
                oT_b = [Buf(f"doT{i}") for i in range(NSEG)]
                bst = sb("dbst", [128, 2, 8], st=ls)
                bst_b = [Buf("dbst0"), Buf("dbst1")]

                norm_mod(l)

                sAB, sABb = ws_get()
                sABv = v8(sAB, 32)
                for c in range(NT):
                    tk = slice(c * 128, (c + 1) * 128)
                    bk, bb = psum()
                    for kt in range(8):
                        S.op("pe", lambda e, bk=bk, kt=kt: e.matmul(bk[:, 0:32], lhsT=hT[:, kt, tk], rhs=sABv[:, kt, :],
                                                                     start=(kt == 0), stop=(kt == 7)),
                             r=[sABb, hT_b[c // 2]], w=[bb])
                    S.op("act", lambda e, bk=bk, c=c: e.copy(out=abraw[:, c, :], in_=bk[:, 0:32]), r=[bb], w=[tk_b])
                ws_release()
                al = prm[:, o_alog + j * 16:o_alog + j * 16 + 16]
                dtb = prm[:, o_dtb + j * 16:o_dtb + j * 16 + 16]
                T = [tk_b, prm_b, cst_b]
                S.op("act", lambda e: e.activation(out=negA[:, :], in_=al, func=AF.Exp), r=T, w=[tk_b])
                S.op("dve", lambda e: e.tensor_scalar(out=negA[:, :], in0=negA[:, :], scalar1=-1.0, scalar2=None,
                                                      op0=ALU.mult), r=T, w=[tk_b])
                S.op("dve", lambda e: e.tensor_tensor(out=U, in0=abraw[:, :, 0:16],
                                                      in1=dtb.unsqueeze(1).to_broadcast([128, NT, 16]), op=ALU.add),
                     r=T, w=[tk_b])
                S.op("dve", lambda e: e.tensor_scalar(out=L1, in0=U, scalar1=-1.0, scalar2=None, op0=ALU.mult), r=T, w=[tk_b])
                S.op("dve", lambda e: e.tensor_tensor(out=L1, in0=L1, in1=U, op=ALU.max), r=T, w=[tk_b])
                S.op("act", lambda e: e.activation(out=L1, in_=L1, func=AF.Exp, scale=-1.0), r=T, w=[tk_b])
                S.op("act", lambda e: e.activation(out=L1, in_=L1, func=AF.Ln, bias=cpar[:, 2:3], scale=1.0), r=T, w=[tk_b])
                S.op("dve", lambda e: e.tensor_scalar(out=U, in0=U, scalar1=0.0, scalar2=None, op0=ALU.max), r=T, w=[tk_b])
                S.op("dve", lambda e: e.tensor_tensor(out=U, in0=U, in1=L1, op=ALU.add), r=T, w=[tk_b])
                S.op("dve", lambda e: e.tensor_tensor(out=GG, in0=U, in1=negA[:, :].unsqueeze(1).to_broadcast([128, NT, 16]),
                                                      op=ALU.mult), r=T, w=[tk_b])
                S.op("dve", lambda e: e.tensor_scalar(out=L1, in0=abraw[:, :, 16:32], scalar1=-1.0, scalar2=None, op0=ALU.mult),
                     r=T, w=[tk_b])
                S.op("dve", lambda e: e.tensor_tensor(out=L1, in0=L1, in1=abraw[:, :, 16:32], op=ALU.max), r=T, w=[tk_b])
                S.op("act", lambda e: e.activation(out=L1, in_=L1, func=AF.Exp, scale=-1.0), r=T, w=[tk_b])
                S.op("act", lambda e: e.activation(out=L1, in_=L1, func=AF.Ln, bias=cpar[:, 2:3], scale=1.0), r=T, w=[tk_b])
                S.op("dve", lambda e: e.tensor_scalar(out=LNB, in0=abraw[:, :, 16:32], scalar1=0.0, scalar2=None,
                                                      op0=ALU.min), r=T, w=[tk_b])
                S.op("dve", lambda e: e.tensor_tensor(out=LNB, in0=LNB, in1=L1, op=ALU.subtract), r=T, w=[tk_b])
                S.op("act", lambda e: e.activation(out=BB, in_=LNB, func=AF.Exp), r=T, w=[tk_b])
                bk, bb = psum()
                for c in range(NT):
                    for d in range(2):
                        S.op("pe", lambda e, c=c, d=d: e.matmul(bk[:, c * 16 + d * 8:c * 16 + d * 8 + 8], lhsT=tri[d][:, :],
                                                                 rhs=GG[:, c, d * 8:d * 8 + 8], start=True, stop=True),
                             r=[tk_b, dc_b], w=[bb])
                S.op("dve", lambda e: e.tensor_copy(out=GT, in_=bk[:, 0:NT * 16].rearrange("p (c x) -> p c x", c=NT)),
                     r=[bb], w=[tk_b])
                S.op("act", lambda e: e.activation(out=EGt, in_=GT, func=AF.Exp), r=T, w=[tk_b])
                S.op("dve", lambda e: e.tensor_tensor(out=NBEG, in0=BB, in1=EGt, op=ALU.mult), r=T, w=[tk_b])
                S.op("dve", lambda e: e.tensor_scalar(out=NBEG, in0=NBEG, scalar1=-1.0, scalar2=None, op0=ALU.mult),
                     r=T, w=[tk_b])
                S.op("dve", lambda e: e.tensor_scalar(out=NEGG, in0=GT, scalar1=-1.0, scalar2=None, op0=ALU.mult),
                     r=T, w=[tk_b])
                S.op("dve", lambda e: e.tensor_tensor(out=GPL, in0=GT, in1=LNB, op=ALU.add), r=T, w=[tk_b])
                for i in range(2):
                    S.op("dve", lambda e, i=i: e.memset(XP[i][:, 0, 0:1], 0.0), w=[XP_b[i]])
                    S.op("dve", lambda e, i=i: e.memset(XP[i][:, NSEG - 1, 257:258], 0.0), w=[XP_b[i]])

                dstage = float(os.environ.get("K_DSTAGE", "9"))
                if dstage == 0:
                    S.barrier()
                    return
                xp_rr = [0]
                for h in range(int(os.environ.get("K_DHEADS", "8"))):
                    sW, sWb = ws_get()
                    sWv = v8(sW)
                    S.dma("sp", dnw[:, :], dnw_d[:, j * 2048 + h * 256:j * 2048 + (h + 1) * 256], w=[dnw_b])
                    for ct in range(4):
                        xi = xp_rr[0] % 2
                        xp_rr[0] += 1
                        xp, xpb = XP[xi], XP_b[xi]
                        for s in range(NSEG):
                            bk, bb = psum()
                            for kt in range(8):
                                S.op("pe", lambda e, bk=bk, kt=kt, s=s: e.matmul(
                                    bk[:, 0:SEGL], lhsT=sWv[:, kt, ct * 128:(ct + 1) * 128],
                                    rhs=hT[:, kt, s * SEGL:(s + 1) * SEGL], start=(kt == 0), stop=(kt == 7)),
                                    r=[sWb, hT_b[s]], w=[bb])
                            S.op("act", lambda e, bk=bk, s=s: e.copy(out=xp[:, s, 1:257], in_=bk[:, 0:SEGL]), r=[bb], w=[xpb])
                        chn = prm[:, o_chain + 1:o_chain + 5]
                        S.op("dve", lambda e: e.tensor_tensor(out=xp[:, 1:5, 0], in0=xp[:, 0:4, 256], in1=chn, op=ALU.mult),
                             r=[xpb, prm_b], w=[xpb])
                        S.op("dve", lambda e: e.tensor_tensor(out=xp[:, 0:4, 257], in0=xp[:, 1:5, 1], in1=chn, op=ALU.mult),
                             r=[xpb, prm_b], w=[xpb])
                        gct = h if ct == 0 else (8 + h if ct == 1 else 16 + 2 * h + (ct - 2))
                        cw = [prm[:, o_convw + (j * 3 + k) * 32 + gct:o_convw + (j * 3 + k) * 32 + gct + 1] for k in range(3)]
                        S.op("dve", lambda e: e.tensor_scalar(out=acc, in0=xp[:, :, 0:256], scalar1=cw[0], scalar2=None,
                                                              op0=ALU.mult), r=[xpb, prm_b], w=[acc_b])
                        S.op("dve", lambda e: e.scalar_tensor_tensor(out=acc, in0=xp[:, :, 1:257], scalar=cw[1],
                                                                     in1=acc, op0=ALU.mult, op1=ALU.add),
                             r=[xpb, prm_b, acc_b], w=[acc_b])
                        S.op("dve", lambda e: e.scalar_tensor_tensor(out=acc, in0=xp[:, :, 2:258], scalar=cw[2],
                                                                     in1=acc, op0=ALU.mult, op1=ALU.add),
                             r=[xpb, prm_b, acc_b], w=[acc_b])
                        accf = acc.rearrange("p s t -> p (s t)")
                        S.op("act", lambda e: e.activation(out=accf, in_=accf, func=AF.Silu), r=[acc_b], w=[acc_b])
                        if ct < 2:
                            S.op("act", lambda e: e.activation(out=tmb[:, :], in_=accf, func=AF.Square), r=[acc_b], w=[tmb_b])
                            dst, dstb = (qT, q_b) if ct == 0 else (kT, k_b)
                            scl = (128.0 ** -0.5) if ct == 0 else 1.0
                            for (t0, n) in BLOCKS:
                                bk, bb = psum()
                                S.op("pe", lambda e, bk=bk: e.matmul(bk[:, 0:n], lhsT=ones_b[:, :], rhs=tmb[:, t0:t0 + n],
                                                                      start=True, stop=True), r=[tmb_b, cst_b], w=[bb])
                                S.op("act", lambda e, bk=bk: e.activation(out=rinv[:, 0:n], in_=bk[:, 0:n], func=AF.Ln,
                                                                          bias=cpar[:, 1:2], scale=1.0),
                                     r=[bb, cst_b], w=[rinv_b])
                                S.op("act", lambda e: e.activation(out=rinv[:, 0:n], in_=rinv[:, 0:n], func=AF.Exp, scale=-0.5),
                                     r=[rinv_b], w=[rinv_b])
                                S.op("dve", lambda e, dst=dst: e.scalar_tensor_tensor(
                                    out=dst[:, t0:t0 + n], in0=accf[:, t0:t0 + n], scalar=scl, in1=rinv[:, 0:n],
                                    op0=ALU.mult, op1=ALU.mult), r=[acc_b, rinv_b], w=[dstb])
                        else:
                            vt = ct - 2
                            S.op("act", lambda e: e.copy(out=tmb[:, :], in_=accf), r=[acc_b], w=[tmb_b])
                            for c in range(NT):
                                tk = slice(c * 128, (c + 1) * 128)
                                bk, bb = psum()
                                bkb = bk[:, :].bitcast(BF16)
                                S.op("pe", lambda e, bkb=bkb: e.transpose(bkb[:, 0:128], tmb[:, tk], ident_b[:]),
                                     r=[tmb_b, cst_b], w=[bb])
                                S.op("dve", lambda e, bkb=bkb, c=c: e.tensor_scalar(
                                    out=bv[:, c * 2 + 0, vt * 128:(vt + 1) * 128], in0=bkb[:, 0:128],
                                    scalar1=BB[:, c, h:h + 1], scalar2=None, op0=ALU.mult), r=[bb, tk_b], w=[bv_b[c]])
                                S.op("act", lambda e, bkb=bkb, c=c: e.activation(
                                    out=bv[:, c * 2 + 1, vt * 128:(vt + 1) * 128], in_=bkb[:, 0:128], func=AF.Copy,
                                    scale=BB[:, c, 8 + h:9 + h]), r=[bb, tk_b], w=[bv_b[c]])
                    ws_release()
                    if dstage == 1:
                        S.barrier()
                        return
                    sZ, sZb = ws_get()
                    sO, sOb = ws_get()
                    sZv = v8(sZ, 256)
                    sOv = sO[:, 0:2048].rearrange("p (v c) -> p v c", v=2)

                    for g in range(NSEG):
                        kkps = []
                        for ci in range(2):
                            c = 2 * g + ci
                            tk = slice(c * 128, (c + 1) * 128)
                            bk, bb = psum()
                            S.op("pe", lambda e, bk=bk: e.matmul(bk[:, 0:128], lhsT=kT[:, tk], rhs=kT[:, tk], start=True, stop=True),
                                 r=[k_b], w=[bb])
                            S.op("pe", lambda e, bk=bk: e.matmul(bk[:, 128:256], lhsT=kT[:, tk], rhs=qT[:, tk], start=True,
                                                                  stop=True), r=[k_b, q_b], w=[bb])
                            bkt = bk[:, :].bitcast(BF16)
                            S.op("pe", lambda e, bkt=bkt: e.transpose(bkt[:, 512:640], kT[:, tk], ident_b[:]),
                                 r=[k_b, cst_b], w=[bb])
                            kkps.append((bk, bb, bkt))
                        if dstage == 1.1:
                            S.barrier()
                            return
                        xg, xgb = Af, Af_b
                        for ci in range(2):
                            c = 2 * g + ci
                            tk = slice(c * 128, (c + 1) * 128)
                            bk, bb, bkt = kkps[ci]
                            for d in range(2):
                                qd = ci * 2 + d
                                cd = c * 2 + d
                                dh = d * 8 + h
                                ei = qd % 2
                                es, esb = Es[ei], Es_b[ei]
                                be, bbe = psum()
                                gcol = GG[:, c, dh:dh + 1].to_broadcast([128, 128])
                                S.op("pe", lambda e, be=be, d=d: e.matmul(be[:, 0:128], lhsT=gcol, rhs=tri[d][:, :], start=True,
                                                                           stop=True), r=[tk_b, dc_b], w=[bbe])
                                S.op("pe", lambda e, be=be, d=d: e.matmul(be[:, 128:256], lhsT=gcol, rhs=tri[d][:, :], start=True,
                                                                           stop=False), r=[tk_b, dc_b], w=[bbe])
                                S.op("pe", lambda e, be=be, d=d: e.matmul(be[:, 128:256], lhsT=ident_f[:, :], rhs=neg3[d][:, :],
                                                                           start=False, stop=True), r=[cst_b, dc_b], w=[bbe])
                                S.op("pe", lambda e, be=be, d=d: e.matmul(be[:, 256:384], lhsT=gcol, rhs=tri[d][:, :], start=True,
                                                                           stop=False), r=[tk_b, dc_b], w=[bbe])
                                S.op("pe", lambda e, be=be, d=d: e.matmul(be[:, 256:384], lhsT=ident_f[:, :], rhs=pos1[d][:, :],
                                                                           start=False, stop=True), r=[cst_b, dc_b], w=[bbe])
                                S.op("act", lambda e, be=be, es=es: e.activation(out=es[:, 0, :], in_=be[:, 0:128], func=AF.Exp),
                                     r=[bbe], w=[esb])
                                S.op("act", lambda e, be=be, es=es, c=c, dh=dh: e.activation(
                                    out=es[:, 1, :], in_=be[:, 128:256], func=AF.Exp, bias=NEGG[:, c, dh:dh + 1], scale=1.0),
                                    r=[bbe, tk_b], w=[esb])
                                S.op("act", lambda e, be=be, es=es, c=c, dh=dh: e.activation(
                                    out=es[:, 2, :], in_=be[:, 256:384], func=AF.Exp, bias=GPL[:, c, dh:dh + 1], scale=-1.0),
                                    r=[bbe, tk_b], w=[esb])
                                last = 127 if d == 0 else 0
                                S.op("dve", lambda e, es=es, cd=cd, last=last: e.tensor_copy(
                                    out=csc[:, 0, cd:cd + 1], in_=es[:, 0, last:last + 1]), r=[esb], w=[csc_b[c]])
                                S.op("dve", lambda e, es=es, cd=cd, last=last: e.tensor_copy(
                                    out=csc[:, 1, cd:cd + 1], in_=es[:, 1, last:last + 1]), r=[esb], w=[csc_b[c]])
                                S.op("dve", lambda e, es=es, bk=bk, qd=qd: e.tensor_tensor(
                                    out=xg[:, qd, :], in0=bk[:, 0:128], in1=es[:, 2, :], op=ALU.mult), r=[bb, esb], w=[xgb])
                                S.op("dve", lambda e, es=es, bk=bk, cd=cd: e.tensor_tensor(
                                    out=PTm[:, cd, :], in0=bk[:, 128:256], in1=es[:, 1, :], op=ALU.mult), r=[bb, esb], w=[pq_b[c]])
                                S.op("dve", lambda e, es=es, cd=cd: e.tensor_tensor(
                                    out=qg[:, cd, :], in0=qT[:, tk], in1=es[:, 0, :], op=ALU.mult), r=[q_b, esb], w=[pq_b[c]])
                                if d == 0:
                                    S.op("dve", lambda e, bkt=bkt, cd=cd: e.tensor_scalar(
                                        out=kd[:, cd, :], in0=bkt[:, 512:640], scalar1=csc[:, 1, cd:cd + 1], scalar2=None,
                                        op0=ALU.mult), r=[bb, csc_b[c]], w=[kd_b[c]])
                                else:
                                    S.op("act", lambda e, bkt=bkt, cd=cd: e.activation(
                                        out=kd[:, cd, :], in_=bkt[:, 512:640], func=AF.Copy, scale=csc[:, 1, cd:cd + 1]),
                                        r=[bb, csc_b[c]], w=[kd_b[c]])
                        if dstage == 1.2:
                            S.barrier()
                            return
                        bt, bbt = psum()
                        btb = bt[:, :].bitcast(BF16)
                        for qd in range(4):
                            S.op("pe", lambda e, qd=qd, btb=btb: e.transpose(btb[:, qd * 128:(qd + 1) * 128], xg[:, qd, :],
                                                                             ident_b[:]), r=[xgb, cst_b], w=[bbt])
                        bt4 = btb[:, 0:512].rearrange("p (q i) -> p q i", q=4)
                        S.op("act", lambda e: e.copy(out=Bf[:, :, :], in_=bt4), r=[bbt], w=[Bf_b])

                        def mk(k_, neg=False):
                            if neg:
                                return bmaskn[:, k_, :, :]
                            return bmask[:, k_, :].unsqueeze(1).to_broadcast([128, 4, 128])

                        S.op("dve", lambda e: e.tensor_tensor(out=Xg[0][:, :, :], in0=xg[:, :, :], in1=mk(0), op=ALU.mult),
                             r=[xgb, dc_b], w=[Xg_b[0]])
                        S.op("dve", lambda e: e.tensor_tensor(out=Yg[0][:, :, :], in0=Bf[:, :, :], in1=mk(0), op=ALU.mult),
                             r=[Bf_b, dc_b], w=[Yg_b[0]])
                        S.op("dve", lambda e: e.scalar_tensor_tensor(out=Pg[0][:, :, :], in0=Yg[0][:, :, :], scalar=-1.0, in1=id4[:, :, :],
                                                                     op0=ALU.mult, op1=ALU.add), r=[Yg_b[0], dc_b], w=[Pg_b[0]])
                        S.op("dve", lambda e: e.scalar_tensor_tensor(out=Qg[0][:, :, :], in0=Xg[0][:, :, :], scalar=-1.0, in1=id4[:, :, :],
                                                                     op0=ALU.mult, op1=ALU.add), r=[Xg_b[0], dc_b], w=[Qg_b[0]])
                        if dstage == 1.3:
                            S.barrier()
                            return

                        def mm4(L, Lb, R, Rb, acc=None):
                            bk_, bb_ = psum()
                            for qd in range(4):
                                o_ = bk_[:, qd * 128:(qd + 1) * 128]
                                if acc is not None:
                                    S.op("pe", lambda e, qd=qd: e.matmul(o_, lhsT=ident_b[:, :], rhs=acc[0][:, qd, :], start=True,
                                                                          stop=False), r=[cst_b, acc[1]], w=[bb_])
                                S.op("pe", lambda e, qd=qd: e.matmul(o_, lhsT=L[:, qd, :], rhs=R[:, qd, :], start=(acc is None),
                                                                      stop=True), r=[Lb, Rb], w=[bb_])
                            return bk_[:, :].rearrange("p (q i) -> p q i", q=4), bb_

                        def ev_copy(eng, dst, dstb, ps, psb):
                            if eng == "act":
                                S.op("act", lambda e: e.copy(out=dst, in_=ps), r=[psb], w=[dstb])
                            else:
                                S.op("dve", lambda e: e.tensor_copy(out=dst, in_=ps), r=[psb], w=[dstb])

                        def ev_add(dst, dstb, ps, psb, old_, oldb):
                            S.op("dve", lambda e: e.tensor_tensor(out=dst, in0=ps, in1=old_[:, :, :], op=ALU.add),
                                 r=[psb, oldb], w=[dstb])

                        def ev_mask(dst, dstb, ps, psb, k_):
                            S.op("dve", lambda e: e.tensor_tensor(out=dst[:, :, :], in0=ps, in1=mk(k_, True), op=ALU.mult),
                                 r=[psb, dc_b], w=[dstb])

                        pi = 0
                        for m in range(1, 4):
                            a, b2 = (m - 1) % 2, m % 2
                            px, pxb = mm4(Yg[a], Yg_b[a], Xg[a], Xg_b[a])
                            py, pyb = mm4(Xg[a], Xg_b[a], Yg[a], Yg_b[a])
                            ev_copy("act", Xg[b2][:, :, :], Xg_b[b2], px, pxb)
                            ev_copy("act", Yg[b2][:, :, :], Yg_b[b2], py, pyb)
                            pp, ppb = mm4(Xg[b2], Xg_b[b2], Pg[pi], Pg_b[pi])
                            pq, pqb = mm4(Yg[b2], Yg_b[b2], Qg[pi], Qg_b[pi])
                            ev_add(Pg[1 - pi][:, :, :], Pg_b[1 - pi], pp, ppb, Pg[pi], Pg_b[pi])
                            ev_add(Qg[1 - pi][:, :, :], Qg_b[1 - pi], pq, pqb, Qg[pi], Qg_b[pi])
                            pi = 1 - pi
                        for s_ in range(1, 4):
                            pw1, pw1b = mm4(xg, xgb, Pg[pi], Pg_b[pi])
                            ev_mask(Wg[0], Wg_b[0], pw1, pw1b, s_)
                            if s_ < 3:
                                pw2, pw2b = mm4(Bf, Bf_b, Qg[pi], Qg_b[pi])
                                ev_mask(Wg[1], Wg_b[1], pw2, pw2b, s_)
                            pp, ppb = mm4(Qg[pi], Qg_b[pi], Wg[0], Wg_b[0])
                            if s_ < 3:
                                pq, pqb = mm4(Pg[pi], Pg_b[pi], Wg[1], Wg_b[1])
                                ev_add(Pg[1 - pi][:, :, :], Pg_b[1 - pi], pp, ppb, Pg[pi], Pg_b[pi])
                                ev_add(Qg[1 - pi][:, :, :], Qg_b[1 - pi], pq, pqb, Qg[pi], Qg_b[pi])
                                pi = 1 - pi
                            else:
                                ev_add(TT[:, g * 4:(g + 1) * 4, :], TT_b[g], pp, ppb, Pg[pi], Pg_b[pi])

                    if dbg_d and h == 0:
                        def dump(name, ap, bufs):
                            if name in dbg_d:
                                S.dma("pool", dbg_d[name], ap, r=bufs)
                        dump("tsc", tsc[:, :, :, :].rearrange("p a c x -> p (a c x)"), [tk_b])
                        dump("qT", qT[:, :], [q_b])
                        dump("kT", kT[:, :], [k_b])
                        dump("bv", bv[:, :, :].rearrange("p a b -> p (a b)"), bv_b)
                        dump("kd", kd[:, :, :].rearrange("p a b -> p (a b)"), kd_b)
                        dump("TT", TT[:, :, :].rearrange("p a b -> p (a b)"), TT_b)
                        dump("PTm", PTm[:, :, :].rearrange("p a b -> p (a b)"), pq_b)
                        dump("qg", qg[:, :, :].rearrange("p a b -> p (a b)"), pq_b)
                        dump("csc", csc[:, :, :].rearrange("p a b -> p (a b)"), csc_b)
                    if dstage == 2:
                        S.barrier()
                        return
                    S.dma("sp", S32[0][:, :], s0d_d[j, 0, h], w=[S32_b[0]])
                    S.op("dve", lambda e: e.memset(S32[1][:, :], 0.0), w=[S32_b[1]])
                    for d in range(2):
                        S.op("act", lambda e, d=d: e.copy(out=S16[d][:, :], in_=S32[d][:, :]), r=[S32_b[d]], w=[S16_b[d]])
                    arrived = [0] * NT
                    seg_done = [0] * NSEG

                    def consume_o(c, bo, bbo, first):
                        if first:
                            S.op("act", lambda e: e.copy(out=ob[:, c, :], in_=bo[:, 0:256]), r=[bbo], w=[ob_b[c]])
                        else:
                            oi = c % 2
                            S.op("dve", lambda e: e.tensor_tensor(out=on[oi][:, :], in0=bo[:, 0:256], in1=ob[:, c, :], op=ALU.add),
                                 r=[bbo, ob_b[c]], w=[on_b[oi]])

                    def finish_chunk(c, first):
                        s = c // 2
                        if first:
                            return
                        oi = c % 2
                        tk = slice(c * 128, (c + 1) * 128)
                        bkz, bbz = psum()
                        for kt in range(8):
                            S.op("pe", lambda e, kt=kt: e.matmul(bkz[:, 0:256], lhsT=hT[:, kt, tk], rhs=sZv[:, kt, :],
                                                                  start=(kt == 0), stop=(kt == 7)), r=[sZb, hT_b[s]], w=[bbz])
                        S.op("act", lambda e: e.activation(out=zs[oi][:, :], in_=bkz[:, 0:256], func=AF.Exp, scale=-1.0),
                             r=[bbz], w=[zs_b[oi]])
                        S.op("act", lambda e: e.activation(out=zs[oi][:, :], in_=zs[oi][:, :], func=AF.Ln, bias=cpar[:, 2:3], scale=1.0),
                             r=[zs_b[oi], cst_b], w=[zs_b[oi]])
                        S.op("act", lambda e: e.activation(out=zs[oi][:, :], in_=zs[oi][:, :], func=AF.Exp, scale=-1.0),
                             r=[zs_b[oi]], w=[zs_b[oi]])
                        S.op("dve", lambda e: e.tensor_tensor(out=zs[oi][:, :], in0=bkz[:, 0:256], in1=zs[oi][:, :], op=ALU.mult),
                             r=[bbz, zs_b[oi]], w=[zs_b[oi]])
                        S.op("dve", lambda e: e.bn_stats(out=bst[:, oi, 0:6], in_=on[oi][:, :]), r=[on_b[oi]], w=[bst_b[oi]])
                        S.op("dve", lambda e: e.bn_aggr(out=bst[:, oi, 6:8], in_=bst[:, oi, 0:6]), r=[bst_b[oi]], w=[bst_b[oi]])
                        S.op("dve", lambda e: e.scalar_tensor_tensor(out=bst[:, oi, 0:1], in0=bst[:, oi, 6:7], scalar=bst[:, oi, 6:7],
                                                                     in1=bst[:, oi, 7:8], op0=ALU.mult, op1=ALU.add),
                             r=[bst_b[oi]], w=[bst_b[oi]])
                        S.op("act", lambda e: e.activation(out=bst[:, oi, 1:2], in_=bst[:, oi, 0:1], func=AF.Ln, bias=cpar[:, 1:2],
                                                           scale=1.0), r=[bst_b[oi], cst_b], w=[bst_b[oi]])
                        S.op("act", lambda e: e.activation(out=bst[:, oi, 1:2], in_=bst[:, oi, 1:2], func=AF.Exp, scale=-0.5),
                             r=[bst_b[oi]], w=[bst_b[oi]])
                        S.op("dve", lambda e: e.scalar_tensor_tensor(out=on[oi][:, :], in0=on[oi][:, :], scalar=bst[:, oi, 1:2],
                                                                     in1=dnw[:, :], op0=ALU.mult, op1=ALU.mult),
                             r=[on_b[oi], bst_b[oi], dnw_b], w=[on_b[oi]])
                        S.op("dve", lambda e: e.tensor_tensor(out=og[oi][:, :], in0=on[oi][:, :], in1=zs[oi][:, :], op=ALU.mult),
                             r=[on_b[oi], zs_b[oi]], w=[og_b[oi]])
                        bk2, bb2 = psum()
                        bk2b = bk2[:, :].bitcast(BF16)
                        for vt in range(2):
                            S.op("pe", lambda e, vt=vt: e.transpose(bk2b[:, vt * 128:(vt + 1) * 128],
                                                                    og[oi][:, vt * 128:(vt + 1) * 128], ident_b[:]),
                                 r=[og_b[oi], cst_b], w=[bb2])
                        lo = (c % 2) * 128
                        S.op("act", lambda e: e.copy(out=oT[s][:, :, lo:lo + 128],
                                                     in_=bk2b[:, 0:256].rearrange("p (v t) -> p v t", v=2)), r=[bb2], w=[oT_b[s]])
                        seg_done[s] += 1
                        if seg_done[s] == 2:
                            seg = slice(s * SEGL, (s + 1) * SEGL)
                            for dt in range(8):
                                bk, bb = psum()
                                for vt in range(2):
                                    S.op("pe", lambda e, vt=vt, dt=dt, bk=bk: e.matmul(
                                        bk[:, 0:SEGL], lhsT=sOv[:, vt, dt * 128:(dt + 1) * 128], rhs=oT[s][:, vt, :],
                                        start=(vt == 0), stop=(vt == 1)), r=[sOb, oT_b[s]], w=[bb])
                                S.op("dve", lambda e, bk=bk, dt=dt: e.scalar_tensor_tensor(
                                    out=xT[:, dt, seg], in0=bk[:, 0:SEGL], scalar=mod_ap(l, 2, dt, s), in1=xT[:, dt, seg],
                                    op0=ALU.mult, op1=ALU.add), r=[bb, modT_b, xT_b[s]], w=[xT_b[s]])

                    for step in range(NT):
                        cs_ = [step, NT - 1 - step]
                        st1 = []
                        for d in range(2):
                            c = cs_[d]
                            tk = slice(c * 128, (c + 1) * 128)
                            bk, bb = psum()
                            S.op("pe", lambda e, bk=bk, d=d: e.matmul(bk[:, 0:256], lhsT=kT[:, tk], rhs=S16[d][:, :], start=True,
                                                                       stop=True), r=[k_b, S16_b[d]], w=[bb])
                            st1.append((bk, bb))
                        for d in range(2):
                            c = cs_[d]
                            bk, bb = st1[d]
                            S.op("dve", lambda e, bk=bk, d=d, c=c: e.scalar_tensor_tensor(
                                out=rr[d][:, :], in0=bk[:, 0:256], scalar=NBEG[:, c, d * 8 + h:d * 8 + h + 1],
                                in1=bv[:, c * 2 + d, :], op0=ALU.mult, op1=ALU.add), r=[bb, tk_b, bv_b[c]], w=[rr_b[d]])
                        st3 = []
                        for d in range(2):
                            c = cs_[d]
                            bk, bb = psum()
                            S.op("pe", lambda e, bk=bk, d=d, c=c: e.matmul(bk[:, 0:256], lhsT=TT[:, c * 2 + d, :], rhs=rr[d][:, :],
                                                                            start=True, stop=True), r=[TT_b[c // 2], rr_b[d]], w=[bb])
                            st3.append((bk, bb))
                        for d in range(2):
                            bk, bb = st3[d]
                            S.op("act", lambda e, bk=bk, d=d: e.copy(out=vn16[d][:, :], in_=bk[:, 0:256]), r=[bb], w=[vn_b[d]])
                        st5 = []
                        for d in range(2):
                            c = cs_[d]
                            bo, bbo = psum()
                            S.op("pe", lambda e, bo=bo, d=d, c=c: e.matmul(bo[:, 0:256], lhsT=qg[:, c * 2 + d, :], rhs=S16[d][:, :],
                                                                            start=True, stop=False), r=[pq_b[c], S16_b[d]], w=[bbo])
                            S.op("pe", lambda e, bo=bo, d=d, c=c: e.matmul(bo[:, 0:256], lhsT=PTm[:, c * 2 + d, :], rhs=vn16[d][:, :],
                                                                            start=False, stop=True), r=[pq_b[c], vn_b[d]], w=[bbo])
                            S.op("pe", lambda e, bo=bo, d=d, c=c: e.matmul(bo[:, 256:512], lhsT=kd[:, c * 2 + d, :], rhs=vn16[d][:, :],
                                                                            start=True, stop=True), r=[kd_b[c], vn_b[d]], w=[bbo])
                            st5.append((bo, bbo))
                        for d in range(2):
                            c = cs_[d]
                            s = c // 2
                            bo, bbo = st5[d]
                            cd = c * 2 + d
                            S.op("dve", lambda e, bo=bo, d=d, cd=cd: e.scalar_tensor_tensor(
                                out=S32[d][:, :], in0=S32[d][:, :], scalar=csc[:, 0, cd:cd + 1], in1=bo[:, 256:512],
                                op0=ALU.mult, op1=ALU.add), r=[bbo, csc_b[c], S32_b[d]], w=[S32_b[d]])
                            seg_end = (c % 2 == 1) if d == 0 else (c % 2 == 0)
                            if seg_end:
                                st_t, st_bf = stage()
                                S.op("act", lambda e, st_t=st_t, d=d: e.copy(out=st_t[:, 0:256], in_=S32[d][:, :]),
                                     r=[S32_b[d]], w=[st_bf])
                                S.dma("sp", nsd_d[s, j, d, h], st_t[:, 0:256], r=[st_bf])
                                if d == 0 and c < NT - 1:
                                    S.op("dve", lambda e, s=s: e.tensor_scalar(out=S32[0][:, :], in0=S32[0][:, :],
                                                                               scalar1=chain_ap(s + 1), scalar2=None, op0=ALU.mult),
                                         r=[S32_b[0], prm_b], w=[S32_b[0]])
                                if d == 1 and c > 0:
                                    if s - 1 == 3:
                                        st2, st2b = stage()
                                        S.dma("sp", st2[:, 0:256], s0d_d[j, 1, h], w=[st2b])
                                        S.op("dve", lambda e, s=s, st2=st2: e.scalar_tensor_tensor(
                                            out=S32[1][:, :], in0=S32[1][:, :], scalar=chain_ap(s), in1=st2[:, 0:256],
                                            op0=ALU.mult, op1=ALU.add), r=[S32_b[1], prm_b, st2b], w=[S32_b[1]])
                                    else:
                                        S.op("dve", lambda e, s=s: e.tensor_scalar(out=S32[1][:, :], in0=S32[1][:, :],
                                                                                   scalar1=chain_ap(s), scalar2=None, op0=ALU.mult),
                                             r=[S32_b[1], prm_b], w=[S32_b[1]])
                            if step < NT - 1:
                                S.op("act", lambda e, d=d: e.copy(out=S16[d][:, :], in_=S32[d][:, :]), r=[S32_b[d]], w=[S16_b[d]])
                        for d in range(2):
                            c = cs_[d]
                            bo, bbo = st5[d]
                            arrived[c] += 1
                            consume_o(c, bo, bbo, arrived[c] == 1)
                        for d in range(2):
                            c = cs_[d]
                            finish_chunk(c, arrived[c] == 1)
                    ws_release()
                    ws_release()
                S.barrier()

        def final_out():
            for s in range(NSEG):
                sq, sqb, rs, rsb = rms_stat(s)
                seg = slice(s * SEGL, (s + 1) * SEGL)
                S.op("dve", lambda e: e.tensor_tensor(out=sq[:, :, :], in0=xT[:, :, seg],
                                                      in1=rs[:, :].unsqueeze(1).to_broadcast([128, 8, 256]),
                                                      op=ALU.mult), r=[xT_b[s], rsb], w=[sqb])
                for dt in range(8):
                    S.op("dve", lambda e, dt=dt: e.tensor_scalar(
                        out=sq[:, dt, :], in0=sq[:, dt, :], scalar1=prm[:, o_normw + 32 + dt:o_normw + 33 + dt],
                        scalar2=32.0, op0=ALU.mult, op1=ALU.mult), r=[sqb, prm_b], w=[sqb])
                for half in range(2):
                    tt = s * 2 + half
                    st_t, st_bf = stage()
                    for g in range(2):
                        bk, bb = psum()
                        for q in range(4):
                            dt = g * 4 + q
                            S.op("pe", lambda e, bk=bk, q=q, dt=dt: e.transpose(
                                bk[:, q * 128:(q + 1) * 128], sq[:, dt, half * 128:(half + 1) * 128], ident_f[:]),
                                r=[sqb, cst_b], w=[bb])
                        if g == 0:
                            S.op("dve", lambda e, bk=bk: e.tensor_copy(out=st_t[:, 0:512], in_=bk[:, :]), r=[bb], w=[st_bf])
                        else:
                            S.op("act", lambda e, bk=bk: e.copy(out=st_t[:, 512:1024], in_=bk[:, :]), r=[bb], w=[st_bf])
                    S.dma("sp", y_d[tt * 128:(tt + 1) * 128, :], st_t[:, :], r=[st_bf])

        for l in range(depth):
            if l % 2 == 0:
                ret_layer(l, l // 2)
            else:
                del_layer(l, l // 2)
        final_out()
        S.barrier()
        print(f"[build] instr={S.ninstr} sems={S.nsem} counts={S.cnt}")
    return nc


def _rope_tables():
    pos = np.arange(1024)
    r = (pos // 64).astype(np.float32)
    col = (pos % 64).astype(np.float32)
    freqs = (10000.0 ** (-np.arange(64, dtype=np.float32) / 64.0)).astype(np.float32)
    ang = np.concatenate([r[:, None] * freqs[None, :], col[:, None] * freqs[None, :]], -1)
    return np.cos(ang).T.astype(np.float32), np.sin(ang).T.astype(np.float32)


def _fm(v, nt):
    return np.ascontiguousarray(np.asarray(v, np.float32).reshape(nt, 128).T)


def make_in_maps(inp):
    f = lambda k: np.ascontiguousarray(np.asarray(inp[k], dtype=np.float32))
    xp, xs = f("x_prompt"), f("x_sample")
    c, c_ctx = f("c"), f("c_ctx")
    sr, sd = f("state_ret"), f("state_delta")
    norm_w, fnw, mod_b = f("norm_w"), f("final_norm_w"), f("mod_b")
    normw = np.concatenate([_fm(norm_w[l], 8) for l in range(4)] + [_fm(fnw, 8)], axis=1)
    modb = np.concatenate([_fm(mod_b[l][p * 1024:(p + 1) * 1024], 8) for l in range(4) for p in range(3)], axis=1)
    rdecb = np.ascontiguousarray(np.broadcast_to(f("ret_decay").reshape(1, 16), (128, 16)))
    gnwb = np.ascontiguousarray(np.broadcast_to(f("ret_gn_w").reshape(1, 4096), (128, 4096)))
    dnwb = np.ascontiguousarray(np.broadcast_to(f("del_norm_w").reshape(1, 4096), (128, 4096)))
    cw = f("del_conv_w")
    convw = np.concatenate([_fm(cw[jj, k], 32) for jj in range(2) for k in range(3)], axis=1)
    alogb = np.ascontiguousarray(np.broadcast_to(f("del_a_log").reshape(1, 32), (128, 32)))
    dtbb = np.ascontiguousarray(np.broadcast_to(f("del_dt_bias").reshape(1, 32), (128, 32)))
    cosS, sinS = _rope_tables()
    ii = np.arange(128)
    same = lambda b: (ii[:, None] // b) == (ii[None, :] // b)
    bm = [same(16), same(32) & ~same(16), same(64) & ~same(32), ~same(64)]
    bmask = np.concatenate([m.astype(np.float32) for m in bm], axis=1)
    shared = dict(bmask=bmask, normw=normw, modb=modb, mod_w=f("mod_w"), ret_w_in=f("ret_w_in"), ret_w_out=f("ret_w_out"),
                  del_w_in=f("del_w_in"), del_w_out=f("del_w_out"), rdecb=rdecb, gnwb=gnwb, dnwb=dnwb, convw=convw,
                  alogb=alogb, dtbb=dtbb)
    maps = []
    for core in range(8):
        m = dict(shared)
        chain = np.zeros((128, 8), np.float32)
        cosT = np.ones((128, NTOK), np.float32)
        sinT = np.zeros((128, NTOK), np.float32)
        if core < 6:
            x = xp[core * 5:(core + 1) * 5].reshape(NTOK, D)
            conds = [c_ctx] * 5
            s0r = np.zeros((2, 2, 4, 256, 512), np.float32)
            s0d = np.zeros((2, 2, 8, 128, 256), np.float32)
        else:
            b = core - 6
            x = np.concatenate([xs[b], xp[30 + b]], axis=0)
            conds = [c[b]] * 4 + [c_ctx]
            chain[:, 1:4] = 1.0
            s0r, s0d = sr[b], sd[b]
            cosT[:, 0:1024] = cosS
            sinT[:, 0:1024] = sinS
        condT = np.zeros((128, 40), np.float32)
        for s in range(5):
            condT[:, s::5] = _fm(conds[s], 8)
        m.update(x=np.ascontiguousarray(x), condT=condT, chainb=chain, s0r=np.ascontiguousarray(s0r),
                 s0d=np.ascontiguousarray(s0d), cosT=cosT, sinT=sinT)
        maps.append(m)
    return maps


def assemble(results):
    y_prompt = np.zeros((32, 256, D), np.float32)
    y_sample = np.zeros((2, 1024, D), np.float32)
    nsr = np.zeros((32, 2, 2, 4, 256, 512), np.float32)
    nsd = np.zeros((32, 2, 2, 8, 128, 256), np.float32)
    for core in range(8):
        r = results[core]
        y = r["y"].reshape(5, 256, D)
        if core < 6:
            y_prompt[core * 5:(core + 1) * 5] = y
            nsr[core * 5:(core + 1) * 5] = r["nsr"]
            nsd[core * 5:(core + 1) * 5] = r["nsd"]
        else:
            b = core - 6
            y_sample[b] = y[0:4].reshape(1024, D)
            y_prompt[30 + b] = y[4]
            nsr[30 + b] = r["nsr"][4]
            nsd[30 + b] = r["nsd"][4]
    return y_prompt, y_sample, nsr, nsd


_NC_CACHE = {}


def kernel(**inputs):
    if "nc" not in _NC_CACHE:
        _NC_CACHE["nc"] = build()
    maps = make_in_maps(inputs)
    res = run_bass_kernel_spmd(_NC_CACHE["nc"], maps, core_ids=list(range(8)))
    return assemble(res.results)
```

```python
import contextlib
import numpy as np
import concourse.bass as bass
import concourse.mybir as mybir
from concourse.bass_utils import run_bass_kernel_spmd

F32 = mybir.dt.float32
BF16 = mybir.dt.bfloat16
AF = mybir.ActivationFunctionType
ALU = mybir.AluOpType

D = 1024
NSEG = 5
SEGL = 256
NTOK = NSEG * SEGL
NT = NTOK // 128
DEPTH = 4
EPS = 1e-6
BLOCKS = [(0, 512), (512, 512), (1024, 256)]
import os
EPOCH = int(os.environ.get("K_EPOCH", "4000"))
BIG = 30000.0


class Buf:
    __slots__ = ("name", "lw", "rd", "dsem", "dcnt")

    def __init__(self, name):
        self.name = name
        self.lw = None
        self.rd = {}
        self.dsem = None
        self.dcnt = 0


class Sched:
    def __init__(self, nc, stack):
        self.nc = nc
        self.stack = stack
        self.engs = {"pe": nc.tensor, "act": nc.scalar, "dve": nc.vector, "pool": nc.gpsimd, "sp": nc.sync}
        self.cnt = {e: 0 for e in self.engs}
        self.esems = {e: [] for e in self.engs}
        self.known = {e: {} for e in self.engs}
        self.dsems = {}
        self.dma_bufs = []
        self.nsem = 0
        self.ninstr = 0
        self.pe_pending = None

    def _newsem(self, name):
        self.nsem += 1
        return self.stack.enter_context(self.nc.semaphore(f"{name}_{self.nsem}"))

    def _esem(self, e, epoch):
        lst = self.esems[e]
        while len(lst) <= epoch:
            lst.append(self._newsem(f"s_{e}_{len(lst)}"))
        return lst[epoch]

    def _need(self, E, ev, raw):
        key, val = ev
        if key == ("e", E) and not raw:
            return
        kn = self.known[E]
        if kn.get(key, 0) >= val:
            return
        kn[key] = val
        eng = self.engs[E]
        if key[0] == "e":
            n = val - 1
            eng.wait_ge(self._esem(key[1], n // EPOCH), n % EPOCH + 1)
        else:
            eng.wait_ge(self.dsems[key], val)
        self.ninstr += 1

    def _deps(self, E, r, w):
        for b in r:
            if b.lw is not None:
                self._need(E, b.lw, True)
        for b in w:
            if b.lw is not None:
                self._need(E, b.lw, False)
            for k, v in b.rd.items():
                self._need(E, (k, v), False)

    def _post(self, ev, r, w):
        k, v = ev
        for b in r:
            if b.rd.get(k, 0) < v:
                b.rd[k] = v
        for b in w:
            b.lw = ev
            b.rd = {}

    def _commit_pe(self):
        if self.pe_pending is None:
            return
        ins, _ = self.pe_pending
        n = self.cnt["pe"]
        ins.then_inc(self._esem("pe", n // EPOCH), 1)
        self.cnt["pe"] = n + 1
        self.pe_pending = None

    def _touch(self, E, r, w):
        if self.pe_pending is None:
            return
        pw = self.pe_pending[1]
        if E == "pe":
            if tuple(id(b) for b in w) != pw:
                self._commit_pe()
        elif any(id(b) in pw for b in r) or any(id(b) in pw for b in w):
            self._commit_pe()

    def op(self, E, fn, r=(), w=()):
        self._touch(E, r, w)
        self._deps(E, r, w)
        ins = fn(self.engs[E])
        if E == "pe":
            self.pe_pending = (ins, tuple(id(b) for b in w))
            self.ninstr += 1
            self._post((("e", "pe"), self.cnt["pe"] + 1), r, w)
            return ins
        n = self.cnt[E]
        ins.then_inc(self._esem(E, n // EPOCH), 1)
        self.cnt[E] = n + 1
        self.ninstr += 1
        self._post((("e", E), n + 1), r, w)
        return ins

    def dma(self, Q, out, in_, r=(), w=(), **kw):
        self._touch(Q, r, w)
        b0 = (list(w) + list(r))[0]
        own = ("d", id(b0))
        for b in r:
            if b.lw is not None:
                self._need(Q, b.lw, True)
        for b in w:
            if b.lw is not None and not (b.lw[0] == own and not b.rd):
                self._need(Q, b.lw, False)
            for k, v in b.rd.items():
                self._need(Q, (k, v), False)
        if b0.dsem is None:
            b0.dsem = self._newsem("d_" + b0.name)
            self.dsems[("d", id(b0))] = b0.dsem
            self.dma_bufs.append(b0)
        ins = self.engs[Q].dma_start(out=out, in_=in_, **kw)
        ins.then_inc(b0.dsem, 16)
        b0.dcnt += 16
        self.ninstr += 1
        self._post((("d", id(b0)), b0.dcnt), r, w)

    def barrier(self):
        self._commit_pe()
        for E in self.engs:
            for e2 in self.engs:
                if self.cnt[e2] > 0:
                    self._need(E, (("e", e2), self.cnt[e2]), True)
            for b in self.dma_bufs:
                self._need(E, (("d", id(b)), b.dcnt), True)


def build(depth=DEPTH, dbg=None):
    nc = bass.Bass("TRN2", target_bir_lowering=False)
    dbg = dbg or {}

    def din(name, shape):
        return nc.dram_tensor(name, list(shape), F32, kind="ExternalInput").ap()

    def dout(name, shape):
        return nc.dram_tensor(name, list(shape), F32, kind="ExternalOutput").ap()

    x_d = din("x", [NTOK, D])
    cond_d = din("condT", [128, 40])
    chain_d = din("chainb", [128, 8])
    normw_d = din("normw", [128, 40])
    modb_d = din("modb", [128, 96])
    modw_d = din("mod_w", [4, D, 3 * D])
    rwin_d = din("ret_w_in", [2, D, 6144])
    rwout_d = din("ret_w_out", [2, 2048, D])
    dwin_d = din("del_w_in", [2, D, 6176])
    dwout_d = din("del_w_out", [2, 2048, D])
    rdec_d = din("rdecb", [128, 16])
    gnw_d = din("gnwb", [128, 4096])
    dnw_d = din("dnwb", [128, 4096])
    convw_d = din("convw", [128, 192])
    alog_d = din("alogb", [128, 32])
    dtb_d = din("dtbb", [128, 32])
    s0r_d = din("s0r", [2, 2, 4, 256, 512])
    s0d_d = din("s0d", [2, 2, 8, 128, 256])
    cos_d = din("cosT", [128, NTOK])
    sin_d = din("sinT", [128, NTOK])
    bmask_d = din("bmask", [128, 512])
    y_d = dout("y", [NTOK, D])
    nsr_d = dout("nsr", [NSEG, 2, 2, 4, 256, 512])
    nsd_d = dout("nsd", [NSEG, 2, 2, 8, 128, 256])
    dbg_d = {k: dout("dbg_" + k, shp) for k, shp in dbg.items()}

    with contextlib.ExitStack() as stack:
        S = Sched(nc, stack)

        sb_n = [0]

        def sb(name, shape, dt=F32, st=None):
            sb_n[0] += 1
            return (st or stack).enter_context(nc.sbuf_tensor(f"sb{sb_n[0]}_{name}", list(shape), dt))

        banks = [stack.enter_context(nc.psum_tensor(f"bank{i}", [128, 512], F32)) for i in range(8)]
        bank_bufs = [Buf(f"bank{i}") for i in range(8)]
        bank_rr = [0]

        def psum():
            i = bank_rr[0] % 8
            bank_rr[0] += 1
            return banks[i], bank_bufs[i]

        xT = sb("xT", [128, 8, NTOK])
        xT_b = [Buf(f"xT{s}") for s in range(NSEG)]
        hT = sb("hT", [128, 8, NTOK], BF16)
        hT_b = [Buf(f"hT{s}") for s in range(NSEG)]
        NSLAB = 3
        ring = [sb(f"slab{i}", [128, 4096], BF16) for i in range(NSLAB)]
        ring_b = [Buf(f"slab{i}") for i in range(NSLAB)]
        ring_rr = [0]

        ident_f = sb("ident_f", [128, 128])
        ident_b = sb("ident_b", [128, 128], BF16)
        ones_f = sb("ones_f", [128, 128])
        ones_b = sb("ones_b", [128, 128], BF16)
        cst_b = Buf("consts")
        cpar = sb("cpar", [128, 8])
        prm = sb("prm", [128, 40 + 8 + 40 + 96 + 16 + 192 + 32 + 32])
        prm_b = Buf("prm")
        o_cond, o_chain, o_normw, o_modb, o_rdec, o_convw, o_alog, o_dtb = 0, 40, 48, 88, 184, 200, 392, 424
        modT = sb("modT", [128, 4 * 3 * 40])
        modT_b = Buf("modT")
        sc1 = sb("sc1", [128, 4 * 40])
        scT = sb("scT", [128, 8, NSEG], BF16)
        sqs = [sb("sq0", [128, 8, 256])]
        sqs_b = [Buf("sq0")]
        rstd = [sb("rstd0", [128, 256])]
        rstd_b = [Buf("rstd0")]
        stg = [sb(f"stg{i}", [128, 1024]) for i in range(2)]
        stg_b = [Buf(f"stg{i}") for i in range(2)]
        stg_rr = [0]

        def stage():
            i = stg_rr[0] % 2
            stg_rr[0] += 1
            return stg[i], stg_b[i]

        def chain_ap(s):
            return prm[:, o_chain + s:o_chain + s + 1]

        S.op("pool", lambda e: e.memset(ident_f[:], 1.0), w=[cst_b])
        S.op("pool", lambda e: e.affine_select(out=ident_f[:], in_=ident_f[:], pattern=[[-1, 128]],
                                                compare_op=ALU.is_equal, fill=0.0, base=0, channel_multiplier=1),
             r=[cst_b], w=[cst_b])
        S.op("pool", lambda e: e.tensor_copy(out=ident_b[:], in_=ident_f[:]), r=[cst_b], w=[cst_b])
        S.op("pool", lambda e: e.memset(ones_f[:], 1.0), w=[cst_b])
        S.op("pool", lambda e: e.memset(ones_b[:], 1.0), w=[cst_b])
        S.op("pool", lambda e: e.memset(cpar[:, 0:1], 1024.0 * EPS), w=[cst_b])
        S.op("pool", lambda e: e.memset(cpar[:, 1:2], EPS), w=[cst_b])
        S.op("pool", lambda e: e.memset(cpar[:, 2:3], 1.0), w=[cst_b])
        S.op("pool", lambda e: e.memset(cpar[:, 3:4], 0.0), w=[cst_b])
        S.op("pool", lambda e: e.memset(cpar[:, 4:5], -0.5), w=[cst_b])

        for off, n, src in ((o_cond, 40, cond_d), (o_chain, 8, chain_d), (o_normw, 40, normw_d), (o_modb, 96, modb_d),
                            (o_rdec, 16, rdec_d), (o_convw, 192, convw_d), (o_alog, 32, alog_d), (o_dtb, 32, dtb_d)):
            S.dma("sp", prm[:, off:off + n], src, w=[prm_b])

        def v8(t, n=512):
            return t[:, 0:8 * n].rearrange("p (kt c) -> p kt c", kt=8)

        def win_view(w2d):
            return w2d.rearrange("(kt p) c -> p kt c", p=128)

        wspecs = []
        for l in range(depth):
            for part in range(3):
                for hf in range(2):
                    c0 = part * 1024 + hf * 512
                    wspecs.append(lambda t, l=l, c0=c0: [(v8(t), win_view(modw_d[l])[:, :, c0:c0 + 512])])
        for l in range(depth):
            j = l // 2
            if l % 2 == 0:
                wv = win_view(rwin_d[j])
                for h in range(4):
                    wspecs.append(lambda t, h=h, wv=wv: [
                        (v8(t)[:, :, 0:256], wv[:, :, h * 256:(h + 1) * 256]),
                        (v8(t)[:, :, 256:512], wv[:, :, 1024 + h * 256:1024 + (h + 1) * 256])])
                    wspecs.append(lambda t, h=h, wv=wv: [(v8(t), wv[:, :, 2048 + h * 512:2048 + (h + 1) * 512])])
                    wspecs.append(lambda t, h=h, wv=wv: [(v8(t), wv[:, :, 4096 + h * 512:4096 + (h + 1) * 512])])
                    wspecs.append(lambda t, h=h, j=j: [
                        (t[:, :].rearrange("p (v c) -> p v c", v=4),
                         rwout_d[j][h * 512:(h + 1) * 512, :].rearrange("(v p) c -> p v c", p=128))])
            else:
                wv = win_view(dwin_d[j])
                wspecs.append(lambda t, wv=wv: [(v8(t, 32), wv[:, :, 6144:6176])])
                for h in range(8):
                    wspecs.append(lambda t, h=h, wv=wv: [
                        (v8(t)[:, :, 0:128], wv[:, :, h * 128:(h + 1) * 128]),
                        (v8(t)[:, :, 128:256], wv[:, :, 1024 + h * 128:1024 + (h + 1) * 128]),
                        (v8(t)[:, :, 256:512], wv[:, :, 2048 + h * 256:2048 + (h + 1) * 256])])
                    wspecs.append(lambda t, h=h, wv=wv: [(v8(t, 256), wv[:, :, 4096 + h * 256:4096 + (h + 1) * 256])])
                    wspecs.append(lambda t, h=h, j=j: [
                        (t[:, 0:2048].rearrange("p (v c) -> p v c", v=2),
                         dwout_d[j][h * 256:(h + 1) * 256, :].rearrange("(v p) c -> p v c", p=128))])
        ws = {"issued": 0, "released": 0, "got": 0}

        def ws_issue():
            while ws["issued"] < len(wspecs) and ws["issued"] < ws["released"] + NSLAB:
                i = ws["issued"]
                for (dst_view, src_ap) in wspecs[i](ring[i % NSLAB]):
                    S.dma("pool", dst_view, src_ap, w=[ring_b[i % NSLAB]])
                ws["issued"] += 1

        def ws_get():
            ws_issue()
            i = ws["got"]
            assert i < ws["issued"], "weight ring over-subscribed"
            ws["got"] += 1
            return ring[i % NSLAB], ring_b[i % NSLAB]

        def ws_release():
            ws["released"] += 1
            ws_issue()

        for tt in range(NT):
            st_t, st_b = stage()
            S.dma("sp", st_t[:, :], x_d[tt * 128:(tt + 1) * 128, :], w=[st_b])
            for half in range(2):
                bk, bb = psum()
                for q in range(4):
                    dt = half * 4 + q
                    S.op("pe", lambda e, bk=bk, q=q, dt=dt, st_t=st_t: e.transpose(
                        bk[:, q * 128:(q + 1) * 128], st_t[:, dt * 128:(dt + 1) * 128], ident_f[:]),
                        r=[st_b, cst_b], w=[bb])
                eng = "dve" if half == 0 else "act"
                dst = xT[:, half * 4:half * 4 + 4, tt * 128:(tt + 1) * 128]
                src = bk[:, :].rearrange("p (q t) -> p q t", q=4)
                if eng == "dve":
                    S.op("dve", lambda e, dst=dst, src=src: e.tensor_copy(out=dst, in_=src), r=[bb], w=[xT_b[tt // 2]])
                else:
                    S.op("act", lambda e, dst=dst, src=src: e.copy(out=dst, in_=src), r=[bb], w=[xT_b[tt // 2]])

        S.op("act", lambda e: e.activation(out=scT[:].rearrange("p a b -> p (a b)"), in_=prm[:, o_cond:o_cond + 40],
                                           func=AF.Silu), r=[prm_b], w=[modT_b])
        for l in range(depth):
            for part in range(3):
                for hf in range(2):
                    c0 = part * 1024 + hf * 512
                    sl, slb = ws_get()
                    slv = v8(sl)
                    bk, bb = psum()
                    for ct in range(4):
                        for kt in range(8):
                            S.op("pe", lambda e, bk=bk, ct=ct, kt=kt, slv=slv: e.matmul(
                                bk[:, ct * 5:ct * 5 + 5], lhsT=slv[:, kt, ct * 128:(ct + 1) * 128], rhs=scT[:, kt, :],
                                start=(kt == 0), stop=(kt == 7)), r=[slb, modT_b], w=[bb])
                    base = (l * 3 + part) * 40 + hf * 20
                    mb = prm[:, o_modb + (l * 3 + part) * 8 + hf * 4: o_modb + (l * 3 + part) * 8 + hf * 4 + 4]
                    S.op("dve", lambda e, bk=bk, base=base, mb=mb: e.tensor_tensor(
                        out=modT[:, base:base + 20].rearrange("p (a b) -> p a b", a=4),
                        in0=bk[:, 0:20].rearrange("p (a b) -> p a b", a=4),
                        in1=mb.unsqueeze(2).to_broadcast([128, 4, 5]), op=ALU.add), r=[bb, prm_b], w=[modT_b])
                    ws_release()
            sv = modT[:, (l * 3 + 1) * 40:(l * 3 + 1) * 40 + 40]
            S.op("dve", lambda e, l=l, sv=sv: e.tensor_scalar(out=sc1[:, l * 40:(l + 1) * 40], in0=sv, scalar1=1.0,
                                                             scalar2=32.0, op0=ALU.add, op1=ALU.mult),
                 r=[modT_b], w=[modT_b])
            S.op("dve", lambda e, l=l: e.tensor_tensor(
                out=sc1[:, l * 40:(l + 1) * 40].rearrange("p (a b) -> p a b", a=8),
                in0=sc1[:, l * 40:(l + 1) * 40].rearrange("p (a b) -> p a b", a=8),
                in1=prm[:, o_normw + l * 8:o_normw + l * 8 + 8].unsqueeze(2).to_broadcast([128, 8, 5]), op=ALU.mult),
                r=[modT_b, prm_b], w=[modT_b])

        def mod_ap(l, part, dt, s):
            o = (l * 3 + part) * 40 + dt * 5 + s
            return modT[:, o:o + 1]

        nrm_rr = [0]

        def rms_stat(s):
            i = 0
            sq, sqb, rs, rsb = sqs[i], sqs_b[i], rstd[i], rstd_b[i]
            seg = slice(s * SEGL, (s + 1) * SEGL)
            S.op("act", lambda e: e.activation(out=sq[:, :, :], in_=xT[:, :, seg], func=AF.Square), r=[xT_b[s]], w=[sqb])
            bk, bb = psum()
            for dt in range(8):
                S.op("pe", lambda e, dt=dt: e.matmul(bk[:, 0:256], lhsT=ones_f[:], rhs=sq[:, dt, :],
                                                      start=(dt == 0), stop=(dt == 7)), r=[sqb, cst_b], w=[bb])
            S.op("act", lambda e: e.activation(out=rs[:, :], in_=bk[:, 0:256], func=AF.Ln, bias=cpar[:, 0:1], scale=1.0),
                 r=[bb, cst_b], w=[rsb])
            S.op("act", lambda e: e.activation(out=rs[:, :], in_=rs[:, :], func=AF.Exp, scale=-0.5), r=[rsb], w=[rsb])
            return sq, sqb, rs, rsb

        def norm_mod(l):
            for s in range(NSEG):
                sq, sqb, rs, rsb = rms_stat(s)
                seg = slice(s * SEGL, (s + 1) * SEGL)
                S.op("dve", lambda e: e.tensor_tensor(out=sq[:, :, :], in0=xT[:, :, seg],
                                                      in1=rs[:, :].unsqueeze(1).to_broadcast([128, 8, 256]),
                                                      op=ALU.mult), r=[xT_b[s], rsb], w=[sqb])
                for dt in range(8):
                    o = l * 40 + dt * 5 + s
                    S.op("act", lambda e, dt=dt, o=o: e.activation(
                        out=hT[:, dt, seg], in_=sq[:, dt, :], func=AF.Identity, bias=mod_ap(l, 0, dt, s),
                        scale=sc1[:, o:o + 1]), r=[sqb, modT_b], w=[hT_b[s]])

        def segs_of(t0, n):
            return list(range(t0 // SEGL, (t0 + n) // SEGL))

        def out_proj(l, sO, sOb, nvt, oT, oT_b):
            sOv = sO[:, 0:nvt * 1024].rearrange("p (v c) -> p v c", v=nvt)
            for (t0, n) in BLOCKS:
                sg = segs_of(t0, n)
                for dt in range(8):
                    bk, bb = psum()
                    for vt in range(nvt):
                        S.op("pe", lambda e, vt=vt, dt=dt, bk=bk: e.matmul(
                            bk[:, 0:n], lhsT=sOv[:, vt, dt * 128:(dt + 1) * 128], rhs=oT[:, vt, t0:t0 + n],
                            start=(vt == 0), stop=(vt == nvt - 1)), r=[sOb] + [oT_b[s] for s in sg], w=[bb])
                    for s in sg:
                        lo = s * SEGL - t0
                        seg = slice(s * SEGL, (s + 1) * SEGL)
                        S.op("dve", lambda e, bk=bk, lo=lo, seg=seg, dt=dt, s=s: e.scalar_tensor_tensor(
                            out=xT[:, dt, seg], in0=bk[:, lo:lo + SEGL], scalar=mod_ap(l, 2, dt, s), in1=xT[:, dt, seg],
                            op0=ALU.mult, op1=ALU.add), r=[bb, modT_b, xT_b[s]], w=[xT_b[s]])

        def ret_layer(l, j):
            with contextlib.ExitStack() as ls:
                cosT = sb("cosT", [128, NTOK], st=ls)
                sinT = sb("sinT", [128, NTOK], st=ls)
                rope_b = Buf("rope")
                S.dma("sp", cosT[:, :], cos_d, w=[rope_b])
                S.dma("sp", sinT[:, :], sin_d, w=[rope_b])
                lc = sb("lc", [128, 64], st=ls)
                lc_b = Buf("lc")
                iot = sb("iot", [128, 128], st=ls)
                iop = sb("iop", [128, 2], st=ls)
                ioi = sb("ioi", [128, 2, 128], st=ls)
                Mh = sb("Mh", [128, 4, 128], st=ls)
                Xi = sb("Xi", [128, 8, 128], BF16, st=ls)
                tmpm = sb("tmpm", [128, 2, 128], st=ls)
                gnw = sb("gnw", [128, 512], st=ls)
                gnw_b = Buf("gnw")
                qT = sb("qT", [128, 2, NTOK], BF16, st=ls)
                kT = sb("kT", [128, 2, NTOK], BF16, st=ls)
                qk_b = [Buf(f"qk{s}") for s in range(NSEG)]
                sq, sqb = sqs[0], sqs_b[0]
                kzf = sb("kzf", [128, NT, 256], BF16, st=ls)
                kzb = sb("kzb", [128, NT, 256], BF16, st=ls)
                kz_b = [Buf(f"kz{c}") for c in range(NT)]
                v16 = sb("v16", [128, NT, 512], BF16, st=ls)
                v_b = [Buf(f"v{c}") for c in range(NT)]
                sm = sb("sm", [128, NT, 128], BF16, st=ls)
                sm_b = [Buf(f"sm{c}") for c in range(NT)]
                qx = [sb(f"qx{i}", [128, 2, 2, 128], BF16, st=ls) for i in range(2)]
                qx_b = [Buf(f"qx{i}") for i in range(2)]
                zs = [sb(f"zs{i}", [128, 512], st=ls) for i in range(2)]
                zs_b = [Buf(f"zs{i}") for i in range(2)]
                Sb16 = sb("Sb16", [128, NT, 2, 512], BF16, st=ls)
                Sb16_b = [Buf(f"Sb16_{c}") for c in range(NT)]
                S32 = [sb(f"S32_{d}", [128, 2, 512], st=ls) for d in range(2)]
                S32_b = [Buf(f"S32_{d}") for d in range(2)]
                S32alt = [None, sb("S32_1b", [128, 2, 512], st=ls)]
                S32alt_b = [None, Buf("S32_1b")]
                S16f = sb("S16f", [128, 2, 512], BF16, st=ls)
                S16f_b = Buf("S16f")
                oT = [sb(f"oT{i}", [128, 4, SEGL], BF16, st=ls) for i in range(2)]
                oT_b = [Buf(f"oT{i}") for i in range(2)]
                bst = sb("bst", [128, 16], st=ls)
                bst_b = Buf("bst")
                on = sb("on", [128, 512], st=ls)
                on_b = Buf("on")
                og = sb("og", [128, 512], BF16, st=ls)
                og_b = Buf("og")

                dec = prm[:, o_rdec + j * 8:o_rdec + j * 8 + 8]
                LG, GC, ZF = lc[:, 0:8], lc[:, 8:16], lc[:, 16:24]
                S.op("act", lambda e: e.activation(out=lc[:, 24:32], in_=dec, func=AF.Exp, scale=-1.0), r=[prm_b], w=[lc_b])
                S.op("act", lambda e: e.activation(out=LG, in_=lc[:, 24:32], func=AF.Ln, bias=cpar[:, 2:3], scale=1.0),
                     r=[lc_b, cst_b], w=[lc_b])
                S.op("dve", lambda e: e.tensor_scalar(out=LG, in0=LG, scalar1=-1.0, scalar2=None, op0=ALU.mult),
                     r=[lc_b], w=[lc_b])
                S.op("act", lambda e: e.activation(out=GC, in_=LG, func=AF.Exp, scale=128.0), r=[lc_b], w=[lc_b])
                S.op("pool", lambda e: e.iota(iot[:, :], pattern=[[1, 128]], base=0, channel_multiplier=-1,
                                               allow_small_or_imprecise_dtypes=True), r=[lc_b], w=[lc_b])
                S.op("pool", lambda e: e.iota(iop[:, 0:1], pattern=[[0, 1]], base=127, channel_multiplier=-1,
                                               allow_small_or_imprecise_dtypes=True), r=[lc_b], w=[lc_b])
                S.op("pool", lambda e: e.iota(iop[:, 1:2], pattern=[[0, 1]], base=0, channel_multiplier=1,
                                               allow_small_or_imprecise_dtypes=True), r=[lc_b], w=[lc_b])
                S.op("pool", lambda e: e.iota(ioi[:, 0, :], pattern=[[1, 128]], base=1, channel_multiplier=0,
                                               allow_small_or_imprecise_dtypes=True), r=[lc_b], w=[lc_b])
                S.op("pool", lambda e: e.iota(ioi[:, 1, :], pattern=[[-1, 128]], base=128, channel_multiplier=0,
                                               allow_small_or_imprecise_dtypes=True), r=[lc_b], w=[lc_b])
                for d in range(2):
                    for h in range(4):
                        c = d * 4 + h
                        S.op("act", lambda e, c=c, d=d: e.activation(out=ZF[:, c:c + 1], in_=iop[:, d:d + 1], func=AF.Exp,
                                                                      scale=LG[:, c:c + 1]), r=[lc_b], w=[lc_b])
                S.op("dve", lambda e: e.tensor_scalar(out=ZF, in0=ZF, scalar1=1.0 / 16.0, scalar2=None, op0=ALU.mult),
                     r=[lc_b], w=[lc_b])
                for h in range(4):
                    S.op("dve", lambda e: e.tensor_scalar(out=tmpm[:, 0, :], in0=iot[:, :], scalar1=0.0, scalar2=None,
                                                          op0=ALU.max), r=[lc_b], w=[lc_b])
                    S.op("act", lambda e, h=h: e.activation(out=tmpm[:, 0, :], in_=tmpm[:, 0, :], func=AF.Exp,
                                                             scale=LG[:, h:h + 1]), r=[lc_b], w=[lc_b])
                    S.op("pool", lambda e: e.affine_select(out=tmpm[:, 0, :], in_=tmpm[:, 0, :], pattern=[[1, 128]],
                                                            compare_op=ALU.is_ge, fill=0.0, base=0, channel_multiplier=-1),
                         r=[lc_b], w=[lc_b])
                    S.op("dve", lambda e: e.tensor_scalar(out=tmpm[:, 1, :], in0=iot[:, :], scalar1=-1.0, scalar2=0.0,
                                                          op0=ALU.mult, op1=ALU.max), r=[lc_b], w=[lc_b])
                    S.op("act", lambda e, h=h: e.activation(out=tmpm[:, 1, :], in_=tmpm[:, 1, :], func=AF.Exp,
                                                             scale=LG[:, 4 + h:5 + h]), r=[lc_b], w=[lc_b])
                    S.op("pool", lambda e: e.affine_select(out=tmpm[:, 1, :], in_=tmpm[:, 1, :], pattern=[[-1, 128]],
                                                            compare_op=ALU.is_ge, fill=0.0, base=0, channel_multiplier=1),
                         r=[lc_b], w=[lc_b])
                    S.op("dve", lambda e, h=h: e.tensor_tensor(out=Mh[:, h, :], in0=tmpm[:, 0, :], in1=tmpm[:, 1, :],
                                                               op=ALU.add), r=[lc_b], w=[lc_b])
                    S.op("dve", lambda e, h=h: e.tensor_scalar(out=Mh[:, h, :], in0=Mh[:, h, :], scalar1=1.0 / 16.0,
                                                               scalar2=None, op0=ALU.mult), r=[lc_b], w=[lc_b])
                    for d in range(2):
                        S.op("act", lambda e, h=h, d=d: e.activation(out=Xi[:, d * 4 + h, :], in_=ioi[:, d, :], func=AF.Exp,
                                                                      scale=LG[:, d * 4 + h:d * 4 + h + 1]),
                             r=[lc_b], w=[lc_b])

                norm_mod(l)

                for h in range(4):
                    sQK, sQKb = ws_get()
                    sV, sVb = ws_get()
                    sQKv, sVv = v8(sQK), v8(sV)
                    S.dma("sp", gnw[:, :], gnw_d[:, j * 2048 + h * 512:j * 2048 + (h + 1) * 512], w=[gnw_b])

                    for s in range(NSEG):
                        t0, n = s * SEGL, SEGL
                        for wi, dst in ((0, qT), (1, kT)):
                            pss = []
                            for half in range(2):
                                bk, bb = psum()
                                c0 = wi * 256 + half * 128
                                for kt in range(8):
                                    S.op("pe", lambda e, bk=bk, kt=kt, c0=c0: e.matmul(
                                        bk[:, 0:n], lhsT=sQKv[:, kt, c0:c0 + 128], rhs=hT[:, kt, t0:t0 + n],
                                        start=(kt == 0), stop=(kt == 7)), r=[sQKb, hT_b[s]], w=[bb])
                                pss.append((bk, bb))
                            cs, sn = cosT[:, t0:t0 + n], sinT[:, t0:t0 + n]
                            for half in range(2):
                                S.op("act", lambda e, half=half: e.copy(out=sq[:, half, :], in_=pss[half][0][:, 0:n]),
                                     r=[pss[half][1]], w=[sqb])
                            S.op("dve", lambda e: e.tensor_tensor(out=sq[:, 2, :], in0=sq[:, 0, :], in1=cs, op=ALU.mult),
                                 r=[sqb, rope_b], w=[sqb])
                            S.op("dve", lambda e: e.tensor_tensor(out=sq[:, 3, :], in0=sq[:, 1, :], in1=sn, op=ALU.mult),
                                 r=[sqb, rope_b], w=[sqb])
                            S.op("dve", lambda e: e.tensor_tensor(out=sq[:, 4, :], in0=sq[:, 0, :], in1=sn, op=ALU.mult),
                                 r=[sqb, rope_b], w=[sqb])
                            S.op("dve", lambda e: e.tensor_tensor(out=sq[:, 5, :], in0=sq[:, 1, :], in1=cs, op=ALU.mult),
                                 r=[sqb, rope_b], w=[sqb])
                            S.op("dve", lambda e, dst=dst: e.tensor_tensor(out=dst[:, 0, t0:t0 + n], in0=sq[:, 2, :],
                                                                          in1=sq[:, 3, :], op=ALU.subtract),
                                 r=[sqb], w=[qk_b[s]])
                            S.op("dve", lambda e, dst=dst: e.tensor_tensor(out=dst[:, 1, t0:t0 + n], in0=sq[:, 4, :],
                                                                          in1=sq[:, 5, :], op=ALU.add),
                                 r=[sqb], w=[qk_b[s]])
                    ws_release()
                    for c in range(NT):
                        tk = slice(c * 128, (c + 1) * 128)
                        bk, bb = psum()
                        for kt in range(8):
                            S.op("pe", lambda e, bk=bk, kt=kt: e.matmul(bk[:, :], lhsT=hT[:, kt, tk], rhs=sVv[:, kt, :],
                                                                         start=(kt == 0), stop=(kt == 7)),
                                 r=[sVb, hT_b[c // 2]], w=[bb])
                        S.op("act", lambda e, bk=bk, c=c: e.copy(out=v16[:, c, :], in_=bk[:, :]), r=[bb], w=[v_b[c]])
                    ws_release()
                    sZ, sZb = ws_get()
                    sO, sOb = ws_get()
                    sZv = v8(sZ)
                    sOv = sO[:, :].rearrange("p (v c) -> p v c", v=4)
                    for c in range(NT):
                        tk = slice(c * 128, (c + 1) * 128)
                        bk, bb = psum()
                        bkb = bk[:, :].bitcast(BF16)
                        for dt in range(2):
                            S.op("pe", lambda e, dt=dt, bkb=bkb: e.transpose(bkb[:, dt * 128:(dt + 1) * 128], kT[:, dt, tk],
                                                                             ident_b[:]),
                                 r=[qk_b[c // 2], cst_b], w=[bb])
                        S.op("dve", lambda e, bkb=bkb, c=c: e.tensor_scalar(out=kzf[:, c, :], in0=bkb[:, 0:256],
                                                                            scalar1=ZF[:, h:h + 1], scalar2=None,
                                                                            op0=ALU.mult), r=[bb, lc_b], w=[kz_b[c]])
                        S.op("act", lambda e, bkb=bkb, c=c: e.activation(out=kzb[:, c, :], in_=bkb[:, 0:256], func=AF.Copy,
                                                                         scale=ZF[:, 4 + h:5 + h]), r=[bb, lc_b], w=[kz_b[c]])
                        bk, bb = psum()
                        for dt in range(2):
                            S.op("pe", lambda e, dt=dt, bk=bk: e.matmul(bk[:, 0:128], lhsT=kT[:, dt, tk], rhs=qT[:, dt, tk],
                                                                         start=(dt == 0), stop=(dt == 1)),
                                 r=[qk_b[c // 2]], w=[bb])
                        S.op("dve", lambda e, bk=bk, c=c: e.tensor_tensor(out=sm[:, c, :], in0=bk[:, 0:128], in1=Mh[:, h, :],
                                                                          op=ALU.mult), r=[bb, lc_b], w=[sm_b[c]])

                    def upd_state(d, c, kz):
                        pss = []
                        for dt in range(2):
                            bk, bb = psum()
                            S.op("pe", lambda e, bk=bk, dt=dt: e.matmul(bk[:, :], lhsT=kz[:, c, dt * 128:(dt + 1) * 128],
                                                                         rhs=v16[:, c, :], start=True, stop=True),
                                 r=[kz_b[c], v_b[c]], w=[bb])
                            pss.append((bk, bb))
                        if S32alt[d] is None:
                            dst, dstb = S32[d], S32_b[d]
                        else:
                            dst, dstb = S32alt[d], S32alt_b[d]
                        for dt in range(2):
                            bk, bb = pss[dt]
                            S.op("dve", lambda e, bk=bk, dt=dt: e.scalar_tensor_tensor(
                                out=dst[:, dt, :], in0=S32[d][:, dt, :], scalar=GC[:, d * 4 + h:d * 4 + h + 1],
                                in1=bk[:, :], op0=ALU.mult, op1=ALU.add), r=[bb, lc_b, S32_b[d]], w=[dstb])
                        if S32alt[d] is not None:
                            S32[d], S32alt[d] = S32alt[d], S32[d]
                            S32_b[d], S32alt_b[d] = S32alt_b[d], S32_b[d]

                    def out_state(d, s):
                        st_t, st_bf = stage()
                        S.op("act", lambda e: e.copy(out=st_t[:, :], in_=S32[d][:, :, :].rearrange("p a b -> p (a b)")),
                             r=[S32_b[d]], w=[st_bf])
                        S.dma("sp", nsr_d[s, j, d, h].rearrange("(a p) v -> p a v", p=128),
                              st_t[:, :].rearrange("p (a v) -> p a v", a=2), r=[st_bf])

                    S.op("dve", lambda e: e.memset(S32[1][:, :, :], 0.0), w=[S32_b[1]])
                    for c in range(NT - 1, -1, -1):
                        s = c // 2
                        S.op("act", lambda e, c=c: e.copy(out=Sb16[:, c, :, :], in_=S32[1][:, :, :]),
                             r=[S32_b[1]], w=[Sb16_b[c]])
                        upd_state(1, c, kzb)
                        if c % 2 == 0:
                            out_state(1, s)
                            if c > 0:
                                if s - 1 == 3:
                                    st_t, st_bf = stage()
                                    S.dma("sp", st_t[:, :].rearrange("p (a v) -> p a v", a=2),
                                          s0r_d[j, 1, h].rearrange("(a p) v -> p a v", p=128), w=[st_bf])
                                    S.op("dve", lambda e, s=s, st_t=st_t: e.scalar_tensor_tensor(
                                        out=S32[1][:, :, :], in0=S32[1][:, :, :], scalar=chain_ap(s),
                                        in1=st_t[:, :].rearrange("p (a v) -> p a v", a=2),
                                        op0=ALU.mult, op1=ALU.add), r=[S32_b[1], prm_b, st_bf], w=[S32_b[1]])
                                else:
                                    S.op("dve", lambda e, s=s: e.tensor_scalar(
                                        out=S32[1][:, :, :], in0=S32[1][:, :, :], scalar1=chain_ap(s), scalar2=None,
                                        op0=ALU.mult), r=[S32_b[1], prm_b], w=[S32_b[1]])
                    S.dma("sp", S32[0][:, :, :], s0r_d[j, 0, h].rearrange("(a p) v -> p a v", p=128), w=[S32_b[0]])
                    S.op("act", lambda e: e.copy(out=S16f[:, :, :], in_=S32[0][:, :, :]), r=[S32_b[0]], w=[S16f_b])
                    for c in range(NT):
                        s = c // 2
                        tk = slice(c * 128, (c + 1) * 128)
                        qi = c % 2
                        for d in range(2):
                            S.op("dve", lambda e, d=d, qi=qi: e.tensor_tensor(
                                out=qx[qi][:, d, :, :], in0=qT[:, :, tk],
                                in1=Xi[:, d * 4 + h, :].unsqueeze(1).to_broadcast([128, 2, 128]), op=ALU.mult),
                                r=[qk_b[s], lc_b], w=[qx_b[qi]])
                        bkz, bbz = psum()
                        for kt in range(8):
                            S.op("pe", lambda e, kt=kt: e.matmul(bkz[:, :], lhsT=hT[:, kt, tk], rhs=sZv[:, kt, :],
                                                                  start=(kt == 0), stop=(kt == 7)), r=[sZb, hT_b[s]], w=[bbz])
                        S.op("act", lambda e, qi=qi: e.activation(out=zs[qi][:, :], in_=bkz[:, :], func=AF.Exp, scale=-1.0),
                             r=[bbz], w=[zs_b[qi]])
                        S.op("act", lambda e, qi=qi: e.activation(out=zs[qi][:, :], in_=zs[qi][:, :], func=AF.Ln, bias=cpar[:, 2:3],
                                                                  scale=1.0), r=[zs_b[qi], cst_b], w=[zs_b[qi]])
                        S.op("act", lambda e, qi=qi: e.activation(out=zs[qi][:, :], in_=zs[qi][:, :], func=AF.Exp, scale=-1.0),
                             r=[zs_b[qi]], w=[zs_b[qi]])
                        S.op("dve", lambda e, qi=qi: e.tensor_tensor(out=zs[qi][:, :], in0=bkz[:, :], in1=zs[qi][:, :], op=ALU.mult),
                             r=[bbz, zs_b[qi]], w=[zs_b[qi]])
                        bk, bb = psum()
                        S.op("pe", lambda e, bk=bk, c=c: e.matmul(bk[:, :], lhsT=sm[:, c, :], rhs=v16[:, c, :],
                                                                   start=True, stop=False), r=[sm_b[c], v_b[c]], w=[bb])
                        for dt in range(2):
                            S.op("pe", lambda e, bk=bk, dt=dt: e.matmul(bk[:, :], lhsT=qx[qi][:, 0, dt, :], rhs=S16f[:, dt, :],
                                                                         start=False, stop=False),
                                 r=[qx_b[qi], S16f_b], w=[bb])
                        for dt in range(2):
                            S.op("pe", lambda e, bk=bk, dt=dt, c=c: e.matmul(bk[:, :], lhsT=qx[qi][:, 1, dt, :],
                                                                              rhs=Sb16[:, c, dt, :], start=False,
                                                                              stop=(dt == 1)),
                                 r=[qx_b[qi], Sb16_b[c]], w=[bb])
                        S.op("dve", lambda e, bk=bk: e.bn_stats(out=bst[:, 0:6], in_=bk[:, :]), r=[bb], w=[bst_b])
                        S.op("dve", lambda e: e.bn_aggr(out=bst[:, 8:10], in_=bst[:, 0:6]), r=[bst_b], w=[bst_b])
                        S.op("act", lambda e: e.activation(out=bst[:, 10:11], in_=bst[:, 9:10], func=AF.Ln, bias=cpar[:, 1:2],
                                                           scale=1.0), r=[bst_b, cst_b], w=[bst_b])
                        S.op("act", lambda e: e.activation(out=bst[:, 10:11], in_=bst[:, 10:11], func=AF.Exp, scale=-0.5),
                             r=[bst_b], w=[bst_b])
                        S.op("dve", lambda e, bk=bk: e.tensor_scalar(out=on[:, :], in0=bk[:, :], scalar1=bst[:, 8:9],
                                                                     scalar2=bst[:, 10:11], op0=ALU.subtract,
                                                                     op1=ALU.mult), r=[bb, bst_b], w=[on_b])
                        S.op("dve", lambda e: e.tensor_tensor(out=on[:, :], in0=on[:, :], in1=gnw[:, :], op=ALU.mult),
                             r=[on_b, gnw_b], w=[on_b])
                        S.op("dve", lambda e, qi=qi: e.tensor_tensor(out=og[:, :], in0=on[:, :], in1=zs[qi][:, :], op=ALU.mult),
                             r=[on_b, zs_b[qi]], w=[og_b])
                        bk2, bb2 = psum()
                        bk2b = bk2[:, :].bitcast(BF16)
                        for vt in range(4):
                            S.op("pe", lambda e, vt=vt, bk2b=bk2b: e.transpose(
                                bk2b[:, vt * 128:(vt + 1) * 128], og[:, vt * 128:(vt + 1) * 128], ident_b[:]),
                                r=[og_b, cst_b], w=[bb2])
                        oi = s % 2
                        lo = (c % 2) * 128
                        S.op("act", lambda e, bk2b=bk2b, oi=oi, lo=lo: e.copy(
                            out=oT[oi][:, :, lo:lo + 128], in_=bk2b[:, 0:512].rearrange("p (v t) -> p v t", v=4)),
                            r=[bb2], w=[oT_b[oi]])
                        upd_state(0, c, kzf)
                        if c % 2 == 1:
                            out_state(0, s)
                            if c < NT - 1:
                                S.op("dve", lambda e, s=s: e.tensor_scalar(
                                    out=S32[0][:, :, :], in0=S32[0][:, :, :], scalar1=chain_ap(s + 1), scalar2=None,
                                    op0=ALU.mult), r=[S32_b[0], prm_b], w=[S32_b[0]])
                        if c < NT - 1:
                            S.op("act", lambda e: e.copy(out=S16f[:, :, :], in_=S32[0][:, :, :]), r=[S32_b[0]], w=[S16f_b])
                        if c % 2 == 1:
                            seg = slice(s * SEGL, (s + 1) * SEGL)
                            for dt in range(8):
                                bk, bb = psum()
                                for vt in range(4):
                                    S.op("pe", lambda e, vt=vt, dt=dt, bk=bk, oi=oi: e.matmul(
                                        bk[:, 0:SEGL], lhsT=sOv[:, vt, dt * 128:(dt + 1) * 128], rhs=oT[oi][:, vt, :],
                                        start=(vt == 0), stop=(vt == 3)), r=[sOb, oT_b[oi]], w=[bb])
                                S.op("dve", lambda e, bk=bk, dt=dt, s=s: e.scalar_tensor_tensor(
                                    out=xT[:, dt, seg], in0=bk[:, 0:SEGL], scalar=mod_ap(l, 2, dt, s), in1=xT[:, dt, seg],
                                    op0=ALU.mult, op1=ALU.add), r=[bb, modT_b, xT_b[s]], w=[xT_b[s]])
                    ws_release()
                    ws_release()
                S.barrier()

        def del_layer(l, j):
            with contextlib.ExitStack() as ls:
                DH = 8
                tri = [sb(f"tri{d}", [128, 128], st=ls) for d in range(2)]
                neg3 = [sb(f"neg3{d}", [128, 128], st=ls) for d in range(2)]
                pos1 = [sb(f"pos1{d}", [128, 128], st=ls) for d in range(2)]
                dc_b = Buf("dconst")
                for d in range(2):
                    S.op("pool", lambda e, d=d: e.memset(tri[d][:, :], 1.0), w=[dc_b])
                    pat, cm = ([[1, 128]], -1) if d == 0 else ([[-1, 128]], 1)
                    S.op("pool", lambda e, d=d, pat=pat, cm=cm: e.affine_select(
                        out=tri[d][:, :], in_=tri[d][:, :], pattern=pat, compare_op=ALU.is_ge, fill=0.0, base=0,
                        channel_multiplier=cm), r=[dc_b], w=[dc_b])
                    S.op("pool", lambda e, d=d: e.memset(neg3[d][:, :], 0.0), w=[dc_b])
                    S.op("pool", lambda e, d=d, pat=pat, cm=cm: e.affine_select(
                        out=neg3[d][:, :], in_=neg3[d][:, :], pattern=pat, compare_op=ALU.is_ge, fill=-BIG, base=0,
                        channel_multiplier=cm), r=[dc_b], w=[dc_b])
                    pat2, cm2 = ([[-1, 128]], 1) if d == 0 else ([[1, 128]], -1)
                    S.op("pool", lambda e, d=d: e.memset(pos1[d][:, :], 0.0), w=[dc_b])
                    S.op("pool", lambda e, d=d, pat2=pat2, cm2=cm2: e.affine_select(
                        out=pos1[d][:, :], in_=pos1[d][:, :], pattern=pat2, compare_op=ALU.is_gt, fill=BIG, base=0,
                        channel_multiplier=cm2), r=[dc_b], w=[dc_b])
                abraw = sb("abraw", [128, NT, 32], st=ls)
                tk_b = Buf("tokscal")
                tsc = sb("tsc", [128, 10, NT, 16], st=ls)
                U, L1, GG, LNB, BB, GT, EGt, NBEG, NEGG, GPL = [tsc[:, i, :, :] for i in range(10)]
                negA = sb("negA", [128, 16], st=ls)
                dnw = sb("dnw", [128, 256], st=ls)
                dnw_b = Buf("dnw")
                XP = [sb("XP0", [128, NSEG, 258], st=ls)] * 2
                XP_b = [Buf("XP0")] * 2
                acc = sqs[0][:, 0:NSEG, :]
                acc_b = sqs_b[0]
                tmb = sb("tmb", [128, NTOK], BF16, st=ls)
                tmb_b = Buf("tmb")
                rinv = sqs[0][:, 5:7, :].rearrange("p a b -> p (a b)")
                rinv_b = sqs_b[0]
                qT = sb("dqT", [128, NTOK], BF16, st=ls)
                kT = sb("dkT", [128, NTOK], BF16, st=ls)
                q_b, k_b = Buf("dq"), Buf("dk")
                bv = sb("bv", [128, NT * 2, 256], BF16, st=ls)
                bv_b = [Buf(f"bv{c}") for c in range(NT)]
                kd = sb("kd", [128, NT * 2, 128], BF16, st=ls)
                kd_b = [Buf(f"kd{c}") for c in range(NT)]
                TT = sb("TT", [128, NT * 2, 128], BF16, st=ls)
                TT_b = [Buf(f"TT{g}") for g in range(NSEG)]
                PTm = sb("PTm", [128, NT * 2, 128], BF16, st=ls)
                qg = sb("qg", [128, NT * 2, 128], BF16, st=ls)
                pq_b = [Buf(f"pq{c}") for c in range(NT)]
                csc = sb("csc", [128, 2, NT * 2], st=ls)
                csc_b = [Buf(f"csc{c}") for c in range(NT)]
                Es = [sb(f"Es{i}", [128, 3, 128], st=ls) for i in range(2)]
                Es_b = [Buf(f"Es{i}") for i in range(2)]
                def g4(nm):
                    return sb(nm, [128, 4, 128], BF16, st=ls), Buf(nm)
                Xg, Xg_b = zip(*[g4(f"Xg{i}") for i in range(2)])
                Yg, Yg_b = zip(*[g4(f"Yg{i}") for i in range(2)])
                Pg, Pg_b = zip(*[g4(f"Pg{i}") for i in range(2)])
                Qg, Qg_b = zip(*[g4(f"Qg{i}") for i in range(2)])
                id4, _ = g4("id4")
                Wg, Wg_b = zip(*[g4(f"Wg{i}") for i in range(2)])
                Af, Af_b = g4("Af")
                Bf, Bf_b = g4("Bf")
                bmask = sb("bmask", [128, 4, 128], BF16, st=ls)
                S.dma("pool", bmask[:, :, :].rearrange("p a b -> p (a b)"), bmask_d, w=[dc_b])
                bmaskn = sb("bmaskn", [128, 4, 4, 128], BF16, st=ls)
                for q_ in range(4):
                    S.op("dve", lambda e, q_=q_: e.tensor_scalar(out=bmaskn[:, :, q_, :], in0=bmask[:, :, :], scalar1=-1.0,
                                                                 scalar2=None, op0=ALU.mult), r=[dc_b], w=[dc_b])
                for q_ in range(4):
                    S.op("dve", lambda e, q_=q_: e.tensor_copy(out=id4[:, q_, :], in_=ident_b[:, :]), r=[cst_b, dc_b], w=[dc_b])
                ob = sb("ob", [128, NT, 256], st=ls)
                ob_b = [Buf(f"ob{c}") for c in range(NT)]
                S32 = [sb(f"dS32_{d}", [128, 256], st=ls) for d in range(2)]
                S32_b = [Buf(f"dS32_{d}") for d in range(2)]
                S16 = [sb(f"dS16_{d}", [128, 256], BF16, st=ls) for d in range(2)]
                S16_b = [Buf(f"dS16_{d}") for d in range(2)]
                rr = [sb(f"rr{d}", [128, 256], BF16, st=ls) for d in range(2)]
                rr_b = [Buf(f"rr{d}") for d in range(2)]
                vn16 = [sb(f"vn{d}", [128, 256], BF16, st=ls) for d in range(2)]
                vn_b = [Buf(f"vn{d}") for d in range(2)]
                zs = [sb(f"dzs{i}", [128, 256], st=ls) for i in range(2)]
                zs_b = [Buf(f"dzs{i}") for i in range(2)]
                on = [sb(f"don{i}", [128, 256], st=ls) for i in range(2)]
                on_b = [Buf(f"don{i}") for i in range(2)]
                og = [sb(f"dog{i}", [128, 256], BF16, st=ls) for i in range(2)]
                og_b = [Buf(f"dog{i}") for i in range(2)]
                oT = [sb(f"doT{i}", [128, 2, SEGL], BF16, st=ls) for i in range(NSEG)]
                oT_b = [Buf(f"doT{i}") for i in range(NSEG)]
                bst = sb("dbst", [128, 2, 8], st=ls)
                bst_b = [Buf("dbst0"), Buf("dbst1")]

                norm_mod(l)

                sAB, sABb = ws_get()
                sABv = v8(sAB, 32)
                for c in range(NT):
                    tk = slice(c * 128, (c + 1) * 128)
                    bk, bb = psum()
                    for kt in range(8):
                        S.op("pe", lambda e, bk=bk, kt=kt: e.matmul(bk[:, 0:32], lhsT=hT[:, kt, tk], rhs=sABv[:, kt, :],
                                                                     start=(kt == 0), stop=(kt == 7)),
                             r=[sABb, hT_b[c // 2]], w=[bb])
                    S.op("act", lambda e, bk=bk, c=c: e.copy(out=abraw[:, c, :], in_=bk[:, 0:32]), r=[bb], w=[tk_b])
                ws_release()
                al = prm[:, o_alog + j * 16:o_alog + j * 16 + 16]
                dtb = prm[:, o_dtb + j * 16:o_dtb + j * 16 + 16]
                T = [tk_b, prm_b, cst_b]
                S.op("act", lambda e: e.activation(out=negA[:, :], in_=al, func=AF.Exp), r=T, w=[tk_b])
                S.op("dve", lambda e: e.tensor_scalar(out=negA[:, :], in0=negA[:, :], scalar1=-1.0, scalar2=None,
                                                      op0=ALU.mult), r=T, w=[tk_b])
                S.op("dve", lambda e: e.tensor_tensor(out=U, in0=abraw[:, :, 0:16],
                                                      in1=dtb.unsqueeze(1).to_broadcast([128, NT, 16]), op=ALU.add),
                     r=T, w=[tk_b])
                S.op("dve", lambda e: e.tensor_scalar(out=L1, in0=U, scalar1=-1.0, scalar2=None, op0=ALU.mult), r=T, w=[tk_b])
                S.op("dve", lambda e: e.tensor_tensor(out=L1, in0=L1, in1=U, op=ALU.max), r=T, w=[tk_b])
                S.op("act", lambda e: e.activation(out=L1, in_=L1, func=AF.Exp, scale=-1.0), r=T, w=[tk_b])
                S.op("act", lambda e: e.activation(out=L1, in_=L1, func=AF.Ln, bias=cpar[:, 2:3], scale=1.0), r=T, w=[tk_b])
                S.op("dve", lambda e: e.tensor_scalar(out=U, in0=U, scalar1=0.0, scalar2=None, op0=ALU.max), r=T, w=[tk_b])
                S.op("dve", lambda e: e.tensor_tensor(out=U, in0=U, in1=L1, op=ALU.add), r=T, w=[tk_b])
                S.op("dve", lambda e: e.tensor_tensor(out=GG, in0=U, in1=negA[:, :].unsqueeze(1).to_broadcast([128, NT, 16]),
                                                      op=ALU.mult), r=T, w=[tk_b])
                S.op("dve", lambda e: e.tensor_scalar(out=L1, in0=abraw[:, :, 16:32], scalar1=-1.0, scalar2=None, op0=ALU.mult),
                     r=T, w=[tk_b])
                S.op("dve", lambda e: e.tensor_tensor(out=L1, in0=L1, in1=abraw[:, :, 16:32], op=ALU.max), r=T, w=[tk_b])
                S.op("act", lambda e: e.activation(out=L1, in_=L1, func=AF.Exp, scale=-1.0), r=T, w=[tk_b])
                S.op("act", lambda e: e.activation(out=L1, in_=L1, func=AF.Ln, bias=cpar[:, 2:3], scale=1.0), r=T, w=[tk_b])
                S.op("dve", lambda e: e.tensor_scalar(out=LNB, in0=abraw[:, :, 16:32], scalar1=0.0, scalar2=None,
                                                      op0=ALU.min), r=T, w=[tk_b])
                S.op("dve", lambda e: e.tensor_tensor(out=LNB, in0=LNB, in1=L1, op=ALU.subtract), r=T, w=[tk_b])
                S.op("act", lambda e: e.activation(out=BB, in_=LNB, func=AF.Exp), r=T, w=[tk_b])
                bk, bb = psum()
                for c in range(NT):
                    for d in range(2):
                        S.op("pe", lambda e, c=c, d=d: e.matmul(bk[:, c * 16 + d * 8:c * 16 + d * 8 + 8], lhsT=tri[d][:, :],
                                                                 rhs=GG[:, c, d * 8:d * 8 + 8], start=True, stop=True),
                             r=[tk_b, dc_b], w=[bb])
                S.op("dve", lambda e: e.tensor_copy(out=GT, in_=bk[:, 0:NT * 16].rearrange("p (c x) -> p c x", c=NT)),
                     r=[bb], w=[tk_b])
                S.op("act", lambda e: e.activation(out=EGt, in_=GT, func=AF.Exp), r=T, w=[tk_b])
                S.op("dve", lambda e: e.tensor_tensor(out=NBEG, in0=BB, in1=EGt, op=ALU.mult), r=T, w=[tk_b])
                S.op("dve", lambda e: e.tensor_scalar(out=NBEG, in0=NBEG, scalar1=-1.0, scalar2=None, op0=ALU.mult),
                     r=T, w=[tk_b])
                S.op("dve", lambda e: e.tensor_scalar(out=NEGG, in0=GT, scalar1=-1.0, scalar2=None, op0=ALU.mult),
                     r=T, w=[tk_b])
                S.op("dve", lambda e: e.tensor_tensor(out=GPL, in0=GT, in1=LNB, op=ALU.add), r=T, w=[tk_b])
                for i in range(2):
                    S.op("dve", lambda e, i=i: e.memset(XP[i][:, 0, 0:1], 0.0), w=[XP_b[i]])
                    S.op("dve", lambda e, i=i: e.memset(XP[i][:, NSEG - 1, 257:258], 0.0), w=[XP_b[i]])

                dstage = float(os.environ.get("K_DSTAGE", "9"))
                if dstage == 0:
                    S.barrier()
                    return
                xp_rr = [0]
                for h in range(int(os.environ.get("K_DHEADS", "8"))):
                    sW, sWb = ws_get()
                    sWv = v8(sW)
                    S.dma("sp", dnw[:, :], dnw_d[:, j * 2048 + h * 256:j * 2048 + (h + 1) * 256], w=[dnw_b])
                    for ct in range(4):
                        xi = xp_rr[0] % 2
                        xp_rr[0] += 1
                        xp, xpb = XP[xi], XP_b[xi]
                        for s in range(NSEG):
                            bk, bb = psum()
                            for kt in range(8):
                                S.op("pe", lambda e, bk=bk, kt=kt, s=s: e.matmul(
                                    bk[:, 0:SEGL], lhsT=sWv[:, kt, ct * 128:(ct + 1) * 128],
                                    rhs=hT[:, kt, s * SEGL:(s + 1) * SEGL], start=(kt == 0), stop=(kt == 7)),
                                    r=[sWb, hT_b[s]], w=[bb])
                            S.op("act", lambda e, bk=bk, s=s: e.copy(out=xp[:, s, 1:257], in_=bk[:, 0:SEGL]), r=[bb], w=[xpb])
                        chn = prm[:, o_chain + 1:o_chain + 5]
                        S.op("dve", lambda e: e.tensor_tensor(out=xp[:, 1:5, 0], in0=xp[:, 0:4, 256], in1=chn, op=ALU.mult),
                             r=[xpb, prm_b], w=[xpb])
                        S.op("dve", lambda e: e.tensor_tensor(out=xp[:, 0:4, 257], in0=xp[:, 1:5, 1], in1=chn, op=ALU.mult),
                             r=[xpb, prm_b], w=[xpb])
                        gct = h if ct == 0 else (8 + h if ct == 1 else 16 + 2 * h + (ct - 2))
                        cw = [prm[:, o_convw + (j * 3 + k) * 32 + gct:o_convw + (j * 3 + k) * 32 + gct + 1] for k in range(3)]
                        S.op("dve", lambda e: e.tensor_scalar(out=acc, in0=xp[:, :, 0:256], scalar1=cw[0], scalar2=None,
                                                              op0=ALU.mult), r=[xpb, prm_b], w=[acc_b])
                        S.op("dve", lambda e: e.scalar_tensor_tensor(out=acc, in0=xp[:, :, 1:257], scalar=cw[1],
                                                                     in1=acc, op0=ALU.mult, op1=ALU.add),
                             r=[xpb, prm_b, acc_b], w=[acc_b])
                        S.op("dve", lambda e: e.scalar_tensor_tensor(out=acc, in0=xp[:, :, 2:258], scalar=cw[2],
                                                                     in1=acc, op0=ALU.mult, op1=ALU.add),
                             r=[xpb, prm_b, acc_b], w=[acc_b])
                        accf = acc.rearrange("p s t -> p (s t)")
                        S.op("act", lambda e: e.activation(out=accf, in_=accf, func=AF.Silu), r=[acc_b], w=[acc_b])
                        if ct < 2:
                            S.op("act", lambda e: e.activation(out=tmb[:, :], in_=accf, func=AF.Square), r=[acc_b], w=[tmb_b])
                            dst, dstb = (qT, q_b) if ct == 0 else (kT, k_b)
                            scl = (128.0 ** -0.5) if ct == 0 else 1.0
                            for (t0, n) in BLOCKS:
                                bk, bb = psum()
                                S.op("pe", lambda e, bk=bk: e.matmul(bk[:, 0:n], lhsT=ones_b[:, :], rhs=tmb[:, t0:t0 + n],
                                                                      start=True, stop=True), r=[tmb_b, cst_b], w=[bb])
                                S.op("act", lambda e, bk=bk: e.activation(out=rinv[:, 0:n], in_=bk[:, 0:n], func=AF.Ln,
                                                                          bias=cpar[:, 1:2], scale=1.0),
                                     r=[bb, cst_b], w=[rinv_b])
                                S.op("act", lambda e: e.activation(out=rinv[:, 0:n], in_=rinv[:, 0:n], func=AF.Exp, scale=-0.5),
                                     r=[rinv_b], w=[rinv_b])
                                S.op("dve", lambda e, dst=dst: e.scalar_tensor_tensor(
                                    out=dst[:, t0:t0 + n], in0=accf[:, t0:t0 + n], scalar=scl, in1=rinv[:, 0:n],
                                    op0=ALU.mult, op1=ALU.mult), r=[acc_b, rinv_b], w=[dstb])
                        else:
                            vt = ct - 2
                            S.op("act", lambda e: e.copy(out=tmb[:, :], in_=accf), r=[acc_b], w=[tmb_b])
                            for c in range(NT):
                                tk = slice(c * 128, (c + 1) * 128)
                                bk, bb = psum()
                                bkb = bk[:, :].bitcast(BF16)
                                S.op("pe", lambda e, bkb=bkb: e.transpose(bkb[:, 0:128], tmb[:, tk], ident_b[:]),
                                     r=[tmb_b, cst_b], w=[bb])
                                S.op("dve", lambda e, bkb=bkb, c=c: e.tensor_scalar(
                                    out=bv[:, c * 2 + 0, vt * 128:(vt + 1) * 128], in0=bkb[:, 0:128],
                                    scalar1=BB[:, c, h:h + 1], scalar2=None, op0=ALU.mult), r=[bb, tk_b], w=[bv_b[c]])
                                S.op("act", lambda e, bkb=bkb, c=c: e.activation(
                                    out=bv[:, c * 2 + 1, vt * 128:(vt + 1) * 128], in_=bkb[:, 0:128], func=AF.Copy,
                                    scale=BB[:, c, 8 + h:9 + h]), r=[bb, tk_b], w=[bv_b[c]])
                    ws_release()
                    if dstage == 1:
                        S.barrier()
                        return
                    sZ, sZb = ws_get()
                    sO, sOb = ws_get()
                    sZv = v8(sZ, 256)
                    sOv = sO[:, 0:2048].rearrange("p (v c) -> p v c", v=2)

                    for g in range(NSEG):
                        kkps = []
                        for ci in range(2):
                            c = 2 * g + ci
                            tk = slice(c * 128, (c + 1) * 128)
                            bk, bb = psum()
                            S.op("pe", lambda e, bk=bk: e.matmul(bk[:, 0:128], lhsT=kT[:, tk], rhs=kT[:, tk], start=True, stop=True),
                                 r=[k_b], w=[bb])
                            S.op("pe", lambda e, bk=bk: e.matmul(bk[:, 128:256], lhsT=kT[:, tk], rhs=qT[:, tk], start=True,
                                                                  stop=True), r=[k_b, q_b], w=[bb])
                            bkt = bk[:, :].bitcast(BF16)
                            S.op("pe", lambda e, bkt=bkt: e.transpose(bkt[:, 512:640], kT[:, tk], ident_b[:]),
                                 r=[k_b, cst_b], w=[bb])
                            kkps.append((bk, bb, bkt))
                        if dstage == 1.1:
                            S.barrier()
                            return
                        xg, xgb = Af, Af_b
                        for ci in range(2):
                            c = 2 * g + ci
                            tk = slice(c * 128, (c + 1) * 128)
                            bk, bb, bkt = kkps[ci]
                            for d in range(2):
                                qd = ci * 2 + d
                                cd = c * 2 + d
                                dh = d * 8 + h
                                ei = qd % 2
                                es, esb = Es[ei], Es_b[ei]
                                be, bbe = psum()
                                gcol = GG[:, c, dh:dh + 1].to_broadcast([128, 128])
                                S.op("pe", lambda e, be=be, d=d: e.matmul(be[:, 0:128], lhsT=gcol, rhs=tri[d][:, :], start=True,
                                                                           stop=True), r=[tk_b, dc_b], w=[bbe])
                                S.op("pe", lambda e, be=be, d=d: e.matmul(be[:, 128:256], lhsT=gcol, rhs=tri[d][:, :], start=True,
                                                                           stop=False), r=[tk_b, dc_b], w=[bbe])
                                S.op("pe", lambda e, be=be, d=d: e.matmul(be[:, 128:256], lhsT=ident_f[:, :], rhs=neg3[d][:, :],
                                                                           start=False, stop=True), r=[cst_b, dc_b], w=[bbe])
                                S.op("pe", lambda e, be=be, d=d: e.matmul(be[:, 256:384], lhsT=gcol, rhs=tri[d][:, :], start=True,
                                                                           stop=False), r=[tk_b, dc_b], w=[bbe])
                                S.op("pe", lambda e, be=be, d=d: e.matmul(be[:, 256:384], lhsT=ident_f[:, :], rhs=pos1[d][:, :],
                                                                           start=False, stop=True), r=[cst_b, dc_b], w=[bbe])
                                S.op("act", lambda e, be=be, es=es: e.activation(out=es[:, 0, :], in_=be[:, 0:128], func=AF.Exp),
                                     r=[bbe], w=[esb])
                                S.op("act", lambda e, be=be, es=es, c=c, dh=dh: e.activation(
                                    out=es[:, 1, :], in_=be[:, 128:256], func=AF.Exp, bias=NEGG[:, c, dh:dh + 1], scale=1.0),
                                    r=[bbe, tk_b], w=[esb])
                                S.op("act", lambda e, be=be, es=es, c=c, dh=dh: e.activation(
                                    out=es[:, 2, :], in_=be[:, 256:384], func=AF.Exp, bias=GPL[:, c, dh:dh + 1], scale=-1.0),
                                    r=[bbe, tk_b], w=[esb])
                                last = 127 if d == 0 else 0
                                S.op("dve", lambda e, es=es, cd=cd, last=last: e.tensor_copy(
                                    out=csc[:, 0, cd:cd + 1], in_=es[:, 0, last:last + 1]), r=[esb], w=[csc_b[c]])
                                S.op("dve", lambda e, es=es, cd=cd, last=last: e.tensor_copy(
                                    out=csc[:, 1, cd:cd + 1], in_=es[:, 1, last:last + 1]), r=[esb], w=[csc_b[c]])
                                S.op("dve", lambda e, es=es, bk=bk, qd=qd: e.tensor_tensor(
                                    out=xg[:, qd, :], in0=bk[:, 0:128], in1=es[:, 2, :], op=ALU.mult), r=[bb, esb], w=[xgb])
                                S.op("dve", lambda e, es=es, bk=bk, cd=cd: e.tensor_tensor(
                                    out=PTm[:, cd, :], in0=bk[:, 128:256], in1=es[:, 1, :], op=ALU.mult), r=[bb, esb], w=[pq_b[c]])
                                S.op("dve", lambda e, es=es, cd=cd: e.tensor_tensor(
                                    out=qg[:, cd, :], in0=qT[:, tk], in1=es[:, 0, :], op=ALU.mult), r=[q_b, esb], w=[pq_b[c]])
                                if d == 0:
                                    S.op("dve", lambda e, bkt=bkt, cd=cd: e.tensor_scalar(
                                        out=kd[:, cd, :], in0=bkt[:, 512:640], scalar1=csc[:, 1, cd:cd + 1], scalar2=None,
                                        op0=ALU.mult), r=[bb, csc_b[c]], w=[kd_b[c]])
                                else:
                                    S.op("act", lambda e, bkt=bkt, cd=cd: e.activation(
                                        out=kd[:, cd, :], in_=bkt[:, 512:640], func=AF.Copy, scale=csc[:, 1, cd:cd + 1]),
                                        r=[bb, csc_b[c]], w=[kd_b[c]])
                        if dstage == 1.2:
                            S.barrier()
                            return
                        bt, bbt = psum()
                        btb = bt[:, :].bitcast(BF16)
                        for qd in range(4):
                            S.op("pe", lambda e, qd=qd, btb=btb: e.transpose(btb[:, qd * 128:(qd + 1) * 128], xg[:, qd, :],
                                                                             ident_b[:]), r=[xgb, cst_b], w=[bbt])
                        bt4 = btb[:, 0:512].rearrange("p (q i) -> p q i", q=4)
                        S.op("act", lambda e: e.copy(out=Bf[:, :, :], in_=bt4), r=[bbt], w=[Bf_b])

                        def mk(k_, neg=False):
                            if neg:
                                return bmaskn[:, k_, :, :]
                            return bmask[:, k_, :].unsqueeze(1).to_broadcast([128, 4, 128])

                        S.op("dve", lambda e: e.tensor_tensor(out=Xg[0][:, :, :], in0=xg[:, :, :], in1=mk(0), op=ALU.mult),
                             r=[xgb, dc_b], w=[Xg_b[0]])
                        S.op("dve", lambda e: e.tensor_tensor(out=Yg[0][:, :, :], in0=Bf[:, :, :], in1=mk(0), op=ALU.mult),
                             r=[Bf_b, dc_b], w=[Yg_b[0]])
                        S.op("dve", lambda e: e.scalar_tensor_tensor(out=Pg[0][:, :, :], in0=Yg[0][:, :, :], scalar=-1.0, in1=id4[:, :, :],
                                                                     op0=ALU.mult, op1=ALU.add), r=[Yg_b[0], dc_b], w=[Pg_b[0]])
                        S.op("dve", lambda e: e.scalar_tensor_tensor(out=Qg[0][:, :, :], in0=Xg[0][:, :, :], scalar=-1.0, in1=id4[:, :, :],
                                                                     op0=ALU.mult, op1=ALU.add), r=[Xg_b[0], dc_b], w=[Qg_b[0]])
                        if dstage == 1.3:
                            S.barrier()
                            return

                        def mm4(L, Lb, R, Rb, acc=None):
                            bk_, bb_ = psum()
                            for qd in range(4):
                                o_ = bk_[:, qd * 128:(qd + 1) * 128]
                                if acc is not None:
                                    S.op("pe", lambda e, qd=qd: e.matmul(o_, lhsT=ident_b[:, :], rhs=acc[0][:, qd, :], start=True,
                                                                          stop=False), r=[cst_b, acc[1]], w=[bb_])
                                S.op("pe", lambda e, qd=qd: e.matmul(o_, lhsT=L[:, qd, :], rhs=R[:, qd, :], start=(acc is None),
                                                                      stop=True), r=[Lb, Rb], w=[bb_])
                            return bk_[:, :].rearrange("p (q i) -> p q i", q=4), bb_

                        def ev_copy(eng, dst, dstb, ps, psb):
                            if eng == "act":
                                S.op("act", lambda e: e.copy(out=dst, in_=ps), r=[psb], w=[dstb])
                            else:
                                S.op("dve", lambda e: e.tensor_copy(out=dst, in_=ps), r=[psb], w=[dstb])

                        def ev_add(dst, dstb, ps, psb, old_, oldb):
                            S.op("dve", lambda e: e.tensor_tensor(out=dst, in0=ps, in1=old_[:, :, :], op=ALU.add),
                                 r=[psb, oldb], w=[dstb])

                        def ev_mask(dst, dstb, ps, psb, k_):
                            S.op("dve", lambda e: e.tensor_tensor(out=dst[:, :, :], in0=ps, in1=mk(k_, True), op=ALU.mult),
                                 r=[psb, dc_b], w=[dstb])

                        pi = 0
                        for m in range(1, 4):
                            a, b2 = (m - 1) % 2, m % 2
                            px, pxb = mm4(Yg[a], Yg_b[a], Xg[a], Xg_b[a])
                            py, pyb = mm4(Xg[a], Xg_b[a], Yg[a], Yg_b[a])
                            ev_copy("act", Xg[b2][:, :, :], Xg_b[b2], px, pxb)
                            ev_copy("act", Yg[b2][:, :, :], Yg_b[b2], py, pyb)
                            pp, ppb = mm4(Xg[b2], Xg_b[b2], Pg[pi], Pg_b[pi])
                            pq, pqb = mm4(Yg[b2], Yg_b[b2], Qg[pi], Qg_b[pi])
                            ev_add(Pg[1 - pi][:, :, :], Pg_b[1 - pi], pp, ppb, Pg[pi], Pg_b[pi])
                            ev_add(Qg[1 - pi][:, :, :], Qg_b[1 - pi], pq, pqb, Qg[pi], Qg_b[pi])
                            pi = 1 - pi
                        for s_ in range(1, 4):
                            pw1, pw1b = mm4(xg, xgb, Pg[pi], Pg_b[pi])
                            ev_mask(Wg[0], Wg_b[0], pw1, pw1b, s_)
                            if s_ < 3:
                                pw2, pw2b = mm4(Bf, Bf_b, Qg[pi], Qg_b[pi])
                                ev_mask(Wg[1], Wg_b[1], pw2, pw2b, s_)
                            pp, ppb = mm4(Qg[pi], Qg_b[pi], Wg[0], Wg_b[0])
                            if s_ < 3:
                                pq, pqb = mm4(Pg[pi], Pg_b[pi], Wg[1], Wg_b[1])
                                ev_add(Pg[1 - pi][:, :, :], Pg_b[1 - pi], pp, ppb, Pg[pi], Pg_b[pi])
                                ev_add(Qg[1 - pi][:, :, :], Qg_b[1 - pi], pq, pqb, Qg[pi], Qg_b[pi])
                                pi = 1 - pi
                            else:
                                ev_add(TT[:, g * 4:(g + 1) * 4, :], TT_b[g], pp, ppb, Pg[pi], Pg_b[pi])

                    if dbg_d and h == 0:
                        def dump(name, ap, bufs):
                            if name in dbg_d:
                                S.dma("pool", dbg_d[name], ap, r=bufs)
                        dump("tsc", tsc[:, :, :, :].rearrange("p a c x -> p (a c x)"), [tk_b])
                        dump("qT", qT[:, :], [q_b])
                        dump("kT", kT[:, :], [k_b])
                        dump("bv", bv[:, :, :].rearrange("p a b -> p (a b)"), bv_b)
                        dump("kd", kd[:, :, :].rearrange("p a b -> p (a b)"), kd_b)
                        dump("TT", TT[:, :, :].rearrange("p a b -> p (a b)"), TT_b)
                        dump("PTm", PTm[:, :, :].rearrange("p a b -> p (a b)"), pq_b)
                        dump("qg", qg[:, :, :].rearrange("p a b -> p (a b)"), pq_b)
                        dump("csc", csc[:, :, :].rearrange("p a b -> p (a b)"), csc_b)
                    if dstage == 2:
                        S.barrier()
                        return
                    S.dma("sp", S32[0][:, :], s0d_d[j, 0, h], w=[S32_b[0]])
                    S.op("dve", lambda e: e.memset(S32[1][:, :], 0.0), w=[S32_b[1]])
                    for d in range(2):
                        S.op("act", lambda e, d=d: e.copy(out=S16[d][:, :], in_=S32[d][:, :]), r=[S32_b[d]], w=[S16_b[d]])
                    arrived = [0] * NT
                    seg_done = [0] * NSEG

                    def consume_o(c, bo, bbo, first):
                        if first:
                            S.op("act", lambda e: e.copy(out=ob[:, c, :], in_=bo[:, 0:256]), r=[bbo], w=[ob_b[c]])
                        else:
                            oi = c % 2
                            S.op("dve", lambda e: e.tensor_tensor(out=on[oi][:, :], in0=bo[:, 0:256], in1=ob[:, c, :], op=ALU.add),
                                 r=[bbo, ob_b[c]], w=[on_b[oi]])

                    def finish_chunk(c, first):
                        s = c // 2
                        if first:
                            return
                        oi = c % 2
                        tk = slice(c * 128, (c + 1) * 128)
                        bkz, bbz = psum()
                        for kt in range(8):
                            S.op("pe", lambda e, kt=kt: e.matmul(bkz[:, 0:256], lhsT=hT[:, kt, tk], rhs=sZv[:, kt, :],
                                                                  start=(kt == 0), stop=(kt == 7)), r=[sZb, hT_b[s]], w=[bbz])
                        S.op("act", lambda e: e.activation(out=zs[oi][:, :], in_=bkz[:, 0:256], func=AF.Exp, scale=-1.0),
                             r=[bbz], w=[zs_b[oi]])
                        S.op("act", lambda e: e.activation(out=zs[oi][:, :], in_=zs[oi][:, :], func=AF.Ln, bias=cpar[:, 2:3], scale=1.0),
                             r=[zs_b[oi], cst_b], w=[zs_b[oi]])
                        S.op("act", lambda e: e.activation(out=zs[oi][:, :], in_=zs[oi][:, :], func=AF.Exp, scale=-1.0),
                             r=[zs_b[oi]], w=[zs_b[oi]])
                        S.op("dve", lambda e: e.tensor_tensor(out=zs[oi][:, :], in0=bkz[:, 0:256], in1=zs[oi][:, :], op=ALU.mult),
                             r=[bbz, zs_b[oi]], w=[zs_b[oi]])
                        S.op("dve", lambda e: e.bn_stats(out=bst[:, oi, 0:6], in_=on[oi][:, :]), r=[on_b[oi]], w=[bst_b[oi]])
                        S.op("dve", lambda e: e.bn_aggr(out=bst[:, oi, 6:8], in_=bst[:, oi, 0:6]), r=[bst_b[oi]], w=[bst_b[oi]])
                        S.op("dve", lambda e: e.scalar_tensor_tensor(out=bst[:, oi, 0:1], in0=bst[:, oi, 6:7], scalar=bst[:, oi, 6:7],
                                                                     in1=bst[:, oi, 7:8], op0=ALU.mult, op1=ALU.add),
                             r=[bst_b[oi]], w=[bst_b[oi]])
                        S.op("act", lambda e: e.activation(out=bst[:, oi, 1:2], in_=bst[:, oi, 0:1], func=AF.Ln, bias=cpar[:, 1:2],
                                                           scale=1.0), r=[bst_b[oi], cst_b], w=[bst_b[oi]])
                        S.op("act", lambda e: e.activation(out=bst[:, oi, 1:2], in_=bst[:, oi, 1:2], func=AF.Exp, scale=-0.5),
                             r=[bst_b[oi]], w=[bst_b[oi]])
                        S.op("dve", lambda e: e.scalar_tensor_tensor(out=on[oi][:, :], in0=on[oi][:, :], scalar=bst[:, oi, 1:2],
                                                                     in1=dnw[:, :], op0=ALU.mult, op1=ALU.mult),
                             r=[on_b[oi], bst_b[oi], dnw_b], w=[on_b[oi]])
                        S.op("dve", lambda e: e.tensor_tensor(out=og[oi][:, :], in0=on[oi][:, :], in1=zs[oi][:, :], op=ALU.mult),
                             r=[on_b[oi], zs_b[oi]], w=[og_b[oi]])
                        bk2, bb2 = psum()
                        bk2b = bk2[:, :].bitcast(BF16)
                        for vt in range(2):
                            S.op("pe", lambda e, vt=vt: e.transpose(bk2b[:, vt * 128:(vt + 1) * 128],
                                                                    og[oi][:, vt * 128:(vt + 1) * 128], ident_b[:]),
                                 r=[og_b[oi], cst_b], w=[bb2])
                        lo = (c % 2) * 128
                        S.op("act", lambda e: e.copy(out=oT[s][:, :, lo:lo + 128],
                                                     in_=bk2b[:, 0:256].rearrange("p (v t) -> p v t", v=2)), r=[bb2], w=[oT_b[s]])
                        seg_done[s] += 1
                        if seg_done[s] == 2:
                            seg = slice(s * SEGL, (s + 1) * SEGL)
                            for dt in range(8):
                                bk, bb = psum()
                                for vt in range(2):
                                    S.op("pe", lambda e, vt=vt, dt=dt, bk=bk: e.matmul(
                                        bk[:, 0:SEGL], lhsT=sOv[:, vt, dt * 128:(dt + 1) * 128], rhs=oT[s][:, vt, :],
                                        start=(vt == 0), stop=(vt == 1)), r=[sOb, oT_b[s]], w=[bb])
                                S.op("dve", lambda e, bk=bk, dt=dt: e.scalar_tensor_tensor(
                                    out=xT[:, dt, seg], in0=bk[:, 0:SEGL], scalar=mod_ap(l, 2, dt, s), in1=xT[:, dt, seg],
                                    op0=ALU.mult, op1=ALU.add), r=[bb, modT_b, xT_b[s]], w=[xT_b[s]])

                    for step in range(NT):
                        cs_ = [step, NT - 1 - step]
                        st1 = []
                        for d in range(2):
                            c = cs_[d]
                            tk = slice(c * 128, (c + 1) * 128)
                            bk, bb = psum()
                            S.op("pe", lambda e, bk=bk, d=d: e.matmul(bk[:, 0:256], lhsT=kT[:, tk], rhs=S16[d][:, :], start=True,
                                                                       stop=True), r=[k_b, S16_b[d]], w=[bb])
                            st1.append((bk, bb))
                        for d in range(2):
                            c = cs_[d]
                            bk, bb = st1[d]
                            S.op("dve", lambda e, bk=bk, d=d, c=c: e.scalar_tensor_tensor(
                                out=rr[d][:, :], in0=bk[:, 0:256], scalar=NBEG[:, c, d * 8 + h:d * 8 + h + 1],
                                in1=bv[:, c * 2 + d, :], op0=ALU.mult, op1=ALU.add), r=[bb, tk_b, bv_b[c]], w=[rr_b[d]])
                        st3 = []
                        for d in range(2):
                            c = cs_[d]
                            bk, bb = psum()
                            S.op("pe", lambda e, bk=bk, d=d, c=c: e.matmul(bk[:, 0:256], lhsT=TT[:, c * 2 + d, :], rhs=rr[d][:, :],
                                                                            start=True, stop=True), r=[TT_b[c // 2], rr_b[d]], w=[bb])
                            st3.append((bk, bb))
                        for d in range(2):
                            bk, bb = st3[d]
                            S.op("act", lambda e, bk=bk, d=d: e.copy(out=vn16[d][:, :], in_=bk[:, 0:256]), r=[bb], w=[vn_b[d]])
                        st5 = []
                        for d in range(2):
                            c = cs_[d]
                            bo, bbo = psum()
                            S.op("pe", lambda e, bo=bo, d=d, c=c: e.matmul(bo[:, 0:256], lhsT=qg[:, c * 2 + d, :], rhs=S16[d][:, :],
                                                                            start=True, stop=False), r=[pq_b[c], S16_b[d]], w=[bbo])
                            S.op("pe", lambda e, bo=bo, d=d, c=c: e.matmul(bo[:, 0:256], lhsT=PTm[:, c * 2 + d, :], rhs=vn16[d][:, :],
                                                                            start=False, stop=True), r=[pq_b[c], vn_b[d]], w=[bbo])
                            S.op("pe", lambda e, bo=bo, d=d, c=c: e.matmul(bo[:, 256:512], lhsT=kd[:, c * 2 + d, :], rhs=vn16[d][:, :],
                                                                            start=True, stop=True), r=[kd_b[c], vn_b[d]], w=[bbo])
                            st5.append((bo, bbo))
                        for d in range(2):
                            c = cs_[d]
                            s = c // 2
                            bo, bbo = st5[d]
                            cd = c * 2 + d
                            S.op("dve", lambda e, bo=bo, d=d, cd=cd: e.scalar_tensor_tensor(
                                out=S32[d][:, :], in0=S32[d][:, :], scalar=csc[:, 0, cd:cd + 1], in1=bo[:, 256:512],
                                op0=ALU.mult, op1=ALU.add), r=[bbo, csc_b[c], S32_b[d]], w=[S32_b[d]])
                            seg_end = (c % 2 == 1) if d == 0 else (c % 2 == 0)
                            if seg_end:
                                st_t, st_bf = stage()
                                S.op("act", lambda e, st_t=st_t, d=d: e.copy(out=st_t[:, 0:256], in_=S32[d][:, :]),
                                     r=[S32_b[d]], w=[st_bf])
                                S.dma("sp", nsd_d[s, j, d, h], st_t[:, 0:256], r=[st_bf])
                                if d == 0 and c < NT - 1:
                                    S.op("dve", lambda e, s=s: e.tensor_scalar(out=S32[0][:, :], in0=S32[0][:, :],
                                                                               scalar1=chain_ap(s + 1), scalar2=None, op0=ALU.mult),
                                         r=[S32_b[0], prm_b], w=[S32_b[0]])
                                if d == 1 and c > 0:
                                    if s - 1 == 3:
                                        st2, st2b = stage()
                                        S.dma("sp", st2[:, 0:256], s0d_d[j, 1, h], w=[st2b])
                                        S.op("dve", lambda e, s=s, st2=st2: e.scalar_tensor_tensor(
                                            out=S32[1][:, :], in0=S32[1][:, :], scalar=chain_ap(s), in1=st2[:, 0:256],
                                            op0=ALU.mult, op1=ALU.add), r=[S32_b[1], prm_b, st2b], w=[S32_b[1]])
                                    else:
                                        S.op("dve", lambda e, s=s: e.tensor_scalar(out=S32[1][:, :], in0=S32[1][:, :],
                                                                                   scalar1=chain_ap(s), scalar2=None, op0=ALU.mult),
                                             r=[S32_b[1], prm_b], w=[S32_b[1]])
                            if step < NT - 1:
                                S.op("act", lambda e, d=d: e.copy(out=S16[d][:, :], in_=S32[d][:, :]), r=[S32_b[d]], w=[S16_b[d]])
                        for d in range(2):
                            c = cs_[d]
                            bo, bbo = st5[d]
                            arrived[c] += 1
                            consume_o(c, bo, bbo, arrived[c] == 1)
                        for d in range(2):
                            c = cs_[d]
                            finish_chunk(c, arrived[c] == 1)
                    ws_release()
                    ws_release()
                S.barrier()

        def final_out():
            for s in range(NSEG):
                sq, sqb, rs, rsb = rms_stat(s)
                seg = slice(s * SEGL, (s + 1) * SEGL)
                S.op("dve", lambda e: e.tensor_tensor(out=sq[:, :, :], in0=xT[:, :, seg],
                                                      in1=rs[:, :].unsqueeze(1).to_broadcast([128, 8, 256]),
                                                      op=ALU.mult), r=[xT_b[s], rsb], w=[sqb])
                for dt in range(8):
                    S.op("dve", lambda e, dt=dt: e.tensor_scalar(
                        out=sq[:, dt, :], in0=sq[:, dt, :], scalar1=prm[:, o_normw + 32 + dt:o_normw + 33 + dt],
                        scalar2=32.0, op0=ALU.mult, op1=ALU.mult), r=[sqb, prm_b], w=[sqb])
                for half in range(2):
                    tt = s * 2 + half
                    st_t, st_bf = stage()
                    for g in range(2):
                        bk, bb = psum()
                        for q in range(4):
                            dt = g * 4 + q
                            S.op("pe", lambda e, bk=bk, q=q, dt=dt: e.transpose(
                                bk[:, q * 128:(q + 1) * 128], sq[:, dt, half * 128:(half + 1) * 128], ident_f[:]),
                                r=[sqb, cst_b], w=[bb])
                        if g == 0:
                            S.op("dve", lambda e, bk=bk: e.tensor_copy(out=st_t[:, 0:512], in_=bk[:, :]), r=[bb], w=[st_bf])
                        else:
                            S.op("act", lambda e, bk=bk: e.copy(out=st_t[:, 512:1024], in_=bk[:, :]), r=[bb], w=[st_bf])
                    S.dma("sp", y_d[tt * 128:(tt + 1) * 128, :], st_t[:, :], r=[st_bf])

        for l in range(depth):
            if l % 2 == 0:
                ret_layer(l, l // 2)
            else:
                del_layer(l, l // 2)
        final_out()
        S.barrier()
        print(f"[build] instr={S.ninstr} sems={S.nsem} counts={S.cnt}")
    return nc


def _rope_tables():
    pos = np.arange(1024)
    r = (pos // 64).astype(np.float32)
    col = (pos % 64).astype(np.float32)
    freqs = (10000.0 ** (-np.arange(64, dtype=np.float32) / 64.0)).astype(np.float32)
    ang = np.concatenate([r[:, None] * freqs[None, :], col[:, None] * freqs[None, :]], -1)
    return np.cos(ang).T.astype(np.float32), np.sin(ang).T.astype(np.float32)


def _fm(v, nt):
    return np.ascontiguousarray(np.asarray(v, np.float32).reshape(nt, 128).T)


def make_in_maps(inp):
    f = lambda k: np.ascontiguousarray(np.asarray(inp[k], dtype=np.float32))
    xp, xs = f("x_prompt"), f("x_sample")
    c, c_ctx = f("c"), f("c_ctx")
    sr, sd = f("state_ret"), f("state_delta")
    norm_w, fnw, mod_b = f("norm_w"), f("final_norm_w"), f("mod_b")
    normw = np.concatenate([_fm(norm_w[l], 8) for l in range(4)] + [_fm(fnw, 8)], axis=1)
    modb = np.concatenate([_fm(mod_b[l][p * 1024:(p + 1) * 1024], 8) for l in range(4) for p in range(3)], axis=1)
    rdecb = np.ascontiguousarray(np.broadcast_to(f("ret_decay").reshape(1, 16), (128, 16)))
    gnwb = np.ascontiguousarray(np.broadcast_to(f("ret_gn_w").reshape(1, 4096), (128, 4096)))
    dnwb = np.ascontiguousarray(np.broadcast_to(f("del_norm_w").reshape(1, 4096), (128, 4096)))
    cw = f("del_conv_w")
    convw = np.concatenate([_fm(cw[jj, k], 32) for jj in range(2) for k in range(3)], axis=1)
    alogb = np.ascontiguousarray(np.broadcast_to(f("del_a_log").reshape(1, 32), (128, 32)))
    dtbb = np.ascontiguousarray(np.broadcast_to(f("del_dt_bias").reshape(1, 32), (128, 32)))
    cosS, sinS = _rope_tables()
    ii = np.arange(128)
    same = lambda b: (ii[:, None] // b) == (ii[None, :] // b)
    bm = [same(16), same(32) & ~same(16), same(64) & ~same(32), ~same(64)]
    bmask = np.concatenate([m.astype(np.float32) for m in bm], axis=1)
    shared = dict(bmask=bmask, normw=normw, modb=modb, mod_w=f("mod_w"), ret_w_in=f("ret_w_in"), ret_w_out=f("ret_w_out"),
                  del_w_in=f("del_w_in"), del_w_out=f("del_w_out"), rdecb=rdecb, gnwb=gnwb, dnwb=dnwb, convw=convw,
                  alogb=alogb, dtbb=dtbb)
    maps = []
    for core in range(8):
        m = dict(shared)
        chain = np.zeros((128, 8), np.float32)
        cosT = np.ones((128, NTOK), np.float32)
        sinT = np.zeros((128, NTOK), np.float32)
        if core < 6:
            x = xp[core * 5:(core + 1) * 5].reshape(NTOK, D)
            conds = [c_ctx] * 5
            s0r = np.zeros((2, 2, 4, 256, 512), np.float32)
            s0d = np.zeros((2, 2, 8, 128, 256), np.float32)
        else:
            b = core - 6
            x = np.concatenate([xs[b], xp[30 + b]], axis=0)
            conds = [c[b]] * 4 + [c_ctx]
            chain[:, 1:4] = 1.0
            s0r, s0d = sr[b], sd[b]
            cosT[:, 0:1024] = cosS
            sinT[:, 0:1024] = sinS
        condT = np.zeros((128, 40), np.float32)
        for s in range(5):
            condT[:, s::5] = _fm(conds[s], 8)
        m.update(x=np.ascontiguousarray(x), condT=condT, chainb=chain, s0r=np.ascontiguousarray(s0r),
                 s0d=np.ascontiguousarray(s0d), cosT=cosT, sinT=sinT)
        maps.append(m)
    return maps


def assemble(results):
    y_prompt = np.zeros((32, 256, D), np.float32)
    y_sample = np.zeros((2, 1024, D), np.float32)
    nsr = np.zeros((32, 2, 2, 4, 256, 512), np.float32)
    nsd = np.zeros((32, 2, 2, 8, 128, 256), np.float32)
    for core in range(8):
        r = results[core]
        y = r["y"].reshape(5, 256, D)
        if core < 6:
            y_prompt[core * 5:(core + 1) * 5] = y
            nsr[core * 5:(core + 1) * 5] = r["nsr"]
            nsd[core * 5:(core + 1) * 5] = r["nsd"]
        else:
            b = core - 6
            y_sample[b] = y[0:4].reshape(1024, D)
            y_prompt[30 + b] = y[4]
            nsr[30 + b] = r["nsr"][4]
            nsd[30 + b] = r["nsd"][4]
    return y_prompt, y_sample, nsr, nsd


_NC_CACHE = {}


def kernel(**inputs):
    if "nc" not in _NC_CACHE:
        _NC_CACHE["nc"] = build()
    maps = make_in_maps(inputs)
    res = run_bass_kernel_spmd(_NC_CACHE["nc"], maps, core_ids=list(range(8)))
    return assemble(res.results)
```

```python
import contextlib
import numpy as np
import concourse.bass as bass
import concourse.mybir as mybir
from concourse.bass_utils import run_bass_kernel_spmd

F32 = mybir.dt.float32
BF16 = mybir.dt.bfloat16
AF = mybir.ActivationFunctionType
ALU = mybir.AluOpType

D = 1024
NSEG = 5
SEGL = 256
NTOK = NSEG * SEGL
NT = NTOK // 128
DEPTH = 4
EPS = 1e-6
BLOCKS = [(0, 512), (512, 512), (1024, 256)]
import os
EPOCH = int(os.environ.get("K_EPOCH", "4000"))
BIG = 30000.0


class Buf:
    __slots__ = ("name", "lw", "rd", "dsem", "dcnt")

    def __init__(self, name):
        self.name = name
        self.lw = None
        self.rd = {}
        self.dsem = None
        self.dcnt = 0


class Sched:
    def __init__(self, nc, stack):
        self.nc = nc
        self.stack = stack
        self.engs = {"pe": nc.tensor, "act": nc.scalar, "dve": nc.vector, "pool": nc.gpsimd, "sp": nc.sync}
        self.cnt = {e: 0 for e in self.engs}
        self.esems = {e: [] for e in self.engs}
        self.known = {e: {} for e in self.engs}
        self.dsems = {}
        self.dma_bufs = []
        self.nsem = 0
        self.ninstr = 0
        self.pe_pending = None

    def _newsem(self, name):
        self.nsem += 1
        return self.stack.enter_context(self.nc.semaphore(f"{name}_{self.nsem}"))

    def _esem(self, e, epoch):
        lst = self.esems[e]
        while len(lst) <= epoch:
            lst.append(self._newsem(f"s_{e}_{len(lst)}"))
        return lst[epoch]

    def _need(self, E, ev, raw):
        key, val = ev
        if key == ("e", E) and not raw:
            return
        kn = self.known[E]
        if kn.get(key, 0) >= val:
            return
        kn[key] = val
        eng = self.engs[E]
        if key[0] == "e":
            n = val - 1
            eng.wait_ge(self._esem(key[1], n // EPOCH), n % EPOCH + 1)
        else:
            eng.wait_ge(self.dsems[key], val)
        self.ninstr += 1

    def _deps(self, E, r, w):
        for b in r:
            if b.lw is not None:
                self._need(E, b.lw, True)
        for b in w:
            if b.lw is not None:
                self._need(E, b.lw, False)
            for k, v in b.rd.items():
                self._need(E, (k, v), False)

    def _post(self, ev, r, w):
        k, v = ev
        for b in r:
            if b.rd.get(k, 0) < v:
                b.rd[k] = v
        for b in w:
            b.lw = ev
            b.rd = {}

    def _commit_pe(self):
        if self.pe_pending is None:
            return
        ins, _ = self.pe_pending
        n = self.cnt["pe"]
        ins.then_inc(self._esem("pe", n // EPOCH), 1)
        self.cnt["pe"] = n + 1
        self.pe_pending = None

    def _touch(self, E, r, w):
        if self.pe_pending is None:
            return
        pw = self.pe_pending[1]
        if E == "pe":
            if tuple(id(b) for b in w) != pw:
                self._commit_pe()
        elif any(id(b) in pw for b in r) or any(id(b) in pw for b in w):
            self._commit_pe()

    def op(self, E, fn, r=(), w=()):
        self._touch(E, r, w)
        self._deps(E, r, w)
        ins = fn(self.engs[E])
        if E == "pe":
            self.pe_pending = (ins, tuple(id(b) for b in w))
            self.ninstr += 1
            self._post((("e", "pe"), self.cnt["pe"] + 1), r, w)
            return ins
        n = self.cnt[E]
        ins.then_inc(self._esem(E, n // EPOCH), 1)
        self.cnt[E] = n + 1
        self.ninstr += 1
        self._post((("e", E), n + 1), r, w)
        return ins

    def dma(self, Q, out, in_, r=(), w=(), **kw):
        self._touch(Q, r, w)
        b0 = (list(w) + list(r))[0]
        own = ("d", id(b0))
        for b in r:
            if b.lw is not None:
                self._need(Q, b.lw, True)
        for b in w:
            if b.lw is not None and not (b.lw[0] == own and not b.rd):
                self._need(Q, b.lw, False)
            for k, v in b.rd.items():
                self._need(Q, (k, v), False)
        if b0.dsem is None:
            b0.dsem = self._newsem("d_" + b0.name)
            self.dsems[("d", id(b0))] = b0.dsem
            self.dma_bufs.append(b0)
        ins = self.engs[Q].dma_start(out=out, in_=in_, **kw)
        ins.then_inc(b0.dsem, 16)
        b0.dcnt += 16
        self.ninstr += 1
        self._post((("d", id(b0)), b0.dcnt), r, w)

    def barrier(self):
        self._commit_pe()
        for E in self.engs:
            for e2 in self.engs:
                if self.cnt[e2] > 0:
                    self._need(E, (("e", e2), self.cnt[e2]), True)
            for b in self.dma_bufs:
                self._need(E, (("d", id(b)), b.dcnt), True)


def build(depth=DEPTH, dbg=None):
    nc = bass.Bass("TRN2", target_bir_lowering=False)
    dbg = dbg or {}

    def din(name, shape):
        return nc.dram_tensor(name, list(shape), F32, kind="ExternalInput").ap()

    def dout(name, shape):
        return nc.dram_tensor(name, list(shape), F32, kind="ExternalOutput").ap()

    x_d = din("x", [NTOK, D])
    cond_d = din("condT", [128, 40])
    chain_d = din("chainb", [128, 8])
    normw_d = din("normw", [128, 40])
    modb_d = din("modb", [128, 96])
    modw_d = din("mod_w", [4, D, 3 * D])
    rwin_d = din("ret_w_in", [2, D, 6144])
    rwout_d = din("ret_w_out", [2, 2048, D])
    dwin_d = din("del_w_in", [2, D, 6176])
    dwout_d = din("del_w_out", [2, 2048, D])
    rdec_d = din("rdecb", [128, 16])
    gnw_d = din("gnwb", [128, 4096])
    dnw_d = din("dnwb", [128, 4096])
    convw_d = din("convw", [128, 192])
    alog_d = din("alogb", [128, 32])
    dtb_d = din("dtbb", [128, 32])
    s0r_d = din("s0r", [2, 2, 4, 256, 512])
    s0d_d = din("s0d", [2, 2, 8, 128, 256])
    cos_d = din("cosT", [128, NTOK])
    sin_d = din("sinT", [128, NTOK])
    bmask_d = din("bmask", [128, 512])
    y_d = dout("y", [NTOK, D])
    nsr_d = dout("nsr", [NSEG, 2, 2, 4, 256, 512])
    nsd_d = dout("nsd", [NSEG, 2, 2, 8, 128, 256])
    dbg_d = {k: dout("dbg_" + k, shp) for k, shp in dbg.items()}

    with contextlib.ExitStack() as stack:
        S = Sched(nc, stack)

        sb_n = [0]

        def sb(name, shape, dt=F32, st=None):
            sb_n[0] += 1
            return (st or stack).enter_context(nc.sbuf_tensor(f"sb{sb_n[0]}_{name}", list(shape), dt))

        banks = [stack.enter_context(nc.psum_tensor(f"bank{i}", [128, 512], F32)) for i in range(8)]
        bank_bufs = [Buf(f"bank{i}") for i in range(8)]
        bank_rr = [0]

        def psum():
            i = bank_rr[0] % 8
            bank_rr[0] += 1
            return banks[i], bank_bufs[i]

        xT = sb("xT", [128, 8, NTOK])
        xT_b = [Buf(f"xT{s}") for s in range(NSEG)]
        hT = sb("hT", [128, 8, NTOK], BF16)
        hT_b = [Buf(f"hT{s}") for s in range(NSEG)]
        NSLAB = 3
        ring = [sb(f"slab{i}", [128, 4096], BF16) for i in range(NSLAB)]
        ring_b = [Buf(f"slab{i}") for i in range(NSLAB)]
        ring_rr = [0]

        ident_f = sb("ident_f", [128, 128])
        ident_b = sb("ident_b", [128, 128], BF16)
        ones_f = sb("ones_f", [128, 128])
        ones_b = sb("ones_b", [128, 128], BF16)
        cst_b = Buf("consts")
        cpar = sb("cpar", [128, 8])
        prm = sb("prm", [128, 40 + 8 + 40 + 96 + 16 + 192 + 32 + 32])
        prm_b = Buf("prm")
        o_cond, o_chain, o_normw, o_modb, o_rdec, o_convw, o_alog, o_dtb = 0, 40, 48, 88, 184, 200, 392, 424
        modT = sb("modT", [128, 4 * 3 * 40])
        modT_b = Buf("modT")
        sc1 = sb("sc1", [128, 4 * 40])
        scT = sb("scT", [128, 8, NSEG], BF16)
        sqs = [sb("sq0", [128, 8, 256])]
        sqs_b = [Buf("sq0")]
        rstd = [sb("rstd0", [128, 256])]
        rstd_b = [Buf("rstd0")]
        stg = [sb(f"stg{i}", [128, 1024]) for i in range(2)]
        stg_b = [Buf(f"stg{i}") for i in range(2)]
        stg_rr = [0]

        def stage():
            i = stg_rr[0] % 2
            stg_rr[0] += 1
            return stg[i], stg_b[i]

        def chain_ap(s):
            return prm[:, o_chain + s:o_chain + s + 1]

        S.op("pool", lambda e: e.memset(ident_f[:], 1.0), w=[cst_b])
        S.op("pool", lambda e: e.affine_select(out=ident_f[:], in_=ident_f[:], pattern=[[-1, 128]],
                                                compare_op=ALU.is_equal, fill=0.0, base=0, channel_multiplier=1),
             r=[cst_b], w=[cst_b])
        S.op("pool", lambda e: e.tensor_copy(out=ident_b[:], in_=ident_f[:]), r=[cst_b], w=[cst_b])
        S.op("pool", lambda e: e.memset(ones_f[:], 1.0), w=[cst_b])
        S.op("pool", lambda e: e.memset(ones_b[:], 1.0), w=[cst_b])
        S.op("pool", lambda e: e.memset(cpar[:, 0:1], 1024.0 * EPS), w=[cst_b])
        S.op("pool", lambda e: e.memset(cpar[:, 1:2], EPS), w=[cst_b])
        S.op("pool", lambda e: e.memset(cpar[:, 2:3], 1.0), w=[cst_b])
        S.op("pool", lambda e: e.memset(cpar[:, 3:4], 0.0), w=[cst_b])
        S.op("pool", lambda e: e.memset(cpar[:, 4:5], -0.5), w=[cst_b])

        for off, n, src in ((o_cond, 40, cond_d), (o_chain, 8, chain_d), (o_normw, 40, normw_d), (o_modb, 96, modb_d),
                            (o_rdec, 16, rdec_d), (o_convw, 192, convw_d), (o_alog, 32, alog_d), (o_dtb, 32, dtb_d)):
            S.dma("sp", prm[:, off:off + n], src, w=[prm_b])

        def v8(t, n=512):
            return t[:, 0:8 * n].rearrange("p (kt c) -> p kt c", kt=8)

        def win_view(w2d):
            return w2d.rearrange("(kt p) c -> p kt c", p=128)

        wspecs = []
        for l in range(depth):
            for part in range(3):
                for hf in range(2):
                    c0 = part * 1024 + hf * 512
                    wspecs.append(lambda t, l=l, c0=c0: [(v8(t), win_view(modw_d[l])[:, :, c0:c0 + 512])])
        for l in range(depth):
            j = l // 2
            if l % 2 == 0:
                wv = win_view(rwin_d[j])
                for h in range(4):
                    wspecs.append(lambda t, h=h, wv=wv: [
                        (v8(t)[:, :, 0:256], wv[:, :, h * 256:(h + 1) * 256]),
                        (v8(t)[:, :, 256:512], wv[:, :, 1024 + h * 256:1024 + (h + 1) * 256])])
                    wspecs.append(lambda t, h=h, wv=wv: [(v8(t), wv[:, :, 2048 + h * 512:2048 + (h + 1) * 512])])
                    wspecs.append(lambda t, h=h, wv=wv: [(v8(t), wv[:, :, 4096 + h * 512:4096 + (h + 1) * 512])])
                    wspecs.append(lambda t, h=h, j=j: [
                        (t[:, :].rearrange("p (v c) -> p v c", v=4),
                         rwout_d[j][h * 512:(h + 1) * 512, :].rearrange("(v p) c -> p v c", p=128))])
            else:
                wv = win_view(dwin_d[j])
                wspecs.append(lambda t, wv=wv: [(v8(t, 32), wv[:, :, 6144:6176])])
                for h in range(8):
                    wspecs.append(lambda t, h=h, wv=wv: [
                        (v8(t)[:, :, 0:128], wv[:, :, h * 128:(h + 1) * 128]),
                        (v8(t)[:, :, 128:256], wv[:, :, 1024 + h * 128:1024 + (h + 1) * 128]),
                        (v8(t)[:, :, 256:512], wv[:, :, 2048 + h * 256:2048 + (h + 1) * 256])])
                    wspecs.append(lambda t, h=h, wv=wv: [(v8(t, 256), wv[:, :, 4096 + h * 256:4096 + (h + 1) * 256])])
                    wspecs.append(lambda t, h=h, j=j: [
                        (t[:, 0:2048].rearrange("p (v c) -> p v c", v=2),
                         dwout_d[j][h * 256:(h + 1) * 256, :].rearrange("(v p) c -> p v c", p=128))])
        ws = {"issued": 0, "released": 0, "got": 0}

        def ws_issue():
            while ws["issued"] < len(wspecs) and ws["issued"] < ws["released"] + NSLAB:
                i = ws["issued"]
                for (dst_view, src_ap) in wspecs[i](ring[i % NSLAB]):
                    S.dma("pool", dst_view, src_ap, w=[ring_b[i % NSLAB]])
                ws["issued"] += 1

        def ws_get():
            ws_issue()
            i = ws["got"]
            assert i < ws["issued"], "weight ring over-subscribed"
            ws["got"] += 1
            return ring[i % NSLAB], ring_b[i % NSLAB]

        def ws_release():
            ws["released"] += 1
            ws_issue()

        for tt in range(NT):
            st_t, st_b = stage()
            S.dma("sp", st_t[:, :], x_d[tt * 128:(tt + 1) * 128, :], w=[st_b])
            for half in range(2):
                bk, bb = psum()
                for q in range(4):
                    dt = half * 4 + q
                    S.op("pe", lambda e, bk=bk, q=q, dt=dt, st_t=st_t: e.transpose(
                        bk[:, q * 128:(q + 1) * 128], st_t[:, dt * 128:(dt + 1) * 128], ident_f[:]),
                        r=[st_b, cst_b], w=[bb])
                eng = "dve" if half == 0 else "act"
                dst = xT[:, half * 4:half * 4 + 4, tt * 128:(tt + 1) * 128]
                src = bk[:, :].rearrange("p (q t) -> p q t", q=4)
                if eng == "dve":
                    S.op("dve", lambda e, dst=dst, src=src: e.tensor_copy(out=dst, in_=src), r=[bb], w=[xT_b[tt // 2]])
                else:
                    S.op("act", lambda e, dst=dst, src=src: e.copy(out=dst, in_=src), r=[bb], w=[xT_b[tt // 2]])

        S.op("act", lambda e: e.activation(out=scT[:].rearrange("p a b -> p (a b)"), in_=prm[:, o_cond:o_cond + 40],
                                           func=AF.Silu), r=[prm_b], w=[modT_b])
        for l in range(depth):
            for part in range(3):
                for hf in range(2):
                    c0 = part * 1024 + hf * 512
                    sl, slb = ws_get()
                    slv = v8(sl)
                    bk, bb = psum()
                    for ct in range(4):
                        for kt in range(8):
                            S.op("pe", lambda e, bk=bk, ct=ct, kt=kt, slv=slv: e.matmul(
                                bk[:, ct * 5:ct * 5 + 5], lhsT=slv[:, kt, ct * 128:(ct + 1) * 128], rhs=scT[:, kt, :],
                                start=(kt == 0), stop=(kt == 7)), r=[slb, modT_b], w=[bb])
                    base = (l * 3 + part) * 40 + hf * 20
                    mb = prm[:, o_modb + (l * 3 + part) * 8 + hf * 4: o_modb + (l * 3 + part) * 8 + hf * 4 + 4]
                    S.op("dve", lambda e, bk=bk, base=base, mb=mb: e.tensor_tensor(
                        out=modT[:, base:base + 20].rearrange("p (a b) -> p a b", a=4),
                        in0=bk[:, 0:20].rearrange("p (a b) -> p a b", a=4),
                        in1=mb.unsqueeze(2).to_broadcast([128, 4, 5]), op=ALU.add), r=[bb, prm_b], w=[modT_b])
                    ws_release()
            sv = modT[:, (l * 3 + 1) * 40:(l * 3 + 1) * 40 + 40]
            S.op("dve", lambda e, l=l, sv=sv: e.tensor_scalar(out=sc1[:, l * 40:(l + 1) * 40], in0=sv, scalar1=1.0,
                                                             scalar2=32.0, op0=ALU.add, op1=ALU.mult),
                 r=[modT_b], w=[modT_b])
            S.op("dve", lambda e, l=l: e.tensor_tensor(
                out=sc1[:, l * 40:(l + 1) * 40].rearrange("p (a b) -> p a b", a=8),
                in0=sc1[:, l * 40:(l + 1) * 40].rearrange("p (a b) -> p a b", a=8),
                in1=prm[:, o_normw + l * 8:o_normw + l * 8 + 8].unsqueeze(2).to_broadcast([128, 8, 5]), op=ALU.mult),
                r=[modT_b, prm_b], w=[modT_b])

        def mod_ap(l, part, dt, s):
            o = (l * 3 + part) * 40 + dt * 5 + s
            return modT[:, o:o + 1]

        nrm_rr = [0]

        def rms_stat(s):
            i = 0
            sq, sqb, rs, rsb = sqs[i], sqs_b[i], rstd[i], rstd_b[i]
            seg = slice(s * SEGL, (s + 1) * SEGL)
            S.op("act", lambda e: e.activation(out=sq[:, :, :], in_=xT[:, :, seg], func=AF.Square), r=[xT_b[s]], w=[sqb])
            bk, bb = psum()
            for dt in range(8):
                S.op("pe", lambda e, dt=dt: e.matmul(bk[:, 0:256], lhsT=ones_f[:], rhs=sq[:, dt, :],
                                                      start=(dt == 0), stop=(dt == 7)), r=[sqb, cst_b], w=[bb])
            S.op("act", lambda e: e.activation(out=rs[:, :], in_=bk[:, 0:256], func=AF.Ln, bias=cpar[:, 0:1], scale=1.0),
                 r=[bb, cst_b], w=[rsb])
            S.op("act", lambda e: e.activation(out=rs[:, :], in_=rs[:, :], func=AF.Exp, scale=-0.5), r=[rsb], w=[rsb])
            return sq, sqb, rs, rsb

        def norm_mod(l):
            for s in range(NSEG):
                sq, sqb, rs, rsb = rms_stat(s)
                seg = slice(s * SEGL, (s + 1) * SEGL)
                S.op("dve", lambda e: e.tensor_tensor(out=sq[:, :, :], in0=xT[:, :, seg],
                                                      in1=rs[:, :].unsqueeze(1).to_broadcast([128, 8, 256]),
                                                      op=ALU.mult), r=[xT_b[s], rsb], w=[sqb])
                for dt in range(8):
                    o = l * 40 + dt * 5 + s
                    S.op("act", lambda e, dt=dt, o=o: e.activation(
                        out=hT[:, dt, seg], in_=sq[:, dt, :], func=AF.Identity, bias=mod_ap(l, 0, dt, s),
                        scale=sc1[:, o:o + 1]), r=[sqb, modT_b], w=[hT_b[s]])

        def segs_of(t0, n):
            return list(range(t0 // SEGL, (t0 + n) // SEGL))

        def out_proj(l, sO, sOb, nvt, oT, oT_b):
            sOv = sO[:, 0:nvt * 1024].rearrange("p (v c) -> p v c", v=nvt)
            for (t0, n) in BLOCKS:
                sg = segs_of(t0, n)
                for dt in range(8):
                    bk, bb = psum()
                    for vt in range(nvt):
                        S.op("pe", lambda e, vt=vt, dt=dt, bk=bk: e.matmul(
                            bk[:, 0:n], lhsT=sOv[:, vt, dt * 128:(dt + 1) * 128], rhs=oT[:, vt, t0:t0 + n],
                            start=(vt == 0), stop=(vt == nvt - 1)), r=[sOb] + [oT_b[s] for s in sg], w=[bb])
                    for s in sg:
                        lo = s * SEGL - t0
                        seg = slice(s * SEGL, (s + 1) * SEGL)
                        S.op("dve", lambda e, bk=bk, lo=lo, seg=seg, dt=dt, s=s: e.scalar_tensor_tensor(
                            out=xT[:, dt, seg], in0=bk[:, lo:lo + SEGL], scalar=mod_ap(l, 2, dt, s), in1=xT[:, dt, seg],
                            op0=ALU.mult, op1=ALU.add), r=[bb, modT_b, xT_b[s]], w=[xT_b[s]])

        def ret_layer(l, j):
            with contextlib.ExitStack() as ls:
                cosT = sb("cosT", [128, NTOK], st=ls)
                sinT = sb("sinT", [128, NTOK], st=ls)
                rope_b = Buf("rope")
                S.dma("sp", cosT[:, :], cos_d, w=[rope_b])
                S.dma("sp", sinT[:, :], sin_d, w=[rope_b])
                lc = sb("lc", [128, 64], st=ls)
                lc_b = Buf("lc")
                iot = sb("iot", [128, 128], st=ls)
                iop = sb("iop", [128, 2], st=ls)
                ioi = sb("ioi", [128, 2, 128], st=ls)
                Mh = sb("Mh", [128, 4, 128], st=ls)
                Xi = sb("Xi", [128, 8, 128], BF16, st=ls)
                tmpm = sb("tmpm", [128, 2, 128], st=ls)
                gnw = sb("gnw", [128, 512], st=ls)
                gnw_b = Buf("gnw")
                qT = sb("qT", [128, 2, NTOK], BF16, st=ls)
                kT = sb("kT", [128, 2, NTOK], BF16, st=ls)
                qk_b = [Buf(f"qk{s}") for s in range(NSEG)]
                sq, sqb = sqs[0], sqs_b[0]
                kzf = sb("kzf", [128, NT, 256], BF16, st=ls)
                kzb = sb("kzb", [128, NT, 256], BF16, st=ls)
                kz_b = [Buf(f"kz{c}") for c in range(NT)]
                v16 = sb("v16", [128, NT, 512], BF16, st=ls)
                v_b = [Buf(f"v{c}") for c in range(NT)]
                sm = sb("sm", [128, NT, 128], BF16, st=ls)
                sm_b = [Buf(f"sm{c}") for c in range(NT)]
                qx = [sb(f"qx{i}", [128, 2, 2, 128], BF16, st=ls) for i in range(2)]
                qx_b = [Buf(f"qx{i}") for i in range(2)]
                zs = [sb(f"zs{i}", [128, 512], st=ls) for i in range(2)]
                zs_b = [Buf(f"zs{i}") for i in range(2)]
                Sb16 = sb("Sb16", [128, NT, 2, 512], BF16, st=ls)
                Sb16_b = [Buf(f"Sb16_{c}") for c in range(NT)]
                S32 = [sb(f"S32_{d}", [128, 2, 512], st=ls) for d in range(2)]
                S32_b = [Buf(f"S32_{d}") for d in range(2)]
                S32alt = [None, sb("S32_1b", [128, 2, 512], st=ls)]
                S32alt_b = [None, Buf("S32_1b")]
                S16f = sb("S16f", [128, 2, 512], BF16, st=ls)
                S16f_b = Buf("S16f")
                oT = [sb(f"oT{i}", [128, 4, SEGL], BF16, st=ls) for i in range(2)]
                oT_b = [Buf(f"oT{i}") for i in range(2)]
                bst = sb("bst", [128, 16], st=ls)
                bst_b = Buf("bst")
                on = sb("on", [128, 512], st=ls)
                on_b = Buf("on")
                og = sb("og", [128, 512], BF16, st=ls)
                og_b = Buf("og")

                dec = prm[:, o_rdec + j * 8:o_rdec + j * 8 + 8]
                LG, GC, ZF = lc[:, 0:8], lc[:, 8:16], lc[:, 16:24]
                S.op("act", lambda e: e.activation(out=lc[:, 24:32], in_=dec, func=AF.Exp, scale=-1.0), r=[prm_b], w=[lc_b])
                S.op("act", lambda e: e.activation(out=LG, in_=lc[:, 24:32], func=AF.Ln, bias=cpar[:, 2:3], scale=1.0),
                     r=[lc_b, cst_b], w=[lc_b])
                S.op("dve", lambda e: e.tensor_scalar(out=LG, in0=LG, scalar1=-1.0, scalar2=None, op0=ALU.mult),
                     r=[lc_b], w=[lc_b])
                S.op("act", lambda e: e.activation(out=GC, in_=LG, func=AF.Exp, scale=128.0), r=[lc_b], w=[lc_b])
                S.op("pool", lambda e: e.iota(iot[:, :], pattern=[[1, 128]], base=0, channel_multiplier=-1,
                                               allow_small_or_imprecise_dtypes=True), r=[lc_b], w=[lc_b])
                S.op("pool", lambda e: e.iota(iop[:, 0:1], pattern=[[0, 1]], base=127, channel_multiplier=-1,
                                               allow_small_or_imprecise_dtypes=True), r=[lc_b], w=[lc_b])
                S.op("pool", lambda e: e.iota(iop[:, 1:2], pattern=[[0, 1]], base=0, channel_multiplier=1,
                                               allow_small_or_imprecise_dtypes=True), r=[lc_b], w=[lc_b])
                S.op("pool", lambda e: e.iota(ioi[:, 0, :], pattern=[[1, 128]], base=1, channel_multiplier=0,
                                               allow_small_or_imprecise_dtypes=True), r=[lc_b], w=[lc_b])
                S.op("pool", lambda e: e.iota(ioi[:, 1, :], pattern=[[-1, 128]], base=128, channel_multiplier=0,
                                               allow_small_or_imprecise_dtypes=True), r=[lc_b], w=[lc_b])
                for d in range(2):
                    for h in range(4):
                        c = d * 4 + h
                        S.op("act", lambda e, c=c, d=d: e.activation(out=ZF[:, c:c + 1], in_=iop[:, d:d + 1], func=AF.Exp,
                                                                      scale=LG[:, c:c + 1]), r=[lc_b], w=[lc_b])
                S.op("dve", lambda e: e.tensor_scalar(out=ZF, in0=ZF, scalar1=1.0 / 16.0, scalar2=None, op0=ALU.mult),
                     r=[lc_b], w=[lc_b])
                for h in range(4):
                    S.op("dve", lambda e: e.tensor_scalar(out=tmpm[:, 0, :], in0=iot[:, :], scalar1=0.0, scalar2=None,
                                                          op0=ALU.max), r=[lc_b], w=[lc_b])
                    S.op("act", lambda e, h=h: e.activation(out=tmpm[:, 0, :], in_=tmpm[:, 0, :], func=AF.Exp,
                                                             scale=LG[:, h:h + 1]), r=[lc_b], w=[lc_b])
                    S.op("pool", lambda e: e.affine_select(out=tmpm[:, 0, :], in_=tmpm[:, 0, :], pattern=[[1, 128]],
                                                            compare_op=ALU.is_ge, fill=0.0, base=0, channel_multiplier=-1),
                         r=[lc_b], w=[lc_b])
                    S.op("dve", lambda e: e.tensor_scalar(out=tmpm[:, 1, :], in0=iot[:, :], scalar1=-1.0, scalar2=0.0,
                                                          op0=ALU.mult, op1=ALU.max), r=[lc_b], w=[lc_b])
                    S.op("act", lambda e, h=h: e.activation(out=tmpm[:, 1, :], in_=tmpm[:, 1, :], func=AF.Exp,
                                                             scale=LG[:, 4 + h:5 + h]), r=[lc_b], w=[lc_b])
                    S.op("pool", lambda e: e.affine_select(out=tmpm[:, 1, :], in_=tmpm[:, 1, :], pattern=[[-1, 128]],
                                                            compare_op=ALU.is_ge, fill=0.0, base=0, channel_multiplier=1),
                         r=[lc_b], w=[lc_b])
                    S.op("dve", lambda e, h=h: e.tensor_tensor(out=Mh[:, h, :], in0=tmpm[:, 0, :], in1=tmpm[:, 1, :],
                                                               op=ALU.add), r=[lc_b], w=[lc_b])
                    S.op("dve", lambda e, h=h: e.tensor_scalar(out=Mh[:, h, :], in0=Mh[:, h, :], scalar1=1.0 / 16.0,
                                                               scalar2=None, op0=ALU.mult), r=[lc_b], w=[lc_b])
                    for d in range(2):
                        S.op("act", lambda e, h=h, d=d: e.activation(out=Xi[:, d * 4 + h, :], in_=ioi[:, d, :], func=AF.Exp,
                                                                      scale=LG[:, d * 4 + h:d * 4 + h + 1]),
                             r=[lc_b], w=[lc_b])

                norm_mod(l)

                for h in range(4):
                    sQK, sQKb = ws_get()
                    sV, sVb = ws_get()
                    sQKv, sVv = v8(sQK), v8(sV)
                    S.dma("sp", gnw[:, :], gnw_d[:, j * 2048 + h * 512:j * 2048 + (h + 1) * 512], w=[gnw_b])

                    for s in range(NSEG):
                        t0, n = s * SEGL, SEGL
                        for wi, dst in ((0, qT), (1, kT)):
                            pss = []
                            for half in range(2):
                                bk, bb = psum()
                                c0 = wi * 256 + half * 128
                                for kt in range(8):
                                    S.op("pe", lambda e, bk=bk, kt=kt, c0=c0: e.matmul(
                                        bk[:, 0:n], lhsT=sQKv[:, kt, c0:c0 + 128], rhs=hT[:, kt, t0:t0 + n],
                                        start=(kt == 0), stop=(kt == 7)), r=[sQKb, hT_b[s]], w=[bb])
                                pss.append((bk, bb))
                            cs, sn = cosT[:, t0:t0 + n], sinT[:, t0:t0 + n]
                            for half in range(2):
                                S.op("act", lambda e, half=half: e.copy(out=sq[:, half, :], in_=pss[half][0][:, 0:n]),
                                     r=[pss[half][1]], w=[sqb])
                            S.op("dve", lambda e: e.tensor_tensor(out=sq[:, 2, :], in0=sq[:, 0, :], in1=cs, op=ALU.mult),
                                 r=[sqb, rope_b], w=[sqb])
                            S.op("dve", lambda e: e.tensor_tensor(out=sq[:, 3, :], in0=sq[:, 1, :], in1=sn, op=ALU.mult),
                                 r=[sqb, rope_b], w=[sqb])
                            S.op("dve", lambda e: e.tensor_tensor(out=sq[:, 4, :], in0=sq[:, 0, :], in1=sn, op=ALU.mult),
                                 r=[sqb, rope_b], w=[sqb])
                            S.op("dve", lambda e: e.tensor_tensor(out=sq[:, 5, :], in0=sq[:, 1, :], in1=cs, op=ALU.mult),
                                 r=[sqb, rope_b], w=[sqb])
                            S.op("dve", lambda e, dst=dst: e.tensor_tensor(out=dst[:, 0, t0:t0 + n], in0=sq[:, 2, :],
                                                                          in1=sq[:, 3, :], op=ALU.subtract),
                                 r=[sqb], w=[qk_b[s]])
                            S.op("dve", lambda e, dst=dst: e.tensor_tensor(out=dst[:, 1, t0:t0 + n], in0=sq[:, 4, :],
                                                                          in1=sq[:, 5, :], op=ALU.add),
                                 r=[sqb], w=[qk_b[s]])
                    ws_release()
                    for c in range(NT):
                        tk = slice(c * 128, (c + 1) * 128)
                        bk, bb = psum()
                        for kt in range(8):
                            S.op("pe", lambda e, bk=bk, kt=kt: e.matmul(bk[:, :], lhsT=hT[:, kt, tk], rhs=sVv[:, kt, :],
                                                                         start=(kt == 0), stop=(kt == 7)),
                                 r=[sVb, hT_b[c // 2]], w=[bb])
                        S.op("act", lambda e, bk=bk, c=c: e.copy(out=v16[:, c, :], in_=bk[:, :]), r=[bb], w=[v_b[c]])
                    ws_release()
                    sZ, sZb = ws_get()
                    sO, sOb = ws_get()
                    sZv = v8(sZ)
                    sOv = sO[:, :].rearrange("p (v c) -> p v c", v=4)
                    for c in range(NT):
                        tk = slice(c * 128, (c + 1) * 128)
                        bk, bb = psum()
                        bkb = bk[:, :].bitcast(BF16)
                        for dt in range(2):
                            S.op("pe", lambda e, dt=dt, bkb=bkb: e.transpose(bkb[:, dt * 128:(dt + 1) * 128], kT[:, dt, tk],
                                                                             ident_b[:]),
                                 r=[qk_b[c // 2], cst_b], w=[bb])
                        S.op("dve", lambda e, bkb=bkb, c=c: e.tensor_scalar(out=kzf[:, c, :], in0=bkb[:, 0:256],
                                                                            scalar1=ZF[:, h:h + 1], scalar2=None,
                                                                            op0=ALU.mult), r=[bb, lc_b], w=[kz_b[c]])
                        S.op("act", lambda e, bkb=bkb, c=c: e.activation(out=kzb[:, c, :], in_=bkb[:, 0:256], func=AF.Copy,
                                                                         scale=ZF[:, 4 + h:5 + h]), r=[bb, lc_b], w=[kz_b[c]])
                        bk, bb = psum()
                        for dt in range(2):
                            S.op("pe", lambda e, dt=dt, bk=bk: e.matmul(bk[:, 0:128], lhsT=kT[:, dt, tk], rhs=qT[:, dt, tk],
                                                                         start=(dt == 0), stop=(dt == 1)),
                                 r=[qk_b[c // 2]], w=[bb])
                        S.op("dve", lambda e, bk=bk, c=c: e.tensor_tensor(out=sm[:, c, :], in0=bk[:, 0:128], in1=Mh[:, h, :],
                                                                          op=ALU.mult), r=[bb, lc_b], w=[sm_b[c]])

                    def upd_state(d, c, kz):
                        pss = []
                        for dt in range(2):
                            bk, bb = psum()
                            S.op("pe", lambda e, bk=bk, dt=dt: e.matmul(bk[:, :], lhsT=kz[:, c, dt * 128:(dt + 1) * 128],
                                                                         rhs=v16[:, c, :], start=True, stop=True),
                                 r=[kz_b[c], v_b[c]], w=[bb])
                            pss.append((bk, bb))
                        if S32alt[d] is None:
                            dst, dstb = S32[d], S32_b[d]
                        else:
                            dst, dstb = S32alt[d], S32alt_b[d]
                        for dt in range(2):
                            bk, bb = pss[dt]
                            S.op("dve", lambda e, bk=bk, dt=dt: e.scalar_tensor_tensor(
                                out=dst[:, dt, :], in0=S32[d][:, dt, :], scalar=GC[:, d * 4 + h:d * 4 + h + 1],
                                in1=bk[:, :], op0=ALU.mult, op1=ALU.add), r=[bb, lc_b, S32_b[d]], w=[dstb])
                        if S32alt[d] is not None:
                            S32[d], S32alt[d] = S32alt[d], S32[d]
                            S32_b[d], S32alt_b[d] = S32alt_b[d], S32_b[d]

                    def out_state(d, s):
                        st_t, st_bf = stage()
                        S.op("act", lambda e: e.copy(out=st_t[:, :], in_=S32[d][:, :, :].rearrange("p a b -> p (a b)")),
                             r=[S32_b[d]], w=[st_bf])
                        S.dma("sp", nsr_d[s, j, d, h].rearrange("(a p) v -> p a v", p=128),
                              st_t[:, :].rearrange("p (a v) -> p a v", a=2), r=[st_bf])

                    S.op("dve", lambda e: e.memset(S32[1][:, :, :], 0.0), w=[S32_b[1]])
                    for c in range(NT - 1, -1, -1):
                        s = c // 2
                        S.op("act", lambda e, c=c: e.copy(out=Sb16[:, c, :, :], in_=S32[1][:, :, :]),
                             r=[S32_b[1]], w=[Sb16_b[c]])
                        upd_state(1, c, kzb)
                        if c % 2 == 0:
                            out_state(1, s)
                            if c > 0:
                                if s - 1 == 3:
                                    st_t, st_bf = stage()
                                    S.dma("sp", st_t[:, :].rearrange("p (a v) -> p a v", a=2),
                                          s0r_d[j, 1, h].rearrange("(a p) v -> p a v", p=128), w=[st_bf])
                                    S.op("dve", lambda e, s=s, st_t=st_t: e.scalar_tensor_tensor(
                                        out=S32[1][:, :, :], in0=S32[1][:, :, :], scalar=chain_ap(s),
                                        in1=st_t[:, :].rearrange("p (a v) -> p a v", a=2),
                                        op0=ALU.mult, op1=ALU.add), r=[S32_b[1], prm_b, st_bf], w=[S32_b[1]])
                                else:
                                    S.op("dve", lambda e, s=s: e.tensor_scalar(
                                        out=S32[1][:, :, :], in0=S32[1][:, :, :], scalar1=chain_ap(s), scalar2=None,
                                        op0=ALU.mult), r=[S32_b[1], prm_b], w=[S32_b[1]])
                    S.dma("sp", S32[0][:, :, :], s0r_d[j, 0, h].rearrange("(a p) v -> p a v", p=128), w=[S32_b[0]])
                    S.op("act", lambda e: e.copy(out=S16f[:, :, :], in_=S32[0][:, :, :]), r=[S32_b[0]], w=[S16f_b])
                    for c in range(NT):
                        s = c // 2
                        tk = slice(c * 128, (c + 1) * 128)
                        qi = c % 2
                        for d in range(2):
                            S.op("dve", lambda e, d=d, qi=qi: e.tensor_tensor(
                                out=qx[qi][:, d, :, :], in0=qT[:, :, tk],
                                in1=Xi[:, d * 4 + h, :].unsqueeze(1).to_broadcast([128, 2, 128]), op=ALU.mult),
                                r=[qk_b[s], lc_b], w=[qx_b[qi]])
                        bkz, bbz = psum()
                        for kt in range(8):
                            S.op("pe", lambda e, kt=kt: e.matmul(bkz[:, :], lhsT=hT[:, kt, tk], rhs=sZv[:, kt, :],
                                                                  start=(kt == 0), stop=(kt == 7)), r=[sZb, hT_b[s]], w=[bbz])
                        S.op("act", lambda e, qi=qi: e.activation(out=zs[qi][:, :], in_=bkz[:, :], func=AF.Exp, scale=-1.0),
                             r=[bbz], w=[zs_b[qi]])
                        S.op("act", lambda e, qi=qi: e.activation(out=zs[qi][:, :], in_=zs[qi][:, :], func=AF.Ln, bias=cpar[:, 2:3],
                                                                  scale=1.0), r=[zs_b[qi], cst_b], w=[zs_b[qi]])
                        S.op("act", lambda e, qi=qi: e.activation(out=zs[qi][:, :], in_=zs[qi][:, :], func=AF.Exp, scale=-1.0),
                             r=[zs_b[qi]], w=[zs_b[qi]])
                        S.op("dve", lambda e, qi=qi: e.tensor_tensor(out=zs[qi][:, :], in0=bkz[:, :], in1=zs[qi][:, :], op=ALU.mult),
                             r=[bbz, zs_b[qi]], w=[zs_b[qi]])
                        bk, bb = psum()
                        S.op("pe", lambda e, bk=bk, c=c: e.matmul(bk[:, :], lhsT=sm[:, c, :], rhs=v16[:, c, :],
                                                                   start=True, stop=False), r=[sm_b[c], v_b[c]], w=[bb])
                        for dt in range(2):
                            S.op("pe", lambda e, bk=bk, dt=dt: e.matmul(bk[:, :], lhsT=qx[qi][:, 0, dt, :], rhs=S16f[:, dt, :],
                                                                         start=False, stop=False),
                                 r=[qx_b[qi], S16f_b], w=[bb])
                        for dt in range(2):
                            S.op("pe", lambda e, bk=bk, dt=dt, c=c: e.matmul(bk[:, :], lhsT=qx[qi][:, 1, dt, :],
                                                                              rhs=Sb16[:, c, dt, :], start=False,
                                                                              stop=(dt == 1)),
                                 r=[qx_b[qi], Sb16_b[c]], w=[bb])
                        S.op("dve", lambda e, bk=bk: e.bn_stats(out=bst[:, 0:6], in_=bk[:, :]), r=[bb], w=[bst_b])
                        S.op("dve", lambda e: e.bn_aggr(out=bst[:, 8:10], in_=bst[:, 0:6]), r=[bst_b], w=[bst_b])
                        S.op("act", lambda e: e.activation(out=bst[:, 10:11], in_=bst[:, 9:10], func=AF.Ln, bias=cpar[:, 1:2],
                                                           scale=1.0), r=[bst_b, cst_b], w=[bst_b])
                        S.op("act", lambda e: e.activation(out=bst[:, 10:11], in_=bst[:, 10:11], func=AF.Exp, scale=-0.5),
                             r=[bst_b], w=[bst_b])
                        S.op("dve", lambda e, bk=bk: e.tensor_scalar(out=on[:, :], in0=bk[:, :], scalar1=bst[:, 8:9],
                                                                     scalar2=bst[:, 10:11], op0=ALU.subtract,
                                                                     op1=ALU.mult), r=[bb, bst_b], w=[on_b])
                        S.op("dve", lambda e: e.tensor_tensor(out=on[:, :], in0=on[:, :], in1=gnw[:, :], op=ALU.mult),
                             r=[on_b, gnw_b], w=[on_b])
                        S.op("dve", lambda e, qi=qi: e.tensor_tensor(out=og[:, :], in0=on[:, :], in1=zs[qi][:, :], op=ALU.mult),
                             r=[on_b, zs_b[qi]], w=[og_b])
                        bk2, bb2 = psum()
                        bk2b = bk2[:, :].bitcast(BF16)
                        for vt in range(4):
                            S.op("pe", lambda e, vt=vt, bk2b=bk2b: e.transpose(
                                bk2b[:, vt * 128:(vt + 1) * 128], og[:, vt * 128:(vt + 1) * 128], ident_b[:]),
                                r=[og_b, cst_b], w=[bb2])
                        oi = s % 2
                        lo = (c % 2) * 128
                        S.op("act", lambda e, bk2b=bk2b, oi=oi, lo=lo: e.copy(
                            out=oT[oi][:, :, lo:lo + 128], in_=bk2b[:, 0:512].rearrange("p (v t) -> p v t", v=4)),
                            r=[bb2], w=[oT_b[oi]])
                        upd_state(0, c, kzf)
                        if c % 2 == 1:
                            out_state(0, s)
                            if c < NT - 1:
                                S.op("dve", lambda e, s=s: e.tensor_scalar(
                                    out=S32[0][:, :, :], in0=S32[0][:, :, :], scalar1=chain_ap(s + 1), scalar2=None,
                                    op0=ALU.mult), r=[S32_b[0], prm_b], w=[S32_b[0]])
                        if c < NT - 1:
                            S.op("act", lambda e: e.copy(out=S16f[:, :, :], in_=S32[0][:, :, :]), r=[S32_b[0]], w=[S16f_b])
                        if c % 2 == 1:
                            seg = slice(s * SEGL, (s + 1) * SEGL)
                            for dt in range(8):
                                bk, bb = psum()
                                for vt in range(4):
                                    S.op("pe", lambda e, vt=vt, dt=dt, bk=bk, oi=oi: e.matmul(
                                        bk[:, 0:SEGL], lhsT=sOv[:, vt, dt * 128:(dt + 1) * 128], rhs=oT[oi][:, vt, :],
                                        start=(vt == 0), stop=(vt == 3)), r=[sOb, oT_b[oi]], w=[bb])
                                S.op("dve", lambda e, bk=bk, dt=dt, s=s: e.scalar_tensor_tensor(
                                    out=xT[:, dt, seg], in0=bk[:, 0:SEGL], scalar=mod_ap(l, 2, dt, s), in1=xT[:, dt, seg],
                                    op0=ALU.mult, op1=ALU.add), r=[bb, modT_b, xT_b[s]], w=[xT_b[s]])
                    ws_release()
                    ws_release()
                S.barrier()

        def del_layer(l, j):
            with contextlib.ExitStack() as ls:
                DH = 8
                tri = [sb(f"tri{d}", [128, 128], st=ls) for d in range(2)]
                neg3 = [sb(f"neg3{d}", [128, 128], st=ls) for d in range(2)]
                pos1 = [sb(f"pos1{d}", [128, 128], st=ls) for d in range(2)]
                dc_b = Buf("dconst")
                for d in range(2):
                    S.op("pool", lambda e, d=d: e.memset(tri[d][:, :], 1.0), w=[dc_b])
                    pat, cm = ([[1, 128]], -1) if d == 0 else ([[-1, 128]], 1)
                    S.op("pool", lambda e, d=d, pat=pat, cm=cm: e.affine_select(
                        out=tri[d][:, :], in_=tri[d][:, :], pattern=pat, compare_op=ALU.is_ge, fill=0.0, base=0,
                        channel_multiplier=cm), r=[dc_b], w=[dc_b])
                    S.op("pool", lambda e, d=d: e.memset(neg3[d][:, :], 0.0), w=[dc_b])
                    S.op("pool", lambda e, d=d, pat=pat, cm=cm: e.affine_select(
                        out=neg3[d][:, :], in_=neg3[d][:, :], pattern=pat, compare_op=ALU.is_ge, fill=-BIG, base=0,
                        channel_multiplier=cm), r=[dc_b], w=[dc_b])
                    pat2, cm2 = ([[-1, 128]], 1) if d == 0 else ([[1, 128]], -1)
                    S.op("pool", lambda e, d=d: e.memset(pos1[d][:, :], 0.0), w=[dc_b])
                    S.op("pool", lambda e, d=d, pat2=pat2, cm2=cm2: e.affine_select(
                        out=pos1[d][:, :], in_=pos1[d][:, :], pattern=pat2, compare_op=ALU.is_gt, fill=BIG, base=0,
                        channel_multiplier=cm2), r=[dc_b], w=[dc_b])
                abraw = sb("abraw", [128, NT, 32], st=ls)
                tk_b = Buf("tokscal")
                tsc = sb("tsc", [128, 10, NT, 16], st=ls)
                U, L1, GG, LNB, BB, GT, EGt, NBEG, NEGG, GPL = [tsc[:, i, :, :] for i in range(10)]
                negA = sb("negA", [128, 16], st=ls)
                dnw = sb("dnw", [128, 256], st=ls)
                dnw_b = Buf("dnw")
                XP = [sb("XP0", [128, NSEG, 258], st=ls)] * 2
                XP_b = [Buf("XP0")] * 2
                acc = sqs[0][:, 0:NSEG, :]
                acc_b = sqs_b[0]
                tmb = sb("tmb", [128, NTOK], BF16, st=ls)
                tmb_b = Buf("tmb")
                rinv = sqs[0][:, 5:7, :].rearrange("p a b -> p (a b)")
                rinv_b = sqs_b[0]
                qT = sb("dqT", [128, NTOK], BF16, st=ls)
                kT = sb("dkT", [128, NTOK], BF16, st=ls)
                q_b, k_b = Buf("dq"), Buf("dk")
                bv = sb("bv", [128, NT * 2, 256], BF16, st=ls)
                bv_b = [Buf(f"bv{c}") for c in range(NT)]
                kd = sb("kd", [128, NT * 2, 128], BF16, st=ls)
                kd_b = [Buf(f"kd{c}") for c in range(NT)]
                TT = sb("TT", [128, NT * 2, 128], BF16, st=ls)
                TT_b = [Buf(f"TT{g}") for g in range(NSEG)]
                PTm = sb("PTm", [128, NT * 2, 128], BF16, st=ls)
                qg = sb("qg", [128, NT * 2, 128], BF16, st=ls)
                pq_b = [Buf(f"pq{c}") for c in range(NT)]
                csc = sb("csc", [128, 2, NT * 2], st=ls)
                csc_b = [Buf(f"csc{c}") for c in range(NT)]
                Es = [sb(f"Es{i}", [128, 3, 128], st=ls) for i in range(2)]
                Es_b = [Buf(f"Es{i}") for i in range(2)]
                def g4(nm):
                    return sb(nm, [128, 4, 128], BF16, st=ls), Buf(nm)
                Xg, Xg_b = zip(*[g4(f"Xg{i}") for i in range(2)])
                Yg, Yg_b = zip(*[g4(f"Yg{i}") for i in range(2)])
                Pg, Pg_b = zip(*[g4(f"Pg{i}") for i in range(2)])
                Qg, Qg_b = zip(*[g4(f"Qg{i}") for i in range(2)])
                id4, _ = g4("id4")
                Wg, Wg_b = zip(*[g4(f"Wg{i}") for i in range(2)])
                Af, Af_b = g4("Af")
                Bf, Bf_b = g4("Bf")
                bmask = sb("bmask", [128, 4, 128], BF16, st=ls)
                S.dma("pool", bmask[:, :, :].rearrange("p a b -> p (a b)"), bmask_d, w=[dc_b])
                bmaskn = sb("bmaskn", [128, 4, 4, 128], BF16, st=ls)
                for q_ in range(4):
                    S.op("dve", lambda e, q_=q_: e.tensor_scalar(out=bmaskn[:, :, q_, :], in0=bmask[:, :, :], scalar1=-1.0,
                                                                 scalar2=None, op0=ALU.mult), r=[dc_b], w=[dc_b])
                for q_ in range(4):
                    S.op("dve", lambda e, q_=q_: e.tensor_copy(out=id4[:, q_, :], in_=ident_b[:, :]), r=[cst_b, dc_b], w=[dc_b])
                ob = sb("ob", [128, NT, 256], st=ls)
                ob_b = [Buf(f"ob{c}") for c in range(NT)]
                S32 = [sb(f"dS32_{d}", [128, 256], st=ls) for d in range(2)]
                S32_b = [Buf(f"dS32_{d}") for d in range(2)]
                S16 = [sb(f"dS16_{d}", [128, 256], BF16, st=ls) for d in range(2)]
                S16_b = [Buf(f"dS16_{d}") for d in range(2)]
                rr = [sb(f"rr{d}", [128, 256], BF16, st=ls) for d in range(2)]
                rr_b = [Buf(f"rr{d}") for d in range(2)]
                vn16 = [sb(f"vn{d}", [128, 256], BF16, st=ls) for d in range(2)]
                vn_b = [Buf(f"vn{d}") for d in range(2)]
                zs = [sb(f"dzs{i}", [128, 256], st=ls) for i in range(2)]
                zs_b = [Buf(f"dzs{i}") for i in range(2)]
                on = [sb(f"don{i}", [128, 256], st=ls) for i in range(2)]
                on_b = [Buf(f"don{i}") for i in range(2)]
                og = [sb(f"dog{i}", [128, 256], BF16, st=ls) for i in range(2)]
                og_b = [Buf(f"dog{i}") for i in range(2)]
                oT = [sb(f"doT{i}", [128, 2, SEGL], BF16, st=ls) for i in range(NSEG)]
                oT_b = [Buf(f"doT{i}") for i in range(NSEG)]
                bst = sb("dbst", [128, 2, 8], st=ls)
                bst_b = [Buf("dbst0"), Buf("dbst1")]

                norm_mod(l)

                sAB, sABb = ws_get()
                sABv = v8(sAB, 32)
                for c in range(NT):
                    tk = slice(c * 128, (c + 1) * 128)
                    bk, bb = psum()
                    for kt in range(8):
                        S.op("pe", lambda e, bk=bk, kt=kt: e.matmul(bk[:, 0:32], lhsT=hT[:, kt, tk], rhs=sABv[:, kt, :],
                                                                     start=(kt == 0), stop=(kt == 7)),
                             r=[sABb, hT_b[c // 2]], w=[bb])
                    S.op("act", lambda e, bk=bk, c=c: e.copy(out=abraw[:, c, :], in_=bk[:, 0:32]), r=[bb], w=[tk_b])
                ws_release()
                al = prm[:, o_alog + j * 16:o_alog + j * 16 + 16]
                dtb = prm[:, o_dtb + j * 16:o_dtb + j * 16 + 16]
                T = [tk_b, prm_b, cst_b]
                S.op("act", lambda e: e.activation(out=negA[:, :], in_=al, func=AF.Exp), r=T, w=[tk_b])
                S.op("dve", lambda e: e.tensor_scalar(out=negA[:, :], in0=negA[:, :], scalar1=-1.0, scalar2=None,
                                                      op0=ALU.mult), r=T, w=[tk_b])
                S.op("dve", lambda e: e.tensor_tensor(out=U, in0=abraw[:, :, 0:16],
                                                      in1=dtb.unsqueeze(1).to_broadcast([128, NT, 16]), op=ALU.add),
                     r=T, w=[tk_b])
                S.op("dve", lambda e: e.tensor_scalar(out=L1, in0=U, scalar1=-1.0, scalar2=None, op0=ALU.mult), r=T, w=[tk_b])
                S.op("dve", lambda e: e.tensor_tensor(out=L1, in0=L1, in1=U, op=ALU.max), r=T, w=[tk_b])
                S.op("act", lambda e: e.activation(out=L1, in_=L1, func=AF.Exp, scale=-1.0), r=T, w=[tk_b])
                S.op("act", lambda e: e.activation(out=L1, in_=L1, func=AF.Ln, bias=cpar[:, 2:3], scale=1.0), r=T, w=[tk_b])
                S.op("dve", lambda e: e.tensor_scalar(out=U, in0=U, scalar1=0.0, scalar2=None, op0=ALU.max), r=T, w=[tk_b])
                S.op("dve", lambda e: e.tensor_tensor(out=U, in0=U, in1=L1, op=ALU.add), r=T, w=[tk_b])
                S.op("dve", lambda e: e.tensor_tensor(out=GG, in0=U, in1=negA[:, :].unsqueeze(1).to_broadcast([128, NT, 16]),
                                                      op=ALU.mult), r=T, w=[tk_b])
                S.op("dve", lambda e: e.tensor_scalar(out=L1, in0=abraw[:, :, 16:32], scalar1=-1.0, scalar2=None, op0=ALU.mult),
                     r=T, w=[tk_b])
                S.op("dve", lambda e: e.tensor_tensor(out=L1, in0=L1, in1=abraw[:, :, 16:32], op=ALU.max), r=T, w=[tk_b])
                S.op("act", lambda e: e.activation(out=L1, in_=L1, func=AF.Exp, scale=-1.0), r=T, w=[tk_b])
                S.op("act", lambda e: e.activation(out=L1, in_=L1, func=AF.Ln, bias=cpar[:, 2:3], scale=1.0), r=T, w=[tk_b])
                S.op("dve", lambda e: e.tensor_scalar(out=LNB, in0=abraw[:, :, 16:32], scalar1=0.0, scalar2=None,
                                                      op0=ALU.min), r=T, w=[tk_b])
                S.op("dve", lambda e: e.tensor_tensor(out=LNB, in0=LNB, in1=L1, op=ALU.subtract), r=T, w=[tk_b])
                S.op("act", lambda e: e.activation(out=BB, in_=LNB, func=AF.Exp), r=T, w=[tk_b])
                bk, bb = psum()
                for c in range(NT):
                    for d in range(2):
                        S.op("pe", lambda e, c=c, d=d: e.matmul(bk[:, c * 16 + d * 8:c * 16 + d * 8 + 8], lhsT=tri[d][:, :],
                                                                 rhs=GG[:, c, d * 8:d * 8 + 8], start=True, stop=True),
                             r=[tk_b, dc_b], w=[bb])
                S.op("dve", lambda e: e.tensor_copy(out=GT, in_=bk[:, 0:NT * 16].rearrange("p (c x) -> p c x", c=NT)),
                     r=[bb], w=[tk_b])
                S.op("act", lambda e: e.activation(out=EGt, in_=GT, func=AF.Exp), r=T, w=[tk_b])
                S.op("dve", lambda e: e.tensor_tensor(out=NBEG, in0=BB, in1=EGt, op=ALU.mult), r=T, w=[tk_b])
                S.op("dve", lambda e: e.tensor_scalar(out=NBEG, in0=NBEG, scalar1=-1.0, scalar2=None, op0=ALU.mult),
                     r=T, w=[tk_b])
                S.op("dve", lambda e: e.tensor_scalar(out=NEGG, in0=GT, scalar1=-1.0, scalar2=None, op0=ALU.mult),
                     r=T, w=[tk_b])
                S.op("dve", lambda e: e.tensor_tensor(out=GPL, in0=GT, in1=LNB, op=ALU.add), r=T, w=[tk_b])
                for i in range(2):
                    S.op("dve", lambda e, i=i: e.memset(XP[i][:, 0, 0:1], 0.0), w=[XP_b[i]])
                    S.op("dve", lambda e, i=i: e.memset(XP[i][:, NSEG - 1, 257:258], 0.0), w=[XP_b[i]])

                dstage = float(os.environ.get("K_DSTAGE", "9"))
                if dstage == 0:
                    S.barrier()
                    return
                xp_rr = [0]
                for h in range(int(os.environ.get("K_DHEADS", "8"))):
                    sW, sWb = ws_get()
                    sWv = v8(sW)
                    S.dma("sp", dnw[:, :], dnw_d[:, j * 2048 + h * 256:j * 2048 + (h + 1) * 256], w=[dnw_b])
                    for ct in range(4):
                        xi = xp_rr[0] % 2
                        xp_rr[0] += 1
                        xp, xpb = XP[xi], XP_b[xi]
                        for s in range(NSEG):
                            bk, bb = psum()
                            for kt in range(8):
                                S.op("pe", lambda e, bk=bk, kt=kt, s=s: e.matmul(
                                    bk[:, 0:SEGL], lhsT=sWv[:, kt, ct * 128:(ct + 1) * 128],
                                    rhs=hT[:, kt, s * SEGL:(s + 1) * SEGL], start=(kt == 0), stop=(kt == 7)),
                                    r=[sWb, hT_b[s]], w=[bb])
                            S.op("act", lambda e, bk=bk, s=s: e.copy(out=xp[:, s, 1:257], in_=bk[:, 0:SEGL]), r=[bb], w=[xpb])
                        chn = prm[:, o_chain + 1:o_chain + 5]
                        S.op("dve", lambda e: e.tensor_tensor(out=xp[:, 1:5, 0], in0=xp[:, 0:4, 256], in1=chn, op=ALU.mult),
                             r=[xpb, prm_b], w=[xpb])
                        S.op("dve", lambda e: e.tensor_tensor(out=xp[:, 0:4, 257], in0=xp[:, 1:5, 1], in1=chn, op=ALU.mult),
                             r=[xpb, prm_b], w=[xpb])
                        gct = h if ct == 0 else (8 + h if ct == 1 else 16 + 2 * h + (ct - 2))
                        cw = [prm[:, o_convw + (j * 3 + k) * 32 + gct:o_convw + (j * 3 + k) * 32 + gct + 1] for k in range(3)]
                        S.op("act", lambda e: e.activation(out=acc, in_=xp[:, :, 0:256], func=AF.Copy, scale=cw[0]),
                             r=[xpb, prm_b], w=[acc_b])
                        S.op("dve", lambda e: e.scalar_tensor_tensor(out=acc, in0=xp[:, :, 1:257], scalar=cw[1],
                                                                     in1=acc, op0=ALU.mult, op1=ALU.add),
                             r=[xpb, prm_b, acc_b], w=[acc_b])
                        S.op("dve", lambda e: e.scalar_tensor_tensor(out=acc, in0=xp[:, :, 2:258], scalar=cw[2],
                                                                     in1=acc, op0=ALU.mult, op1=ALU.add),
                             r=[xpb, prm_b, acc_b], w=[acc_b])
                        accf = acc.rearrange("p s t -> p (s t)")
                        S.op("act", lambda e: e.activation(out=accf, in_=accf, func=AF.Silu), r=[acc_b], w=[acc_b])
                        if ct < 2:
                            S.op("act", lambda e: e.activation(out=tmb[:, :], in_=accf, func=AF.Square), r=[acc_b], w=[tmb_b])
                            dst, dstb = (qT, q_b) if ct == 0 else (kT, k_b)
                            scl = (128.0 ** -0.5) if ct == 0 else 1.0
                            for (t0, n) in BLOCKS:
                                bk, bb = psum()
                                S.op("pe", lambda e, bk=bk: e.matmul(bk[:, 0:n], lhsT=ones_b[:, :], rhs=tmb[:, t0:t0 + n],
                                                                      start=True, stop=True), r=[tmb_b, cst_b], w=[bb])
                                S.op("act", lambda e, bk=bk: e.activation(out=rinv[:, 0:n], in_=bk[:, 0:n], func=AF.Ln,
                                                                          bias=cpar[:, 1:2], scale=1.0),
                                     r=[bb, cst_b], w=[rinv_b])
                                S.op("act", lambda e: e.activation(out=rinv[:, 0:n], in_=rinv[:, 0:n], func=AF.Exp, scale=-0.5),
                                     r=[rinv_b], w=[rinv_b])
                                S.op("dve", lambda e, dst=dst: e.scalar_tensor_tensor(
                                    out=dst[:, t0:t0 + n], in0=accf[:, t0:t0 + n], scalar=scl, in1=rinv[:, 0:n],
                                    op0=ALU.mult, op1=ALU.mult), r=[acc_b, rinv_b], w=[dstb])
                        else:
                            vt = ct - 2
                            S.op("act", lambda e: e.copy(out=tmb[:, :], in_=accf), r=[acc_b], w=[tmb_b])
                            for c in range(NT):
                                tk = slice(c * 128, (c + 1) * 128)
                                bk, bb = psum()
                                bkb = bk[:, :].bitcast(BF16)
                                S.op("pe", lambda e, bkb=bkb: e.transpose(bkb[:, 0:128], tmb[:, tk], ident_b[:]),
                                     r=[tmb_b, cst_b], w=[bb])
                                S.op("dve", lambda e, bkb=bkb, c=c: e.tensor_scalar(
                                    out=bv[:, c * 2 + 0, vt * 128:(vt + 1) * 128], in0=bkb[:, 0:128],
                                    scalar1=BB[:, c, h:h + 1], scalar2=None, op0=ALU.mult), r=[bb, tk_b], w=[bv_b[c]])
                                S.op("act", lambda e, bkb=bkb, c=c: e.activation(
                                    out=bv[:, c * 2 + 1, vt * 128:(vt + 1) * 128], in_=bkb[:, 0:128], func=AF.Copy,
                                    scale=BB[:, c, 8 + h:9 + h]), r=[bb, tk_b], w=[bv_b[c]])
                    ws_release()
                    if dstage == 1:
                        S.barrier()
                        return
                    sZ, sZb = ws_get()
                    sO, sOb = ws_get()
                    sZv = v8(sZ, 256)
                    sOv = sO[:, 0:2048].rearrange("p (v c) -> p v c", v=2)

                    for g in range(NSEG):
                        kkps = []
                        for ci in range(2):
                            c = 2 * g + ci
                            tk = slice(c * 128, (c + 1) * 128)
                            bk, bb = psum()
                            S.op("pe", lambda e, bk=bk: e.matmul(bk[:, 0:128], lhsT=kT[:, tk], rhs=kT[:, tk], start=True, stop=True),
                                 r=[k_b], w=[bb])
                            S.op("pe", lambda e, bk=bk: e.matmul(bk[:, 128:256], lhsT=kT[:, tk], rhs=qT[:, tk], start=True,
                                                                  stop=True), r=[k_b, q_b], w=[bb])
                            bkt = bk[:, :].bitcast(BF16)
                            S.op("pe", lambda e, bkt=bkt: e.transpose(bkt[:, 512:640], kT[:, tk], ident_b[:]),
                                 r=[k_b, cst_b], w=[bb])
                            kkps.append((bk, bb, bkt))
                        if dstage == 1.1:
                            S.barrier()
                            return
                        xg, xgb = Af, Af_b
                        for ci in range(2):
                            c = 2 * g + ci
                            tk = slice(c * 128, (c + 1) * 128)
                            bk, bb, bkt = kkps[ci]
                            for d in range(2):
                                qd = ci * 2 + d
                                cd = c * 2 + d
                                dh = d * 8 + h
                                ei = qd % 2
                                es, esb = Es[ei], Es_b[ei]
                                be, bbe = psum()
                                gcol = GG[:, c, dh:dh + 1].to_broadcast([128, 128])
                                S.op("pe", lambda e, be=be, d=d: e.matmul(be[:, 0:128], lhsT=gcol, rhs=tri[d][:, :], start=True,
                                                                           stop=True), r=[tk_b, dc_b], w=[bbe])
                                S.op("pe", lambda e, be=be, d=d: e.matmul(be[:, 128:256], lhsT=gcol, rhs=tri[d][:, :], start=True,
                                                                           stop=False), r=[tk_b, dc_b], w=[bbe])
                                S.op("pe", lambda e, be=be, d=d: e.matmul(be[:, 128:256], lhsT=ident_f[:, :], rhs=neg3[d][:, :],
                                                                           start=False, stop=True), r=[cst_b, dc_b], w=[bbe])
                                S.op("pe", lambda e, be=be, d=d: e.matmul(be[:, 256:384], lhsT=gcol, rhs=tri[d][:, :], start=True,
                                                                           stop=False), r=[tk_b, dc_b], w=[bbe])
                                S.op("pe", lambda e, be=be, d=d: e.matmul(be[:, 256:384], lhsT=ident_f[:, :], rhs=pos1[d][:, :],
                                                                           start=False, stop=True), r=[cst_b, dc_b], w=[bbe])
                                S.op("act", lambda e, be=be, es=es: e.activation(out=es[:, 0, :], in_=be[:, 0:128], func=AF.Exp),
                                     r=[bbe], w=[esb])
                                S.op("act", lambda e, be=be, es=es, c=c, dh=dh: e.activation(
                                    out=es[:, 1, :], in_=be[:, 128:256], func=AF.Exp, bias=NEGG[:, c, dh:dh + 1], scale=1.0),
                                    r=[bbe, tk_b], w=[esb])
                                S.op("act", lambda e, be=be, es=es, c=c, dh=dh: e.activation(
                                    out=es[:, 2, :], in_=be[:, 256:384], func=AF.Exp, bias=GPL[:, c, dh:dh + 1], scale=-1.0),
                                    r=[bbe, tk_b], w=[esb])
                                last = 127 if d == 0 else 0
                                S.op("dve", lambda e, es=es, cd=cd, last=last: e.tensor_copy(
                                    out=csc[:, 0, cd:cd + 1], in_=es[:, 0, last:last + 1]), r=[esb], w=[csc_b[c]])
                                S.op("dve", lambda e, es=es, cd=cd, last=last: e.tensor_copy(
                                    out=csc[:, 1, cd:cd + 1], in_=es[:, 1, last:last + 1]), r=[esb], w=[csc_b[c]])
                                S.op("dve", lambda e, es=es, bk=bk, qd=qd: e.tensor_tensor(
                                    out=xg[:, qd, :], in0=bk[:, 0:128], in1=es[:, 2, :], op=ALU.mult), r=[bb, esb], w=[xgb])
                                S.op("dve", lambda e, es=es, bk=bk, cd=cd: e.tensor_tensor(
                                    out=PTm[:, cd, :], in0=bk[:, 128:256], in1=es[:, 1, :], op=ALU.mult), r=[bb, esb], w=[pq_b[c]])
                                S.op("dve", lambda e, es=es, cd=cd: e.tensor_tensor(
                                    out=qg[:, cd, :], in0=qT[:, tk], in1=es[:, 0, :], op=ALU.mult), r=[q_b, esb], w=[pq_b[c]])
                                if d == 0:
                                    S.op("dve", lambda e, bkt=bkt, cd=cd: e.tensor_scalar(
                                        out=kd[:, cd, :], in0=bkt[:, 512:640], scalar1=csc[:, 1, cd:cd + 1], scalar2=None,
                                        op0=ALU.mult), r=[bb, csc_b[c]], w=[kd_b[c]])
                                else:
                                    S.op("act", lambda e, bkt=bkt, cd=cd: e.activation(
                                        out=kd[:, cd, :], in_=bkt[:, 512:640], func=AF.Copy, scale=csc[:, 1, cd:cd + 1]),
                                        r=[bb, csc_b[c]], w=[kd_b[c]])
                        if dstage == 1.2:
                            S.barrier()
                            return
                        bt, bbt = psum()
                        btb = bt[:, :].bitcast(BF16)
                        for qd in range(4):
                            S.op("pe", lambda e, qd=qd, btb=btb: e.transpose(btb[:, qd * 128:(qd + 1) * 128], xg[:, qd, :],
                                                                             ident_b[:]), r=[xgb, cst_b], w=[bbt])
                        bt4 = btb[:, 0:512].rearrange("p (q i) -> p q i", q=4)
                        S.op("act", lambda e: e.copy(out=Bf[:, :, :], in_=bt4), r=[bbt], w=[Bf_b])

                        def mk(k_, neg=False):
                            if neg:
                                return bmaskn[:, k_, :, :]
                            return bmask[:, k_, :].unsqueeze(1).to_broadcast([128, 4, 128])

                        S.op("dve", lambda e: e.tensor_tensor(out=Xg[0][:, :, :], in0=xg[:, :, :], in1=mk(0), op=ALU.mult),
                             r=[xgb, dc_b], w=[Xg_b[0]])
                        S.op("dve", lambda e: e.tensor_tensor(out=Yg[0][:, :, :], in0=Bf[:, :, :], in1=mk(0), op=ALU.mult),
                             r=[Bf_b, dc_b], w=[Yg_b[0]])
                        S.op("dve", lambda e: e.scalar_tensor_tensor(out=Pg[0][:, :, :], in0=Yg[0][:, :, :], scalar=-1.0, in1=id4[:, :, :],
                                                                     op0=ALU.mult, op1=ALU.add), r=[Yg_b[0], dc_b], w=[Pg_b[0]])
                        S.op("dve", lambda e: e.scalar_tensor_tensor(out=Qg[0][:, :, :], in0=Xg[0][:, :, :], scalar=-1.0, in1=id4[:, :, :],
                                                                     op0=ALU.mult, op1=ALU.add), r=[Xg_b[0], dc_b], w=[Qg_b[0]])
                        if dstage == 1.3:
                            S.barrier()
                            return

                        def mm4(L, Lb, R, Rb, acc=None):
                            bk_, bb_ = psum()
                            for qd in range(4):
                                o_ = bk_[:, qd * 128:(qd + 1) * 128]
                                if acc is not None:
                                    S.op("pe", lambda e, qd=qd: e.matmul(o_, lhsT=ident_b[:, :], rhs=acc[0][:, qd, :], start=True,
                                                                          stop=False), r=[cst_b, acc[1]], w=[bb_])
                                S.op("pe", lambda e, qd=qd: e.matmul(o_, lhsT=L[:, qd, :], rhs=R[:, qd, :], start=(acc is None),
                                                                      stop=True), r=[Lb, Rb], w=[bb_])
                            return bk_[:, :].rearrange("p (q i) -> p q i", q=4), bb_

                        def ev_copy(eng, dst, dstb, ps, psb):
                            if eng == "act":
                                S.op("act", lambda e: e.copy(out=dst, in_=ps), r=[psb], w=[dstb])
                            else:
                                S.op("dve", lambda e: e.tensor_copy(out=dst, in_=ps), r=[psb], w=[dstb])

                        def ev_add(dst, dstb, ps, psb, old_, oldb):
                            S.op("dve", lambda e: e.tensor_tensor(out=dst, in0=ps, in1=old_[:, :, :], op=ALU.add),
                                 r=[psb, oldb], w=[dstb])

                        def ev_mask(dst, dstb, ps, psb, k_):
                            S.op("dve", lambda e: e.tensor_tensor(out=dst[:, :, :], in0=ps, in1=mk(k_, True), op=ALU.mult),
                                 r=[psb, dc_b], w=[dstb])

                        pi = 0
                        for m in range(1, 4):
                            a, b2 = (m - 1) % 2, m % 2
                            px, pxb = mm4(Yg[a], Yg_b[a], Xg[a], Xg_b[a])
                            py, pyb = mm4(Xg[a], Xg_b[a], Yg[a], Yg_b[a])
                            ev_copy("act", Xg[b2][:, :, :], Xg_b[b2], px, pxb)
                            ev_copy("act", Yg[b2][:, :, :], Yg_b[b2], py, pyb)
                            pp, ppb = mm4(Xg[b2], Xg_b[b2], Pg[pi], Pg_b[pi])
                            pq, pqb = mm4(Yg[b2], Yg_b[b2], Qg[pi], Qg_b[pi])
                            ev_add(Pg[1 - pi][:, :, :], Pg_b[1 - pi], pp, ppb, Pg[pi], Pg_b[pi])
                            ev_add(Qg[1 - pi][:, :, :], Qg_b[1 - pi], pq, pqb, Qg[pi], Qg_b[pi])
                            pi = 1 - pi
                        for s_ in range(1, 4):
                            pw1, pw1b = mm4(xg, xgb, Pg[pi], Pg_b[pi])
                            ev_mask(Wg[0], Wg_b[0], pw1, pw1b, s_)
                            if s_ < 3:
                                pw2, pw2b = mm4(Bf, Bf_b, Qg[pi], Qg_b[pi])
                                ev_mask(Wg[1], Wg_b[1], pw2, pw2b, s_)
                            pp, ppb = mm4(Qg[pi], Qg_b[pi], Wg[0], Wg_b[0])
                            if s_ < 3:
                                pq, pqb = mm4(Pg[pi], Pg_b[pi], Wg[1], Wg_b[1])
                                ev_add(Pg[1 - pi][:, :, :], Pg_b[1 - pi], pp, ppb, Pg[pi], Pg_b[pi])
                                ev_add(Qg[1 - pi][:, :, :], Qg_b[1 - pi], pq, pqb, Qg[pi], Qg_b[pi])
                                pi = 1 - pi
                            else:
                                ev_add(TT[:, g * 4:(g + 1) * 4, :], TT_b[g], pp, ppb, Pg[pi], Pg_b[pi])

                    if dbg_d and h == 0:
                        def dump(name, ap, bufs):
                            if name in dbg_d:
                                S.dma("pool", dbg_d[name], ap, r=bufs)
                        dump("tsc", tsc[:, :, :, :].rearrange("p a c x -> p (a c x)"), [tk_b])
                        dump("qT", qT[:, :], [q_b])
                        dump("kT", kT[:, :], [k_b])
                        dump("bv", bv[:, :, :].rearrange("p a b -> p (a b)"), bv_b)
                        dump("kd", kd[:, :, :].rearrange("p a b -> p (a b)"), kd_b)
                        dump("TT", TT[:, :, :].rearrange("p a b -> p (a b)"), TT_b)
                        dump("PTm", PTm[:, :, :].rearrange("p a b -> p (a b)"), pq_b)
                        dump("qg", qg[:, :, :].rearrange("p a b -> p (a b)"), pq_b)
                        dump("csc", csc[:, :, :].rearrange("p a b -> p (a b)"), csc_b)
                    if dstage == 2:
                        S.barrier()
                        return
                    S.dma("sp", S32[0][:, :], s0d_d[j, 0, h], w=[S32_b[0]])
                    S.op("dve", lambda e: e.memset(S32[1][:, :], 0.0), w=[S32_b[1]])
                    for d in range(2):
                        S.op("act", lambda e, d=d: e.copy(out=S16[d][:, :], in_=S32[d][:, :]), r=[S32_b[d]], w=[S16_b[d]])
                    arrived = [0] * NT
                    seg_done = [0] * NSEG

                    def consume_o(c, bo, bbo, first):
                        if first:
                            S.op("act", lambda e: e.copy(out=ob[:, c, :], in_=bo[:, 0:256]), r=[bbo], w=[ob_b[c]])
                        else:
                            oi = c % 2
                            S.op("dve", lambda e: e.tensor_tensor(out=on[oi][:, :], in0=bo[:, 0:256], in1=ob[:, c, :], op=ALU.add),
                                 r=[bbo, ob_b[c]], w=[on_b[oi]])

                    def finish_chunk(c, first):
                        s = c // 2
                        if first:
                            return
                        oi = c % 2
                        tk = slice(c * 128, (c + 1) * 128)
                        bkz, bbz = psum()
                        for kt in range(8):
                            S.op("pe", lambda e, kt=kt: e.matmul(bkz[:, 0:256], lhsT=hT[:, kt, tk], rhs=sZv[:, kt, :],
                                                                  start=(kt == 0), stop=(kt == 7)), r=[sZb, hT_b[s]], w=[bbz])
                        S.op("act", lambda e: e.activation(out=zs[oi][:, :], in_=bkz[:, 0:256], func=AF.Exp, scale=-1.0),
                             r=[bbz], w=[zs_b[oi]])
                        S.op("act", lambda e: e.activation(out=zs[oi][:, :], in_=zs[oi][:, :], func=AF.Ln, bias=cpar[:, 2:3], scale=1.0),
                             r=[zs_b[oi], cst_b], w=[zs_b[oi]])
                        S.op("act", lambda e: e.activation(out=zs[oi][:, :], in_=zs[oi][:, :], func=AF.Exp, scale=-1.0),
                             r=[zs_b[oi]], w=[zs_b[oi]])
                        S.op("dve", lambda e: e.tensor_tensor(out=zs[oi][:, :], in0=bkz[:, 0:256], in1=zs[oi][:, :], op=ALU.mult),
                             r=[bbz, zs_b[oi]], w=[zs_b[oi]])
                        S.op("dve", lambda e: e.bn_stats(out=bst[:, oi, 0:6], in_=on[oi][:, :]), r=[on_b[oi]], w=[bst_b[oi]])
                        S.op("dve", lambda e: e.bn_aggr(out=bst[:, oi, 6:8], in_=bst[:, oi, 0:6]), r=[bst_b[oi]], w=[bst_b[oi]])
                        S.op("dve", lambda e: e.scalar_tensor_tensor(out=bst[:, oi, 0:1], in0=bst[:, oi, 6:7], scalar=bst[:, oi, 6:7],
                                                                     in1=bst[:, oi, 7:8], op0=ALU.mult, op1=ALU.add),
                             r=[bst_b[oi]], w=[bst_b[oi]])
                        S.op("act", lambda e: e.activation(out=bst[:, oi, 1:2], in_=bst[:, oi, 0:1], func=AF.Ln, bias=cpar[:, 1:2],
                                                           scale=1.0), r=[bst_b[oi], cst_b], w=[bst_b[oi]])
                        S.op("act", lambda e: e.activation(out=bst[:, oi, 1:2], in_=bst[:, oi, 1:2], func=AF.Exp, scale=-0.5),
                             r=[bst_b[oi]], w=[bst_b[oi]])
                        S.op("dve", lambda e: e.scalar_tensor_tensor(out=on[oi][:, :], in0=on[oi][:, :], scalar=bst[:, oi, 1:2],
                                                                     in1=dnw[:, :], op0=ALU.mult, op1=ALU.mult),
                             r=[on_b[oi], bst_b[oi], dnw_b], w=[on_b[oi]])
                        S.op("dve", lambda e: e.tensor_tensor(out=og[oi][:, :], in0=on[oi][:, :], in1=zs[oi][:, :], op=ALU.mult),
                             r=[on_b[oi], zs_b[oi]], w=[og_b[oi]])
                        bk2, bb2 = psum()
                        bk2b = bk2[:, :].bitcast(BF16)
                        for vt in range(2):
                            S.op("pe", lambda e, vt=vt: e.transpose(bk2b[:, vt * 128:(vt + 1) * 128],
                                                                    og[oi][:, vt * 128:(vt + 1) * 128], ident_b[:]),
                                 r=[og_b[oi], cst_b], w=[bb2])
                        lo = (c % 2) * 128
                        S.op("act", lambda e: e.copy(out=oT[s][:, :, lo:lo + 128],
                                                     in_=bk2b[:, 0:256].rearrange("p (v t) -> p v t", v=2)), r=[bb2], w=[oT_b[s]])
                        seg_done[s] += 1
                        if seg_done[s] == 2:
                            seg = slice(s * SEGL, (s + 1) * SEGL)
                            for dt in range(8):
                                bk, bb = psum()
                                for vt in range(2):
                                    S.op("pe", lambda e, vt=vt, dt=dt, bk=bk: e.matmul(
                                        bk[:, 0:SEGL], lhsT=sOv[:, vt, dt * 128:(dt + 1) * 128], rhs=oT[s][:, vt, :],
                                        start=(vt == 0), stop=(vt == 1)), r=[sOb, oT_b[s]], w=[bb])
                                S.op("dve", lambda e, bk=bk, dt=dt: e.scalar_tensor_tensor(
                                    out=xT[:, dt, seg], in0=bk[:, 0:SEGL], scalar=mod_ap(l, 2, dt, s), in1=xT[:, dt, seg],
                                    op0=ALU.mult, op1=ALU.add), r=[bb, modT_b, xT_b[s]], w=[xT_b[s]])

                    for step in range(NT):
                        cs_ = [step, NT - 1 - step]
                        st1 = []
                        for d in range(2):
                            c = cs_[d]
                            tk = slice(c * 128, (c + 1) * 128)
                            bk, bb = psum()
                            S.op("pe", lambda e, bk=bk, d=d: e.matmul(bk[:, 0:256], lhsT=kT[:, tk], rhs=S16[d][:, :], start=True,
                                                                       stop=True), r=[k_b, S16_b[d]], w=[bb])
                            st1.append((bk, bb))
                        for d in range(2):
                            c = cs_[d]
                            bk, bb = st1[d]
                            S.op("dve", lambda e, bk=bk, d=d, c=c: e.scalar_tensor_tensor(
                                out=rr[d][:, :], in0=bk[:, 0:256], scalar=NBEG[:, c, d * 8 + h:d * 8 + h + 1],
                                in1=bv[:, c * 2 + d, :], op0=ALU.mult, op1=ALU.add), r=[bb, tk_b, bv_b[c]], w=[rr_b[d]])
                        st3 = []
                        for d in range(2):
                            c = cs_[d]
                            bk, bb = psum()
                            S.op("pe", lambda e, bk=bk, d=d, c=c: e.matmul(bk[:, 0:256], lhsT=TT[:, c * 2 + d, :], rhs=rr[d][:, :],
                                                                            start=True, stop=True), r=[TT_b[c // 2], rr_b[d]], w=[bb])
                            st3.append((bk, bb))
                        for d in range(2):
                            bk, bb = st3[d]
                            S.op("act", lambda e, bk=bk, d=d: e.copy(out=vn16[d][:, :], in_=bk[:, 0:256]), r=[bb], w=[vn_b[d]])
                        st5 = []
                        for d in range(2):
                            c = cs_[d]
                            bo, bbo = psum()
                            S.op("pe", lambda e, bo=bo, d=d, c=c: e.matmul(bo[:, 0:256], lhsT=qg[:, c * 2 + d, :], rhs=S16[d][:, :],
                                                                            start=True, stop=False), r=[pq_b[c], S16_b[d]], w=[bbo])
                            S.op("pe", lambda e, bo=bo, d=d, c=c: e.matmul(bo[:, 0:256], lhsT=PTm[:, c * 2 + d, :], rhs=vn16[d][:, :],
                                                                            start=False, stop=True), r=[pq_b[c], vn_b[d]], w=[bbo])
                            S.op("pe", lambda e, bo=bo, d=d, c=c: e.matmul(bo[:, 256:512], lhsT=kd[:, c * 2 + d, :], rhs=vn16[d][:, :],
                                                                            start=True, stop=True), r=[kd_b[c], vn_b[d]], w=[bbo])
                            st5.append((bo, bbo))
                        for d in range(2):
                            c = cs_[d]
                            s = c // 2
                            bo, bbo = st5[d]
                            cd = c * 2 + d
                            S.op("dve", lambda e, bo=bo, d=d, cd=cd: e.scalar_tensor_tensor(
                                out=S32[d][:, :], in0=S32[d][:, :], scalar=csc[:, 0, cd:cd + 1], in1=bo[:, 256:512],
                                op0=ALU.mult, op1=ALU.add), r=[bbo, csc_b[c], S32_b[d]], w=[S32_b[d]])
                            seg_end = (c % 2 == 1) if d == 0 else (c % 2 == 0)
                            if seg_end:
                                st_t, st_bf = stage()
                                S.op("act", lambda e, st_t=st_t, d=d: e.copy(out=st_t[:, 0:256], in_=S32[d][:, :]),
                                     r=[S32_b[d]], w=[st_bf])
                                S.dma("sp", nsd_d[s, j, d, h], st_t[:, 0:256], r=[st_bf])
                                if d == 0 and c < NT - 1:
                                    S.op("dve", lambda e, s=s: e.tensor_scalar(out=S32[0][:, :], in0=S32[0][:, :],
                                                                               scalar1=chain_ap(s + 1), scalar2=None, op0=ALU.mult),
                                         r=[S32_b[0], prm_b], w=[S32_b[0]])
                                if d == 1 and c > 0:
                                    if s - 1 == 3:
                                        st2, st2b = stage()
                                        S.dma("sp", st2[:, 0:256], s0d_d[j, 1, h], w=[st2b])
                                        S.op("dve", lambda e, s=s, st2=st2: e.scalar_tensor_tensor(
                                            out=S32[1][:, :], in0=S32[1][:, :], scalar=chain_ap(s), in1=st2[:, 0:256],
                                            op0=ALU.mult, op1=ALU.add), r=[S32_b[1], prm_b, st2b], w=[S32_b[1]])
                                    else:
                                        S.op("dve", lambda e, s=s: e.tensor_scalar(out=S32[1][:, :], in0=S32[1][:, :],
                                                                                   scalar1=chain_ap(s), scalar2=None, op0=ALU.mult),
                                             r=[S32_b[1], prm_b], w=[S32_b[1]])
                            if step < NT - 1:
                                S.op("act", lambda e, d=d: e.copy(out=S16[d][:, :], in_=S32[d][:, :]), r=[S32_b[d]], w=[S16_b[d]])
                        for d in range(2):
                            c = cs_[d]
                            bo, bbo = st5[d]
                            arrived[c] += 1
                            consume_o(c, bo, bbo, arrived[c] == 1)
                        for d in range(2):
                            c = cs_[d]
                            finish_chunk(c, arrived[c] == 1)
                    ws_release()
                    ws_release()
                S.barrier()

        def final_out():
            for s in range(NSEG):
                sq, sqb, rs, rsb = rms_stat(s)
                seg = slice(s * SEGL, (s + 1) * SEGL)
                S.op("dve", lambda e: e.tensor_tensor(out=sq[:, :, :], in0=xT[:, :, seg],
                                                      in1=rs[:, :].unsqueeze(1).to_broadcast([128, 8, 256]),
                                                      op=ALU.mult), r=[xT_b[s], rsb], w=[sqb])
                for dt in range(8):
                    S.op("dve", lambda e, dt=dt: e.tensor_scalar(
                        out=sq[:, dt, :], in0=sq[:, dt, :], scalar1=prm[:, o_normw + 32 + dt:o_normw + 33 + dt],
                        scalar2=32.0, op0=ALU.mult, op1=ALU.mult), r=[sqb, prm_b], w=[sqb])
                for half in range(2):
                    tt = s * 2 + half
                    st_t, st_bf = stage()
                    for g in range(2):
                        bk, bb = psum()
                        for q in range(4):
                            dt = g * 4 + q
                            S.op("pe", lambda e, bk=bk, q=q, dt=dt: e.transpose(
                                bk[:, q * 128:(q + 1) * 128], sq[:, dt, half * 128:(half + 1) * 128], ident_f[:]),
                                r=[sqb, cst_b], w=[bb])
                        if g == 0:
                            S.op("dve", lambda e, bk=bk: e.tensor_copy(out=st_t[:, 0:512], in_=bk[:, :]), r=[bb], w=[st_bf])
                        else:
                            S.op("act", lambda e, bk=bk: e.copy(out=st_t[:, 512:1024], in_=bk[:, :]), r=[bb], w=[st_bf])
                    S.dma("sp", y_d[tt * 128:(tt + 1) * 128, :], st_t[:, :], r=[st_bf])

        for l in range(depth):
            if l % 2 == 0:
                ret_layer(l, l // 2)
            else:
                del_layer(l, l // 2)
        final_out()
        S.barrier()
        print(f"[build] instr={S.ninstr} sems={S.nsem} counts={S.cnt}")
    return nc


def _rope_tables():
    pos = np.arange(1024)
    r = (pos // 64).astype(np.float32)
    col = (pos % 64).astype(np.float32)
    freqs = (10000.0 ** (-np.arange(64, dtype=np.float32) / 64.0)).astype(np.float32)
    ang = np.concatenate([r[:, None] * freqs[None, :], col[:, None] * freqs[None, :]], -1)
    return np.cos(ang).T.astype(np.float32), np.sin(ang).T.astype(np.float32)


def _fm(v, nt):
    return np.ascontiguousarray(np.asarray(v, np.float32).reshape(nt, 128).T)


def make_in_maps(inp):
    f = lambda k: np.ascontiguousarray(np.asarray(inp[k], dtype=np.float32))
    xp, xs = f("x_prompt"), f("x_sample")
    c, c_ctx = f("c"), f("c_ctx")
    sr, sd = f("state_ret"), f("state_delta")
    norm_w, fnw, mod_b = f("norm_w"), f("final_norm_w"), f("mod_b")
    normw = np.concatenate([_fm(norm_w[l], 8) for l in range(4)] + [_fm(fnw, 8)], axis=1)
    modb = np.concatenate([_fm(mod_b[l][p * 1024:(p + 1) * 1024], 8) for l in range(4) for p in range(3)], axis=1)
    rdecb = np.ascontiguousarray(np.broadcast_to(f("ret_decay").reshape(1, 16), (128, 16)))
    gnwb = np.ascontiguousarray(np.broadcast_to(f("ret_gn_w").reshape(1, 4096), (128, 4096)))
    dnwb = np.ascontiguousarray(np.broadcast_to(f("del_norm_w").reshape(1, 4096), (128, 4096)))
    cw = f("del_conv_w")
    convw = np.concatenate([_fm(cw[jj, k], 32) for jj in range(2) for k in range(3)], axis=1)
    alogb = np.ascontiguousarray(np.broadcast_to(f("del_a_log").reshape(1, 32), (128, 32)))
    dtbb = np.ascontiguousarray(np.broadcast_to(f("del_dt_bias").reshape(1, 32), (128, 32)))
    cosS, sinS = _rope_tables()
    ii = np.arange(128)
    same = lambda b: (ii[:, None] // b) == (ii[None, :] // b)
    bm = [same(16), same(32) & ~same(16), same(64) & ~same(32), ~same(64)]
    bmask = np.concatenate([m.astype(np.float32) for m in bm], axis=1)
    shared = dict(bmask=bmask, normw=normw, modb=modb, mod_w=f("mod_w"), ret_w_in=f("ret_w_in"), ret_w_out=f("ret_w_out"),
                  del_w_in=f("del_w_in"), del_w_out=f("del_w_out"), rdecb=rdecb, gnwb=gnwb, dnwb=dnwb, convw=convw,
                  alogb=alogb, dtbb=dtbb)
    maps = []
    for core in range(8):
        m = dict(shared)
        chain = np.zeros((128, 8), np.float32)
        cosT = np.ones((128, NTOK), np.float32)
        sinT = np.zeros((128, NTOK), np.float32)
        if core < 6:
            x = xp[core * 5:(core + 1) * 5].reshape(NTOK, D)
            conds = [c_ctx] * 5
            s0r = np.zeros((2, 2, 4, 256, 512), np.float32)
            s0d = np.zeros((2, 2, 8, 128, 256), np.float32)
        else:
            b = core - 6
            x = np.concatenate([xs[b], xp[30 + b]], axis=0)
            conds = [c[b]] * 4 + [c_ctx]
            chain[:, 1:4] = 1.0
            s0r, s0d = sr[b], sd[b]
            cosT[:, 0:1024] = cosS
            sinT[:, 0:1024] = sinS
        condT = np.zeros((128, 40), np.float32)
        for s in range(5):
            condT[:, s::5] = _fm(conds[s], 8)
        m.update(x=np.ascontiguousarray(x), condT=condT, chainb=chain, s0r=np.ascontiguousarray(s0r),
                 s0d=np.ascontiguousarray(s0d), cosT=cosT, sinT=sinT)
        maps.append(m)
    return maps


def assemble(results):
    y_prompt = np.zeros((32, 256, D), np.float32)
    y_sample = np.zeros((2, 1024, D), np.float32)
    nsr = np.zeros((32, 2, 2, 4, 256, 512), np.float32)
    nsd = np.zeros((32, 2, 2, 8, 128, 256), np.float32)
    for core in range(8):
        r = results[core]
        y = r["y"].reshape(5, 256, D)
        if core < 6:
            y_prompt[core * 5:(core + 1) * 5] = y
            nsr[core * 5:(core + 1) * 5] = r["nsr"]
            nsd[core * 5:(core + 1) * 5] = r["nsd"]
        else:
            b = core - 6
            y_sample[b] = y[0:4].reshape(1024, D)
            y_prompt[30 + b] = y[4]
            nsr[30 + b] = r["nsr"][4]
            nsd[30 + b] = r["nsd"][4]
    return y_prompt, y_sample, nsr, nsd


_NC_CACHE = {}


def kernel(**inputs):
    if "nc" not in _NC_CACHE:
        _NC_CACHE["nc"] = build()
    maps = make_in_maps(inputs)
    res = run_bass_kernel_spmd(_NC_CACHE["nc"], maps, core_ids=list(range(8)))
    return assemble(res.results)
```
